# Optimizing a Trainium2 kernel written in Bass

```python
import math
import jax
import jax.numpy as jnp
from jax import lax
import numpy as np

D_MODEL = 1024
BATCH = 16
SEQ = 2048
DEPTH = 2

GRID_W = 64
CTX_LEN = 256
Q_BLOCK = 128
ROPE_THETA = 10000.0
EPS = 1e-6
N_MOD = 9

HEAD_DIM = 64
GROUP_WIDTH = D_MODEL // 4
N_HEADS_A = GROUP_WIDTH // HEAD_DIM
N_KV_A = N_HEADS_A // 2
GQA_GROUP = N_HEADS_A // N_KV_A
N_HEADS_C = GROUP_WIDTH // HEAD_DIM
DIFF_DIM = HEAD_DIM // 2
DIFF_VDIM = HEAD_DIM
GMLP_GROUPS = GROUP_WIDTH // HEAD_DIM
GMLP_GROUP_DIM = HEAD_DIM
GMLP_WIDTH = GROUP_WIDTH
CHUNK = 128
CONV_CH = GROUP_WIDTH
CONV_K = 31
D_FF = ((8 * D_MODEL // 3 + 127) // 128) * 128

IN_SPLITS = (
    N_HEADS_A * HEAD_DIM,
    N_HEADS_C * 2 * DIFF_DIM,
    N_KV_A * HEAD_DIM,
    N_KV_A * HEAD_DIM,
    N_HEADS_C * 2 * DIFF_DIM,
    N_HEADS_C * DIFF_VDIM,
    2 * GMLP_WIDTH,
    2 * CONV_CH,
)
IN_COLS = sum(IN_SPLITS)
KV_LO = IN_SPLITS[0] + IN_SPLITS[1]
KV_HI = KV_LO + sum(IN_SPLITS[2:6])

kernel_name = "hybrid_parallel_group_dit_block"


def rms_norm(x, g):
    x32 = x.astype(jnp.float32)
    y = x32 * lax.rsqrt(jnp.mean(x32 * x32, axis=-1, keepdims=True) + EPS)
    return y.astype(x.dtype) * g


def split_cols(x, sizes):
    return jnp.split(x, np.cumsum(sizes)[:-1].tolist(), axis=-1)


def adaln(cvec, w, b):
    m = jax.nn.silu(cvec) @ w + b
    return m.reshape(m.shape[:-1] + (N_MOD, 1, D_MODEL))


def modulate(h, g, mod, i):
    return rms_norm(h, g) * (1 + mod[..., i + 1, :, :]) + mod[..., i, :, :]


def swiglu(x, w_in, w_out):
    a, b = jnp.split(x @ w_in, 2, axis=-1)
    return (jax.nn.silu(a) * b) @ w_out


def axial_rope_tables(row, col, head_dim):
    quarter = head_dim // 4
    inv = ROPE_THETA ** (-jnp.arange(quarter, dtype=jnp.float32) / quarter)
    ang = jnp.concatenate([row.astype(jnp.float32)[:, None] * inv,
                           col.astype(jnp.float32)[:, None] * inv], axis=-1)
    return jnp.cos(ang), jnp.sin(ang)


def apply_rope(x, cos, sin):
    x1, x2 = jnp.split(x, 2, axis=-1)
    cos = cos.astype(x.dtype)
    sin = sin.astype(x.dtype)
    return jnp.concatenate([x1 * cos - x2 * sin, x1 * sin + x2 * cos], axis=-1)


def sweep_query_blocks(fn, q):
    n, d = q.shape[-2], q.shape[-1]
    qb = jnp.moveaxis(q.reshape(q.shape[:-2] + (n // Q_BLOCK, Q_BLOCK, d)), -3, 0)
    out = jnp.moveaxis(lax.map(fn, qb), 0, -3)
    return out.reshape(out.shape[:-3] + (n, out.shape[-1]))


def to_heads(x, n_heads, d):
    b, n, _ = x.shape
    return x.reshape(b, n, n_heads, d).transpose(0, 2, 1, 3)


def gqa_q_heads(q, g):
    b, n, _ = q.shape
    q = rms_norm(q.reshape(b, n, N_KV_A, GQA_GROUP, HEAD_DIM), g)
    return q.transpose(0, 2, 3, 1, 4)


def gqa_attend(q, k, v):
    s = jnp.einsum('bkgqd,bksd->bkgqs', q, k) * HEAD_DIM ** -0.5
    p = jax.nn.softmax(s.astype(jnp.float32), axis=-1).astype(v.dtype)
    return jnp.einsum('bkgqs,bksd->bkgqd', p, v)


def merge_gqa(o):
    b, kv, g, n, d = o.shape
    return o.transpose(0, 3, 1, 2, 4).reshape(b, n, kv * g * d)


def diff_qk_heads(x):
    b, n, _ = x.shape
    return x.reshape(b, n, N_HEADS_C, 2, DIFF_DIM).transpose(0, 2, 3, 1, 4)


def diff_lambda(lp, lam_init):
    lp = lp.astype(jnp.float32)
    return jnp.exp(jnp.sum(lp[0] * lp[1])) - jnp.exp(jnp.sum(lp[2] * lp[3])) + lam_init


def diff_attend(q, k, v, lam):
    s = jnp.einsum('bhcqd,bhcsd->bhcqs', q, k) * DIFF_DIM ** -0.5
    p = jax.nn.softmax(s.astype(jnp.float32), axis=-1)
    a = (p[:, :, 0] - lam * p[:, :, 1]).astype(v.dtype)
    return jnp.einsum('bhqs,bhsd->bhqd', a, v)


def diff_finish(o, g_sub, lam_init):
    o = rms_norm(o, g_sub) * (1.0 - lam_init)
    b, h, n, d = o.shape
    return o.transpose(0, 2, 1, 3).reshape(b, n, h * d)


def chunk_gmlp(uv, g_v, w_s, b_s):
    u, v = jnp.split(jax.nn.gelu(uv), 2, axis=-1)
    v = rms_norm(v, g_v)
    b, n, _ = v.shape
    vb = v.reshape(b, n // CHUNK, CHUNK, GMLP_GROUPS, GMLP_GROUP_DIM)
    mixed = jnp.einsum('gpq,bcqgd->bcpgd', w_s, vb) + jnp.transpose(b_s)[:, :, None]
    return u * mixed.reshape(b, n, GMLP_WIDTH)


def conformer_conv(glu_in, w_dw, b_dw, g_n):
    a, gate = jnp.split(glu_in, 2, axis=-1)
    y = a * jax.nn.sigmoid(gate)
    y = lax.conv_general_dilated(
        y, w_dw[:, None, :], window_strides=(1,), padding=[(CONV_K // 2, CONV_K // 2)],
        dimension_numbers=('NWC', 'WIO', 'NWC'), feature_group_count=CONV_CH) + b_dw
    return jax.nn.silu(rms_norm(y, g_n))


def setup_inputs(seed: int = 0) -> dict:
    key = jax.random.key(seed)
    ks = jax.random.split(key, 24)
    f32 = jnp.float32

    def nrm(k, shape, scale):
        return jax.random.normal(k, shape, f32) * scale

    return {
        "x": nrm(ks[0], (BATCH, SEQ, D_MODEL), 1.0),
        "c": nrm(ks[1], (BATCH, D_MODEL), 1.0),
        "ctx": nrm(ks[2], (BATCH, CTX_LEN, D_MODEL), 1.0),
        "c_ctx": nrm(ks[3], (D_MODEL,), 1.0),
        "w_ada": nrm(ks[4], (DEPTH, D_MODEL, N_MOD * D_MODEL), 0.5 * D_MODEL ** -0.5),
        "b_ada": nrm(ks[5], (DEPTH, N_MOD * D_MODEL), 0.02),
        "g_norm": 1.0 + nrm(ks[6], (DEPTH, 3, D_MODEL), 0.02),
        "w_ff1_in": nrm(ks[7], (DEPTH, D_MODEL, 2 * D_FF), D_MODEL ** -0.5),
        "w_ff1_out": nrm(ks[8], (DEPTH, D_FF, D_MODEL), D_FF ** -0.5),
        "w_ff2_in": nrm(ks[9], (DEPTH, D_MODEL, 2 * D_FF), D_MODEL ** -0.5),
        "w_ff2_out": nrm(ks[10], (DEPTH, D_FF, D_MODEL), D_FF ** -0.5),
        "w_in": nrm(ks[11], (DEPTH, D_MODEL, IN_COLS), D_MODEL ** -0.5),
        "w_out": nrm(ks[12], (DEPTH, D_MODEL, D_MODEL), D_MODEL ** -0.5),
        "g_q_a": 1.0 + nrm(ks[13], (DEPTH, HEAD_DIM), 0.02),
        "g_k_a": 1.0 + nrm(ks[14], (DEPTH, HEAD_DIM), 0.02),
        "lam_c": nrm(ks[15], (DEPTH, 4, DIFF_DIM), 0.1),
        "g_sub_c": 1.0 + nrm(ks[16], (DEPTH, DIFF_VDIM), 0.02),
        "g_v_b": 1.0 + nrm(ks[17], (DEPTH, GMLP_WIDTH), 0.02),
        "w_s_b": nrm(ks[18], (DEPTH, GMLP_GROUPS, CHUNK, CHUNK), CHUNK ** -0.5),
        "b_s_b": 1.0 + nrm(ks[19], (DEPTH, GMLP_GROUPS, CHUNK), 0.02),
        "w_dw_d": nrm(ks[20], (DEPTH, CONV_K, CONV_CH), CONV_K ** -0.5),
        "b_dw_d": nrm(ks[21], (DEPTH, CONV_CH), 0.02),
        "g_conv_d": 1.0 + nrm(ks[22], (DEPTH, CONV_CH), 0.02),
        "g_final": 1.0 + nrm(ks[23], (D_MODEL,), 0.02),
    }


def reference(x, c, ctx, c_ctx, w_ada, b_ada, g_norm, w_ff1_in, w_ff1_out, w_ff2_in,
              w_ff2_out, w_in, w_out, g_q_a, g_k_a, lam_c, g_sub_c, g_v_b, w_s_b, b_s_b,
              w_dw_d, b_dw_d, g_conv_d, g_final):
    n = x.shape[1]
    rows = n // GRID_W
    row = jnp.repeat(jnp.arange(rows, dtype=jnp.int32), GRID_W)
    col = jnp.tile(jnp.arange(GRID_W, dtype=jnp.int32), rows)
    cos_a, sin_a = axial_rope_tables(row, col, HEAD_DIM)
    cos_c, sin_c = axial_rope_tables(row, col, DIFF_DIM)

    h, hc = x, ctx
    for l in range(DEPTH):
        last = l == DEPTH - 1
        m = adaln(c, w_ada[l], b_ada[l])
        mc = adaln(c_ctx, w_ada[l], b_ada[l])

        h = h + 0.5 * m[..., 2, :, :] * swiglu(modulate(h, g_norm[l, 0], m, 0), w_ff1_in[l], w_ff1_out[l])
        hc = hc + 0.5 * mc[..., 2, :, :] * swiglu(modulate(hc, g_norm[l, 0], mc, 0), w_ff1_in[l], w_ff1_out[l])

        hn = modulate(h, g_norm[l, 1], m, 3)
        hcn = modulate(hc, g_norm[l, 1], mc, 3)
        qa, qc, ka, va, kc, vc, uv, glu = split_cols(hn @ w_in[l], IN_SPLITS)
        if last:
            ka_x, va_x, kc_x, vc_x = split_cols(hcn @ w_in[l, :, KV_LO:KV_HI], IN_SPLITS[2:6])
        else:
            qa_x, qc_x, ka_x, va_x, kc_x, vc_x, uv_x, glu_x = split_cols(hcn @ w_in[l], IN_SPLITS)
        ka_x = rms_norm(to_heads(ka_x, N_KV_A, HEAD_DIM), g_k_a[l])
        va_x = to_heads(va_x, N_KV_A, HEAD_DIM)
        kc_x = diff_qk_heads(kc_x)
        vc_x = to_heads(vc_x, N_HEADS_C, DIFF_VDIM)
        lam_init = 0.8 - 0.6 * math.exp(-0.3 * l)
        lam = diff_lambda(lam_c[l], lam_init)

        qa_h = apply_rope(gqa_q_heads(qa, g_q_a[l]), cos_a, sin_a)
        ka_all = jnp.concatenate(
            [apply_rope(rms_norm(to_heads(ka, N_KV_A, HEAD_DIM), g_k_a[l]), cos_a, sin_a), ka_x], axis=2)
        va_all = jnp.concatenate([to_heads(va, N_KV_A, HEAD_DIM), va_x], axis=2)
        oa = merge_gqa(sweep_query_blocks(lambda qb: gqa_attend(qb, ka_all, va_all), qa_h))

        qc_h = apply_rope(diff_qk_heads(qc), cos_c, sin_c)
        kc_all = jnp.concatenate([apply_rope(diff_qk_heads(kc), cos_c, sin_c), kc_x], axis=3)
        vc_all = jnp.concatenate([to_heads(vc, N_HEADS_C, DIFF_VDIM), vc_x], axis=2)
        oc = diff_finish(sweep_query_blocks(lambda qb: diff_attend(qb, kc_all, vc_all, lam), qc_h),
                         g_sub_c[l], lam_init)

        ob = chunk_gmlp(uv, g_v_b[l], w_s_b[l], b_s_b[l])
        od = conformer_conv(glu, w_dw_d[l], b_dw_d[l], g_conv_d[l])

        mix = jnp.concatenate([oa, oc, ob, od], axis=-1) @ w_out[l]
        h = h + m[..., 5, :, :] * mix

        if not last:
            oa_x = merge_gqa(gqa_attend(gqa_q_heads(qa_x, g_q_a[l]), ka_x, va_x))
            oc_x = diff_finish(diff_attend(diff_qk_heads(qc_x), kc_x, vc_x, lam), g_sub_c[l], lam_init)
            ob_x = chunk_gmlp(uv_x, g_v_b[l], w_s_b[l], b_s_b[l])
            od_x = conformer_conv(glu_x, w_dw_d[l], b_dw_d[l], g_conv_d[l])
            mix_x = jnp.concatenate([oa_x, oc_x, ob_x, od_x], axis=-1) @ w_out[l]
            hc = hc + mc[..., 5, :, :] * mix_x

        h = h + 0.5 * m[..., 8, :, :] * swiglu(modulate(h, g_norm[l, 2], m, 6), w_ff2_in[l], w_ff2_out[l])
        if not last:
            hc = hc + 0.5 * mc[..., 8, :, :] * swiglu(modulate(hc, g_norm[l, 2], mc, 6), w_ff2_in[l], w_ff2_out[l])

    return rms_norm(h, g_final)
```

```python
import math
import contextlib
import numpy as np
import ml_dtypes
import concourse.bass as bass
import concourse.mybir as mybir
from concourse.bass_utils import run_bass_kernel_spmd

F32 = mybir.dt.float32
BF16 = mybir.dt.bfloat16
AF = mybir.ActivationFunctionType
ALU = mybir.AluOpType

ENGS = ["pe", "act", "dve", "pool", "sp"]

D = 1024
NLAT = 2048
NCTX = 256
NTOK = NLAT + NCTX
DFF = 2816
DEPTH = 2
EPS = 1e-6
NCORES = 8
TBS = [(0, 512), (512, 512), (1024, 512), (1536, 512), (2048, 256)]
DEBUG_PARTS = set("ACBD")
WARM_N = 0


class Op:
    __slots__ = ("eng", "fn", "deps", "sig", "sigval", "ch", "inc", "dma", "idx")


class Sched:
    def __init__(self, nc):
        self.nc = nc
        self.ops = {e: [] for e in ENGS}
        self.last_w = {}
        self.readers = {}
        self.ch_eng = {}
        self.pending_bar = {e: [] for e in ENGS}
        self.nops = 0
        self.last_on_ch = {}
        self.dma_ring = {}
        self.DMA_RING = {"sp": 16, "pool": 24}

    def add(self, eng, fn, reads=(), writes=(), ch=None, dma=False):
        op = Op()
        op.eng = eng
        op.fn = fn
        op.sig = bool(dma)
        op.sigval = None
        op.dma = dma
        op.ch = ch if ch is not None else eng
        if dma:
            k = self.dma_ring.get(eng, 0)
            self.dma_ring[eng] = k + 1
            op.ch = f"{eng}_d{k % self.DMA_RING[eng]}"
        op.inc = 16 if dma else 1
        op.idx = self.nops
        self.nops += 1
        if op.ch in self.ch_eng:
            assert self.ch_eng[op.ch] == eng, (op.ch, eng)
        else:
            self.ch_eng[op.ch] = eng
        deps = {}
        for k in reads:
            w = self.last_w.get(k)
            if w is not None:
                deps[w.idx] = (w, True)
        for k in writes:
            w = self.last_w.get(k)
            if w is not None and w.idx not in deps:
                deps[w.idx] = (w, False)
            for r in self.readers.get(k, ()):
                if r.idx not in deps:
                    deps[r.idx] = (r, False)
        need = []
        for (d, raw) in deps.values():
            if d.eng == eng and not d.dma and not dma and not raw:
                continue
            if d.eng == eng and eng == "pe" and not d.dma and not dma:
                continue
            need.append(d)
        if dma:
            prev = self.last_on_ch.get(op.ch)
            if prev is not None:
                need.append(prev)
        for d in self.pending_bar[eng]:
            need.append(d)
        self.pending_bar[eng] = []
        for d in need:
            d.sig = True
        op.deps = need
        for k in writes:
            self.last_w[k] = op
            self.readers[k] = []
        for k in reads:
            if k in writes:
                continue
            self.readers.setdefault(k, []).append(op)
        self.ops[eng].append(op)
        self.last_on_ch[op.ch] = op
        return op

    def barrier(self):
        frontier = list(self.last_on_ch.values())
        for e in ENGS:
            self.pending_bar[e] = list(frontier)

    def emit(self):
        nc = self.nc
        chans = list(self.ch_eng.keys())
        cum = {c: 0 for c in chans}
        for e in ENGS:
            for op in self.ops[e]:
                if op.sig:
                    cum[op.ch] += op.inc
                    op.sigval = cum[op.ch]
        with contextlib.ExitStack() as st:
            sems = {c: st.enter_context(nc.semaphore("s_" + c)) for c in chans}
            block = st.enter_context(nc.Block())

            def run(engname, engine):
                waited = {}
                for op in self.ops[engname]:
                    for d in op.deps:
                        if waited.get(d.ch, 0) < d.sigval:
                            engine.wait_ge(sems[d.ch], d.sigval)
                            waited[d.ch] = d.sigval
                    ins = op.fn(engine)
                    if op.sig:
                        assert ins is not None
                        ins.then_inc(sems[op.ch], op.inc)

            @block.tensor
            def _(eng):
                run("pe", eng)

            @block.scalar
            def _(eng):
                run("act", eng)

            @block.vector
            def _(eng):
                run("dve", eng)

            @block.gpsimd
            def _(eng):
                run("pool", eng)

            @block.sync
            def _(eng):
                run("sp", eng)


class Alloc:
    def __init__(self, nc, limit=229344, base=16512):
        self.nc = nc
        self.off = base
        self.limit = limit
        self.n = 0
        self.peak = base

    def mark(self):
        return self.off

    def release(self, m):
        self.off = m

    def t(self, name, shape, dtype):
        esz = 4 if dtype == F32 else 2
        nbytes = int(np.prod(shape[1:])) * esz
        nbytes = (nbytes + 63) // 64 * 64
        assert self.off + nbytes <= self.limit, (name, self.off, nbytes, self.limit)
        self.n += 1
        h = self.nc.alloc_sbuf_tensor_at(f"{name}_{self.n}", list(shape), dtype, offset=self.off)
        self.off += nbytes
        self.peak = max(self.peak, self.off)
        return h


class Ring:
    def __init__(self, items):
        self.items = list(items)
        self.i = 0

    def next(self):
        v = self.items[self.i % len(self.items)]
        self.i += 1
        return v


def build_program(nb=2, depth=DEPTH, stop=None):
    nc = bass.Bass("TRN2", target_bir_lowering=False)
    S = Sched(nc)
    A = Alloc(nc)

    def din(name, shape, dt=F32):
        return nc.dram_tensor(name, list(shape), dt, kind="ExternalInput").ap()

    xT = din("xT", [nb, D, NLAT])
    ctxT = din("ctxT", [nb, D, NCTX])
    cT_d = din("cT", [128, 8, 3])
    w_ada = din("w_ada", [DEPTH, D, 9 * D])
    b_adaT_d = din("b_adaT", [128, DEPTH, 72])
    g_normT_d = din("g_normT", [128, DEPTH, 3, 8])
    g_finalT_d = din("g_finalT", [128, 8])
    w_ff_in = [din("w_ff1_in", [DEPTH, D, 2 * DFF]), din("w_ff2_in", [DEPTH, D, 2 * DFF])]
    w_ff_out = [din("w_ff1_out", [DEPTH, DFF, D]), din("w_ff2_out", [DEPTH, DFF, D])]
    w_in_p = din("w_in_p", [DEPTH, D, 2304])
    w_out = din("w_out", [DEPTH, D, D])
    gqk_d = din("gqk", [128, DEPTH, 2])
    lamw_d = din("lamw", [64, DEPTH, 4, 32])
    gsub_d = din("gsub", [64, DEPTH])
    gv_d = din("gv_bc", [128, DEPTH, 256])
    wsT_d = din("wsT", [DEPTH, 128, 4, 128])
    bs_d = din("bs_tbl", [64, DEPTH, 4, 512])
    wdw_d = din("wdw", [128, DEPTH, 2, 31])
    bdw_d = din("bdw", [128, DEPTH, 2])
    gconv_d = din("gconv", [128, DEPTH, 2])
    rope_d = din("rope", [4, 128, NLAT])
    cmat_d = din("cmat", [6, 128, 128])
    xscr = nc.dram_tensor("xscr", [128, 8, NTOK], BF16, kind="ExternalOutput").ap()
    if stop is None:
        outT = nc.dram_tensor("outT", [nb, D, NLAT], F32, kind="ExternalOutput").ap()
    else:
        dbg = nc.dram_tensor("dbg", [nb, 128, 8, NTOK], F32, kind="ExternalOutput").ap()

    hT = A.t("hT", [128, 8, NTOK], F32)
    cmat = A.t("cmat", [128, 6, 128], BF16)
    identF = A.t("identF", [128, 128], F32)
    modall = A.t("modall", [128, DEPTH, 72, 3], F32)
    gs = A.t("gs", [128, DEPTH, 3, 8, 3], F32)
    hg = A.t("hg", [128, DEPTH, 3, 8, 3], F32)
    b_adaT = A.t("b_adaT", [128, DEPTH, 72], F32)
    g_normT = A.t("g_normT", [128, DEPTH, 3, 8], F32)
    g_finalT = A.t("g_finalT", [128, 8], F32)
    gqk = A.t("gqk", [128, DEPTH, 2], F32)
    gsub = A.t("gsub", [64, DEPTH], F32)
    gsub1m = A.t("gsub1m", [64, DEPTH], F32)
    neglam = A.t("neglam", [64, DEPTH], F32)
    wdw = A.t("wdw", [128, DEPTH, 2, 31], F32)
    bdw = A.t("bdw", [128, DEPTH, 2], F32)
    gconv = A.t("gconv", [128, DEPTH, 2], F32)
    epsc = A.t("epsc", [128, 1], F32)
    sq = A.t("sq", [128, 8, 512], BF16)
    rt = A.t("rt", [128, 512], F32)
    rstd = A.t("rstd", [128, 512], F32)
    tmpx = [A.t("tmpx0", [128, 512], F32), A.t("tmpx1", [128, 512], F32)]
    PERSIST = A.mark()

    ps = [nc.alloc_psum_tensor(f"ps{i}", [128, 512], F32) for i in range(8)]
    onesD = cmat[:, 0, :]
    blk64 = cmat[:, 1, :]
    ones256 = cmat[:, 2, :]
    permA = cmat[:, 3, :]
    permC = cmat[:, 4, :]

    cnt = [0]

    def uid():
        cnt[0] += 1
        return cnt[0]

    def PK(b):
        return ("ps", b)

    def dma_sp(out, in_, reads, writes):
        S.add("sp", lambda e: e.dma_start(out=out, in_=in_), reads=reads, writes=writes, ch="dq_sp", dma=True)

    def dma_w(out, in_, reads, writes):
        S.add("pool", lambda e: e.dma_start(out=out, in_=in_), reads=reads, writes=writes, ch="dq_w", dma=True)

    dma_w(cmat[:], cmat_d.rearrange("m p n -> p m n"), [], ["cmat"])
    dma_sp(identF[:], cmat_d[5], [], ["identF"])
    dma_sp(b_adaT[:], b_adaT_d, [], ["b_adaT"])
    dma_sp(g_normT[:], g_normT_d, [], ["g_normT"])
    dma_sp(g_finalT[:], g_finalT_d, [], ["g_finalT"])
    dma_sp(gqk[:], gqk_d, [], ["gqk"])
    dma_sp(gsub[:], gsub_d, [], ["gsub"])
    dma_sp(wdw[:], wdw_d, [], ["wdw"])
    dma_sp(bdw[:], bdw_d, [], ["bdw"])
    dma_sp(gconv[:], gconv_d, [], ["gconv"])
    S.add("dve", lambda e: e.memset(epsc[:], EPS), writes=["epsc"])

    m0 = A.mark()
    cTs = A.t("cTs", [128, 8, 3], F32)
    scT = A.t("scT", [128, 8, 3], BF16)
    lamw = A.t("lamw", [64, DEPTH, 4, 32], F32)
    lamp = A.t("lamp", [64, DEPTH, 2, 32], F32)
    lams = A.t("lams", [64, DEPTH, 2], F32)
    lame = A.t("lame", [64, DEPTH, 2], F32)
    wab = [A.t(f"wab{i}", [128, 8, 512], BF16) for i in range(3)]
    dma_sp(cTs[:], cT_d, [], ["cTs"])
    dma_sp(lamw[:], lamw_d, [], ["lamw"])
    S.add("act", lambda e: e.activation(out=scT[:], in_=cTs[:], func=AF.Silu), reads=["cTs"], writes=["scT"])
    for l in range(depth):
        for cb in range(18):
            bi = (l * 18 + cb) % 3
            wt = wab[bi]
            dma_w(wt[:], w_ada[l][:, cb * 512:(cb + 1) * 512].rearrange("(k p) n -> p k n", p=128), [], [("wab", bi)])
            if cb == 0:
                pass

            def mm(e, wt=wt, cb=cb):
                ins = None
                for cc in range(4):
                    chn = cb * 4 + cc
                    for k in range(8):
                        ins = e.matmul(ps[7][:, chn * 3:(chn + 1) * 3], lhsT=wt[:, k, cc * 128:(cc + 1) * 128],
                                       rhs=scT[:, k, :], start=(k == 0), stop=(k == 7))
                return ins
            S.add("pe", mm, reads=[("wab", bi), "scT"], writes=[PK(7)])
        psm = ps[7][:, 0:216].rearrange("p (c j) -> p c j", j=3)
        for j in range(3):
            S.add("dve", lambda e, l=l, j=j, psm=psm: e.tensor_tensor(out=modall[:, l, :, j], in0=psm[:, :, j], in1=b_adaT[:, l, :], op=ALU.add),
                  reads=[PK(7), "b_adaT"], writes=["modall"])
        for s in range(3):
            for j in range(3):
                S.add("dve", lambda e, l=l, s=s, j=j: e.tensor_scalar(out=gs[:, l, s, :, j], in0=modall[:, l, (3 * s + 1) * 8:(3 * s + 2) * 8, j],
                                                                        scalar1=1.0, scalar2=None, op0=ALU.add),
                      reads=["modall"], writes=["gs"])
                S.add("dve", lambda e, l=l, s=s, j=j: e.tensor_tensor(out=gs[:, l, s, :, j], in0=gs[:, l, s, :, j], in1=g_normT[:, l, s, :], op=ALU.mult),
                      reads=["gs", "g_normT"], writes=["gs"])
                S.add("dve", lambda e, l=l, s=s, j=j: e.tensor_scalar(out=hg[:, l, s, :, j], in0=modall[:, l, (3 * s + 2) * 8:(3 * s + 3) * 8, j],
                                                                        scalar1=(1.0 if s == 1 else 0.5), scalar2=None, op0=ALU.mult),
                      reads=["modall"], writes=["hg"])
        lam_init = 0.8 - 0.6 * math.exp(-0.3 * l)
        for q in range(2):
            S.add("dve", lambda e, l=l, q=q: e.tensor_tensor(out=lamp[:, l, q, :], in0=lamw[:, l, 2 * q, :], in1=lamw[:, l, 2 * q + 1, :], op=ALU.mult),
                  reads=["lamw"], writes=["lamp"])
            S.add("dve", lambda e, l=l, q=q: e.tensor_reduce(out=lams[:, l, q:q + 1], in_=lamp[:, l, q, :], axis=mybir.AxisListType.X, op=ALU.add),
                  reads=["lamp"], writes=["lams"])
        S.add("act", lambda e, l=l: e.activation(out=lame[:, l, :], in_=lams[:, l, :], func=AF.Exp), reads=["lams"], writes=["lame"])
        S.add("dve", lambda e, l=l: e.tensor_tensor(out=neglam[:, l:l + 1], in0=lame[:, l, 1:2], in1=lame[:, l, 0:1], op=ALU.subtract),
              reads=["lame"], writes=["neglam"])
        S.add("dve", lambda e, l=l, li=lam_init: e.tensor_scalar(out=neglam[:, l:l + 1], in0=neglam[:, l:l + 1], scalar1=-li, scalar2=None, op0=ALU.add),
              reads=["neglam"], writes=["neglam"])
        S.add("dve", lambda e, l=l, li=lam_init: e.tensor_scalar(out=gsub1m[:, l:l + 1], in0=gsub[:, l:l + 1], scalar1=(1.0 - li), scalar2=None, op0=ALU.mult),
              reads=["gsub"], writes=["gsub1m"])
    S.barrier()
    A.release(m0)

    def hkeys(tb, cs=range(8)):
        return [("h", c, tb) for c in cs]

    def jmod(bl, tb):
        return 2 if tb == 4 else bl

    def rsqrt_from_psum(pb, n, parts=128, scale=1.0):
        S.add("act", lambda e: e.activation(out=rt[0:parts, 0:n], in_=ps[pb][0:parts, 0:n], func=AF.Sqrt, bias=epsc[0:parts, 0:1], scale=scale),
              reads=[PK(pb), "epsc"], writes=["rt"])
        S.add("dve", lambda e: e.reciprocal(out=rstd[0:parts, 0:n], in_=rt[0:parts, 0:n]), reads=["rt"], writes=["rstd"])

    def make_xn(bl, l, s, tb, dst, dkeys, final=False, pbank=7):
        t0, n = TBS[tb]
        j = jmod(bl, tb)
        for c in range(8):
            if c % 2 == 0:
                S.add("act", lambda e, c=c: e.activation(out=sq[:, c, 0:n], in_=hT[:, c, t0:t0 + n], func=AF.Square),
                      reads=[("h", c, tb)], writes=[("sq", c)])
            else:
                S.add("pool", lambda e, c=c: e.tensor_tensor(out=sq[:, c, 0:n], in0=hT[:, c, t0:t0 + n], in1=hT[:, c, t0:t0 + n], op=ALU.mult),
                      reads=[("h", c, tb)], writes=[("sq", c)])

        def mm(e):
            ins = None
            for c in range(8):
                ins = e.matmul(ps[pbank][:, 0:n], lhsT=onesD, rhs=sq[:, c, 0:n], start=(c == 0), stop=(c == 7))
            return ins
        S.add("pe", mm, reads=[("sq", c) for c in range(8)] + ["cmat"], writes=[PK(pbank)])
        rsqrt_from_psum(pbank, n)
        for c in range(8):
            tx = tmpx[c % 2]
            S.add("pool", lambda e, c=c, tx=tx: e.tensor_tensor(out=tx[:, 0:n], in0=hT[:, c, t0:t0 + n], in1=rstd[:, 0:n], op=ALU.mult),
                  reads=[("h", c, tb), "rstd"], writes=[("tmpx", c % 2)])
            if final:
                S.add("act", lambda e, c=c, tx=tx: e.activation(out=dst(c), in_=tx[:, 0:n], func=AF.Identity, scale=g_finalT[:, c:c + 1]),
                      reads=[("tmpx", c % 2), "g_finalT"], writes=[dkeys(c)])
            else:
                S.add("act", lambda e, c=c, tx=tx: e.activation(out=dst(c), in_=tx[:, 0:n], func=AF.Identity,
                                                                 scale=gs[:, l, s, c, j:j + 1], bias=modall[:, l, 3 * s * 8 + c, j:j + 1]),
                      reads=[("tmpx", c % 2), "gs", "modall"], writes=[dkeys(c)])

    def resid_add(bl, l, s, tb, oc, pb, n):
        t0, _ = TBS[tb]
        j = jmod(bl, tb)
        S.add("dve", lambda e: e.scalar_tensor_tensor(out=hT[:, oc, t0:t0 + n], in0=ps[pb][:, 0:n], scalar=hg[:, l, s, oc, j:j + 1],
                                                       in1=hT[:, oc, t0:t0 + n], op0=ALU.mult, op1=ALU.add),
              reads=[PK(pb), "hg", ("h", oc, tb)], writes=[("h", oc, tb)])

    def dump_and_end(bl):
        dma_sp(dbg[bl], hT[:], [("h", c, tb) for c in range(8) for tb in range(5)], ["dbg"])

    def ffn(bl, l, which, blocks):
        s = 0 if which == 0 else 2
        m = A.mark()
        xn = A.t("xn", [128, 8, NTOK], BF16)
        wa = [A.t(f"wa{i}", [128, 8, 256], BF16) for i in range(2)]
        wb = [A.t(f"wb{i}", [128, 8, 256], BF16) for i in range(2)]
        wo = [A.t(f"wo{i}", [128, 2, 1024], BF16) for i in range(2)]
        mid = [A.t(f"mid{i}", [128, 2, NTOK], BF16) for i in range(2)]
        sa = [A.t(f"sa{i}", [128, 512], F32) for i in range(2)]
        u = uid()
        for tb in blocks:
            t0, n = TBS[tb]
            make_xn(bl, l, s, tb, lambda c, t0=t0, n=n: xn[:, c, t0:t0 + n], lambda c, tb=tb: ("xn", u, c, tb))
        win = w_ff_in[which][l]
        wout = w_ff_out[which][l]
        NG = DFF // 256
        ring1 = Ring([0, 1, 2, 3])
        ring2 = Ring([4, 5, 6])
        sar = Ring([0, 1])

        def load(g):
            bi = g % 2
            dma_w(wa[bi][:], win[:, g * 256:(g + 1) * 256].rearrange("(k p) n -> p k n", p=128), [], [("wa", bi)])
            dma_w(wb[bi][:], win[:, DFF + g * 256:DFF + (g + 1) * 256].rearrange("(k p) n -> p k n", p=128), [], [("wb", bi)])
            dma_w(wo[bi][:], wout[g * 256:(g + 1) * 256, :].rearrange("(j p) n -> p j n", p=128), [], [("wo", bi)])

        def phase1(g):
            bi = g % 2
            for tb in blocks:
                t0, n = TBS[tb]
                for jj in range(2):
                    pa = ring1.next()
                    pb = ring1.next()

                    def mm(e, w, pbk, jj=jj, t0=t0, n=n):
                        ins = None
                        for k in range(8):
                            ins = e.matmul(ps[pbk][:, 0:n], lhsT=w[:, k, jj * 128:(jj + 1) * 128], rhs=xn[:, k, t0:t0 + n],
                                           start=(k == 0), stop=(k == 7))
                        return ins
                    xk = [("xn", u, c, tb) for c in range(8)]
                    S.add("pe", lambda e, w=wa[bi], pbk=pa, mm=mm: mm(e, w, pbk), reads=xk + [("wa", bi)], writes=[PK(pa)])
                    S.add("pe", lambda e, w=wb[bi], pbk=pb, mm=mm: mm(e, w, pbk), reads=xk + [("wb", bi)], writes=[PK(pb)])
                    si = sar.next()
                    S.add("act", lambda e, pa=pa, si=si, n=n: e.activation(out=sa[si][:, 0:n], in_=ps[pa][:, 0:n], func=AF.Silu),
                          reads=[PK(pa)], writes=[("sa", si)])
                    S.add("dve", lambda e, pb=pb, si=si, n=n, jj=jj, t0=t0: e.tensor_tensor(out=mid[bi][:, jj, t0:t0 + n], in0=sa[si][:, 0:n],
                                                                                         in1=ps[pb][:, 0:n], op=ALU.mult),
                          reads=[PK(pb), ("sa", si)], writes=[("mid", bi, jj, tb)])

        def phase2(g):
            bi = g % 2
            for tb in blocks:
                t0, n = TBS[tb]
                for oc in range(8):
                    po = ring2.next()

                    def mm(e, po=po, oc=oc, t0=t0, n=n):
                        ins = None
                        for jj in range(2):
                            ins = e.matmul(ps[po][:, 0:n], lhsT=wo[bi][:, jj, oc * 128:(oc + 1) * 128], rhs=mid[bi][:, jj, t0:t0 + n],
                                           start=(jj == 0), stop=(jj == 1))
                        return ins
                    S.add("pe", mm, reads=[("mid", bi, 0, tb), ("mid", bi, 1, tb), ("wo", bi)], writes=[PK(po)])
                    resid_add(bl, l, s, tb, oc, po, n)

        load(0)
        load(1)
        phase1(0)
        for g in range(1, NG):
            phase1(g)
            phase2(g - 1)
            if g + 1 < NG:
                load(g + 1)
        phase2(NG - 1)
        S.barrier()
        A.release(m)

    def mixer(bl, l):
        last = (l == DEPTH - 1)
        blocks = [0, 1, 2, 3, 4]
        m = A.mark()
        qT = A.t("qT", [128, 4, NTOK], BF16)
        kT = A.t("kT", [128, 3, NTOK], BF16)
        vaug = A.t("vaug", [128, 18, 6, 128], BF16)
        m1 = A.mark()
        xnb = [A.t(f"xnb{i}", [128, 8, 512], BF16) for i in range(2)]
        wq = A.t("wq", [128, 8, 512], BF16)
        wk = A.t("wk", [128, 8, 384], BF16)
        wv = A.t("wv", [128, 8, 384], BF16)
        ropet = A.t("ropet", [128, 4, 512], F32)
        sqb = A.t("sqb", [128, 512], BF16)
        qn = [A.t(f"qn{i}", [128, 512], BF16) for i in range(2)]
        t1 = A.t("t1", [128, 512], F32)
        t2 = A.t("t2", [128, 512], F32)
        u = uid()
        S.add("pool", lambda e: e.memset(vaug[:], 1.0), writes=[("vaug", kc) for kc in range(18)])
        W = w_in_p[l]
        prj = Ring([0, 1, 2])
        aux = Ring([3, 4])
        vring = Ring([5, 6])
        qnr = Ring([0, 1])
        def pass1_pre(tb):
            t0, n = TBS[tb]
            xb = xnb[tb % 2]
            xkeys = [("xnb", tb % 2, c) for c in range(8)]
            make_xn(bl, l, 1, tb, lambda c, xb=xb, n=n: xb[:, c, 0:n], lambda c, tb=tb: ("xnb", tb % 2, c))
            dma_sp(xscr[:, :, t0:t0 + n], xb[:, :, 0:n], xkeys, [("xscr", tb)])

        def pass1(tb):
            t0, n = TBS[tb]
            xb = xnb[tb % 2]
            xkeys = [("xnb", tb % 2, c) for c in range(8)]
            if tb != 4:
                dma_sp(ropet[:], rope_d[:, :, t0:t0 + n].rearrange("m p n -> p m n"), [], ["ropet"])
            for ci in range(7):
                if ci < 4:
                    wt, wkey, co = wq, "wq", ci * 128
                    dest = qT[:, ci, t0:t0 + n]
                    dkey = ("qT", ci, tb)
                else:
                    wt, wkey, co = wk, "wk", (ci - 4) * 128
                    dest = kT[:, ci - 4, t0:t0 + n]
                    dkey = ("kT", ci - 4, tb)
                isA = ci in (0, 1, 4)
                pb = prj.next()

                def mm(e, wt=wt, co=co, pb=pb, xb=xb, n=n):
                    ins = None
                    for k in range(8):
                        ins = e.matmul(ps[pb][:, 0:n], lhsT=wt[:, k, co:co + 128], rhs=xb[:, k, 0:n], start=(k == 0), stop=(k == 7))
                    return ins
                S.add("pe", mm, reads=xkeys + [wkey], writes=[PK(pb)])
                norope = (tb == 4)
                qi = qnr.next()
                qdst = dest if norope else qn[qi][:, 0:n]
                qkey = dkey if norope else ("qn", qi)
                if isA:
                    S.add("act", lambda e, pb=pb, n=n: e.activation(out=sqb[:, 0:n], in_=ps[pb][:, 0:n], func=AF.Square),
                          reads=[PK(pb)], writes=["sqb"])
                    pa = aux.next()
                    S.add("pe", lambda e, pa=pa, n=n: e.matmul(ps[pa][:, 0:n], lhsT=blk64, rhs=sqb[:, 0:n], start=True, stop=True),
                          reads=["sqb", "cmat"], writes=[PK(pa)])
                    rsqrt_from_psum(pa, n)
                    gi = 0 if ci < 4 else 1
                    S.add("dve", lambda e, pb=pb, n=n, qdst=qdst, gi=gi: e.scalar_tensor_tensor(out=qdst, in0=ps[pb][:, 0:n], scalar=gqk[:, l, gi:gi + 1],
                                                                                         in1=rstd[:, 0:n], op0=ALU.mult, op1=ALU.mult),
                          reads=[PK(pb), "rstd", "gqk"], writes=[qkey])
                else:
                    S.add("act", lambda e, pb=pb, n=n, qdst=qdst: e.activation(out=qdst, in_=ps[pb][:, 0:n], func=AF.Copy),
                          reads=[PK(pb)], writes=[qkey])
                if not norope:
                    pr = aux.next()
                    pm = permA if isA else permC
                    ti = 0 if isA else 2
                    S.add("pe", lambda e, pr=pr, n=n, qi=qi, pm=pm: e.matmul(ps[pr][:, 0:n], lhsT=pm, rhs=qn[qi][:, 0:n], start=True, stop=True),
                          reads=[("qn", qi), "cmat"], writes=[PK(pr)])
                    S.add("dve", lambda e, n=n, qi=qi, ti=ti: e.tensor_tensor(out=t1[:, 0:n], in0=qn[qi][:, 0:n], in1=ropet[:, ti, 0:n], op=ALU.mult),
                          reads=[("qn", qi), "ropet"], writes=["t1"])
                    S.add("dve", lambda e, n=n, pr=pr, ti=ti: e.tensor_tensor(out=t2[:, 0:n], in0=ps[pr][:, 0:n], in1=ropet[:, ti + 1, 0:n], op=ALU.mult),
                          reads=[PK(pr), "ropet"], writes=["t2"])
                    S.add("pool", lambda e, n=n, dest=dest: e.tensor_tensor(out=dest, in0=t1[:, 0:n], in1=t2[:, 0:n], op=ALU.add),
                          reads=["t1", "t2"], writes=[dkey])
            for sb in range(n // 128):
                kc = (t0 // 128) + sb
                pv = vring.next()

                def mmv(e, pv=pv, sb=sb, xb=xb):
                    ins = None
                    for k in range(8):
                        ins = e.matmul(ps[pv][:, 0:384], lhsT=xb[:, k, sb * 128:(sb + 1) * 128], rhs=wv[:, k, :], start=(k == 0), stop=(k == 7))
                    return ins
                S.add("pe", mmv, reads=xkeys + ["wv"], writes=[PK(pv)])
                S.add("act", lambda e, pv=pv, kc=kc: e.activation(out=vaug[:, kc, :, 0:64], in_=ps[pv][:, 0:384].rearrange("p (h d) -> p h d", d=64), func=AF.Copy),
                      reads=[PK(pv)], writes=[("vaug", kc)])

        dma_w(wq[:], W[:, 0:512].rearrange("(k p) n -> p k n", p=128), [], ["wq"])
        dma_w(wk[:], W[:, 512:896].rearrange("(k p) n -> p k n", p=128), [], ["wk"])
        dma_w(wv[:], W[:, 896:1280].rearrange("(k p) n -> p k n", p=128), [], ["wv"])
        pass1_pre(blocks[0])
        for i_, tb in enumerate(blocks):
            if i_ + 1 < len(blocks):
                pass1_pre(blocks[i_ + 1])
            pass1(tb)
        S.barrier()
        A.release(m1)
        if stop == f"p1_{l}":
            pass

        woh = A.t("woh", [128, 4, 1024], BF16)
        cat = [A.t(f"cat{i}", [128, 4, 512], BF16) for i in range(2)]
        lnb = A.t("lnb", [64, 512], F32)
        pT = [A.t(f"pT{i}", [128, 512], BF16) for i in range(3)]
        rden = [A.t(f"rden{i}", [64, 512], F32) for i in range(2)]
        tt0 = A.t("tt0", [64, 512], F32)
        tt1 = A.t("tt1", [64, 512], F32)
        od_ = A.t("od_", [64, 512], F32)
        odn = A.t("odn", [64, 512], F32)
        sq64 = A.t("sq64", [64, 512], BF16)
        qpad = [A.t(f"qpad{i}", [128, 512], BF16) for i in range(3)]
        qpr = Ring([0, 1, 2])
        dma_w(woh[:], w_out[l][0:512, :].rearrange("(c p) n -> p c n", p=128), [], ["woh"])
        sring = Ring([0, 1, 6])
        oring = Ring([2, 3, 4, 5])
        ptr = Ring([0, 1, 2])
        rdr = Ring([0, 1])
        qblocks = [0, 1, 2, 3] if last else [0, 1, 2, 3, 4]
        steps = []

        def qblock(qi_, tb):
            t0, n = TBS[tb]
            kcs = list(range(18)) if tb != 4 else [16, 17]
            ct = cat[qi_ % 2]
            ci_ = qi_ % 2

            def warm():
                wbk = sring.next()

                def burst(e):
                    ins = None
                    for _ in range(WARM_N):
                        ins = e.matmul(ps[wbk][:, 0:512], lhsT=woh[:, 0, 0:128], rhs=woh[:, 1, 0:512], start=True, stop=True)
                    return ins
                S.add("pe", burst, reads=["woh"], writes=[PK(wbk)])
            if WARM_N > 0:
                steps.append((warm, lambda: None, lambda: None, None))

            hm_list = []

            def prep_qpad(hm):
                qi = qpr.next()
                hm["qp"] = qi
                r0, r1 = hm["krows"]
                S.add("pool", lambda e: e.memset(qpad[qi][:, 0:n], 0.0), writes=[("qpad", qi)])
                S.add("pool", lambda e: e.tensor_copy(out=qpad[qi][r0:r1, 0:n], in_=qT[r0:r1, hm["qchunk"], t0:t0 + n]),
                      reads=[("qT", hm["qchunk"], tb)], writes=[("qpad", qi)])

            def head_pass(krows, kchunk, qchunk, vslot, scale, po, tp, post):
                hm = dict(krows=krows, qchunk=qchunk)
                hm_list.append(hm)
                myidx = len(hm_list) - 1
                for ii, kc in enumerate(kcs):
                    st = {}
                    ktb = kc // 4 if kc < 16 else 4

                    def qk(st=st, kc=kc, ktb=ktb, ii=ii):
                        if ii == 0:
                            if "qp" not in hm:
                                prep_qpad(hm)
                            if myidx + 1 < len(hm_list) and "qp" not in hm_list[myidx + 1]:
                                prep_qpad(hm_list[myidx + 1])
                        sb_ = sring.next()
                        st["sb"] = sb_
                        qi = hm["qp"]

                        def mms(e):
                            return e.matmul(ps[sb_][:, 0:n], lhsT=kT[:, kchunk, kc * 128:(kc + 1) * 128],
                                            rhs=qpad[qi][:, 0:n], start=True, stop=True)
                        S.add("pe", mms, reads=[("kT", kchunk, ktb), ("qpad", qi)], writes=[PK(sb_)])

                    def ex(st=st):
                        sb_ = st["sb"]
                        pi = ptr.next()
                        st["pi"] = pi
                        S.add("act", lambda e: e.activation(out=pT[pi][:, 0:n], in_=ps[sb_][:, 0:n], func=AF.Exp, scale=scale),
                              reads=[PK(sb_)], writes=[("pT", pi)])

                    def pv(st=st, kc=kc, ii=ii):
                        pi = st["pi"]
                        S.add("pe", lambda e: e.matmul(ps[po][:, 0:n], lhsT=vaug[:, kc, vslot, :], rhs=pT[pi][:, 0:n],
                                                       start=(ii == 0), stop=(ii == len(kcs) - 1)),
                              reads=[("pT", pi), ("vaug", kc)], writes=[PK(po)])
                    steps.append((qk, ex, pv, post if ii == len(kcs) - 1 else None))

            for h in range(4):
                r0 = 64 * (h // 2)
                po = oring.next()

                def postA(po=po, h=h):
                    ri = rdr.next()
                    S.add("dve", lambda e: e.reciprocal(out=rden[ri][:, 0:n], in_=ps[po][64:128, 0:n]), reads=[PK(po)], writes=[("rden", ri)])
                    p0 = 64 * (h % 2)
                    S.add("dve", lambda e: e.tensor_tensor(out=ct[p0:p0 + 64, h // 2, 0:n], in0=ps[po][0:64, 0:n], in1=rden[ri][:, 0:n], op=ALU.mult),
                          reads=[PK(po), ("rden", ri)], writes=[("cat", ci_, h)])
                    return []
                head_pass((r0, r0 + 64), 0, h % 2, h // 2, 0.125, po, None, postA)
            for h in range(4):
                base = 64 * (h % 2)
                pos = [oring.next(), oring.next()]

                def postC(pos=pos, h=h):
                    ri0 = rdr.next()
                    ri1 = rdr.next()
                    p0 = 64 * (h % 2)
                    S.add("dve", lambda e: e.reciprocal(out=rden[ri0][:, 0:n], in_=ps[pos[0]][64:128, 0:n]), reads=[PK(pos[0])], writes=[("rden", ri0)])
                    S.add("dve", lambda e: e.tensor_tensor(out=tt0[:, 0:n], in0=ps[pos[0]][0:64, 0:n], in1=rden[ri0][:, 0:n], op=ALU.mult),
                          reads=[PK(pos[0]), ("rden", ri0)], writes=["tt0"])
                    S.add("dve", lambda e: e.reciprocal(out=rden[ri1][:, 0:n], in_=ps[pos[1]][64:128, 0:n]), reads=[PK(pos[1])], writes=[("rden", ri1)])
                    S.add("dve", lambda e: e.tensor_tensor(out=tt1[:, 0:n], in0=ps[pos[1]][0:64, 0:n], in1=rden[ri1][:, 0:n], op=ALU.mult),
                          reads=[PK(pos[1]), ("rden", ri1)], writes=["tt1"])
                    S.add("dve", lambda e: e.scalar_tensor_tensor(out=od_[:, 0:n], in0=tt1[:, 0:n], scalar=neglam[:, l:l + 1], in1=tt0[:, 0:n],
                                                                    op0=ALU.mult, op1=ALU.add),
                          reads=["tt0", "tt1", "neglam"], writes=["od_"])

                    def st1():
                        S.add("act", lambda e: e.activation(out=sq64[:, 0:n], in_=od_[:, 0:n], func=AF.Square), reads=["od_"], writes=["sq64"])

                    def st2():
                        S.add("pe", lambda e: e.matmul(ps[7][0:64, 0:n], lhsT=blk64[0:64, 0:64], rhs=sq64[:, 0:n], start=True, stop=True),
                              reads=["sq64", "cmat"], writes=[PK(7)])

                    def st3():
                        S.add("act", lambda e: e.activation(out=lnb[:, 0:n], in_=ps[7][0:64, 0:n], func=AF.Ln, bias=epsc[0:64, 0:1], scale=1.0),
                              reads=[PK(7), "epsc"], writes=["lnb"])
                        S.add("act", lambda e: e.activation(out=odn[:, 0:n], in_=lnb[:, 0:n], func=AF.Exp, scale=-0.5), reads=["lnb"], writes=["odn"])

                    def st4():
                        S.add("dve", lambda e: e.scalar_tensor_tensor(out=ct[p0:p0 + 64, 2 + h // 2, 0:n], in0=od_[:, 0:n], scalar=gsub1m[:, l:l + 1],
                                                                        in1=odn[:, 0:n], op0=ALU.mult, op1=ALU.mult),
                              reads=["od_", "odn", "gsub1m"], writes=[("cat", ci_, 4 + h)])
                    tasks = [(20, st1), (23, st2), (26, st3), (30, st4)]
                    if h == 3:
                        def outproj():
                            for oc in range(8):
                                pb = 7

                                def mmo(e, oc=oc, pb=pb):
                                    ins = None
                                    for hh in range(4):
                                        ins = e.matmul(ps[pb][:, 0:n], lhsT=woh[:, hh, oc * 128:(oc + 1) * 128], rhs=ct[:, hh, 0:n],
                                                       start=(hh == 0), stop=(hh == 3))
                                    return ins
                                S.add("pe", mmo, reads=[("cat", ci_, hh) for hh in range(8)] + ["woh"], writes=[PK(pb)])
                                resid_add(bl, l, 1, tb, oc, pb, n)
                        tasks.append((34, outproj))
                    return tasks
                postC.is_c = True
                for c in range(2):
                    r0 = base + 32 * c
                    head_pass((r0, r0 + 32), 1 + h // 2, 2 + h // 2, 2 + h, 32 ** -0.5, pos[c], (r0, 0), postC if c == 1 else None)

        for qi_, tb in enumerate(qblocks):
            qblock(qi_, tb)
        LA = 2
        deferred = []
        NS = len(steps)
        for i in range(min(LA, NS)):
            steps[i][0]()
        for i in range(NS):
            if i + LA < NS:
                steps[i + LA][0]()
            steps[i][1]()
            steps[i][2]()
            if steps[i][3] is not None:
                if getattr(steps[i][3], "is_c", False):
                    for d in sorted(deferred, key=lambda d: d[0]):
                        d[1]()
                    deferred = []
                for (dl, fn) in steps[i][3]():
                    deferred.append((i + dl, fn))
            ready = [d for d in deferred if d[0] <= i]
            deferred = [d for d in deferred if d[0] > i]
            for d in ready:
                d[1]()
        for d in sorted(deferred, key=lambda d: d[0]):
            d[1]()
        S.barrier()
        A.release(m)

        m2 = A.mark()
        blocks2 = [0, 1, 2, 3] if last else [0, 1, 2, 3, 4]
        xnb = [A.t(f"xnb{i}", [128, 8, 512], BF16) for i in range(2)]
        wu = A.t("wu", [128, 8, 256], BF16)
        wvg = A.t("wvg", [128, 8, 256], BF16)
        wgl = A.t("wgl", [128, 8, 512], BF16)
        wob = A.t("wob", [64, 4, 1024], BF16)
        wod = A.t("wod", [128, 2, 1024], BF16)
        wsT = A.t("wsT", [128, 4, 128], BF16)
        gvb = A.t("gvb", [128, 256], F32)
        bst = A.t("bst", [64, 4, 512], F32)
        diag = A.t("diag", [128, 2, 31, 128], BF16)
        ypl = A.t("ypl", [128, 2, NLAT + 30], BF16)
        ypc = A.t("ypc", [128, 2, NCTX + 30], BF16)
        ug = A.t("ug", [64, 4, 512], F32)
        vge = A.t("vge", [128, 256], F32)
        junk = A.t("junk", [128, 256], BF16)
        ssum = A.t("ssum", [128, 2], F32)
        vn = A.t("vn", [128, 4, 256], BF16)
        tmb = A.t("tmb", [64, 512], F32)
        catb = A.t("catb", [64, 4, 512], BF16)
        sg = A.t("sg", [128, 512], F32)
        zz = A.t("zz", [128, 2, 512], F32)
        sqz = A.t("sqz", [128, 2, 512], BF16)
        odd = A.t("odd", [128, 2, 512], BF16)
        dma_w(wob[:], w_out[l][512:768, :].rearrange("(g d) n -> d g n", d=64), [], ["wob"])
        dma_w(wod[:], w_out[l][768:1024, :].rearrange("(c p) n -> p c n", p=128), [], ["wod"])
        dma_w(wsT[:], wsT_d[l], [], ["wsT"])
        dma_sp(gvb[:], gv_d[:, l, :], [], ["gvb"])
        dma_sp(bst[:], bs_d[:, l, :, :], [], ["bst"])
        for c in range(2):
            for k in range(31):
                S.add("dve", lambda e, c=c, k=k: e.tensor_scalar(out=diag[:, c, k, :], in0=identF[:], scalar1=wdw[:, l, c, k:k + 1], scalar2=None, op0=ALU.mult),
                      reads=["identF", "wdw"], writes=[("diag", c)])
        S.add("pool", lambda e: e.memset(ypl[:], 0.0), writes=[("ypl", c, tb) for c in range(2) for tb in range(4)] + ["yplpad"])
        S.add("pool", lambda e: e.memset(ypc[:], 0.0), writes=[("ypc", c) for c in range(2)])
        pr2 = Ring([0, 1, 2, 3])
        outr = Ring([4, 5])
        def pass2a_pre(tb):
            t0, n = TBS[tb]
            xb = xnb[tb % 2]
            xkeys = [("xnb", tb % 2, c) for c in range(8)]
            dma_sp(xb[:, :, 0:n], xscr[:, :, t0:t0 + n], [("xscr", tb)], xkeys)

        def pass2a(tb):
            t0, n = TBS[tb]
            xb = xnb[tb % 2]
            xkeys = [("xnb", tb % 2, c) for c in range(8)]
            for g in range(4):
                pb = pr2.next()

                def mmu(e, pb=pb, g=g, xb=xb, n=n):
                    ins = None
                    for k in range(8):
                        ins = e.matmul(ps[pb][0:64, 0:n], lhsT=wu[:, k, g * 64:(g + 1) * 64], rhs=xb[:, k, 0:n], start=(k == 0), stop=(k == 7))
                    return ins
                S.add("pe", mmu, reads=xkeys + ["wu"], writes=[PK(pb)])
                S.add("act", lambda e, pb=pb, g=g, n=n: e.activation(out=ug[:, g, 0:n], in_=ps[pb][0:64, 0:n], func=AF.Gelu_apprx_tanh),
                      reads=[PK(pb)], writes=[("ug", g)])
            nsb = n // 128
            for sb in range(nsb):
                pb = pr2.next()

                def mmv(e, pb=pb, sb=sb, xb=xb):
                    ins = None
                    for k in range(8):
                        ins = e.matmul(ps[pb][:, 0:256], lhsT=xb[:, k, sb * 128:(sb + 1) * 128], rhs=wvg[:, k, :], start=(k == 0), stop=(k == 7))
                    return ins
                S.add("pe", mmv, reads=xkeys + ["wvg"], writes=[PK(pb)])
                S.add("act", lambda e, pb=pb: e.activation(out=vge[:], in_=ps[pb][:, 0:256], func=AF.Gelu_apprx_tanh), reads=[PK(pb)], writes=["vge"])
                S.add("dve", lambda e: e.memset(ssum[:], 0.0), writes=["ssum"])
                S.add("act", lambda e: e.activation(out=junk[:], in_=vge[:], func=AF.Square, accum_out=ssum[:, 0:1]), reads=["vge", "ssum"], writes=["junk", "ssum"])
                S.add("act", lambda e: e.activation(out=ssum[:, 1:2], in_=ssum[:, 0:1], func=AF.Sqrt, bias=epsc[:, 0:1], scale=1.0 / 256.0),
                      reads=["ssum", "epsc"], writes=["ssum"])
                S.add("dve", lambda e: e.reciprocal(out=ssum[:, 1:2], in_=ssum[:, 1:2]), reads=["ssum"], writes=["ssum"])
                S.add("dve", lambda e, sb=sb: e.scalar_tensor_tensor(out=vn[:, sb, :], in0=vge[:], scalar=ssum[:, 1:2], in1=gvb[:], op0=ALU.mult, op1=ALU.mult),
                      reads=["vge", "ssum", "gvb"], writes=[("vn", sb)])
            for g in range(4):
                pb = pr2.next()

                def mmm(e, pb=pb, g=g, nsb=nsb):
                    ins = None
                    for sb in range(nsb):
                        ins = e.matmul(ps[pb][0:64, sb * 128:(sb + 1) * 128], lhsT=vn[:, sb, g * 64:(g + 1) * 64], rhs=wsT[:, g, :], start=True, stop=True)
                    return ins
                S.add("pe", mmm, reads=[("vn", sb) for sb in range(nsb)] + ["wsT"], writes=[PK(pb)])
                S.add("dve", lambda e, pb=pb, g=g, n=n: e.tensor_tensor(out=tmb[:, 0:n], in0=ps[pb][0:64, 0:n], in1=bst[:, g, 0:n], op=ALU.add),
                      reads=[PK(pb), "bst"], writes=["tmb"])
                S.add("pool", lambda e, g=g, n=n: e.tensor_tensor(out=catb[:, g, 0:n], in0=ug[:, g, 0:n], in1=tmb[:, 0:n], op=ALU.mult),
                      reads=["tmb", ("ug", g)], writes=[("catb", g)])
            for oc in range(8):
                po = outr.next()

                def mmo(e, po=po, oc=oc, n=n):
                    ins = None
                    for g in range(4):
                        ins = e.matmul(ps[po][:, 0:n], lhsT=wob[:, g, oc * 128:(oc + 1) * 128], rhs=catb[:, g, 0:n], start=(g == 0), stop=(g == 3))
                    return ins
                S.add("pe", mmo, reads=[("catb", g) for g in range(4)] + ["wob"], writes=[PK(po)])
                if "B" in DEBUG_PARTS:
                    resid_add(bl, l, 1, tb, oc, po, n)
            for c in range(2):
                pa = pr2.next()
                pg = pr2.next()

                def mmg(e, pbk, co, xb=xb, n=n):
                    ins = None
                    for k in range(8):
                        ins = e.matmul(ps[pbk][:, 0:n], lhsT=wgl[:, k, co:co + 128], rhs=xb[:, k, 0:n], start=(k == 0), stop=(k == 7))
                    return ins
                S.add("pe", lambda e, pa=pa, c=c, mmg=mmg: mmg(e, pa, c * 128), reads=xkeys + ["wgl"], writes=[PK(pa)])
                S.add("pe", lambda e, pg=pg, c=c, mmg=mmg: mmg(e, pg, 256 + c * 128), reads=xkeys + ["wgl"], writes=[PK(pg)])
                S.add("act", lambda e, pg=pg, n=n: e.activation(out=sg[:, 0:n], in_=ps[pg][:, 0:n], func=AF.Sigmoid), reads=[PK(pg)], writes=["sg"])
                if tb != 4:
                    S.add("dve", lambda e, pa=pa, c=c, t0=t0, n=n: e.tensor_tensor(out=ypl[:, c, 15 + t0:15 + t0 + n], in0=ps[pa][:, 0:n], in1=sg[:, 0:n], op=ALU.mult),
                          reads=[PK(pa), "sg", "yplpad"], writes=[("ypl", c, tb)])
                else:
                    S.add("dve", lambda e, pa=pa, c=c, n=n: e.tensor_tensor(out=ypc[:, c, 15:15 + n], in0=ps[pa][:, 0:n], in1=sg[:, 0:n], op=ALU.mult),
                          reads=[PK(pa), "sg"], writes=[("ypc", c)])
        dma_w(wu[:], W[:, 1280:1536].rearrange("(k p) n -> p k n", p=128), [], ["wu"])
        dma_w(wvg[:], W[:, 1536:1792].rearrange("(k p) n -> p k n", p=128), [], ["wvg"])
        dma_w(wgl[:], W[:, 1792:2304].rearrange("(k p) n -> p k n", p=128), [], ["wgl"])
        pass2a_pre(blocks2[0])
        for i_, tb in enumerate(blocks2):
            if i_ + 1 < len(blocks2):
                pass2a_pre(blocks2[i_ + 1])
            pass2a(tb)

        def pass2b(tb):
            t0, n = TBS[tb]
            for c in range(2):
                pz = pr2.next()
                if tb != 4:
                    yk = [("ypl", c, t) for t in range(4)] + ["yplpad"]
                    ysrc = lambda k, c=c, t0=t0, n=n: ypl[:, c, t0 + k:t0 + k + n]
                else:
                    yk = [("ypc", c)]
                    ysrc = lambda k, c=c, n=n: ypc[:, c, k:k + n]

                def mmc(e, pz=pz, c=c, ysrc=ysrc, n=n):
                    ins = None
                    for k in range(31):
                        ins = e.matmul(ps[pz][:, 0:n], lhsT=diag[:, c, k, :], rhs=ysrc(k), start=(k == 0), stop=(k == 30))
                    return ins
                S.add("pe", mmc, reads=yk + [("diag", c)], writes=[PK(pz)])
                S.add("act", lambda e, pz=pz, c=c, n=n: e.activation(out=zz[:, c, 0:n], in_=ps[pz][:, 0:n], func=AF.Identity, bias=bdw[:, l, c:c + 1]),
                      reads=[PK(pz), "bdw"], writes=[("zz", c)])
                S.add("act", lambda e, c=c, n=n: e.activation(out=sqz[:, c, 0:n], in_=zz[:, c, 0:n], func=AF.Square), reads=[("zz", c)], writes=[("sqz", c)])
            pn = pr2.next()

            def mmn(e, pn=pn, n=n):
                ins = None
                for c in range(2):
                    ins = e.matmul(ps[pn][:, 0:n], lhsT=ones256, rhs=sqz[:, c, 0:n], start=(c == 0), stop=(c == 1))
                return ins
            S.add("pe", mmn, reads=[("sqz", 0), ("sqz", 1), "cmat"], writes=[PK(pn)])
            rsqrt_from_psum(pn, n)
            for c in range(2):
                tx = tmpx[c]
                S.add("pool", lambda e, c=c, tx=tx, n=n: e.tensor_tensor(out=tx[:, 0:n], in0=zz[:, c, 0:n], in1=rstd[:, 0:n], op=ALU.mult),
                      reads=[("zz", c), "rstd"], writes=[("tmpx", c)])
                S.add("act", lambda e, c=c, tx=tx, n=n: e.activation(out=odd[:, c, 0:n], in_=tx[:, 0:n], func=AF.Silu, scale=gconv[:, l, c:c + 1]),
                      reads=[("tmpx", c), "gconv"], writes=[("odd", c)])
            for oc in range(8):
                po = outr.next()

                def mmo2(e, po=po, oc=oc, n=n):
                    ins = None
                    for c in range(2):
                        ins = e.matmul(ps[po][:, 0:n], lhsT=wod[:, c, oc * 128:(oc + 1) * 128], rhs=odd[:, c, 0:n], start=(c == 0), stop=(c == 1))
                    return ins
                S.add("pe", mmo2, reads=[("odd", 0), ("odd", 1), "wod"], writes=[PK(po)])
                if "D" in DEBUG_PARTS:
                    resid_add(bl, l, 1, tb, oc, po, n)

        for tb in blocks2:
            pass2b(tb)
        S.barrier()
        A.release(m2)

    done = False
    for bl in range(nb):
        for c in range(8):
            dma_sp(hT[:, c, 0:NLAT], xT[bl][c * 128:(c + 1) * 128, :], [], [("h", c, tb) for tb in range(4)])
        dma_sp(hT[:, :, NLAT:NTOK], ctxT[bl].rearrange("(c p) t -> p c t", p=128), [], [("h", c, 4) for c in range(8)])
        for l in range(depth):
            last = (l == DEPTH - 1)
            ffn(bl, l, 0, [0, 1, 2, 3, 4])
            if stop == f"ffn1_{l}":
                dump_and_end(bl)
                done = True
                break
            mixer(bl, l)
            if stop == f"mix_{l}":
                dump_and_end(bl)
                done = True
                break
            ffn(bl, l, 1, [0, 1, 2, 3] if last else [0, 1, 2, 3, 4])
            if stop == f"ffn2_{l}":
                dump_and_end(bl)
                done = True
                break
        if done:
            continue
        mf = A.mark()
        ob = [A.t(f"ob{i}", [128, 8, 512], F32) for i in range(2)]
        for tb in range(4):
            t0, n = TBS[tb]
            o = ob[tb % 2]
            make_xn(bl, 0, 0, tb, lambda c, o=o: o[:, c, :], lambda c, tb=tb: ("ob", tb % 2, c), final=True)
            dma_sp(outT[bl][:, t0:t0 + n].rearrange("(c p) t -> p c t", p=128), o[:], [("ob", tb % 2, c) for c in range(8)], [("ob", tb % 2, c) for c in range(8)] + ["outT"])
        S.barrier()
        A.release(mf)
    S.add("sp", lambda e: None, reads=["outT" if stop is None else "dbg"])
    S.emit()
    return nc


def _rope_tables():
    t = np.arange(NLAT)
    row = (t // 64).astype(np.float32)
    col = (t % 64).astype(np.float32)
    out = np.zeros((4, 128, NLAT), np.float32)
    for ti, hd in ((0, 64), (2, 32)):
        quarter = hd // 4
        inv = (np.float32(10000.0) ** (-np.arange(quarter, dtype=np.float32) / np.float32(quarter))).astype(np.float32)
        ang = np.concatenate([row[:, None] * inv[None, :], col[:, None] * inv[None, :]], axis=-1).astype(np.float32)
        cos = np.cos(ang).astype(np.float32)
        sin = np.sin(ang).astype(np.float32)
        half = hd // 2
        for p in range(128):
            d = p % hd
            j = d % half
            out[ti, p] = cos[:, j]
            out[ti + 1, p] = (-sin[:, j]) if d < half else sin[:, j]
    return out


def _const_mats():
    m = np.zeros((6, 128, 128), np.float32)
    m[0] = 1.0 / 1024.0
    m[1, 0:64, 0:64] = 1.0 / 64.0
    m[1, 64:128, 64:128] = 1.0 / 64.0
    m[2] = 1.0 / 256.0
    for mm_ in range(128):
        pa = mm_ + 32 if (mm_ % 64) < 32 else mm_ - 32
        m[3, pa, mm_] = 1.0
        pc = mm_ + 16 if (mm_ % 32) < 16 else mm_ - 16
        m[4, pc, mm_] = 1.0
    m[5] = np.eye(128, dtype=np.float32)
    return m


def _col_perm():
    qa = lambda h: list(range(h * 64, (h + 1) * 64))
    perm = qa(0) + qa(2) + qa(1) + qa(3)
    perm += list(range(256, 512))
    perm += list(range(512, 640))
    perm += list(range(768, 1024))
    perm += list(range(640, 768))
    perm += list(range(1024, 1280))
    perm += list(range(1280, 2304))
    return np.array(perm)


def prep_shared(inp):
    f = lambda a: np.ascontiguousarray(np.asarray(a, dtype=np.float32))
    sh = {}
    sh["w_ada"] = f(inp["w_ada"])
    sh["b_adaT"] = f(np.asarray(inp["b_ada"]).reshape(DEPTH, 72, 128).transpose(2, 0, 1))
    sh["g_normT"] = f(np.asarray(inp["g_norm"]).reshape(DEPTH, 3, 8, 128).transpose(3, 0, 1, 2))
    sh["g_finalT"] = f(np.asarray(inp["g_final"]).reshape(8, 128).T)
    for k in ("w_ff1_in", "w_ff1_out", "w_ff2_in", "w_ff2_out", "w_out"):
        sh[k] = f(inp[k])
    sh["w_in_p"] = f(np.asarray(inp["w_in"])[:, :, _col_perm()])
    gq = np.asarray(inp["g_q_a"])
    gk = np.asarray(inp["g_k_a"])
    gqk = np.stack([np.tile(gq, (1, 2)), np.tile(gk, (1, 2))], axis=-1)
    sh["gqk"] = f(gqk.transpose(1, 0, 2))
    sh["lamw"] = f(np.broadcast_to(np.asarray(inp["lam_c"])[None], (64, DEPTH, 4, 32)))
    sh["gsub"] = f(np.asarray(inp["g_sub_c"]).T)
    sh["gv_bc"] = f(np.broadcast_to(np.asarray(inp["g_v_b"])[None], (128, DEPTH, 256)))
    sh["wsT"] = f(np.asarray(inp["w_s_b"]).transpose(0, 3, 1, 2))
    bs = np.asarray(inp["b_s_b"])
    sh["bs_tbl"] = f(np.broadcast_to(np.tile(bs, (1, 1, 4))[None], (64, DEPTH, 4, 512)))
    sh["wdw"] = f(np.asarray(inp["w_dw_d"]).reshape(DEPTH, 31, 2, 128).transpose(3, 0, 2, 1))
    sh["bdw"] = f(np.asarray(inp["b_dw_d"]).reshape(DEPTH, 2, 128).transpose(2, 0, 1))
    sh["gconv"] = f(np.asarray(inp["g_conv_d"]).reshape(DEPTH, 2, 128).transpose(2, 0, 1))
    sh["rope"] = _rope_tables()
    sh["cmat"] = _const_mats()
    return sh


def prep_core(inp, bids):
    x = np.asarray(inp["x"])
    ctx = np.asarray(inp["ctx"])
    c = np.asarray(inp["c"])
    cc = np.asarray(inp["c_ctx"])
    d = {}
    d["xT"] = np.ascontiguousarray(np.stack([x[b].T for b in bids]).astype(np.float32))
    d["ctxT"] = np.ascontiguousarray(np.stack([ctx[b].T for b in bids]).astype(np.float32))
    vecs = [c[b] for b in bids]
    while len(vecs) < 2:
        vecs.append(c[bids[0]])
    vecs.append(cc)
    cT = np.stack(vecs, axis=-1).reshape(8, 128, 3).transpose(1, 0, 2)
    d["cT"] = np.ascontiguousarray(cT.astype(np.float32))
    return d


_NC_CACHE = {}


def kernel(**inputs):
    B = np.asarray(inputs["x"]).shape[0]
    nb = B // NCORES
    if "prog" not in _NC_CACHE:
        _NC_CACHE["prog"] = build_program(nb=nb)
    nc = _NC_CACHE["prog"]
    sh = prep_shared(inputs)
    in_maps = []
    for i in range(NCORES):
        d = dict(sh)
        d.update(prep_core(inputs, list(range(i * nb, (i + 1) * nb))))
        in_maps.append(d)
    res = run_bass_kernel_spmd(nc, in_maps, core_ids=list(range(NCORES)))
    out = np.empty((B, NLAT, D), np.float32)
    for i in range(NCORES):
        o = res.results[i]["outT"]
        for jb in range(nb):
            out[i * nb + jb] = o[jb].T
    return out
```

```python
import math
import contextlib
import numpy as np
import ml_dtypes
import concourse.bass as bass
import concourse.mybir as mybir
from concourse.bass_utils import run_bass_kernel_spmd

F32 = mybir.dt.float32
BF16 = mybir.dt.bfloat16
AF = mybir.ActivationFunctionType
ALU = mybir.AluOpType

ENGS = ["pe", "act", "dve", "pool", "sp"]

D = 1024
NLAT = 2048
NCTX = 256
NTOK = NLAT + NCTX
DFF = 2816
DEPTH = 2
EPS = 1e-6
NCORES = 8
TBS = [(0, 512), (512, 512), (1024, 512), (1536, 512), (2048, 256)]
DEBUG_PARTS = set("ACBD")
WARM_N = 0


class Op:
    __slots__ = ("eng", "fn", "deps", "sig", "sigval", "ch", "inc", "dma", "idx")


class Sched:
    def __init__(self, nc):
        self.nc = nc
        self.ops = {e: [] for e in ENGS}
        self.last_w = {}
        self.readers = {}
        self.ch_eng = {}
        self.pending_bar = {e: [] for e in ENGS}
        self.nops = 0
        self.last_on_ch = {}
        self.dma_ring = {}
        self.DMA_RING = {"sp": 16, "pool": 24}

    def add(self, eng, fn, reads=(), writes=(), ch=None, dma=False):
        op = Op()
        op.eng = eng
        op.fn = fn
        op.sig = bool(dma)
        op.sigval = None
        op.dma = dma
        op.ch = ch if ch is not None else eng
        if dma:
            k = self.dma_ring.get(eng, 0)
            self.dma_ring[eng] = k + 1
            op.ch = f"{eng}_d{k % self.DMA_RING[eng]}"
        op.inc = 16 if dma else 1
        op.idx = self.nops
        self.nops += 1
        if op.ch in self.ch_eng:
            assert self.ch_eng[op.ch] == eng, (op.ch, eng)
        else:
            self.ch_eng[op.ch] = eng
        deps = {}
        for k in reads:
            w = self.last_w.get(k)
            if w is not None:
                deps[w.idx] = (w, True)
        for k in writes:
            w = self.last_w.get(k)
            if w is not None and w.idx not in deps:
                deps[w.idx] = (w, False)
            for r in self.readers.get(k, ()):
                if r.idx not in deps:
                    deps[r.idx] = (r, False)
        need = []
        for (d, raw) in deps.values():
            if d.eng == eng and not d.dma and not dma and not raw:
                continue
            if d.eng == eng and eng == "pe" and not d.dma and not dma:
                continue
            need.append(d)
        if dma:
            prev = self.last_on_ch.get(op.ch)
            if prev is not None:
                need.append(prev)
        for d in self.pending_bar[eng]:
            need.append(d)
        self.pending_bar[eng] = []
        for d in need:
            d.sig = True
        op.deps = need
        for k in writes:
            self.last_w[k] = op
            self.readers[k] = []
        for k in reads:
            if k in writes:
                continue
            self.readers.setdefault(k, []).append(op)
        self.ops[eng].append(op)
        self.last_on_ch[op.ch] = op
        return op

    def barrier(self):
        frontier = list(self.last_on_ch.values())
        for e in ENGS:
            self.pending_bar[e] = list(frontier)

    def emit(self):
        nc = self.nc
        chans = list(self.ch_eng.keys())
        cum = {c: 0 for c in chans}
        for e in ENGS:
            for op in self.ops[e]:
                if op.sig:
                    cum[op.ch] += op.inc
                    op.sigval = cum[op.ch]
        with contextlib.ExitStack() as st:
            sems = {c: st.enter_context(nc.semaphore("s_" + c)) for c in chans}
            block = st.enter_context(nc.Block())

            def run(engname, engine):
                waited = {}
                for op in self.ops[engname]:
                    for d in op.deps:
                        if waited.get(d.ch, 0) < d.sigval:
                            engine.wait_ge(sems[d.ch], d.sigval)
                            waited[d.ch] = d.sigval
                    ins = op.fn(engine)
                    if op.sig:
                        assert ins is not None
                        ins.then_inc(sems[op.ch], op.inc)

            @block.tensor
            def _(eng):
                run("pe", eng)

            @block.scalar
            def _(eng):
                run("act", eng)

            @block.vector
            def _(eng):
                run("dve", eng)

            @block.gpsimd
            def _(eng):
                run("pool", eng)

            @block.sync
            def _(eng):
                run("sp", eng)


class Alloc:
    def __init__(self, nc, limit=229344, base=16512):
        self.nc = nc
        self.off = base
        self.limit = limit
        self.n = 0
        self.peak = base

    def mark(self):
        return self.off

    def release(self, m):
        self.off = m

    def t(self, name, shape, dtype):
        esz = 4 if dtype == F32 else 2
        nbytes = int(np.prod(shape[1:])) * esz
        nbytes = (nbytes + 63) // 64 * 64
        assert self.off + nbytes <= self.limit, (name, self.off, nbytes, self.limit)
        self.n += 1
        h = self.nc.alloc_sbuf_tensor_at(f"{name}_{self.n}", list(shape), dtype, offset=self.off)
        self.off += nbytes
        self.peak = max(self.peak, self.off)
        return h


class Ring:
    def __init__(self, items):
        self.items = list(items)
        self.i = 0

    def next(self):
        v = self.items[self.i % len(self.items)]
        self.i += 1
        return v


def build_program(nb=2, depth=DEPTH, stop=None):
    nc = bass.Bass("TRN2", target_bir_lowering=False)
    S = Sched(nc)
    A = Alloc(nc)

    def din(name, shape, dt=F32):
        return nc.dram_tensor(name, list(shape), dt, kind="ExternalInput").ap()

    xT = din("xT", [nb, D, NLAT])
    ctxT = din("ctxT", [nb, D, NCTX])
    cT_d = din("cT", [128, 8, 3])
    w_ada = din("w_ada", [DEPTH, D, 9 * D])
    b_adaT_d = din("b_adaT", [128, DEPTH, 72])
    g_normT_d = din("g_normT", [128, DEPTH, 3, 8])
    g_finalT_d = din("g_finalT", [128, 8])
    w_ff_in = [din("w_ff1_in", [DEPTH, D, 2 * DFF]), din("w_ff2_in", [DEPTH, D, 2 * DFF])]
    w_ff_out = [din("w_ff1_out", [DEPTH, DFF, D]), din("w_ff2_out", [DEPTH, DFF, D])]
    w_in_p = din("w_in_p", [DEPTH, D, 2304])
    w_out = din("w_out", [DEPTH, D, D])
    gqk_d = din("gqk", [128, DEPTH, 2])
    lamw_d = din("lamw", [64, DEPTH, 4, 32])
    gsub_d = din("gsub", [64, DEPTH])
    gv_d = din("gv_bc", [128, DEPTH, 256])
    wsT_d = din("wsT", [DEPTH, 128, 4, 128])
    bs_d = din("bs_tbl", [64, DEPTH, 4, 512])
    wdw_d = din("wdw", [128, DEPTH, 2, 31])
    bdw_d = din("bdw", [128, DEPTH, 2])
    gconv_d = din("gconv", [128, DEPTH, 2])
    rope_d = din("rope", [4, 128, NLAT])
    cmat_d = din("cmat", [6, 128, 128])
    xscr = nc.dram_tensor("xscr", [128, 8, NTOK], BF16, kind="ExternalOutput").ap()
    if stop is None:
        outT = nc.dram_tensor("outT", [nb, D, NLAT], F32, kind="ExternalOutput").ap()
    else:
        dbg = nc.dram_tensor("dbg", [nb, 128, 8, NTOK], F32, kind="ExternalOutput").ap()

    hT = A.t("hT", [128, 8, NTOK], F32)
    cmat = A.t("cmat", [128, 6, 128], BF16)
    identF = A.t("identF", [128, 128], F32)
    modall = A.t("modall", [128, DEPTH, 72, 3], F32)
    gs = A.t("gs", [128, DEPTH, 3, 8, 3], F32)
    hg = A.t("hg", [128, DEPTH, 3, 8, 3], F32)
    b_adaT = A.t("b_adaT", [128, DEPTH, 72], F32)
    g_normT = A.t("g_normT", [128, DEPTH, 3, 8], F32)
    g_finalT = A.t("g_finalT", [128, 8], F32)
    gqk = A.t("gqk", [128, DEPTH, 2], F32)
    gsub = A.t("gsub", [64, DEPTH], F32)
    gsub1m = A.t("gsub1m", [64, DEPTH], F32)
    neglam = A.t("neglam", [64, DEPTH], F32)
    wdw = A.t("wdw", [128, DEPTH, 2, 31], F32)
    bdw = A.t("bdw", [128, DEPTH, 2], F32)
    gconv = A.t("gconv", [128, DEPTH, 2], F32)
    epsc = A.t("epsc", [128, 1], F32)
    scT = A.t("scT", [128, 8, 3], BF16)
    sq = A.t("sq", [128, 8, 512], BF16)
    rt = A.t("rt", [128, 512], F32)
    rstd = A.t("rstd", [128, 512], F32)
    tmpx = [A.t("tmpx0", [128, 512], F32), A.t("tmpx1", [128, 512], F32)]
    PERSIST = A.mark()

    ps = [nc.alloc_psum_tensor(f"ps{i}", [128, 512], F32) for i in range(8)]
    onesD = cmat[:, 0, :]
    blk64 = cmat[:, 1, :]
    ones256 = cmat[:, 2, :]
    permA = cmat[:, 3, :]
    permC = cmat[:, 4, :]

    cnt = [0]

    def uid():
        cnt[0] += 1
        return cnt[0]

    def PK(b):
        return ("ps", b)

    def dma_sp(out, in_, reads, writes):
        S.add("sp", lambda e: e.dma_start(out=out, in_=in_), reads=reads, writes=writes, ch="dq_sp", dma=True)

    def dma_w(out, in_, reads, writes):
        S.add("pool", lambda e: e.dma_start(out=out, in_=in_), reads=reads, writes=writes, ch="dq_w", dma=True)

    dma_w(cmat[:], cmat_d.rearrange("m p n -> p m n"), [], ["cmat"])
    dma_sp(identF[:], cmat_d[5], [], ["identF"])
    dma_sp(b_adaT[:], b_adaT_d, [], ["b_adaT"])
    dma_sp(g_normT[:], g_normT_d, [], ["g_normT"])
    dma_sp(g_finalT[:], g_finalT_d, [], ["g_finalT"])
    dma_sp(gqk[:], gqk_d, [], ["gqk"])
    dma_sp(gsub[:], gsub_d, [], ["gsub"])
    dma_sp(wdw[:], wdw_d, [], ["wdw"])
    dma_sp(bdw[:], bdw_d, [], ["bdw"])
    dma_sp(gconv[:], gconv_d, [], ["gconv"])
    S.add("dve", lambda e: e.memset(epsc[:], EPS), writes=["epsc"])

    m0 = A.mark()
    cTs = A.t("cTs", [128, 8, 3], F32)
    lamw = A.t("lamw", [64, DEPTH, 4, 32], F32)
    lamp = A.t("lamp", [64, DEPTH, 2, 32], F32)
    lams = A.t("lams", [64, DEPTH, 2], F32)
    lame = A.t("lame", [64, DEPTH, 2], F32)
    wab = [A.t(f"wab{i}", [128, 8, 512], BF16) for i in range(3)]
    dma_sp(cTs[:], cT_d, [], ["cTs"])
    dma_sp(lamw[:], lamw_d, [], ["lamw"])
    S.add("act", lambda e: e.activation(out=scT[:], in_=cTs[:], func=AF.Silu), reads=["cTs"], writes=["scT"])
    def adaln_steps(l, wabufs, uidx):
        stepsl = []
        for cb in range(18):
            def one(cb=cb):
                bi = cb % len(wabufs)
                wt = wabufs[bi]
                dma_w(wt[:], w_ada[l][:, cb * 512:(cb + 1) * 512].rearrange("(k p) n -> p k n", p=128), [], [("wab", uidx, bi)])

                def mm(e):
                    ins = None
                    for cc in range(4):
                        chn = cb * 4 + cc
                        for k in range(8):
                            ins = e.matmul(ps[7][:, chn * 3:(chn + 1) * 3], lhsT=wt[:, k, cc * 128:(cc + 1) * 128],
                                           rhs=scT[:, k, :], start=(k == 0), stop=(k == 7))
                    return ins
                S.add("pe", mm, reads=[("wab", uidx, bi), "scT"], writes=[PK(7)])
            stepsl.append(one)

        def fin():
            psm = ps[7][:, 0:216].rearrange("p (c j) -> p c j", j=3)
            for j in range(3):
                S.add("dve", lambda e, j=j: e.tensor_tensor(out=modall[:, l, :, j], in0=psm[:, :, j], in1=b_adaT[:, l, :], op=ALU.add),
                      reads=[PK(7), "b_adaT"], writes=[("modall", l)])
            for s_ in range(3):
                for j in range(3):
                    S.add("dve", lambda e, s_=s_, j=j: e.tensor_scalar(out=gs[:, l, s_, :, j], in0=modall[:, l, (3 * s_ + 1) * 8:(3 * s_ + 2) * 8, j],
                                                                         scalar1=1.0, scalar2=None, op0=ALU.add),
                          reads=[("modall", l)], writes=[("gs", l)])
                    S.add("dve", lambda e, s_=s_, j=j: e.tensor_tensor(out=gs[:, l, s_, :, j], in0=gs[:, l, s_, :, j], in1=g_normT[:, l, s_, :], op=ALU.mult),
                          reads=[("gs", l), "g_normT"], writes=[("gs", l)])
                    S.add("dve", lambda e, s_=s_, j=j: e.tensor_scalar(out=hg[:, l, s_, :, j], in0=modall[:, l, (3 * s_ + 2) * 8:(3 * s_ + 3) * 8, j],
                                                                         scalar1=(1.0 if s_ == 1 else 0.5), scalar2=None, op0=ALU.mult),
                          reads=[("modall", l)], writes=[("hg", l)])
        return stepsl, fin

    st0, fin0 = adaln_steps(0, wab, 0)
    for f_ in st0:
        f_()
    fin0()
    for l in range(depth):
        lam_init = 0.8 - 0.6 * math.exp(-0.3 * l)
        for q in range(2):
            S.add("dve", lambda e, l=l, q=q: e.tensor_tensor(out=lamp[:, l, q, :], in0=lamw[:, l, 2 * q, :], in1=lamw[:, l, 2 * q + 1, :], op=ALU.mult),
                  reads=["lamw"], writes=["lamp"])
            S.add("dve", lambda e, l=l, q=q: e.tensor_reduce(out=lams[:, l, q:q + 1], in_=lamp[:, l, q, :], axis=mybir.AxisListType.X, op=ALU.add),
                  reads=["lamp"], writes=["lams"])
        S.add("act", lambda e, l=l: e.activation(out=lame[:, l, :], in_=lams[:, l, :], func=AF.Exp), reads=["lams"], writes=["lame"])
        S.add("dve", lambda e, l=l: e.tensor_tensor(out=neglam[:, l:l + 1], in0=lame[:, l, 1:2], in1=lame[:, l, 0:1], op=ALU.subtract),
              reads=["lame"], writes=["neglam"])
        S.add("dve", lambda e, l=l, li=lam_init: e.tensor_scalar(out=neglam[:, l:l + 1], in0=neglam[:, l:l + 1], scalar1=-li, scalar2=None, op0=ALU.add),
              reads=["neglam"], writes=["neglam"])
        S.add("dve", lambda e, l=l, li=lam_init: e.tensor_scalar(out=gsub1m[:, l:l + 1], in0=gsub[:, l:l + 1], scalar1=(1.0 - li), scalar2=None, op0=ALU.mult),
              reads=["gsub"], writes=["gsub1m"])
    S.barrier()
    A.release(m0)

    def hkeys(tb, cs=range(8)):
        return [("h", c, tb) for c in cs]

    def jmod(bl, tb):
        return 2 if tb == 4 else bl

    def rsqrt_from_psum(pb, n, parts=128, scale=1.0):
        S.add("act", lambda e: e.activation(out=rt[0:parts, 0:n], in_=ps[pb][0:parts, 0:n], func=AF.Sqrt, bias=epsc[0:parts, 0:1], scale=scale),
              reads=[PK(pb), "epsc"], writes=["rt"])
        S.add("dve", lambda e: e.reciprocal(out=rstd[0:parts, 0:n], in_=rt[0:parts, 0:n]), reads=["rt"], writes=["rstd"])

    def make_xn(bl, l, s, tb, dst, dkeys, final=False, pbank=7):
        t0, n = TBS[tb]
        j = jmod(bl, tb)
        for c in range(8):
            if c % 2 == 0:
                S.add("act", lambda e, c=c: e.activation(out=sq[:, c, 0:n], in_=hT[:, c, t0:t0 + n], func=AF.Square),
                      reads=[("h", c, tb)], writes=[("sq", c)])
            else:
                S.add("pool", lambda e, c=c: e.tensor_tensor(out=sq[:, c, 0:n], in0=hT[:, c, t0:t0 + n], in1=hT[:, c, t0:t0 + n], op=ALU.mult),
                      reads=[("h", c, tb)], writes=[("sq", c)])

        def mm(e):
            ins = None
            for c in range(8):
                ins = e.matmul(ps[pbank][:, 0:n], lhsT=onesD, rhs=sq[:, c, 0:n], start=(c == 0), stop=(c == 7))
            return ins
        S.add("pe", mm, reads=[("sq", c) for c in range(8)] + ["cmat"], writes=[PK(pbank)])
        rsqrt_from_psum(pbank, n)
        for c in range(8):
            tx = tmpx[c % 2]
            S.add("pool", lambda e, c=c, tx=tx: e.tensor_tensor(out=tx[:, 0:n], in0=hT[:, c, t0:t0 + n], in1=rstd[:, 0:n], op=ALU.mult),
                  reads=[("h", c, tb), "rstd"], writes=[("tmpx", c % 2)])
            if final:
                S.add("act", lambda e, c=c, tx=tx: e.activation(out=dst(c), in_=tx[:, 0:n], func=AF.Identity, scale=g_finalT[:, c:c + 1]),
                      reads=[("tmpx", c % 2), "g_finalT"], writes=[dkeys(c)])
            else:
                S.add("act", lambda e, c=c, tx=tx: e.activation(out=dst(c), in_=tx[:, 0:n], func=AF.Identity,
                                                                 scale=gs[:, l, s, c, j:j + 1], bias=modall[:, l, 3 * s * 8 + c, j:j + 1]),
                      reads=[("tmpx", c % 2), ("gs", l), ("modall", l)], writes=[dkeys(c)])

    def resid_add(bl, l, s, tb, oc, pb, n):
        t0, _ = TBS[tb]
        j = jmod(bl, tb)
        S.add("dve", lambda e: e.scalar_tensor_tensor(out=hT[:, oc, t0:t0 + n], in0=ps[pb][:, 0:n], scalar=hg[:, l, s, oc, j:j + 1],
                                                       in1=hT[:, oc, t0:t0 + n], op0=ALU.mult, op1=ALU.add),
              reads=[PK(pb), ("hg", l), ("h", oc, tb)], writes=[("h", oc, tb)])

    def dump_and_end(bl):
        dma_sp(dbg[bl], hT[:], [("h", c, tb) for c in range(8) for tb in range(5)], ["dbg"])

    def ffn(bl, l, which, blocks, ada_next=None):
        s = 0 if which == 0 else 2
        m = A.mark()
        extra, extra_fin = [], None
        if ada_next is not None:
            wabx = [A.t(f"wabx{i}", [128, 8, 512], BF16) for i in range(3)]
            extra, extra_fin = adaln_steps(ada_next, wabx, uid())
        xn = A.t("xn", [128, 8, NTOK], BF16)
        wa = [A.t(f"wa{i}", [128, 8, 256], BF16) for i in range(2)]
        wb = [A.t(f"wb{i}", [128, 8, 256], BF16) for i in range(2)]
        wo = [A.t(f"wo{i}", [128, 2, 1024], BF16) for i in range(2)]
        mid = [A.t(f"mid{i}", [128, 2, NTOK], BF16) for i in range(2)]
        sa = [A.t(f"sa{i}", [128, 512], F32) for i in range(2)]
        u = uid()
        for tb in blocks:
            t0, n = TBS[tb]
            make_xn(bl, l, s, tb, lambda c, t0=t0, n=n: xn[:, c, t0:t0 + n], lambda c, tb=tb: ("xn", u, c, tb))
        win = w_ff_in[which][l]
        wout = w_ff_out[which][l]
        NG = DFF // 256
        ring1 = Ring([0, 1, 2, 3])
        ring2 = Ring([4, 5, 6])
        sar = Ring([0, 1])

        def load(g):
            bi = g % 2
            dma_w(wa[bi][:], win[:, g * 256:(g + 1) * 256].rearrange("(k p) n -> p k n", p=128), [], [("wa", bi)])
            dma_w(wb[bi][:], win[:, DFF + g * 256:DFF + (g + 1) * 256].rearrange("(k p) n -> p k n", p=128), [], [("wb", bi)])
            dma_w(wo[bi][:], wout[g * 256:(g + 1) * 256, :].rearrange("(j p) n -> p j n", p=128), [], [("wo", bi)])

        def phase1(g):
            bi = g % 2
            for tb in blocks:
                t0, n = TBS[tb]
                for jj in range(2):
                    pa = ring1.next()
                    pb = ring1.next()

                    def mm(e, w, pbk, jj=jj, t0=t0, n=n):
                        ins = None
                        for k in range(8):
                            ins = e.matmul(ps[pbk][:, 0:n], lhsT=w[:, k, jj * 128:(jj + 1) * 128], rhs=xn[:, k, t0:t0 + n],
                                           start=(k == 0), stop=(k == 7))
                        return ins
                    xk = [("xn", u, c, tb) for c in range(8)]
                    S.add("pe", lambda e, w=wa[bi], pbk=pa, mm=mm: mm(e, w, pbk), reads=xk + [("wa", bi)], writes=[PK(pa)])
                    S.add("pe", lambda e, w=wb[bi], pbk=pb, mm=mm: mm(e, w, pbk), reads=xk + [("wb", bi)], writes=[PK(pb)])
                    si = sar.next()
                    S.add("act", lambda e, pa=pa, si=si, n=n: e.activation(out=sa[si][:, 0:n], in_=ps[pa][:, 0:n], func=AF.Silu),
                          reads=[PK(pa)], writes=[("sa", si)])
                    S.add("dve", lambda e, pb=pb, si=si, n=n, jj=jj, t0=t0: e.tensor_tensor(out=mid[bi][:, jj, t0:t0 + n], in0=sa[si][:, 0:n],
                                                                                         in1=ps[pb][:, 0:n], op=ALU.mult),
                          reads=[PK(pb), ("sa", si)], writes=[("mid", bi, jj, tb)])

        def phase2(g):
            bi = g % 2
            for tb in blocks:
                t0, n = TBS[tb]
                for oc in range(8):
                    po = ring2.next()

                    def mm(e, po=po, oc=oc, t0=t0, n=n):
                        ins = None
                        for jj in range(2):
                            ins = e.matmul(ps[po][:, 0:n], lhsT=wo[bi][:, jj, oc * 128:(oc + 1) * 128], rhs=mid[bi][:, jj, t0:t0 + n],
                                           start=(jj == 0), stop=(jj == 1))
                        return ins
                    S.add("pe", mm, reads=[("mid", bi, 0, tb), ("mid", bi, 1, tb), ("wo", bi)], writes=[PK(po)])
                    resid_add(bl, l, s, tb, oc, po, n)

        load(0)
        load(1)
        phase1(0)
        for g in range(1, NG):
            phase1(g)
            phase2(g - 1)
            if g + 1 < NG:
                load(g + 1)
            for _ in range(2):
                if extra:
                    extra.pop(0)()
        phase2(NG - 1)
        while extra:
            extra.pop(0)()
        if extra_fin is not None:
            extra_fin()
        S.barrier()
        A.release(m)

    def mixer(bl, l):
        last = (l == DEPTH - 1)
        blocks = [0, 1, 2, 3, 4]
        m = A.mark()
        qT = A.t("qT", [128, 4, NTOK], BF16)
        kT = A.t("kT", [128, 3, NTOK], BF16)
        vaug = A.t("vaug", [128, 18, 6, 128], BF16)
        m1 = A.mark()
        xnb = [A.t(f"xnb{i}", [128, 8, 512], BF16) for i in range(2)]
        wq = A.t("wq", [128, 8, 512], BF16)
        wk = A.t("wk", [128, 8, 384], BF16)
        wv = A.t("wv", [128, 8, 384], BF16)
        ropet = A.t("ropet", [128, 4, 512], F32)
        sqb = A.t("sqb", [128, 512], BF16)
        qn = [A.t(f"qn{i}", [128, 512], BF16) for i in range(2)]
        t1 = A.t("t1", [128, 512], F32)
        t2 = A.t("t2", [128, 512], F32)
        u = uid()
        S.add("pool", lambda e: e.memset(vaug[:], 1.0), writes=[("vaug", kc) for kc in range(18)])
        W = w_in_p[l]
        prj = Ring([0, 1, 2])
        aux = Ring([3, 4])
        vring = Ring([5, 6])
        qnr = Ring([0, 1])
        def pass1_pre(tb):
            t0, n = TBS[tb]
            xb = xnb[tb % 2]
            xkeys = [("xnb", tb % 2, c) for c in range(8)]
            make_xn(bl, l, 1, tb, lambda c, xb=xb, n=n: xb[:, c, 0:n], lambda c, tb=tb: ("xnb", tb % 2, c))
            dma_sp(xscr[:, :, t0:t0 + n], xb[:, :, 0:n], xkeys, [("xscr", tb)])

        def pass1(tb):
            t0, n = TBS[tb]
            xb = xnb[tb % 2]
            xkeys = [("xnb", tb % 2, c) for c in range(8)]
            if tb != 4:
                dma_sp(ropet[:], rope_d[:, :, t0:t0 + n].rearrange("m p n -> p m n"), [], ["ropet"])
            for ci in range(7):
                if ci < 4:
                    wt, wkey, co = wq, "wq", ci * 128
                    dest = qT[:, ci, t0:t0 + n]
                    dkey = ("qT", ci, tb)
                else:
                    wt, wkey, co = wk, "wk", (ci - 4) * 128
                    dest = kT[:, ci - 4, t0:t0 + n]
                    dkey = ("kT", ci - 4, tb)
                isA = ci in (0, 1, 4)
                pb = prj.next()

                def mm(e, wt=wt, co=co, pb=pb, xb=xb, n=n):
                    ins = None
                    for k in range(8):
                        ins = e.matmul(ps[pb][:, 0:n], lhsT=wt[:, k, co:co + 128], rhs=xb[:, k, 0:n], start=(k == 0), stop=(k == 7))
                    return ins
                S.add("pe", mm, reads=xkeys + [wkey], writes=[PK(pb)])
                norope = (tb == 4)
                qi = qnr.next()
                qdst = dest if norope else qn[qi][:, 0:n]
                qkey = dkey if norope else ("qn", qi)
                if isA:
                    S.add("act", lambda e, pb=pb, n=n: e.activation(out=sqb[:, 0:n], in_=ps[pb][:, 0:n], func=AF.Square),
                          reads=[PK(pb)], writes=["sqb"])
                    pa = aux.next()
                    S.add("pe", lambda e, pa=pa, n=n: e.matmul(ps[pa][:, 0:n], lhsT=blk64, rhs=sqb[:, 0:n], start=True, stop=True),
                          reads=["sqb", "cmat"], writes=[PK(pa)])
                    rsqrt_from_psum(pa, n)
                    gi = 0 if ci < 4 else 1
                    S.add("dve", lambda e, pb=pb, n=n, qdst=qdst, gi=gi: e.scalar_tensor_tensor(out=qdst, in0=ps[pb][:, 0:n], scalar=gqk[:, l, gi:gi + 1],
                                                                                         in1=rstd[:, 0:n], op0=ALU.mult, op1=ALU.mult),
                          reads=[PK(pb), "rstd", "gqk"], writes=[qkey])
                else:
                    S.add("act", lambda e, pb=pb, n=n, qdst=qdst: e.activation(out=qdst, in_=ps[pb][:, 0:n], func=AF.Copy),
                          reads=[PK(pb)], writes=[qkey])
                if not norope:
                    pr = aux.next()
                    pm = permA if isA else permC
                    ti = 0 if isA else 2
                    S.add("pe", lambda e, pr=pr, n=n, qi=qi, pm=pm: e.matmul(ps[pr][:, 0:n], lhsT=pm, rhs=qn[qi][:, 0:n], start=True, stop=True),
                          reads=[("qn", qi), "cmat"], writes=[PK(pr)])
                    S.add("dve", lambda e, n=n, qi=qi, ti=ti: e.tensor_tensor(out=t1[:, 0:n], in0=qn[qi][:, 0:n], in1=ropet[:, ti, 0:n], op=ALU.mult),
                          reads=[("qn", qi), "ropet"], writes=["t1"])
                    S.add("dve", lambda e, n=n, pr=pr, ti=ti: e.tensor_tensor(out=t2[:, 0:n], in0=ps[pr][:, 0:n], in1=ropet[:, ti + 1, 0:n], op=ALU.mult),
                          reads=[PK(pr), "ropet"], writes=["t2"])
                    S.add("pool", lambda e, n=n, dest=dest: e.tensor_tensor(out=dest, in0=t1[:, 0:n], in1=t2[:, 0:n], op=ALU.add),
                          reads=["t1", "t2"], writes=[dkey])
            for sb in range(n // 128):
                kc = (t0 // 128) + sb
                pv = vring.next()

                def mmv(e, pv=pv, sb=sb, xb=xb):
                    ins = None
                    for k in range(8):
                        ins = e.matmul(ps[pv][:, 0:384], lhsT=xb[:, k, sb * 128:(sb + 1) * 128], rhs=wv[:, k, :], start=(k == 0), stop=(k == 7))
                    return ins
                S.add("pe", mmv, reads=xkeys + ["wv"], writes=[PK(pv)])
                S.add("act", lambda e, pv=pv, kc=kc: e.activation(out=vaug[:, kc, :, 0:64], in_=ps[pv][:, 0:384].rearrange("p (h d) -> p h d", d=64), func=AF.Copy),
                      reads=[PK(pv)], writes=[("vaug", kc)])

        dma_w(wq[:], W[:, 0:512].rearrange("(k p) n -> p k n", p=128), [], ["wq"])
        dma_w(wk[:], W[:, 512:896].rearrange("(k p) n -> p k n", p=128), [], ["wk"])
        dma_w(wv[:], W[:, 896:1280].rearrange("(k p) n -> p k n", p=128), [], ["wv"])
        pass1_pre(blocks[0])
        for i_, tb in enumerate(blocks):
            if i_ + 1 < len(blocks):
                pass1_pre(blocks[i_ + 1])
            pass1(tb)
        S.barrier()
        A.release(m1)
        if stop == f"p1_{l}":
            pass

        woh = A.t("woh", [128, 4, 1024], BF16)
        cat = [A.t(f"cat{i}", [128, 4, 512], BF16) for i in range(2)]
        lnb = A.t("lnb", [64, 512], F32)
        pT = [A.t(f"pT{i}", [128, 512], BF16) for i in range(3)]
        rden = [A.t(f"rden{i}", [64, 512], F32) for i in range(2)]
        tt0 = A.t("tt0", [64, 512], F32)
        tt1 = A.t("tt1", [64, 512], F32)
        od_ = A.t("od_", [64, 512], F32)
        odn = A.t("odn", [64, 512], F32)
        sq64 = A.t("sq64", [64, 512], BF16)
        qpad = [A.t(f"qpad{i}", [128, 512], BF16) for i in range(3)]
        qpr = Ring([0, 1, 2])
        dma_w(woh[:], w_out[l][0:512, :].rearrange("(c p) n -> p c n", p=128), [], ["woh"])
        sring = Ring([0, 1, 6])
        oring = Ring([2, 3, 4, 5])
        ptr = Ring([0, 1, 2])
        rdr = Ring([0, 1])
        qblocks = [0, 1, 2, 3] if last else [0, 1, 2, 3, 4]
        steps = []

        def qblock(qi_, tb):
            t0, n = TBS[tb]
            kcs = list(range(18)) if tb != 4 else [16, 17]
            ct = cat[qi_ % 2]
            ci_ = qi_ % 2

            def warm():
                wbk = sring.next()

                def burst(e):
                    ins = None
                    for _ in range(WARM_N):
                        ins = e.matmul(ps[wbk][:, 0:512], lhsT=woh[:, 0, 0:128], rhs=woh[:, 1, 0:512], start=True, stop=True)
                    return ins
                S.add("pe", burst, reads=["woh"], writes=[PK(wbk)])
            if WARM_N > 0:
                steps.append((warm, lambda: None, lambda: None, None))

            hm_list = []

            def prep_qpad(hm):
                qi = qpr.next()
                hm["qp"] = qi
                r0, r1 = hm["krows"]
                S.add("pool", lambda e: e.memset(qpad[qi][:, 0:n], 0.0), writes=[("qpad", qi)])
                S.add("pool", lambda e: e.tensor_copy(out=qpad[qi][r0:r1, 0:n], in_=qT[r0:r1, hm["qchunk"], t0:t0 + n]),
                      reads=[("qT", hm["qchunk"], tb)], writes=[("qpad", qi)])

            def head_pass(krows, kchunk, qchunk, vslot, scale, po, tp, post):
                hm = dict(krows=krows, qchunk=qchunk)
                hm_list.append(hm)
                myidx = len(hm_list) - 1
                for ii, kc in enumerate(kcs):
                    st = {}
                    ktb = kc // 4 if kc < 16 else 4

                    def qk(st=st, kc=kc, ktb=ktb, ii=ii):
                        if ii == 0:
                            if "qp" not in hm:
                                prep_qpad(hm)
                            if myidx + 1 < len(hm_list) and "qp" not in hm_list[myidx + 1]:
                                prep_qpad(hm_list[myidx + 1])
                        sb_ = sring.next()
                        st["sb"] = sb_
                        qi = hm["qp"]

                        def mms(e):
                            return e.matmul(ps[sb_][:, 0:n], lhsT=kT[:, kchunk, kc * 128:(kc + 1) * 128],
                                            rhs=qpad[qi][:, 0:n], start=True, stop=True)
                        S.add("pe", mms, reads=[("kT", kchunk, ktb), ("qpad", qi)], writes=[PK(sb_)])

                    def ex(st=st):
                        sb_ = st["sb"]
                        pi = ptr.next()
                        st["pi"] = pi
                        S.add("act", lambda e: e.activation(out=pT[pi][:, 0:n], in_=ps[sb_][:, 0:n], func=AF.Exp, scale=scale),
                              reads=[PK(sb_)], writes=[("pT", pi)])

                    def pv(st=st, kc=kc, ii=ii):
                        pi = st["pi"]
                        S.add("pe", lambda e: e.matmul(ps[po][:, 0:n], lhsT=vaug[:, kc, vslot, :], rhs=pT[pi][:, 0:n],
                                                       start=(ii == 0), stop=(ii == len(kcs) - 1)),
                              reads=[("pT", pi), ("vaug", kc)], writes=[PK(po)])
                    steps.append((qk, ex, pv, post if ii == len(kcs) - 1 else None))

            for h in range(4):
                r0 = 64 * (h // 2)
                po = oring.next()

                def postA(po=po, h=h):
                    ri = rdr.next()
                    S.add("dve", lambda e: e.reciprocal(out=rden[ri][:, 0:n], in_=ps[po][64:128, 0:n]), reads=[PK(po)], writes=[("rden", ri)])
                    p0 = 64 * (h % 2)
                    S.add("dve", lambda e: e.tensor_tensor(out=ct[p0:p0 + 64, h // 2, 0:n], in0=ps[po][0:64, 0:n], in1=rden[ri][:, 0:n], op=ALU.mult),
                          reads=[PK(po), ("rden", ri)], writes=[("cat", ci_, h)])
                    return []
                head_pass((r0, r0 + 64), 0, h % 2, h // 2, 0.125, po, None, postA)
            for h in range(4):
                base = 64 * (h % 2)
                pos = [oring.next(), oring.next()]

                def postC(pos=pos, h=h):
                    ri0 = rdr.next()
                    ri1 = rdr.next()
                    p0 = 64 * (h % 2)
                    S.add("dve", lambda e: e.reciprocal(out=rden[ri0][:, 0:n], in_=ps[pos[0]][64:128, 0:n]), reads=[PK(pos[0])], writes=[("rden", ri0)])
                    S.add("dve", lambda e: e.tensor_tensor(out=tt0[:, 0:n], in0=ps[pos[0]][0:64, 0:n], in1=rden[ri0][:, 0:n], op=ALU.mult),
                          reads=[PK(pos[0]), ("rden", ri0)], writes=["tt0"])
                    S.add("dve", lambda e: e.reciprocal(out=rden[ri1][:, 0:n], in_=ps[pos[1]][64:128, 0:n]), reads=[PK(pos[1])], writes=[("rden", ri1)])
                    S.add("dve", lambda e: e.tensor_tensor(out=tt1[:, 0:n], in0=ps[pos[1]][0:64, 0:n], in1=rden[ri1][:, 0:n], op=ALU.mult),
                          reads=[PK(pos[1]), ("rden", ri1)], writes=["tt1"])
                    S.add("dve", lambda e: e.scalar_tensor_tensor(out=od_[:, 0:n], in0=tt1[:, 0:n], scalar=neglam[:, l:l + 1], in1=tt0[:, 0:n],
                                                                    op0=ALU.mult, op1=ALU.add),
                          reads=["tt0", "tt1", "neglam"], writes=["od_"])

                    def st1():
                        S.add("act", lambda e: e.activation(out=sq64[:, 0:n], in_=od_[:, 0:n], func=AF.Square), reads=["od_"], writes=["sq64"])

                    def st2():
                        S.add("pe", lambda e: e.matmul(ps[7][0:64, 0:n], lhsT=blk64[0:64, 0:64], rhs=sq64[:, 0:n], start=True, stop=True),
                              reads=["sq64", "cmat"], writes=[PK(7)])

                    def st3():
                        S.add("act", lambda e: e.activation(out=lnb[:, 0:n], in_=ps[7][0:64, 0:n], func=AF.Ln, bias=epsc[0:64, 0:1], scale=1.0),
                              reads=[PK(7), "epsc"], writes=["lnb"])
                        S.add("act", lambda e: e.activation(out=odn[:, 0:n], in_=lnb[:, 0:n], func=AF.Exp, scale=-0.5), reads=["lnb"], writes=["odn"])

                    def st4():
                        S.add("dve", lambda e: e.scalar_tensor_tensor(out=ct[p0:p0 + 64, 2 + h // 2, 0:n], in0=od_[:, 0:n], scalar=gsub1m[:, l:l + 1],
                                                                        in1=odn[:, 0:n], op0=ALU.mult, op1=ALU.mult),
                              reads=["od_", "odn", "gsub1m"], writes=[("cat", ci_, 4 + h)])
                    tasks = [(20, st1), (23, st2), (26, st3), (30, st4)]
                    if h == 3:
                        def outproj():
                            for oc in range(8):
                                pb = 7

                                def mmo(e, oc=oc, pb=pb):
                                    ins = None
                                    for hh in range(4):
                                        ins = e.matmul(ps[pb][:, 0:n], lhsT=woh[:, hh, oc * 128:(oc + 1) * 128], rhs=ct[:, hh, 0:n],
                                                       start=(hh == 0), stop=(hh == 3))
                                    return ins
                                S.add("pe", mmo, reads=[("cat", ci_, hh) for hh in range(8)] + ["woh"], writes=[PK(pb)])
                                resid_add(bl, l, 1, tb, oc, pb, n)
                        tasks.append((34, outproj))
                    return tasks
                postC.is_c = True
                for c in range(2):
                    r0 = base + 32 * c
                    head_pass((r0, r0 + 32), 1 + h // 2, 2 + h // 2, 2 + h, 32 ** -0.5, pos[c], (r0, 0), postC if c == 1 else None)

        for qi_, tb in enumerate(qblocks):
            qblock(qi_, tb)
        LA = 2
        deferred = []
        NS = len(steps)
        for i in range(min(LA, NS)):
            steps[i][0]()
        for i in range(NS):
            if i + LA < NS:
                steps[i + LA][0]()
            steps[i][1]()
            steps[i][2]()
            if steps[i][3] is not None:
                if getattr(steps[i][3], "is_c", False):
                    for d in sorted(deferred, key=lambda d: d[0]):
                        d[1]()
                    deferred = []
                for (dl, fn) in steps[i][3]():
                    deferred.append((i + dl, fn))
            ready = [d for d in deferred if d[0] <= i]
            deferred = [d for d in deferred if d[0] > i]
            for d in ready:
                d[1]()
        for d in sorted(deferred, key=lambda d: d[0]):
            d[1]()
        S.barrier()
        A.release(m)

        m2 = A.mark()
        blocks2 = [0, 1, 2, 3] if last else [0, 1, 2, 3, 4]
        xnb = [A.t(f"xnb{i}", [128, 8, 512], BF16) for i in range(2)]
        wu = A.t("wu", [128, 8, 256], BF16)
        wvg = A.t("wvg", [128, 8, 256], BF16)
        wgl = A.t("wgl", [128, 8, 512], BF16)
        wob = A.t("wob", [64, 4, 1024], BF16)
        wod = A.t("wod", [128, 2, 1024], BF16)
        wsT = A.t("wsT", [128, 4, 128], BF16)
        gvb = A.t("gvb", [128, 256], F32)
        bst = A.t("bst", [64, 4, 512], F32)
        diag = A.t("diag", [128, 2, 31, 128], BF16)
        ypl = A.t("ypl", [128, 2, NLAT + 30], BF16)
        ypc = A.t("ypc", [128, 2, NCTX + 30], BF16)
        ug = A.t("ug", [64, 4, 512], F32)
        vge = A.t("vge", [128, 256], F32)
        junk = A.t("junk", [128, 256], BF16)
        ssum = A.t("ssum", [128, 2], F32)
        vn = A.t("vn", [128, 4, 256], BF16)
        tmb = A.t("tmb", [64, 512], F32)
        catb = A.t("catb", [64, 4, 512], BF16)
        sg = A.t("sg", [128, 512], F32)
        zz = A.t("zz", [128, 2, 512], F32)
        sqz = A.t("sqz", [128, 2, 512], BF16)
        odd = A.t("odd", [128, 2, 512], BF16)
        dma_w(wob[:], w_out[l][512:768, :].rearrange("(g d) n -> d g n", d=64), [], ["wob"])
        dma_w(wod[:], w_out[l][768:1024, :].rearrange("(c p) n -> p c n", p=128), [], ["wod"])
        dma_w(wsT[:], wsT_d[l], [], ["wsT"])
        dma_sp(gvb[:], gv_d[:, l, :], [], ["gvb"])
        dma_sp(bst[:], bs_d[:, l, :, :], [], ["bst"])
        for c in range(2):
            for k in range(31):
                S.add("dve", lambda e, c=c, k=k: e.tensor_scalar(out=diag[:, c, k, :], in0=identF[:], scalar1=wdw[:, l, c, k:k + 1], scalar2=None, op0=ALU.mult),
                      reads=["identF", "wdw"], writes=[("diag", c)])
        S.add("pool", lambda e: e.memset(ypl[:], 0.0), writes=[("ypl", c, tb) for c in range(2) for tb in range(4)] + ["yplpad"])
        S.add("pool", lambda e: e.memset(ypc[:], 0.0), writes=[("ypc", c) for c in range(2)])
        pr2 = Ring([0, 1, 2, 3])
        outr = Ring([4, 5])
        def pass2a_pre(tb):
            t0, n = TBS[tb]
            xb = xnb[tb % 2]
            xkeys = [("xnb", tb % 2, c) for c in range(8)]
            dma_sp(xb[:, :, 0:n], xscr[:, :, t0:t0 + n], [("xscr", tb)], xkeys)

        def pass2a(tb):
            t0, n = TBS[tb]
            xb = xnb[tb % 2]
            xkeys = [("xnb", tb % 2, c) for c in range(8)]
            for g in range(4):
                pb = pr2.next()

                def mmu(e, pb=pb, g=g, xb=xb, n=n):
                    ins = None
                    for k in range(8):
                        ins = e.matmul(ps[pb][0:64, 0:n], lhsT=wu[:, k, g * 64:(g + 1) * 64], rhs=xb[:, k, 0:n], start=(k == 0), stop=(k == 7))
                    return ins
                S.add("pe", mmu, reads=xkeys + ["wu"], writes=[PK(pb)])
                S.add("act", lambda e, pb=pb, g=g, n=n: e.activation(out=ug[:, g, 0:n], in_=ps[pb][0:64, 0:n], func=AF.Gelu_apprx_tanh),
                      reads=[PK(pb)], writes=[("ug", g)])
            nsb = n // 128
            for sb in range(nsb):
                pb = pr2.next()

                def mmv(e, pb=pb, sb=sb, xb=xb):
                    ins = None
                    for k in range(8):
                        ins = e.matmul(ps[pb][:, 0:256], lhsT=xb[:, k, sb * 128:(sb + 1) * 128], rhs=wvg[:, k, :], start=(k == 0), stop=(k == 7))
                    return ins
                S.add("pe", mmv, reads=xkeys + ["wvg"], writes=[PK(pb)])
                S.add("act", lambda e, pb=pb: e.activation(out=vge[:], in_=ps[pb][:, 0:256], func=AF.Gelu_apprx_tanh), reads=[PK(pb)], writes=["vge"])
                S.add("dve", lambda e: e.memset(ssum[:], 0.0), writes=["ssum"])
                S.add("act", lambda e: e.activation(out=junk[:], in_=vge[:], func=AF.Square, accum_out=ssum[:, 0:1]), reads=["vge", "ssum"], writes=["junk", "ssum"])
                S.add("act", lambda e: e.activation(out=ssum[:, 1:2], in_=ssum[:, 0:1], func=AF.Sqrt, bias=epsc[:, 0:1], scale=1.0 / 256.0),
                      reads=["ssum", "epsc"], writes=["ssum"])
                S.add("dve", lambda e: e.reciprocal(out=ssum[:, 1:2], in_=ssum[:, 1:2]), reads=["ssum"], writes=["ssum"])
                S.add("dve", lambda e, sb=sb: e.scalar_tensor_tensor(out=vn[:, sb, :], in0=vge[:], scalar=ssum[:, 1:2], in1=gvb[:], op0=ALU.mult, op1=ALU.mult),
                      reads=["vge", "ssum", "gvb"], writes=[("vn", sb)])
            for g in range(4):
                pb = pr2.next()

                def mmm(e, pb=pb, g=g, nsb=nsb):
                    ins = None
                    for sb in range(nsb):
                        ins = e.matmul(ps[pb][0:64, sb * 128:(sb + 1) * 128], lhsT=vn[:, sb, g * 64:(g + 1) * 64], rhs=wsT[:, g, :], start=True, stop=True)
                    return ins
                S.add("pe", mmm, reads=[("vn", sb) for sb in range(nsb)] + ["wsT"], writes=[PK(pb)])
                S.add("dve", lambda e, pb=pb, g=g, n=n: e.tensor_tensor(out=tmb[:, 0:n], in0=ps[pb][0:64, 0:n], in1=bst[:, g, 0:n], op=ALU.add),
                      reads=[PK(pb), "bst"], writes=["tmb"])
                S.add("pool", lambda e, g=g, n=n: e.tensor_tensor(out=catb[:, g, 0:n], in0=ug[:, g, 0:n], in1=tmb[:, 0:n], op=ALU.mult),
                      reads=["tmb", ("ug", g)], writes=[("catb", g)])
            for oc in range(8):
                po = outr.next()

                def mmo(e, po=po, oc=oc, n=n):
                    ins = None
                    for g in range(4):
                        ins = e.matmul(ps[po][:, 0:n], lhsT=wob[:, g, oc * 128:(oc + 1) * 128], rhs=catb[:, g, 0:n], start=(g == 0), stop=(g == 3))
                    return ins
                S.add("pe", mmo, reads=[("catb", g) for g in range(4)] + ["wob"], writes=[PK(po)])
                if "B" in DEBUG_PARTS:
                    resid_add(bl, l, 1, tb, oc, po, n)
            for c in range(2):
                pa = pr2.next()
                pg = pr2.next()

                def mmg(e, pbk, co, xb=xb, n=n):
                    ins = None
                    for k in range(8):
                        ins = e.matmul(ps[pbk][:, 0:n], lhsT=wgl[:, k, co:co + 128], rhs=xb[:, k, 0:n], start=(k == 0), stop=(k == 7))
                    return ins
                S.add("pe", lambda e, pa=pa, c=c, mmg=mmg: mmg(e, pa, c * 128), reads=xkeys + ["wgl"], writes=[PK(pa)])
                S.add("pe", lambda e, pg=pg, c=c, mmg=mmg: mmg(e, pg, 256 + c * 128), reads=xkeys + ["wgl"], writes=[PK(pg)])
                S.add("act", lambda e, pg=pg, n=n: e.activation(out=sg[:, 0:n], in_=ps[pg][:, 0:n], func=AF.Sigmoid), reads=[PK(pg)], writes=["sg"])
                if tb != 4:
                    S.add("dve", lambda e, pa=pa, c=c, t0=t0, n=n: e.tensor_tensor(out=ypl[:, c, 15 + t0:15 + t0 + n], in0=ps[pa][:, 0:n], in1=sg[:, 0:n], op=ALU.mult),
                          reads=[PK(pa), "sg", "yplpad"], writes=[("ypl", c, tb)])
                else:
                    S.add("dve", lambda e, pa=pa, c=c, n=n: e.tensor_tensor(out=ypc[:, c, 15:15 + n], in0=ps[pa][:, 0:n], in1=sg[:, 0:n], op=ALU.mult),
                          reads=[PK(pa), "sg"], writes=[("ypc", c)])
        dma_w(wu[:], W[:, 1280:1536].rearrange("(k p) n -> p k n", p=128), [], ["wu"])
        dma_w(wvg[:], W[:, 1536:1792].rearrange("(k p) n -> p k n", p=128), [], ["wvg"])
        dma_w(wgl[:], W[:, 1792:2304].rearrange("(k p) n -> p k n", p=128), [], ["wgl"])
        pass2a_pre(blocks2[0])
        for i_, tb in enumerate(blocks2):
            if i_ + 1 < len(blocks2):
                pass2a_pre(blocks2[i_ + 1])
            pass2a(tb)

        def pass2b(tb):
            t0, n = TBS[tb]
            for c in range(2):
                pz = pr2.next()
                if tb != 4:
                    yk = [("ypl", c, t) for t in range(4)] + ["yplpad"]
                    ysrc = lambda k, c=c, t0=t0, n=n: ypl[:, c, t0 + k:t0 + k + n]
                else:
                    yk = [("ypc", c)]
                    ysrc = lambda k, c=c, n=n: ypc[:, c, k:k + n]

                def mmc(e, pz=pz, c=c, ysrc=ysrc, n=n):
                    ins = None
                    for k in range(31):
                        ins = e.matmul(ps[pz][:, 0:n], lhsT=diag[:, c, k, :], rhs=ysrc(k), start=(k == 0), stop=(k == 30))
                    return ins
                S.add("pe", mmc, reads=yk + [("diag", c)], writes=[PK(pz)])
                S.add("act", lambda e, pz=pz, c=c, n=n: e.activation(out=zz[:, c, 0:n], in_=ps[pz][:, 0:n], func=AF.Identity, bias=bdw[:, l, c:c + 1]),
                      reads=[PK(pz), "bdw"], writes=[("zz", c)])
                S.add("act", lambda e, c=c, n=n: e.activation(out=sqz[:, c, 0:n], in_=zz[:, c, 0:n], func=AF.Square), reads=[("zz", c)], writes=[("sqz", c)])
            pn = pr2.next()

            def mmn(e, pn=pn, n=n):
                ins = None
                for c in range(2):
                    ins = e.matmul(ps[pn][:, 0:n], lhsT=ones256, rhs=sqz[:, c, 0:n], start=(c == 0), stop=(c == 1))
                return ins
            S.add("pe", mmn, reads=[("sqz", 0), ("sqz", 1), "cmat"], writes=[PK(pn)])
            rsqrt_from_psum(pn, n)
            for c in range(2):
                tx = tmpx[c]
                S.add("pool", lambda e, c=c, tx=tx, n=n: e.tensor_tensor(out=tx[:, 0:n], in0=zz[:, c, 0:n], in1=rstd[:, 0:n], op=ALU.mult),
                      reads=[("zz", c), "rstd"], writes=[("tmpx", c)])
                S.add("act", lambda e, c=c, tx=tx, n=n: e.activation(out=odd[:, c, 0:n], in_=tx[:, 0:n], func=AF.Silu, scale=gconv[:, l, c:c + 1]),
                      reads=[("tmpx", c), "gconv"], writes=[("odd", c)])
            for oc in range(8):
                po = outr.next()

                def mmo2(e, po=po, oc=oc, n=n):
                    ins = None
                    for c in range(2):
                        ins = e.matmul(ps[po][:, 0:n], lhsT=wod[:, c, oc * 128:(oc + 1) * 128], rhs=odd[:, c, 0:n], start=(c == 0), stop=(c == 1))
                    return ins
                S.add("pe", mmo2, reads=[("odd", 0), ("odd", 1), "wod"], writes=[PK(po)])
                if "D" in DEBUG_PARTS:
                    resid_add(bl, l, 1, tb, oc, po, n)

        for tb in blocks2:
            pass2b(tb)
        S.barrier()
        A.release(m2)

    done = False
    for bl in range(nb):
        for tb in range(4):
            t0_, n_ = TBS[tb]
            dma_sp(hT[:, :, t0_:t0_ + n_], xT[bl][:, t0_:t0_ + n_].rearrange("(c p) t -> p c t", p=128), [], [("h", c, tb) for c in range(8)])
        dma_sp(hT[:, :, NLAT:NTOK], ctxT[bl].rearrange("(c p) t -> p c t", p=128), [], [("h", c, 4) for c in range(8)])
        for l in range(depth):
            last = (l == DEPTH - 1)
            ffn(bl, l, 0, [0, 1, 2, 3, 4], ada_next=(l + 1 if (bl == 0 and l + 1 < depth) else None))
            if stop == f"ffn1_{l}":
                dump_and_end(bl)
                done = True
                break
            mixer(bl, l)
            if stop == f"mix_{l}":
                dump_and_end(bl)
                done = True
                break
            ffn(bl, l, 1, [0, 1, 2, 3] if last else [0, 1, 2, 3, 4])
            if stop == f"ffn2_{l}":
                dump_and_end(bl)
                done = True
                break
        if done:
            continue
        mf = A.mark()
        ob = [A.t(f"ob{i}", [128, 8, 512], F32) for i in range(2)]
        for tb in range(4):
            t0, n = TBS[tb]
            o = ob[tb % 2]
            make_xn(bl, 0, 0, tb, lambda c, o=o: o[:, c, :], lambda c, tb=tb: ("ob", tb % 2, c), final=True)
            dma_sp(outT[bl][:, t0:t0 + n].rearrange("(c p) t -> p c t", p=128), o[:], [("ob", tb % 2, c) for c in range(8)], [("ob", tb % 2, c) for c in range(8)] + ["outT"])
        S.barrier()
        A.release(mf)
    S.add("sp", lambda e: None, reads=["outT" if stop is None else "dbg"])
    S.emit()
    return nc


def _rope_tables():
    t = np.arange(NLAT)
    row = (t // 64).astype(np.float32)
    col = (t % 64).astype(np.float32)
    out = np.zeros((4, 128, NLAT), np.float32)
    for ti, hd in ((0, 64), (2, 32)):
        quarter = hd // 4
        inv = (np.float32(10000.0) ** (-np.arange(quarter, dtype=np.float32) / np.float32(quarter))).astype(np.float32)
        ang = np.concatenate([row[:, None] * inv[None, :], col[:, None] * inv[None, :]], axis=-1).astype(np.float32)
        cos = np.cos(ang).astype(np.float32)
        sin = np.sin(ang).astype(np.float32)
        half = hd // 2
        for p in range(128):
            d = p % hd
            j = d % half
            out[ti, p] = cos[:, j]
            out[ti + 1, p] = (-sin[:, j]) if d < half else sin[:, j]
    return out


def _const_mats():
    m = np.zeros((6, 128, 128), np.float32)
    m[0] = 1.0 / 1024.0
    m[1, 0:64, 0:64] = 1.0 / 64.0
    m[1, 64:128, 64:128] = 1.0 / 64.0
    m[2] = 1.0 / 256.0
    for mm_ in range(128):
        pa = mm_ + 32 if (mm_ % 64) < 32 else mm_ - 32
        m[3, pa, mm_] = 1.0
        pc = mm_ + 16 if (mm_ % 32) < 16 else mm_ - 16
        m[4, pc, mm_] = 1.0
    m[5] = np.eye(128, dtype=np.float32)
    return m


def _col_perm():
    qa = lambda h: list(range(h * 64, (h + 1) * 64))
    perm = qa(0) + qa(2) + qa(1) + qa(3)
    perm += list(range(256, 512))
    perm += list(range(512, 640))
    perm += list(range(768, 1024))
    perm += list(range(640, 768))
    perm += list(range(1024, 1280))
    perm += list(range(1280, 2304))
    return np.array(perm)


def prep_shared(inp):
    f = lambda a: np.ascontiguousarray(np.asarray(a, dtype=np.float32))
    sh = {}
    sh["w_ada"] = f(inp["w_ada"])
    sh["b_adaT"] = f(np.asarray(inp["b_ada"]).reshape(DEPTH, 72, 128).transpose(2, 0, 1))
    sh["g_normT"] = f(np.asarray(inp["g_norm"]).reshape(DEPTH, 3, 8, 128).transpose(3, 0, 1, 2))
    sh["g_finalT"] = f(np.asarray(inp["g_final"]).reshape(8, 128).T)
    for k in ("w_ff1_in", "w_ff1_out", "w_ff2_in", "w_ff2_out", "w_out"):
        sh[k] = f(inp[k])
    sh["w_in_p"] = f(np.asarray(inp["w_in"])[:, :, _col_perm()])
    gq = np.asarray(inp["g_q_a"])
    gk = np.asarray(inp["g_k_a"])
    gqk = np.stack([np.tile(gq, (1, 2)), np.tile(gk, (1, 2))], axis=-1)
    sh["gqk"] = f(gqk.transpose(1, 0, 2))
    sh["lamw"] = f(np.broadcast_to(np.asarray(inp["lam_c"])[None], (64, DEPTH, 4, 32)))
    sh["gsub"] = f(np.asarray(inp["g_sub_c"]).T)
    sh["gv_bc"] = f(np.broadcast_to(np.asarray(inp["g_v_b"])[None], (128, DEPTH, 256)))
    sh["wsT"] = f(np.asarray(inp["w_s_b"]).transpose(0, 3, 1, 2))
    bs = np.asarray(inp["b_s_b"])
    sh["bs_tbl"] = f(np.broadcast_to(np.tile(bs, (1, 1, 4))[None], (64, DEPTH, 4, 512)))
    sh["wdw"] = f(np.asarray(inp["w_dw_d"]).reshape(DEPTH, 31, 2, 128).transpose(3, 0, 2, 1))
    sh["bdw"] = f(np.asarray(inp["b_dw_d"]).reshape(DEPTH, 2, 128).transpose(2, 0, 1))
    sh["gconv"] = f(np.asarray(inp["g_conv_d"]).reshape(DEPTH, 2, 128).transpose(2, 0, 1))
    sh["rope"] = _rope_tables()
    sh["cmat"] = _const_mats()
    return sh


def prep_core(inp, bids):
    x = np.asarray(inp["x"])
    ctx = np.asarray(inp["ctx"])
    c = np.asarray(inp["c"])
    cc = np.asarray(inp["c_ctx"])
    d = {}
    d["xT"] = np.ascontiguousarray(np.stack([x[b].T for b in bids]).astype(np.float32))
    d["ctxT"] = np.ascontiguousarray(np.stack([ctx[b].T for b in bids]).astype(np.float32))
    vecs = [c[b] for b in bids]
    while len(vecs) < 2:
        vecs.append(c[bids[0]])
    vecs.append(cc)
    cT = np.stack(vecs, axis=-1).reshape(8, 128, 3).transpose(1, 0, 2)
    d["cT"] = np.ascontiguousarray(cT.astype(np.float32))
    return d


_NC_CACHE = {}


def kernel(**inputs):
    B = np.asarray(inputs["x"]).shape[0]
    nb = B // NCORES
    if "prog" not in _NC_CACHE:
        _NC_CACHE["prog"] = build_program(nb=nb)
    nc = _NC_CACHE["prog"]
    sh = prep_shared(inputs)
    in_maps = []
    for i in range(NCORES):
        d = dict(sh)
        d.update(prep_core(inputs, list(range(i * nb, (i + 1) * nb))))
        in_maps.append(d)
    res = run_bass_kernel_spmd(nc, in_maps, core_ids=list(range(NCORES)))
    out = np.empty((B, NLAT, D), np.float32)
    for i in range(NCORES):
        o = res.results[i]["outT"]
        for jb in range(nb):
            out[i * nb + jb] = o[jb].T
    return out
```

```python
import math
import contextlib
import numpy as np
import ml_dtypes
import concourse.bass as bass
import concourse.mybir as mybir
from concourse.bass_utils import run_bass_kernel_spmd

F32 = mybir.dt.float32
BF16 = mybir.dt.bfloat16
AF = mybir.ActivationFunctionType
ALU = mybir.AluOpType

ENGS = ["pe", "act", "dve", "pool", "sp"]

D = 1024
NLAT = 2048
NCTX = 256
NTOK = NLAT + NCTX
DFF = 2816
DEPTH = 2
EPS = 1e-6
NCORES = 8
TBS = [(0, 512), (512, 512), (1024, 512), (1536, 512), (2048, 256)]
DEBUG_PARTS = set("ACBD")
WARM_N = 0


class Op:
    __slots__ = ("eng", "fn", "deps", "sig", "sigval", "ch", "inc", "dma", "idx")


class Sched:
    def __init__(self, nc):
        self.nc = nc
        self.ops = {e: [] for e in ENGS}
        self.last_w = {}
        self.readers = {}
        self.ch_eng = {}
        self.pending_bar = {e: [] for e in ENGS}
        self.nops = 0
        self.last_on_ch = {}
        self.dma_ring = {}
        self.DMA_RING = {"sp": 16, "pool": 24}

    def add(self, eng, fn, reads=(), writes=(), ch=None, dma=False):
        op = Op()
        op.eng = eng
        op.fn = fn
        op.sig = bool(dma)
        op.sigval = None
        op.dma = dma
        op.ch = ch if ch is not None else eng
        if dma:
            k = self.dma_ring.get(eng, 0)
            self.dma_ring[eng] = k + 1
            op.ch = f"{eng}_d{k % self.DMA_RING[eng]}"
        op.inc = 16 if dma else 1
        op.idx = self.nops
        self.nops += 1
        if op.ch in self.ch_eng:
            assert self.ch_eng[op.ch] == eng, (op.ch, eng)
        else:
            self.ch_eng[op.ch] = eng
        deps = {}
        for k in reads:
            w = self.last_w.get(k)
            if w is not None:
                deps[w.idx] = (w, True)
        for k in writes:
            w = self.last_w.get(k)
            if w is not None and w.idx not in deps:
                deps[w.idx] = (w, False)
            for r in self.readers.get(k, ()):
                if r.idx not in deps:
                    deps[r.idx] = (r, False)
        need = []
        for (d, raw) in deps.values():
            if d.eng == eng and not d.dma and not dma and not raw:
                continue
            if d.eng == eng and eng == "pe" and not d.dma and not dma:
                continue
            need.append(d)
        if dma:
            prev = self.last_on_ch.get(op.ch)
            if prev is not None:
                need.append(prev)
        for d in self.pending_bar[eng]:
            need.append(d)
        self.pending_bar[eng] = []
        for d in need:
            d.sig = True
        op.deps = need
        for k in writes:
            self.last_w[k] = op
            self.readers[k] = []
        for k in reads:
            if k in writes:
                continue
            self.readers.setdefault(k, []).append(op)
        self.ops[eng].append(op)
        self.last_on_ch[op.ch] = op
        return op

    def barrier(self):
        frontier = list(self.last_on_ch.values())
        for e in ENGS:
            self.pending_bar[e] = list(frontier)

    def emit(self):
        nc = self.nc
        chans = list(self.ch_eng.keys())
        cum = {c: 0 for c in chans}
        for e in ENGS:
            for op in self.ops[e]:
                if op.sig:
                    cum[op.ch] += op.inc
                    op.sigval = cum[op.ch]
        with contextlib.ExitStack() as st:
            sems = {c: st.enter_context(nc.semaphore("s_" + c)) for c in chans}
            block = st.enter_context(nc.Block())

            def run(engname, engine):
                waited = {}
                for op in self.ops[engname]:
                    for d in op.deps:
                        if waited.get(d.ch, 0) < d.sigval:
                            engine.wait_ge(sems[d.ch], d.sigval)
                            waited[d.ch] = d.sigval
                    ins = op.fn(engine)
                    if op.sig:
                        assert ins is not None
                        ins.then_inc(sems[op.ch], op.inc)

            @block.tensor
            def _(eng):
                run("pe", eng)

            @block.scalar
            def _(eng):
                run("act", eng)

            @block.vector
            def _(eng):
                run("dve", eng)

            @block.gpsimd
            def _(eng):
                run("pool", eng)

            @block.sync
            def _(eng):
                run("sp", eng)


class Alloc:
    def __init__(self, nc, limit=229344, base=16512):
        self.nc = nc
        self.off = base
        self.limit = limit
        self.n = 0
        self.peak = base

    def mark(self):
        return self.off

    def release(self, m):
        self.off = m

    def t(self, name, shape, dtype):
        esz = 4 if dtype == F32 else 2
        nbytes = int(np.prod(shape[1:])) * esz
        nbytes = (nbytes + 63) // 64 * 64
        assert self.off + nbytes <= self.limit, (name, self.off, nbytes, self.limit)
        self.n += 1
        h = self.nc.alloc_sbuf_tensor_at(f"{name}_{self.n}", list(shape), dtype, offset=self.off)
        self.off += nbytes
        self.peak = max(self.peak, self.off)
        return h


class Ring:
    def __init__(self, items):
        self.items = list(items)
        self.i = 0

    def next(self):
        v = self.items[self.i % len(self.items)]
        self.i += 1
        return v


def build_program(nb=2, depth=DEPTH, stop=None):
    nc = bass.Bass("TRN2", target_bir_lowering=False)
    S = Sched(nc)
    A = Alloc(nc)

    def din(name, shape, dt=F32):
        return nc.dram_tensor(name, list(shape), dt, kind="ExternalInput").ap()

    xT = din("xT", [nb, D, NLAT])
    ctxT = din("ctxT", [nb, D, NCTX])
    cT_d = din("cT", [128, 8, 3])
    w_ada = din("w_ada", [DEPTH, D, 9 * D])
    b_adaT_d = din("b_adaT", [128, DEPTH, 72])
    g_normT_d = din("g_normT", [128, DEPTH, 3, 8])
    g_finalT_d = din("g_finalT", [128, 8])
    w_ff_in = [din("w_ff1_in", [DEPTH, D, 2 * DFF]), din("w_ff2_in", [DEPTH, D, 2 * DFF])]
    w_ff_out = [din("w_ff1_out", [DEPTH, DFF, D]), din("w_ff2_out", [DEPTH, DFF, D])]
    w_in_p = din("w_in_p", [DEPTH, D, 2304])
    w_out = din("w_out", [DEPTH, D, D])
    gqk_d = din("gqk", [128, DEPTH, 2])
    lamw_d = din("lamw", [64, DEPTH, 4, 32])
    gsub_d = din("gsub", [64, DEPTH])
    gv_d = din("gv_bc", [128, DEPTH, 256])
    wsT_d = din("wsT", [DEPTH, 128, 4, 128])
    bs_d = din("bs_tbl", [64, DEPTH, 4, 512])
    wdw_d = din("wdw", [128, DEPTH, 2, 31])
    bdw_d = din("bdw", [128, DEPTH, 2])
    gconv_d = din("gconv", [128, DEPTH, 2])
    rope_d = din("rope", [4, 128, NLAT])
    cmat_d = din("cmat", [6, 128, 128])
    xscr = nc.dram_tensor("xscr", [128, 8, NTOK], BF16, kind="ExternalOutput").ap()
    if stop is None:
        outT = nc.dram_tensor("outT", [nb, D, NLAT], F32, kind="ExternalOutput").ap()
    else:
        dbg = nc.dram_tensor("dbg", [nb, 128, 8, NTOK], F32, kind="ExternalOutput").ap()

    hT = A.t("hT", [128, 8, NTOK], F32)
    cmat = A.t("cmat", [128, 6, 128], BF16)
    identF = A.t("identF", [128, 128], F32)
    modall = A.t("modall", [128, DEPTH, 72, 3], F32)
    gs = A.t("gs", [128, DEPTH, 3, 8, 3], F32)
    hg = A.t("hg", [128, DEPTH, 3, 8, 3], F32)
    b_adaT = A.t("b_adaT", [128, DEPTH, 72], F32)
    g_normT = A.t("g_normT", [128, DEPTH, 3, 8], F32)
    g_finalT = A.t("g_finalT", [128, 8], F32)
    gqk = A.t("gqk", [128, DEPTH, 2], F32)
    gsub = A.t("gsub", [64, DEPTH], F32)
    gsub1m = A.t("gsub1m", [64, DEPTH], F32)
    neglam = A.t("neglam", [64, DEPTH], F32)
    wdw = A.t("wdw", [128, DEPTH, 2, 31], F32)
    bdw = A.t("bdw", [128, DEPTH, 2], F32)
    gconv = A.t("gconv", [128, DEPTH, 2], F32)
    epsc = A.t("epsc", [128, 1], F32)
    scT = A.t("scT", [128, 8, 3], BF16)
    sq = A.t("sq", [128, 8, 512], BF16)
    rt = A.t("rt", [128, 512], F32)
    rstd = A.t("rstd", [128, 512], F32)
    tmpx = [A.t("tmpx0", [128, 512], F32), A.t("tmpx1", [128, 512], F32)]
    PERSIST = A.mark()

    ps = [nc.alloc_psum_tensor(f"ps{i}", [128, 512], F32) for i in range(8)]
    onesD = cmat[:, 0, :]
    blk64 = cmat[:, 1, :]
    ones256 = cmat[:, 2, :]
    permA = cmat[:, 3, :]
    permC = cmat[:, 4, :]

    cnt = [0]

    def uid():
        cnt[0] += 1
        return cnt[0]

    def PK(b):
        return ("ps", b)

    def dma_sp(out, in_, reads, writes):
        S.add("sp", lambda e: e.dma_start(out=out, in_=in_), reads=reads, writes=writes, ch="dq_sp", dma=True)

    def dma_w(out, in_, reads, writes):
        S.add("pool", lambda e: e.dma_start(out=out, in_=in_), reads=reads, writes=writes, ch="dq_w", dma=True)

    dma_w(cmat[:], cmat_d.rearrange("m p n -> p m n"), [], ["cmat"])
    dma_sp(identF[:], cmat_d[5], [], ["identF"])
    dma_sp(b_adaT[:], b_adaT_d, [], ["b_adaT"])
    dma_sp(g_normT[:], g_normT_d, [], ["g_normT"])
    dma_sp(g_finalT[:], g_finalT_d, [], ["g_finalT"])
    dma_sp(gqk[:], gqk_d, [], ["gqk"])
    dma_sp(gsub[:], gsub_d, [], ["gsub"])
    dma_sp(wdw[:], wdw_d, [], ["wdw"])
    dma_sp(bdw[:], bdw_d, [], ["bdw"])
    dma_sp(gconv[:], gconv_d, [], ["gconv"])
    S.add("dve", lambda e: e.memset(epsc[:], EPS), writes=["epsc"])

    m0 = A.mark()
    cTs = A.t("cTs", [128, 8, 3], F32)
    lamw = A.t("lamw", [64, DEPTH, 4, 32], F32)
    lamp = A.t("lamp", [64, DEPTH, 2, 32], F32)
    lams = A.t("lams", [64, DEPTH, 2], F32)
    lame = A.t("lame", [64, DEPTH, 2], F32)
    wab = [A.t(f"wab{i}", [128, 8, 512], BF16) for i in range(3)]
    dma_sp(cTs[:], cT_d, [], ["cTs"])
    dma_sp(lamw[:], lamw_d, [], ["lamw"])
    S.add("act", lambda e: e.activation(out=scT[:], in_=cTs[:], func=AF.Silu), reads=["cTs"], writes=["scT"])
    def adaln_steps(l, wabufs, uidx):
        stepsl = []
        for cb in range(18):
            def one(cb=cb):
                bi = cb % len(wabufs)
                wt = wabufs[bi]
                dma_w(wt[:], w_ada[l][:, cb * 512:(cb + 1) * 512].rearrange("(k p) n -> p k n", p=128), [], [("wab", uidx, bi)])

                def mm(e):
                    ins = None
                    for cc in range(4):
                        chn = cb * 4 + cc
                        for k in range(8):
                            ins = e.matmul(ps[7][:, chn * 3:(chn + 1) * 3], lhsT=wt[:, k, cc * 128:(cc + 1) * 128],
                                           rhs=scT[:, k, :], start=(k == 0), stop=(k == 7))
                    return ins
                S.add("pe", mm, reads=[("wab", uidx, bi), "scT"], writes=[PK(7)])
            stepsl.append(one)

        def fin():
            psm = ps[7][:, 0:216].rearrange("p (c j) -> p c j", j=3)
            for j in range(3):
                S.add("dve", lambda e, j=j: e.tensor_tensor(out=modall[:, l, :, j], in0=psm[:, :, j], in1=b_adaT[:, l, :], op=ALU.add),
                      reads=[PK(7), "b_adaT"], writes=[("modall", l)])
            for s_ in range(3):
                for j in range(3):
                    S.add("dve", lambda e, s_=s_, j=j: e.tensor_scalar(out=gs[:, l, s_, :, j], in0=modall[:, l, (3 * s_ + 1) * 8:(3 * s_ + 2) * 8, j],
                                                                         scalar1=1.0, scalar2=None, op0=ALU.add),
                          reads=[("modall", l)], writes=[("gs", l)])
                    S.add("dve", lambda e, s_=s_, j=j: e.tensor_tensor(out=gs[:, l, s_, :, j], in0=gs[:, l, s_, :, j], in1=g_normT[:, l, s_, :], op=ALU.mult),
                          reads=[("gs", l), "g_normT"], writes=[("gs", l)])
                    S.add("dve", lambda e, s_=s_, j=j: e.tensor_scalar(out=hg[:, l, s_, :, j], in0=modall[:, l, (3 * s_ + 2) * 8:(3 * s_ + 3) * 8, j],
                                                                         scalar1=(1.0 if s_ == 1 else 0.5), scalar2=None, op0=ALU.mult),
                          reads=[("modall", l)], writes=[("hg", l)])
        return stepsl, fin

    st0, fin0 = adaln_steps(0, wab, 0)
    for f_ in st0:
        f_()
    fin0()
    for l in range(depth):
        lam_init = 0.8 - 0.6 * math.exp(-0.3 * l)
        for q in range(2):
            S.add("dve", lambda e, l=l, q=q: e.tensor_tensor(out=lamp[:, l, q, :], in0=lamw[:, l, 2 * q, :], in1=lamw[:, l, 2 * q + 1, :], op=ALU.mult),
                  reads=["lamw"], writes=["lamp"])
            S.add("dve", lambda e, l=l, q=q: e.tensor_reduce(out=lams[:, l, q:q + 1], in_=lamp[:, l, q, :], axis=mybir.AxisListType.X, op=ALU.add),
                  reads=["lamp"], writes=["lams"])
        S.add("act", lambda e, l=l: e.activation(out=lame[:, l, :], in_=lams[:, l, :], func=AF.Exp), reads=["lams"], writes=["lame"])
        S.add("dve", lambda e, l=l: e.tensor_tensor(out=neglam[:, l:l + 1], in0=lame[:, l, 1:2], in1=lame[:, l, 0:1], op=ALU.subtract),
              reads=["lame"], writes=["neglam"])
        S.add("dve", lambda e, l=l, li=lam_init: e.tensor_scalar(out=neglam[:, l:l + 1], in0=neglam[:, l:l + 1], scalar1=-li, scalar2=None, op0=ALU.add),
              reads=["neglam"], writes=["neglam"])
        S.add("dve", lambda e, l=l, li=lam_init: e.tensor_scalar(out=gsub1m[:, l:l + 1], in0=gsub[:, l:l + 1], scalar1=(1.0 - li), scalar2=None, op0=ALU.mult),
              reads=["gsub"], writes=["gsub1m"])
    S.barrier()
    A.release(m0)

    def hkeys(tb, cs=range(8)):
        return [("h", c, tb) for c in cs]

    def jmod(bl, tb):
        return 2 if tb == 4 else bl

    def rsqrt_from_psum(pb, n, parts=128, scale=1.0):
        S.add("act", lambda e: e.activation(out=rt[0:parts, 0:n], in_=ps[pb][0:parts, 0:n], func=AF.Sqrt, bias=epsc[0:parts, 0:1], scale=scale),
              reads=[PK(pb), "epsc"], writes=["rt"])
        S.add("dve", lambda e: e.reciprocal(out=rstd[0:parts, 0:n], in_=rt[0:parts, 0:n]), reads=["rt"], writes=["rstd"])

    def make_xn(bl, l, s, tb, dst, dkeys, final=False, pbank=7):
        t0, n = TBS[tb]
        j = jmod(bl, tb)
        for c in range(8):
            if c % 2 == 0:
                S.add("act", lambda e, c=c: e.activation(out=sq[:, c, 0:n], in_=hT[:, c, t0:t0 + n], func=AF.Square),
                      reads=[("h", c, tb)], writes=[("sq", c)])
            else:
                S.add("pool", lambda e, c=c: e.tensor_tensor(out=sq[:, c, 0:n], in0=hT[:, c, t0:t0 + n], in1=hT[:, c, t0:t0 + n], op=ALU.mult),
                      reads=[("h", c, tb)], writes=[("sq", c)])

        def mm(e):
            ins = None
            for c in range(8):
                ins = e.matmul(ps[pbank][:, 0:n], lhsT=onesD, rhs=sq[:, c, 0:n], start=(c == 0), stop=(c == 7))
            return ins
        S.add("pe", mm, reads=[("sq", c) for c in range(8)] + ["cmat"], writes=[PK(pbank)])
        rsqrt_from_psum(pbank, n)
        for c in range(8):
            tx = tmpx[c % 2]
            S.add("pool", lambda e, c=c, tx=tx: e.tensor_tensor(out=tx[:, 0:n], in0=hT[:, c, t0:t0 + n], in1=rstd[:, 0:n], op=ALU.mult),
                  reads=[("h", c, tb), "rstd"], writes=[("tmpx", c % 2)])
            if final:
                S.add("act", lambda e, c=c, tx=tx: e.activation(out=dst(c), in_=tx[:, 0:n], func=AF.Identity, scale=g_finalT[:, c:c + 1]),
                      reads=[("tmpx", c % 2), "g_finalT"], writes=[dkeys(c)])
            else:
                S.add("act", lambda e, c=c, tx=tx: e.activation(out=dst(c), in_=tx[:, 0:n], func=AF.Identity,
                                                                 scale=gs[:, l, s, c, j:j + 1], bias=modall[:, l, 3 * s * 8 + c, j:j + 1]),
                      reads=[("tmpx", c % 2), ("gs", l), ("modall", l)], writes=[dkeys(c)])

    def resid_add(bl, l, s, tb, oc, pb, n):
        t0, _ = TBS[tb]
        j = jmod(bl, tb)
        S.add("dve", lambda e: e.scalar_tensor_tensor(out=hT[:, oc, t0:t0 + n], in0=ps[pb][:, 0:n], scalar=hg[:, l, s, oc, j:j + 1],
                                                       in1=hT[:, oc, t0:t0 + n], op0=ALU.mult, op1=ALU.add),
              reads=[PK(pb), ("hg", l), ("h", oc, tb)], writes=[("h", oc, tb)])

    def dump_and_end(bl):
        dma_sp(dbg[bl], hT[:], [("h", c, tb) for c in range(8) for tb in range(5)], ["dbg"])

    def ffn(bl, l, which, blocks, ada_next=None):
        s = 0 if which == 0 else 2
        m = A.mark()
        extra, extra_fin = [], None
        if ada_next is not None:
            wabx = [A.t(f"wabx{i}", [128, 8, 512], BF16) for i in range(3)]
            extra, extra_fin = adaln_steps(ada_next, wabx, uid())
        xn = A.t("xn", [128, 8, NTOK], BF16)
        wa = [A.t(f"wa{i}", [128, 8, 256], BF16) for i in range(2)]
        wb = [A.t(f"wb{i}", [128, 8, 256], BF16) for i in range(2)]
        wo = [A.t(f"wo{i}", [128, 2, 1024], BF16) for i in range(2)]
        mid = [A.t(f"mid{i}", [128, 2, NTOK], BF16) for i in range(2)]
        sa = [A.t(f"sa{i}", [128, 512], F32) for i in range(2)]
        u = uid()
        for tb in blocks:
            t0, n = TBS[tb]
            make_xn(bl, l, s, tb, lambda c, t0=t0, n=n: xn[:, c, t0:t0 + n], lambda c, tb=tb: ("xn", u, c, tb))
        win = w_ff_in[which][l]
        wout = w_ff_out[which][l]
        NG = DFF // 256
        ring1 = Ring([0, 1, 2, 3])
        ring2 = Ring([4, 5, 6])
        sar = Ring([0, 1])

        def load(g):
            bi = g % 2
            dma_w(wa[bi][:], win[:, g * 256:(g + 1) * 256].rearrange("(k p) n -> p k n", p=128), [], [("wa", bi)])
            dma_w(wb[bi][:], win[:, DFF + g * 256:DFF + (g + 1) * 256].rearrange("(k p) n -> p k n", p=128), [], [("wb", bi)])
            dma_w(wo[bi][:], wout[g * 256:(g + 1) * 256, :].rearrange("(j p) n -> p j n", p=128), [], [("wo", bi)])

        def phase1(g):
            bi = g % 2
            for tb in blocks:
                t0, n = TBS[tb]
                for jj in range(2):
                    pa = ring1.next()
                    pb = ring1.next()

                    def mm(e, w, pbk, jj=jj, t0=t0, n=n):
                        ins = None
                        for k in range(8):
                            ins = e.matmul(ps[pbk][:, 0:n], lhsT=w[:, k, jj * 128:(jj + 1) * 128], rhs=xn[:, k, t0:t0 + n],
                                           start=(k == 0), stop=(k == 7))
                        return ins
                    xk = [("xn", u, c, tb) for c in range(8)]
                    S.add("pe", lambda e, w=wa[bi], pbk=pa, mm=mm: mm(e, w, pbk), reads=xk + [("wa", bi)], writes=[PK(pa)])
                    S.add("pe", lambda e, w=wb[bi], pbk=pb, mm=mm: mm(e, w, pbk), reads=xk + [("wb", bi)], writes=[PK(pb)])
                    si = sar.next()
                    S.add("act", lambda e, pa=pa, si=si, n=n: e.activation(out=sa[si][:, 0:n], in_=ps[pa][:, 0:n], func=AF.Silu),
                          reads=[PK(pa)], writes=[("sa", si)])
                    S.add("dve", lambda e, pb=pb, si=si, n=n, jj=jj, t0=t0: e.tensor_tensor(out=mid[bi][:, jj, t0:t0 + n], in0=sa[si][:, 0:n],
                                                                                         in1=ps[pb][:, 0:n], op=ALU.mult),
                          reads=[PK(pb), ("sa", si)], writes=[("mid", bi, jj, tb)])

        def phase2(g):
            bi = g % 2
            for tb in blocks:
                t0, n = TBS[tb]
                for oc in range(8):
                    po = ring2.next()

                    def mm(e, po=po, oc=oc, t0=t0, n=n):
                        ins = None
                        for jj in range(2):
                            ins = e.matmul(ps[po][:, 0:n], lhsT=wo[bi][:, jj, oc * 128:(oc + 1) * 128], rhs=mid[bi][:, jj, t0:t0 + n],
                                           start=(jj == 0), stop=(jj == 1))
                        return ins
                    S.add("pe", mm, reads=[("mid", bi, 0, tb), ("mid", bi, 1, tb), ("wo", bi)], writes=[PK(po)])
                    resid_add(bl, l, s, tb, oc, po, n)

        load(0)
        load(1)
        phase1(0)
        for g in range(1, NG):
            phase1(g)
            phase2(g - 1)
            if g + 1 < NG:
                load(g + 1)
            for _ in range(2):
                if extra:
                    extra.pop(0)()
        phase2(NG - 1)
        while extra:
            extra.pop(0)()
        if extra_fin is not None:
            extra_fin()
        S.barrier()
        A.release(m)

    def mixer(bl, l):
        last = (l == DEPTH - 1)
        blocks = [0, 1, 2, 3, 4]
        m = A.mark()
        qT = A.t("qT", [128, 4, NTOK], BF16)
        kT = A.t("kT", [128, 3, NTOK], BF16)
        vaug = A.t("vaug", [128, 18, 6, 128], BF16)
        m1 = A.mark()
        xnb = [A.t(f"xnb{i}", [128, 8, 512], BF16) for i in range(2)]
        wq = A.t("wq", [128, 8, 512], BF16)
        wk = A.t("wk", [128, 8, 384], BF16)
        wv = A.t("wv", [128, 8, 384], BF16)
        ropet = A.t("ropet", [128, 4, 512], F32)
        sqb = A.t("sqb", [128, 512], BF16)
        qn = [A.t(f"qn{i}", [128, 512], BF16) for i in range(2)]
        t1 = A.t("t1", [128, 512], F32)
        t2 = A.t("t2", [128, 512], F32)
        u = uid()
        S.add("pool", lambda e: e.memset(vaug[:], 1.0), writes=[("vaug", kc) for kc in range(18)])
        W = w_in_p[l]
        prj = Ring([0, 1, 2])
        aux = Ring([3, 4])
        vring = Ring([5, 6])
        qnr = Ring([0, 1])
        def pass1_pre(tb):
            t0, n = TBS[tb]
            xb = xnb[tb % 2]
            xkeys = [("xnb", tb % 2, c) for c in range(8)]
            make_xn(bl, l, 1, tb, lambda c, xb=xb, n=n: xb[:, c, 0:n], lambda c, tb=tb: ("xnb", tb % 2, c))
            dma_sp(xscr[:, :, t0:t0 + n], xb[:, :, 0:n], xkeys, [("xscr", tb)])

        def pass1(tb):
            t0, n = TBS[tb]
            xb = xnb[tb % 2]
            xkeys = [("xnb", tb % 2, c) for c in range(8)]
            if tb != 4:
                dma_sp(ropet[:], rope_d[:, :, t0:t0 + n].rearrange("m p n -> p m n"), [], ["ropet"])
            for ci in range(7):
                if ci < 4:
                    wt, wkey, co = wq, "wq", ci * 128
                    dest = qT[:, ci, t0:t0 + n]
                    dkey = ("qT", ci, tb)
                else:
                    wt, wkey, co = wk, "wk", (ci - 4) * 128
                    dest = kT[:, ci - 4, t0:t0 + n]
                    dkey = ("kT", ci - 4, tb)
                isA = ci in (0, 1, 4)
                pb = prj.next()

                def mm(e, wt=wt, co=co, pb=pb, xb=xb, n=n):
                    ins = None
                    for k in range(8):
                        ins = e.matmul(ps[pb][:, 0:n], lhsT=wt[:, k, co:co + 128], rhs=xb[:, k, 0:n], start=(k == 0), stop=(k == 7))
                    return ins
                S.add("pe", mm, reads=xkeys + [wkey], writes=[PK(pb)])
                norope = (tb == 4)
                qi = qnr.next()
                qdst = dest if norope else qn[qi][:, 0:n]
                qkey = dkey if norope else ("qn", qi)
                if isA:
                    S.add("act", lambda e, pb=pb, n=n: e.activation(out=sqb[:, 0:n], in_=ps[pb][:, 0:n], func=AF.Square),
                          reads=[PK(pb)], writes=["sqb"])
                    pa = aux.next()
                    S.add("pe", lambda e, pa=pa, n=n: e.matmul(ps[pa][:, 0:n], lhsT=blk64, rhs=sqb[:, 0:n], start=True, stop=True),
                          reads=["sqb", "cmat"], writes=[PK(pa)])
                    rsqrt_from_psum(pa, n)
                    gi = 0 if ci < 4 else 1
                    S.add("dve", lambda e, pb=pb, n=n, qdst=qdst, gi=gi: e.scalar_tensor_tensor(out=qdst, in0=ps[pb][:, 0:n], scalar=gqk[:, l, gi:gi + 1],
                                                                                         in1=rstd[:, 0:n], op0=ALU.mult, op1=ALU.mult),
                          reads=[PK(pb), "rstd", "gqk"], writes=[qkey])
                else:
                    S.add("act", lambda e, pb=pb, n=n, qdst=qdst: e.activation(out=qdst, in_=ps[pb][:, 0:n], func=AF.Copy),
                          reads=[PK(pb)], writes=[qkey])
                if not norope:
                    pr = aux.next()
                    pm = permA if isA else permC
                    ti = 0 if isA else 2
                    S.add("pe", lambda e, pr=pr, n=n, qi=qi, pm=pm: e.matmul(ps[pr][:, 0:n], lhsT=pm, rhs=qn[qi][:, 0:n], start=True, stop=True),
                          reads=[("qn", qi), "cmat"], writes=[PK(pr)])
                    S.add("dve", lambda e, n=n, qi=qi, ti=ti: e.tensor_tensor(out=t1[:, 0:n], in0=qn[qi][:, 0:n], in1=ropet[:, ti, 0:n], op=ALU.mult),
                          reads=[("qn", qi), "ropet"], writes=["t1"])
                    S.add("dve", lambda e, n=n, pr=pr, ti=ti: e.tensor_tensor(out=t2[:, 0:n], in0=ps[pr][:, 0:n], in1=ropet[:, ti + 1, 0:n], op=ALU.mult),
                          reads=[PK(pr), "ropet"], writes=["t2"])
                    S.add("pool", lambda e, n=n, dest=dest: e.tensor_tensor(out=dest, in0=t1[:, 0:n], in1=t2[:, 0:n], op=ALU.add),
                          reads=["t1", "t2"], writes=[dkey])
            for sb in range(n // 128):
                kc = (t0 // 128) + sb
                pv = vring.next()

                def mmv(e, pv=pv, sb=sb, xb=xb):
                    ins = None
                    for k in range(8):
                        ins = e.matmul(ps[pv][:, 0:384], lhsT=xb[:, k, sb * 128:(sb + 1) * 128], rhs=wv[:, k, :], start=(k == 0), stop=(k == 7))
                    return ins
                S.add("pe", mmv, reads=xkeys + ["wv"], writes=[PK(pv)])
                S.add("act", lambda e, pv=pv, kc=kc: e.activation(out=vaug[:, kc, :, 0:64], in_=ps[pv][:, 0:384].rearrange("p (h d) -> p h d", d=64), func=AF.Copy),
                      reads=[PK(pv)], writes=[("vaug", kc)])

        dma_w(wq[:], W[:, 0:512].rearrange("(k p) n -> p k n", p=128), [], ["wq"])
        dma_w(wk[:], W[:, 512:896].rearrange("(k p) n -> p k n", p=128), [], ["wk"])
        dma_w(wv[:], W[:, 896:1280].rearrange("(k p) n -> p k n", p=128), [], ["wv"])
        pass1_pre(blocks[0])
        for i_, tb in enumerate(blocks):
            if i_ + 1 < len(blocks):
                pass1_pre(blocks[i_ + 1])
            pass1(tb)
        S.barrier()
        A.release(m1)
        if stop == f"p1_{l}":
            pass

        woh = A.t("woh", [128, 4, 1024], BF16)
        cat = [A.t(f"cat{i}", [128, 4, 512], BF16) for i in range(2)]
        lnb = A.t("lnb", [64, 512], F32)
        pT = [A.t(f"pT{i}", [128, 512], BF16) for i in range(3)]
        rden = [A.t(f"rden{i}", [64, 512], F32) for i in range(2)]
        tt0 = A.t("tt0", [64, 512], F32)
        tt1 = A.t("tt1", [64, 512], F32)
        od_ = A.t("od_", [64, 512], F32)
        odn = A.t("odn", [64, 512], F32)
        sq64 = A.t("sq64", [64, 512], BF16)
        qpad = [A.t(f"qpad{i}", [128, 512], BF16) for i in range(3)]
        qpr = Ring([0, 1, 2])
        dma_w(woh[:], w_out[l][0:512, :].rearrange("(c p) n -> p c n", p=128), [], ["woh"])
        sring = Ring([0, 1, 6])
        oring = Ring([2, 3, 4, 5])
        ptr = Ring([0, 1, 2])
        rdr = Ring([0, 1])
        qblocks = [0, 1, 2, 3] if last else [0, 1, 2, 3, 4]
        steps = []

        def qblock(qi_, tb):
            t0, n = TBS[tb]
            kcs = list(range(18)) if tb != 4 else [16, 17]
            ct = cat[qi_ % 2]
            ci_ = qi_ % 2

            def warm():
                wbk = sring.next()

                def burst(e):
                    ins = None
                    for _ in range(WARM_N):
                        ins = e.matmul(ps[wbk][:, 0:512], lhsT=woh[:, 0, 0:128], rhs=woh[:, 1, 0:512], start=True, stop=True)
                    return ins
                S.add("pe", burst, reads=["woh"], writes=[PK(wbk)])
            if WARM_N > 0:
                steps.append((warm, lambda: None, lambda: None, None))

            hm_list = []

            def prep_qpad(hm):
                qi = qpr.next()
                hm["qp"] = qi
                r0, r1 = hm["krows"]
                S.add("pool", lambda e: e.memset(qpad[qi][:, 0:n], 0.0), writes=[("qpad", qi)])
                S.add("pool", lambda e: e.tensor_copy(out=qpad[qi][r0:r1, 0:n], in_=qT[r0:r1, hm["qchunk"], t0:t0 + n]),
                      reads=[("qT", hm["qchunk"], tb)], writes=[("qpad", qi)])

            def head_pass(krows, kchunk, qchunk, vslot, scale, po, tp, post):
                hm = dict(krows=krows, qchunk=qchunk)
                hm_list.append(hm)
                myidx = len(hm_list) - 1
                for ii, kc in enumerate(kcs):
                    st = {}
                    ktb = kc // 4 if kc < 16 else 4

                    def qk(st=st, kc=kc, ktb=ktb, ii=ii):
                        if ii == 0:
                            if "qp" not in hm:
                                prep_qpad(hm)
                            if myidx + 1 < len(hm_list) and "qp" not in hm_list[myidx + 1]:
                                prep_qpad(hm_list[myidx + 1])
                        sb_ = sring.next()
                        st["sb"] = sb_
                        qi = hm["qp"]

                        def mms(e):
                            return e.matmul(ps[sb_][:, 0:n], lhsT=kT[:, kchunk, kc * 128:(kc + 1) * 128],
                                            rhs=qpad[qi][:, 0:n], start=True, stop=True)
                        S.add("pe", mms, reads=[("kT", kchunk, ktb), ("qpad", qi)], writes=[PK(sb_)])

                    def ex(st=st):
                        sb_ = st["sb"]
                        pi = ptr.next()
                        st["pi"] = pi
                        S.add("act", lambda e: e.activation(out=pT[pi][:, 0:n], in_=ps[sb_][:, 0:n], func=AF.Exp, scale=scale),
                              reads=[PK(sb_)], writes=[("pT", pi)])

                    def pv(st=st, kc=kc, ii=ii):
                        pi = st["pi"]
                        S.add("pe", lambda e: e.matmul(ps[po][:, 0:n], lhsT=vaug[:, kc, vslot, :], rhs=pT[pi][:, 0:n],
                                                       start=(ii == 0), stop=(ii == len(kcs) - 1)),
                              reads=[("pT", pi), ("vaug", kc)], writes=[PK(po)])
                    steps.append((qk, ex, pv, post if ii == len(kcs) - 1 else None))

            for h in range(4):
                r0 = 64 * (h // 2)
                po = oring.next()

                def postA(po=po, h=h):
                    ri = rdr.next()
                    S.add("dve", lambda e: e.reciprocal(out=rden[ri][:, 0:n], in_=ps[po][64:128, 0:n]), reads=[PK(po)], writes=[("rden", ri)])
                    p0 = 64 * (h % 2)
                    S.add("dve", lambda e: e.tensor_tensor(out=ct[p0:p0 + 64, h // 2, 0:n], in0=ps[po][0:64, 0:n], in1=rden[ri][:, 0:n], op=ALU.mult),
                          reads=[PK(po), ("rden", ri)], writes=[("cat", ci_, h)])
                    return []
                head_pass((r0, r0 + 64), 0, h % 2, h // 2, 0.125, po, None, postA)
            for h in range(4):
                base = 64 * (h % 2)
                pos = [oring.next(), oring.next()]

                def postC(pos=pos, h=h):
                    ri0 = rdr.next()
                    ri1 = rdr.next()
                    p0 = 64 * (h % 2)
                    S.add("dve", lambda e: e.reciprocal(out=rden[ri0][:, 0:n], in_=ps[pos[0]][64:128, 0:n]), reads=[PK(pos[0])], writes=[("rden", ri0)])
                    S.add("dve", lambda e: e.tensor_tensor(out=tt0[:, 0:n], in0=ps[pos[0]][0:64, 0:n], in1=rden[ri0][:, 0:n], op=ALU.mult),
                          reads=[PK(pos[0]), ("rden", ri0)], writes=["tt0"])
                    S.add("dve", lambda e: e.reciprocal(out=rden[ri1][:, 0:n], in_=ps[pos[1]][64:128, 0:n]), reads=[PK(pos[1])], writes=[("rden", ri1)])
                    S.add("dve", lambda e: e.tensor_tensor(out=tt1[:, 0:n], in0=ps[pos[1]][0:64, 0:n], in1=rden[ri1][:, 0:n], op=ALU.mult),
                          reads=[PK(pos[1]), ("rden", ri1)], writes=["tt1"])
                    S.add("dve", lambda e: e.scalar_tensor_tensor(out=od_[:, 0:n], in0=tt1[:, 0:n], scalar=neglam[:, l:l + 1], in1=tt0[:, 0:n],
                                                                    op0=ALU.mult, op1=ALU.add),
                          reads=["tt0", "tt1", "neglam"], writes=["od_"])

                    def st1():
                        S.add("act", lambda e: e.activation(out=sq64[:, 0:n], in_=od_[:, 0:n], func=AF.Square), reads=["od_"], writes=["sq64"])

                    def st2():
                        S.add("pe", lambda e: e.matmul(ps[7][0:64, 0:n], lhsT=blk64[0:64, 0:64], rhs=sq64[:, 0:n], start=True, stop=True),
                              reads=["sq64", "cmat"], writes=[PK(7)])

                    def st3():
                        S.add("act", lambda e: e.activation(out=lnb[:, 0:n], in_=ps[7][0:64, 0:n], func=AF.Ln, bias=epsc[0:64, 0:1], scale=1.0),
                              reads=[PK(7), "epsc"], writes=["lnb"])
                        S.add("act", lambda e: e.activation(out=odn[:, 0:n], in_=lnb[:, 0:n], func=AF.Exp, scale=-0.5), reads=["lnb"], writes=["odn"])

                    def st4():
                        S.add("dve", lambda e: e.scalar_tensor_tensor(out=ct[p0:p0 + 64, 2 + h // 2, 0:n], in0=od_[:, 0:n], scalar=gsub1m[:, l:l + 1],
                                                                        in1=odn[:, 0:n], op0=ALU.mult, op1=ALU.mult),
                              reads=["od_", "odn", "gsub1m"], writes=[("cat", ci_, 4 + h)])
                    tasks = [(20, st1), (23, st2), (26, st3), (30, st4)]
                    if h == 3:
                        def outproj():
                            for oc in range(8):
                                pb = 7

                                def mmo(e, oc=oc, pb=pb):
                                    ins = None
                                    for hh in range(4):
                                        ins = e.matmul(ps[pb][:, 0:n], lhsT=woh[:, hh, oc * 128:(oc + 1) * 128], rhs=ct[:, hh, 0:n],
                                                       start=(hh == 0), stop=(hh == 3))
                                    return ins
                                S.add("pe", mmo, reads=[("cat", ci_, hh) for hh in range(8)] + ["woh"], writes=[PK(pb)])
                                resid_add(bl, l, 1, tb, oc, pb, n)
                        tasks.append((34, outproj))
                    return tasks
                postC.is_c = True
                for c in range(2):
                    r0 = base + 32 * c
                    head_pass((r0, r0 + 32), 1 + h // 2, 2 + h // 2, 2 + h, 32 ** -0.5, pos[c], (r0, 0), postC if c == 1 else None)

        for qi_, tb in enumerate(qblocks):
            qblock(qi_, tb)
        LA = 2
        deferred = []
        NS = len(steps)
        for i in range(min(LA, NS)):
            steps[i][0]()
        for i in range(NS):
            if i + LA < NS:
                steps[i + LA][0]()
            steps[i][1]()
            steps[i][2]()
            if steps[i][3] is not None:
                if getattr(steps[i][3], "is_c", False):
                    for d in sorted(deferred, key=lambda d: d[0]):
                        d[1]()
                    deferred = []
                for (dl, fn) in steps[i][3]():
                    deferred.append((i + dl, fn))
            ready = [d for d in deferred if d[0] <= i]
            deferred = [d for d in deferred if d[0] > i]
            for d in ready:
                d[1]()
        for d in sorted(deferred, key=lambda d: d[0]):
            d[1]()
        S.barrier()
        A.release(m)

        m2 = A.mark()
        blocks2 = [0, 1, 2, 3] if last else [0, 1, 2, 3, 4]
        xnb = [A.t(f"xnb{i}", [128, 8, 512], BF16) for i in range(2)]
        wu = A.t("wu", [128, 8, 256], BF16)
        wvg = A.t("wvg", [128, 8, 256], BF16)
        wgl = A.t("wgl", [128, 8, 512], BF16)
        wob = A.t("wob", [128, 2, 1024], BF16)
        wod = A.t("wod", [128, 2, 1024], BF16)
        wsT = A.t("wsT", [128, 4, 128], BF16)
        gvb = A.t("gvb", [128, 256], F32)
        bst = A.t("bst", [64, 4, 512], F32)
        diag = A.t("diag", [128, 2, 31, 128], BF16)
        ypl = A.t("ypl", [128, 2, NLAT + 30], BF16)
        ypc = A.t("ypc", [128, 2, NCTX + 30], BF16)
        ug = A.t("ug", [64, 4, 512], F32)
        vge = A.t("vge", [128, 4, 256], F32)
        junk = A.t("junk", [128, 256], BF16)
        ssum = A.t("ssum", [128, 8], F32)
        vn = A.t("vn", [128, 4, 256], BF16)
        tmb = [A.t("tmb0", [64, 512], F32), A.t("tmb1", [64, 512], F32)]
        catb = A.t("catb", [128, 2, 512], BF16)
        sg = A.t("sg", [128, 512], F32)
        zz = A.t("zz", [128, 2, 512], F32)
        sqz = A.t("sqz", [128, 2, 512], BF16)
        odd = A.t("odd", [128, 2, 512], BF16)
        dma_w(wob[:], w_out[l][512:768, :].rearrange("(c p) n -> p c n", p=128), [], ["wob"])
        dma_w(wod[:], w_out[l][768:1024, :].rearrange("(c p) n -> p c n", p=128), [], ["wod"])
        dma_w(wsT[:], wsT_d[l], [], ["wsT"])
        dma_sp(gvb[:], gv_d[:, l, :], [], ["gvb"])
        dma_sp(bst[:], bs_d[:, l, :, :], [], ["bst"])
        for c in range(2):
            for k in range(31):
                S.add("dve", lambda e, c=c, k=k: e.tensor_scalar(out=diag[:, c, k, :], in0=identF[:], scalar1=wdw[:, l, c, k:k + 1], scalar2=None, op0=ALU.mult),
                      reads=["identF", "wdw"], writes=[("diag", c)])
        S.add("pool", lambda e: e.memset(ypl[:], 0.0), writes=[("ypl", c, tb) for c in range(2) for tb in range(4)] + ["yplpad"])
        S.add("pool", lambda e: e.memset(ypc[:], 0.0), writes=[("ypc", c) for c in range(2)])
        pr2 = Ring([0, 1, 2, 3])
        outr = Ring([4])
        pr2b = Ring([5, 6])
        outrb = Ring([7])
        def pass2a_pre(tb):
            t0, n = TBS[tb]
            xb = xnb[tb % 2]
            xkeys = [("xnb", tb % 2, c) for c in range(8)]
            dma_sp(xb[:, :, 0:n], xscr[:, :, t0:t0 + n], [("xscr", tb)], xkeys)

        def pass2a(tb):
            t0, n = TBS[tb]
            xb = xnb[tb % 2]
            xkeys = [("xnb", tb % 2, c) for c in range(8)]
            nsb = n // 128
            for g in range(4):
                pb = pr2.next()

                def mmu(e, pb=pb, g=g):
                    ins = None
                    for k in range(8):
                        ins = e.matmul(ps[pb][0:64, 0:n], lhsT=wu[:, k, g * 64:(g + 1) * 64], rhs=xb[:, k, 0:n], start=(k == 0), stop=(k == 7))
                    return ins
                S.add("pe", mmu, reads=xkeys + ["wu"], writes=[PK(pb)])
                S.add("act", lambda e, pb=pb, g=g: e.activation(out=ug[:, g, 0:n], in_=ps[pb][0:64, 0:n], func=AF.Gelu_apprx_tanh),
                      reads=[PK(pb)], writes=[("ug", g)])
            S.add("dve", lambda e: e.memset(ssum[:, 0:4], 0.0), writes=[("ssum", sb) for sb in range(4)])
            for sb in range(nsb):
                pb = pr2.next()

                def mmv(e, pb=pb, sb=sb):
                    ins = None
                    for k in range(8):
                        ins = e.matmul(ps[pb][:, 0:256], lhsT=xb[:, k, sb * 128:(sb + 1) * 128], rhs=wvg[:, k, :], start=(k == 0), stop=(k == 7))
                    return ins
                S.add("pe", mmv, reads=xkeys + ["wvg"], writes=[PK(pb)])
                S.add("act", lambda e, pb=pb, sb=sb: e.activation(out=vge[:, sb, :], in_=ps[pb][:, 0:256], func=AF.Gelu_apprx_tanh),
                      reads=[PK(pb)], writes=[("vge", sb)])
                S.add("act", lambda e, sb=sb: e.activation(out=junk[:], in_=vge[:, sb, :], func=AF.Square, accum_out=ssum[:, sb:sb + 1]),
                      reads=[("vge", sb), ("ssum", sb)], writes=["junk", ("ssum", sb)])
            for c in range(2):
                pa = pr2.next()
                pg = pr2.next()

                def mmg(e, pbk, co):
                    ins = None
                    for k in range(8):
                        ins = e.matmul(ps[pbk][:, 0:n], lhsT=wgl[:, k, co:co + 128], rhs=xb[:, k, 0:n], start=(k == 0), stop=(k == 7))
                    return ins
                S.add("pe", lambda e, pa=pa, c=c, mmg=mmg: mmg(e, pa, c * 128), reads=xkeys + ["wgl"], writes=[PK(pa)])
                S.add("pe", lambda e, pg=pg, c=c, mmg=mmg: mmg(e, pg, 256 + c * 128), reads=xkeys + ["wgl"], writes=[PK(pg)])
                S.add("act", lambda e, pg=pg: e.activation(out=sg[:, 0:n], in_=ps[pg][:, 0:n], func=AF.Sigmoid), reads=[PK(pg)], writes=["sg"])
                if tb != 4:
                    S.add("dve", lambda e, pa=pa, c=c: e.tensor_tensor(out=ypl[:, c, 15 + t0:15 + t0 + n], in0=ps[pa][:, 0:n], in1=sg[:, 0:n], op=ALU.mult),
                          reads=[PK(pa), "sg", "yplpad"], writes=[("ypl", c, tb)])
                else:
                    S.add("dve", lambda e, pa=pa, c=c: e.tensor_tensor(out=ypc[:, c, 15:15 + n], in0=ps[pa][:, 0:n], in1=sg[:, 0:n], op=ALU.mult),
                          reads=[PK(pa), "sg"], writes=[("ypc", c)])
            S.add("act", lambda e: e.activation(out=ssum[:, 4:4 + nsb], in_=ssum[:, 0:nsb], func=AF.Sqrt, bias=epsc[:, 0:1], scale=1.0 / 256.0),
                  reads=[("ssum", sb) for sb in range(nsb)] + ["epsc"], writes=["ssr"])
            S.add("dve", lambda e: e.reciprocal(out=ssum[:, 4:4 + nsb], in_=ssum[:, 4:4 + nsb]), reads=["ssr"], writes=["ssr"])
            for sb in range(nsb):
                S.add("dve", lambda e, sb=sb: e.scalar_tensor_tensor(out=vn[:, sb, :], in0=vge[:, sb, :], scalar=ssum[:, 4 + sb:5 + sb], in1=gvb[:],
                                                                       op0=ALU.mult, op1=ALU.mult),
                      reads=[("vge", sb), "ssr", "gvb"], writes=[("vn", sb)])
            for g in range(4):
                pb = pr2.next()
                ti = g % 2
                p0 = 64 * (g % 2)

                def mmm(e, pb=pb, g=g):
                    ins = None
                    for sb in range(nsb):
                        ins = e.matmul(ps[pb][0:64, sb * 128:(sb + 1) * 128], lhsT=vn[:, sb, g * 64:(g + 1) * 64], rhs=wsT[:, g, :], start=True, stop=True)
                    return ins
                S.add("pe", mmm, reads=[("vn", sb) for sb in range(nsb)] + ["wsT"], writes=[PK(pb)])
                S.add("dve", lambda e, pb=pb, g=g, ti=ti: e.tensor_tensor(out=tmb[ti][:, 0:n], in0=ps[pb][0:64, 0:n], in1=bst[:, g, 0:n], op=ALU.add),
                      reads=[PK(pb), "bst"], writes=[("tmb", ti)])
                S.add("dve", lambda e, g=g, ti=ti, p0=p0: e.tensor_tensor(out=catb[p0:p0 + 64, g // 2, 0:n], in0=ug[:, g, 0:n], in1=tmb[ti][:, 0:n], op=ALU.mult),
                      reads=[("tmb", ti), ("ug", g)], writes=[("catb", g)])
            for oc in range(8):
                po = outr.next()

                def mmo(e, po=po, oc=oc):
                    ins = None
                    for cc in range(2):
                        ins = e.matmul(ps[po][:, 0:n], lhsT=wob[:, cc, oc * 128:(oc + 1) * 128], rhs=catb[:, cc, 0:n], start=(cc == 0), stop=(cc == 1))
                    return ins
                S.add("pe", mmo, reads=[("catb", g) for g in range(4)] + ["wob"], writes=[PK(po)])
                if "B" in DEBUG_PARTS:
                    resid_add(bl, l, 1, tb, oc, po, n)

        dma_w(wu[:], W[:, 1280:1536].rearrange("(k p) n -> p k n", p=128), [], ["wu"])
        dma_w(wvg[:], W[:, 1536:1792].rearrange("(k p) n -> p k n", p=128), [], ["wvg"])
        dma_w(wgl[:], W[:, 1792:2304].rearrange("(k p) n -> p k n", p=128), [], ["wgl"])

        def pass2b(tb):
            t0, n = TBS[tb]
            for c in range(2):
                pz = pr2b.next()
                if tb != 4:
                    yk = [("ypl", c, t) for t in range(max(0, tb - 1), min(3, tb + 1) + 1)] + ["yplpad"]
                    ysrc = lambda k, c=c, t0=t0, n=n: ypl[:, c, t0 + k:t0 + k + n]
                else:
                    yk = [("ypc", c)]
                    ysrc = lambda k, c=c, n=n: ypc[:, c, k:k + n]

                def mmc(e, pz=pz, c=c, ysrc=ysrc, n=n):
                    ins = None
                    for k in range(31):
                        ins = e.matmul(ps[pz][:, 0:n], lhsT=diag[:, c, k, :], rhs=ysrc(k), start=(k == 0), stop=(k == 30))
                    return ins
                S.add("pe", mmc, reads=yk + [("diag", c)], writes=[PK(pz)])
                S.add("act", lambda e, pz=pz, c=c, n=n: e.activation(out=zz[:, c, 0:n], in_=ps[pz][:, 0:n], func=AF.Identity, bias=bdw[:, l, c:c + 1]),
                      reads=[PK(pz), "bdw"], writes=[("zz", c)])
                S.add("act", lambda e, c=c, n=n: e.activation(out=sqz[:, c, 0:n], in_=zz[:, c, 0:n], func=AF.Square), reads=[("zz", c)], writes=[("sqz", c)])
            pn = pr2b.next()

            def mmn(e, pn=pn, n=n):
                ins = None
                for c in range(2):
                    ins = e.matmul(ps[pn][:, 0:n], lhsT=ones256, rhs=sqz[:, c, 0:n], start=(c == 0), stop=(c == 1))
                return ins
            S.add("pe", mmn, reads=[("sqz", 0), ("sqz", 1), "cmat"], writes=[PK(pn)])
            rsqrt_from_psum(pn, n)
            for c in range(2):
                tx = tmpx[c]
                S.add("pool", lambda e, c=c, tx=tx, n=n: e.tensor_tensor(out=tx[:, 0:n], in0=zz[:, c, 0:n], in1=rstd[:, 0:n], op=ALU.mult),
                      reads=[("zz", c), "rstd"], writes=[("tmpx", c)])
                S.add("act", lambda e, c=c, tx=tx, n=n: e.activation(out=odd[:, c, 0:n], in_=tx[:, 0:n], func=AF.Silu, scale=gconv[:, l, c:c + 1]),
                      reads=[("tmpx", c), "gconv"], writes=[("odd", c)])
            for oc in range(8):
                po = outrb.next()

                def mmo2(e, po=po, oc=oc, n=n):
                    ins = None
                    for c in range(2):
                        ins = e.matmul(ps[po][:, 0:n], lhsT=wod[:, c, oc * 128:(oc + 1) * 128], rhs=odd[:, c, 0:n], start=(c == 0), stop=(c == 1))
                    return ins
                S.add("pe", mmo2, reads=[("odd", 0), ("odd", 1), "wod"], writes=[PK(po)])
                if "D" in DEBUG_PARTS:
                    resid_add(bl, l, 1, tb, oc, po, n)

        def record(fn):
            items = []
            S.add = lambda *a_, **k_: items.append((a_, k_))
            try:
                fn()
            finally:
                del S.add
            return items

        def merge_emit(la, lb):
            i = j = 0
            na, nb_ = len(la), len(lb)
            while i < na or j < nb_:
                if j >= nb_ or (i < na and i * nb_ <= j * na):
                    S.add(*la[i][0], **la[i][1])
                    i += 1
                else:
                    S.add(*lb[j][0], **lb[j][1])
                    j += 1

        def do2a(i_):
            if i_ + 1 < len(blocks2):
                pass2a_pre(blocks2[i_ + 1])
            pass2a(blocks2[i_])

        pass2a_pre(blocks2[0])
        NB2 = len(blocks2)
        for i_ in range(NB2 + 2):
            ia, ib = i_, i_ - 2
            la = record(lambda: do2a(ia)) if ia < NB2 else []
            lb = record(lambda: pass2b(blocks2[ib])) if 0 <= ib < NB2 else []
            merge_emit(la, lb)
        S.barrier()
        A.release(m2)

    done = False
    for bl in range(nb):
        for tb in range(4):
            t0_, n_ = TBS[tb]
            dma_sp(hT[:, :, t0_:t0_ + n_], xT[bl][:, t0_:t0_ + n_].rearrange("(c p) t -> p c t", p=128), [], [("h", c, tb) for c in range(8)])
        dma_sp(hT[:, :, NLAT:NTOK], ctxT[bl].rearrange("(c p) t -> p c t", p=128), [], [("h", c, 4) for c in range(8)])
        for l in range(depth):
            last = (l == DEPTH - 1)
            ffn(bl, l, 0, [0, 1, 2, 3, 4], ada_next=(l + 1 if (bl == 0 and l + 1 < depth) else None))
            if stop == f"ffn1_{l}":
                dump_and_end(bl)
                done = True
                break
            mixer(bl, l)
            if stop == f"mix_{l}":
                dump_and_end(bl)
                done = True
                break
            ffn(bl, l, 1, [0, 1, 2, 3] if last else [0, 1, 2, 3, 4])
            if stop == f"ffn2_{l}":
                dump_and_end(bl)
                done = True
                break
        if done:
            continue
        mf = A.mark()
        ob = [A.t(f"ob{i}", [128, 8, 512], F32) for i in range(2)]
        for tb in range(4):
            t0, n = TBS[tb]
            o = ob[tb % 2]
            make_xn(bl, 0, 0, tb, lambda c, o=o: o[:, c, :], lambda c, tb=tb: ("ob", tb % 2, c), final=True)
            dma_sp(outT[bl][:, t0:t0 + n].rearrange("(c p) t -> p c t", p=128), o[:], [("ob", tb % 2, c) for c in range(8)], [("ob", tb % 2, c) for c in range(8)] + ["outT"])
        S.barrier()
        A.release(mf)
    S.add("sp", lambda e: None, reads=["outT" if stop is None else "dbg"])
    S.emit()
    return nc


def _rope_tables():
    t = np.arange(NLAT)
    row = (t // 64).astype(np.float32)
    col = (t % 64).astype(np.float32)
    out = np.zeros((4, 128, NLAT), np.float32)
    for ti, hd in ((0, 64), (2, 32)):
        quarter = hd // 4
        inv = (np.float32(10000.0) ** (-np.arange(quarter, dtype=np.float32) / np.float32(quarter))).astype(np.float32)
        ang = np.concatenate([row[:, None] * inv[None, :], col[:, None] * inv[None, :]], axis=-1).astype(np.float32)
        cos = np.cos(ang).astype(np.float32)
        sin = np.sin(ang).astype(np.float32)
        half = hd // 2
        for p in range(128):
            d = p % hd
            j = d % half
            out[ti, p] = cos[:, j]
            out[ti + 1, p] = (-sin[:, j]) if d < half else sin[:, j]
    return out


def _const_mats():
    m = np.zeros((6, 128, 128), np.float32)
    m[0] = 1.0 / 1024.0
    m[1, 0:64, 0:64] = 1.0 / 64.0
    m[1, 64:128, 64:128] = 1.0 / 64.0
    m[2] = 1.0 / 256.0
    for mm_ in range(128):
        pa = mm_ + 32 if (mm_ % 64) < 32 else mm_ - 32
        m[3, pa, mm_] = 1.0
        pc = mm_ + 16 if (mm_ % 32) < 16 else mm_ - 16
        m[4, pc, mm_] = 1.0
    m[5] = np.eye(128, dtype=np.float32)
    return m


def _col_perm():
    qa = lambda h: list(range(h * 64, (h + 1) * 64))
    perm = qa(0) + qa(2) + qa(1) + qa(3)
    perm += list(range(256, 512))
    perm += list(range(512, 640))
    perm += list(range(768, 1024))
    perm += list(range(640, 768))
    perm += list(range(1024, 1280))
    perm += list(range(1280, 2304))
    return np.array(perm)


def prep_shared(inp):
    f = lambda a: np.ascontiguousarray(np.asarray(a, dtype=np.float32))
    sh = {}
    sh["w_ada"] = f(inp["w_ada"])
    sh["b_adaT"] = f(np.asarray(inp["b_ada"]).reshape(DEPTH, 72, 128).transpose(2, 0, 1))
    sh["g_normT"] = f(np.asarray(inp["g_norm"]).reshape(DEPTH, 3, 8, 128).transpose(3, 0, 1, 2))
    sh["g_finalT"] = f(np.asarray(inp["g_final"]).reshape(8, 128).T)
    for k in ("w_ff1_in", "w_ff1_out", "w_ff2_in", "w_ff2_out", "w_out"):
        sh[k] = f(inp[k])
    sh["w_in_p"] = f(np.asarray(inp["w_in"])[:, :, _col_perm()])
    gq = np.asarray(inp["g_q_a"])
    gk = np.asarray(inp["g_k_a"])
    gqk = np.stack([np.tile(gq, (1, 2)), np.tile(gk, (1, 2))], axis=-1)
    sh["gqk"] = f(gqk.transpose(1, 0, 2))
    sh["lamw"] = f(np.broadcast_to(np.asarray(inp["lam_c"])[None], (64, DEPTH, 4, 32)))
    sh["gsub"] = f(np.asarray(inp["g_sub_c"]).T)
    sh["gv_bc"] = f(np.broadcast_to(np.asarray(inp["g_v_b"])[None], (128, DEPTH, 256)))
    sh["wsT"] = f(np.asarray(inp["w_s_b"]).transpose(0, 3, 1, 2))
    bs = np.asarray(inp["b_s_b"])
    sh["bs_tbl"] = f(np.broadcast_to(np.tile(bs, (1, 1, 4))[None], (64, DEPTH, 4, 512)))
    sh["wdw"] = f(np.asarray(inp["w_dw_d"]).reshape(DEPTH, 31, 2, 128).transpose(3, 0, 2, 1))
    sh["bdw"] = f(np.asarray(inp["b_dw_d"]).reshape(DEPTH, 2, 128).transpose(2, 0, 1))
    sh["gconv"] = f(np.asarray(inp["g_conv_d"]).reshape(DEPTH, 2, 128).transpose(2, 0, 1))
    sh["rope"] = _rope_tables()
    sh["cmat"] = _const_mats()
    return sh


def prep_core(inp, bids):
    x = np.asarray(inp["x"])
    ctx = np.asarray(inp["ctx"])
    c = np.asarray(inp["c"])
    cc = np.asarray(inp["c_ctx"])
    d = {}
    d["xT"] = np.ascontiguousarray(np.stack([x[b].T for b in bids]).astype(np.float32))
    d["ctxT"] = np.ascontiguousarray(np.stack([ctx[b].T for b in bids]).astype(np.float32))
    vecs = [c[b] for b in bids]
    while len(vecs) < 2:
        vecs.append(c[bids[0]])
    vecs.append(cc)
    cT = np.stack(vecs, axis=-1).reshape(8, 128, 3).transpose(1, 0, 2)
    d["cT"] = np.ascontiguousarray(cT.astype(np.float32))
    return d


_NC_CACHE = {}


def kernel(**inputs):
    B = np.asarray(inputs["x"]).shape[0]
    nb = B // NCORES
    if "prog" not in _NC_CACHE:
        _NC_CACHE["prog"] = build_program(nb=nb)
    nc = _NC_CACHE["prog"]
    sh = prep_shared(inputs)
    in_maps = []
    for i in range(NCORES):
        d = dict(sh)
        d.update(prep_core(inputs, list(range(i * nb, (i + 1) * nb))))
        in_maps.append(d)
    res = run_bass_kernel_spmd(nc, in_maps, core_ids=list(range(NCORES)))
    out = np.empty((B, NLAT, D), np.float32)
    for i in range(NCORES):
        o = res.results[i]["outT"]
        for jb in range(nb):
            out[i * nb + jb] = o[jb].T
    return out
```

```python
import math
import contextlib
import numpy as np
import ml_dtypes
import concourse.bass as bass
import concourse.mybir as mybir
from concourse.bass_utils import run_bass_kernel_spmd

F32 = mybir.dt.float32
BF16 = mybir.dt.bfloat16
AF = mybir.ActivationFunctionType
ALU = mybir.AluOpType

ENGS = ["pe", "act", "dve", "pool", "sp"]

D = 1024
NLAT = 2048
NCTX = 256
NTOK = NLAT + NCTX
DFF = 2816
DEPTH = 2
EPS = 1e-6
NCORES = 8
TBS = [(0, 512), (512, 512), (1024, 512), (1536, 512), (2048, 256)]
DEBUG_PARTS = set("ACBD")
WARM_N = 0


class Op:
    __slots__ = ("eng", "fn", "deps", "sig", "sigval", "ch", "inc", "dma", "idx")


class Sched:
    def __init__(self, nc):
        self.nc = nc
        self.ops = {e: [] for e in ENGS}
        self.last_w = {}
        self.readers = {}
        self.ch_eng = {}
        self.pending_bar = {e: [] for e in ENGS}
        self.nops = 0
        self.last_on_ch = {}
        self.dma_ring = {}
        self.DMA_RING = {"sp": 16, "pool": 24}

    def add(self, eng, fn, reads=(), writes=(), ch=None, dma=False):
        op = Op()
        op.eng = eng
        op.fn = fn
        op.sig = bool(dma)
        op.sigval = None
        op.dma = dma
        op.ch = ch if ch is not None else eng
        if dma:
            k = self.dma_ring.get(eng, 0)
            self.dma_ring[eng] = k + 1
            op.ch = f"{eng}_d{k % self.DMA_RING[eng]}"
        op.inc = 16 if dma else 1
        op.idx = self.nops
        self.nops += 1
        if op.ch in self.ch_eng:
            assert self.ch_eng[op.ch] == eng, (op.ch, eng)
        else:
            self.ch_eng[op.ch] = eng
        deps = {}
        for k in reads:
            w = self.last_w.get(k)
            if w is not None:
                deps[w.idx] = (w, True)
        for k in writes:
            w = self.last_w.get(k)
            if w is not None and w.idx not in deps:
                deps[w.idx] = (w, False)
            for r in self.readers.get(k, ()):
                if r.idx not in deps:
                    deps[r.idx] = (r, False)
        need = []
        for (d, raw) in deps.values():
            if d.eng == eng and not d.dma and not dma and not raw:
                continue
            if d.eng == eng and eng == "pe" and not d.dma and not dma:
                continue
            need.append(d)
        if dma:
            prev = self.last_on_ch.get(op.ch)
            if prev is not None:
                need.append(prev)
        for d in self.pending_bar[eng]:
            need.append(d)
        self.pending_bar[eng] = []
        for d in need:
            d.sig = True
        op.deps = need
        for k in writes:
            self.last_w[k] = op
            self.readers[k] = []
        for k in reads:
            if k in writes:
                continue
            self.readers.setdefault(k, []).append(op)
        self.ops[eng].append(op)
        self.last_on_ch[op.ch] = op
        return op

    def barrier(self):
        frontier = list(self.last_on_ch.values())
        for e in ENGS:
            self.pending_bar[e] = list(frontier)

    def emit(self):
        nc = self.nc
        chans = list(self.ch_eng.keys())
        cum = {c: 0 for c in chans}
        for e in ENGS:
            for op in self.ops[e]:
                if op.sig:
                    cum[op.ch] += op.inc
                    op.sigval = cum[op.ch]
        with contextlib.ExitStack() as st:
            sems = {c: st.enter_context(nc.semaphore("s_" + c)) for c in chans}
            block = st.enter_context(nc.Block())

            def run(engname, engine):
                waited = {}
                for op in self.ops[engname]:
                    for d in op.deps:
                        if waited.get(d.ch, 0) < d.sigval:
                            engine.wait_ge(sems[d.ch], d.sigval)
                            waited[d.ch] = d.sigval
                    ins = op.fn(engine)
                    if op.sig:
                        assert ins is not None
                        ins.then_inc(sems[op.ch], op.inc)

            @block.tensor
            def _(eng):
                run("pe", eng)

            @block.scalar
            def _(eng):
                run("act", eng)

            @block.vector
            def _(eng):
                run("dve", eng)

            @block.gpsimd
            def _(eng):
                run("pool", eng)

            @block.sync
            def _(eng):
                run("sp", eng)


class Alloc:
    def __init__(self, nc, limit=229344, base=16512):
        self.nc = nc
        self.off = base
        self.limit = limit
        self.n = 0
        self.peak = base

    def mark(self):
        return self.off

    def release(self, m):
        self.off = m

    def t(self, name, shape, dtype):
        esz = 4 if dtype == F32 else 2
        nbytes = int(np.prod(shape[1:])) * esz
        nbytes = (nbytes + 63) // 64 * 64
        assert self.off + nbytes <= self.limit, (name, self.off, nbytes, self.limit)
        self.n += 1
        h = self.nc.alloc_sbuf_tensor_at(f"{name}_{self.n}", list(shape), dtype, offset=self.off)
        self.off += nbytes
        self.peak = max(self.peak, self.off)
        return h


class Ring:
    def __init__(self, items):
        self.items = list(items)
        self.i = 0

    def next(self):
        v = self.items[self.i % len(self.items)]
        self.i += 1
        return v


def build_program(nb=2, depth=DEPTH, stop=None):
    nc = bass.Bass("TRN2", target_bir_lowering=False)
    S = Sched(nc)
    A = Alloc(nc)

    def din(name, shape, dt=F32):
        return nc.dram_tensor(name, list(shape), dt, kind="ExternalInput").ap()

    xT = din("xT", [nb, D, NLAT])
    ctxT = din("ctxT", [nb, D, NCTX])
    cT_d = din("cT", [128, 8, 3])
    w_ada = din("w_ada", [DEPTH, D, 9 * D])
    b_adaT_d = din("b_adaT", [128, DEPTH, 72])
    g_normT_d = din("g_normT", [128, DEPTH, 3, 8])
    g_finalT_d = din("g_finalT", [128, 8])
    w_ff_in = [din("w_ff1_in", [DEPTH, D, 2 * DFF]), din("w_ff2_in", [DEPTH, D, 2 * DFF])]
    w_ff_out = [din("w_ff1_out", [DEPTH, DFF, D]), din("w_ff2_out", [DEPTH, DFF, D])]
    w_in_p = din("w_in_p", [DEPTH, D, 2304])
    w_out = din("w_out", [DEPTH, D, D])
    gqk_d = din("gqk", [128, DEPTH, 2])
    lamw_d = din("lamw", [64, DEPTH, 4, 32])
    gsub_d = din("gsub", [64, DEPTH])
    gv_d = din("gv_bc", [128, DEPTH, 256])
    wsT_d = din("wsT", [DEPTH, 128, 4, 128])
    bs_d = din("bs_tbl", [64, DEPTH, 4, 512])
    wdw_d = din("wdw", [128, DEPTH, 2, 31])
    bdw_d = din("bdw", [128, DEPTH, 2])
    gconv_d = din("gconv", [128, DEPTH, 2])
    rope_d = din("rope", [4, 128, NLAT])
    cmat_d = din("cmat", [6, 128, 128])
    xscr = nc.dram_tensor("xscr", [128, 8, NTOK], BF16, kind="ExternalOutput").ap()
    if stop is None:
        outT = nc.dram_tensor("outT", [nb, D, NLAT], F32, kind="ExternalOutput").ap()
    else:
        dbg = nc.dram_tensor("dbg", [nb, 128, 8, NTOK], F32, kind="ExternalOutput").ap()

    hT = A.t("hT", [128, 8, NTOK], F32)
    cmat = A.t("cmat", [128, 6, 128], BF16)
    identF = A.t("identF", [128, 128], F32)
    modall = A.t("modall", [128, DEPTH, 72, 3], F32)
    gs = A.t("gs", [128, DEPTH, 3, 8, 3], F32)
    hg = A.t("hg", [128, DEPTH, 3, 8, 3], F32)
    b_adaT = A.t("b_adaT", [128, DEPTH, 72], F32)
    g_normT = A.t("g_normT", [128, DEPTH, 3, 8], F32)
    g_finalT = A.t("g_finalT", [128, 8], F32)
    gqk = A.t("gqk", [128, DEPTH, 2], F32)
    gsub = A.t("gsub", [64, DEPTH], F32)
    gsub1m = A.t("gsub1m", [64, DEPTH], F32)
    neglam = A.t("neglam", [64, DEPTH], F32)
    wdw = A.t("wdw", [128, DEPTH, 2, 31], F32)
    bdw = A.t("bdw", [128, DEPTH, 2], F32)
    gconv = A.t("gconv", [128, DEPTH, 2], F32)
    epsc = A.t("epsc", [128, 1], F32)
    scT = A.t("scT", [128, 8, 3], BF16)
    sq = A.t("sq", [128, 8, 512], BF16)
    rt = A.t("rt", [128, 512], F32)
    rstd = A.t("rstd", [128, 512], F32)
    tmpx = [A.t("tmpx0", [128, 512], F32), A.t("tmpx1", [128, 512], F32)]
    PERSIST = A.mark()

    ps = [nc.alloc_psum_tensor(f"ps{i}", [128, 512], F32) for i in range(8)]
    onesD = cmat[:, 0, :]
    blk64 = cmat[:, 1, :]
    ones256 = cmat[:, 2, :]
    permA = cmat[:, 3, :]
    permC = cmat[:, 4, :]

    cnt = [0]

    def uid():
        cnt[0] += 1
        return cnt[0]

    def PK(b):
        return ("ps", b)

    def dma_sp(out, in_, reads, writes):
        S.add("sp", lambda e: e.dma_start(out=out, in_=in_), reads=reads, writes=writes, ch="dq_sp", dma=True)

    def dma_w(out, in_, reads, writes):
        S.add("pool", lambda e: e.dma_start(out=out, in_=in_), reads=reads, writes=writes, ch="dq_w", dma=True)

    dma_w(cmat[:], cmat_d.rearrange("m p n -> p m n"), [], ["cmat"])
    dma_sp(identF[:], cmat_d[5], [], ["identF"])
    dma_sp(b_adaT[:], b_adaT_d, [], ["b_adaT"])
    dma_sp(g_normT[:], g_normT_d, [], ["g_normT"])
    dma_sp(g_finalT[:], g_finalT_d, [], ["g_finalT"])
    dma_sp(gqk[:], gqk_d, [], ["gqk"])
    dma_sp(gsub[:], gsub_d, [], ["gsub"])
    dma_sp(wdw[:], wdw_d, [], ["wdw"])
    dma_sp(bdw[:], bdw_d, [], ["bdw"])
    dma_sp(gconv[:], gconv_d, [], ["gconv"])
    S.add("dve", lambda e: e.memset(epsc[:], EPS), writes=["epsc"])

    m0 = A.mark()
    cTs = A.t("cTs", [128, 8, 3], F32)
    lamw = A.t("lamw", [64, DEPTH, 4, 32], F32)
    lamp = A.t("lamp", [64, DEPTH, 2, 32], F32)
    lams = A.t("lams", [64, DEPTH, 2], F32)
    lame = A.t("lame", [64, DEPTH, 2], F32)
    wab = [A.t(f"wab{i}", [128, 8, 512], BF16) for i in range(3)]
    dma_sp(cTs[:], cT_d, [], ["cTs"])
    dma_sp(lamw[:], lamw_d, [], ["lamw"])
    S.add("act", lambda e: e.activation(out=scT[:], in_=cTs[:], func=AF.Silu), reads=["cTs"], writes=["scT"])
    def adaln_steps(l, wabufs, uidx):
        stepsl = []
        for cb in range(18):
            def dm(cb=cb):
                bi = cb % len(wabufs)
                wt = wabufs[bi]
                dma_w(wt[:], w_ada[l][:, cb * 512:(cb + 1) * 512].rearrange("(k p) n -> p k n", p=128), [], [("wab", uidx, bi)])

            def one(cb=cb):
                bi = cb % len(wabufs)
                wt = wabufs[bi]

                def mm(e):
                    ins = None
                    for cc in range(4):
                        chn = cb * 4 + cc
                        for k in range(8):
                            ins = e.matmul(ps[7][:, chn * 3:(chn + 1) * 3], lhsT=wt[:, k, cc * 128:(cc + 1) * 128],
                                           rhs=scT[:, k, :], start=(k == 0), stop=(k == 7))
                    return ins
                S.add("pe", mm, reads=[("wab", uidx, bi), "scT"], writes=[PK(7)])
            stepsl.append((dm, one))

        def fin():
            psm = ps[7][:, 0:216].rearrange("p (c j) -> p c j", j=3)
            for j in range(3):
                S.add("dve", lambda e, j=j: e.tensor_tensor(out=modall[:, l, :, j], in0=psm[:, :, j], in1=b_adaT[:, l, :], op=ALU.add),
                      reads=[PK(7), "b_adaT"], writes=[("modall", l)])
            for s_ in range(3):
                for j in range(3):
                    S.add("dve", lambda e, s_=s_, j=j: e.tensor_scalar(out=gs[:, l, s_, :, j], in0=modall[:, l, (3 * s_ + 1) * 8:(3 * s_ + 2) * 8, j],
                                                                         scalar1=1.0, scalar2=None, op0=ALU.add),
                          reads=[("modall", l)], writes=[("gs", l)])
                    S.add("dve", lambda e, s_=s_, j=j: e.tensor_tensor(out=gs[:, l, s_, :, j], in0=gs[:, l, s_, :, j], in1=g_normT[:, l, s_, :], op=ALU.mult),
                          reads=[("gs", l), "g_normT"], writes=[("gs", l)])
                    S.add("dve", lambda e, s_=s_, j=j: e.tensor_scalar(out=hg[:, l, s_, :, j], in0=modall[:, l, (3 * s_ + 2) * 8:(3 * s_ + 3) * 8, j],
                                                                         scalar1=(1.0 if s_ == 1 else 0.5), scalar2=None, op0=ALU.mult),
                          reads=[("modall", l)], writes=[("hg", l)])
        return stepsl, fin

    def ada_run(stl, k):
        if k + 2 < len(stl):
            stl[k + 2][0]()
        stl[k][1]()

    st0, fin0 = adaln_steps(0, wab, 0)
    st0[0][0]()
    st0[1][0]()
    for k_ in range(len(st0)):
        ada_run(st0, k_)
    fin0()
    for l in range(depth):
        lam_init = 0.8 - 0.6 * math.exp(-0.3 * l)
        for q in range(2):
            S.add("dve", lambda e, l=l, q=q: e.tensor_tensor(out=lamp[:, l, q, :], in0=lamw[:, l, 2 * q, :], in1=lamw[:, l, 2 * q + 1, :], op=ALU.mult),
                  reads=["lamw"], writes=["lamp"])
            S.add("dve", lambda e, l=l, q=q: e.tensor_reduce(out=lams[:, l, q:q + 1], in_=lamp[:, l, q, :], axis=mybir.AxisListType.X, op=ALU.add),
                  reads=["lamp"], writes=["lams"])
        S.add("act", lambda e, l=l: e.activation(out=lame[:, l, :], in_=lams[:, l, :], func=AF.Exp), reads=["lams"], writes=["lame"])
        S.add("dve", lambda e, l=l: e.tensor_tensor(out=neglam[:, l:l + 1], in0=lame[:, l, 1:2], in1=lame[:, l, 0:1], op=ALU.subtract),
              reads=["lame"], writes=["neglam"])
        S.add("dve", lambda e, l=l, li=lam_init: e.tensor_scalar(out=neglam[:, l:l + 1], in0=neglam[:, l:l + 1], scalar1=-li, scalar2=None, op0=ALU.add),
              reads=["neglam"], writes=["neglam"])
        S.add("dve", lambda e, l=l, li=lam_init: e.tensor_scalar(out=gsub1m[:, l:l + 1], in0=gsub[:, l:l + 1], scalar1=(1.0 - li), scalar2=None, op0=ALU.mult),
              reads=["gsub"], writes=["gsub1m"])
    S.barrier()
    A.release(m0)

    def hkeys(tb, cs=range(8)):
        return [("h", c, tb) for c in cs]

    def jmod(bl, tb):
        return 2 if tb == 4 else bl

    def rsqrt_from_psum(pb, n, parts=128, scale=1.0):
        S.add("act", lambda e: e.activation(out=rt[0:parts, 0:n], in_=ps[pb][0:parts, 0:n], func=AF.Sqrt, bias=epsc[0:parts, 0:1], scale=scale),
              reads=[PK(pb), "epsc"], writes=["rt"])
        S.add("dve", lambda e: e.reciprocal(out=rstd[0:parts, 0:n], in_=rt[0:parts, 0:n]), reads=["rt"], writes=["rstd"])

    def make_xn(bl, l, s, tb, dst, dkeys, final=False, pbank=7):
        t0, n = TBS[tb]
        j = jmod(bl, tb)
        for c in range(8):
            if c % 2 == 0:
                S.add("act", lambda e, c=c: e.activation(out=sq[:, c, 0:n], in_=hT[:, c, t0:t0 + n], func=AF.Square),
                      reads=[("h", c, tb)], writes=[("sq", c)])
            else:
                S.add("pool", lambda e, c=c: e.tensor_tensor(out=sq[:, c, 0:n], in0=hT[:, c, t0:t0 + n], in1=hT[:, c, t0:t0 + n], op=ALU.mult),
                      reads=[("h", c, tb)], writes=[("sq", c)])

        def mm(e):
            ins = None
            for c in range(8):
                ins = e.matmul(ps[pbank][:, 0:n], lhsT=onesD, rhs=sq[:, c, 0:n], start=(c == 0), stop=(c == 7))
            return ins
        S.add("pe", mm, reads=[("sq", c) for c in range(8)] + ["cmat"], writes=[PK(pbank)])
        rsqrt_from_psum(pbank, n)
        for c in range(8):
            tx = tmpx[c % 2]
            S.add("pool", lambda e, c=c, tx=tx: e.tensor_tensor(out=tx[:, 0:n], in0=hT[:, c, t0:t0 + n], in1=rstd[:, 0:n], op=ALU.mult),
                  reads=[("h", c, tb), "rstd"], writes=[("tmpx", c % 2)])
            if final:
                S.add("act", lambda e, c=c, tx=tx: e.activation(out=dst(c), in_=tx[:, 0:n], func=AF.Identity, scale=g_finalT[:, c:c + 1]),
                      reads=[("tmpx", c % 2), "g_finalT"], writes=[dkeys(c)])
            else:
                S.add("act", lambda e, c=c, tx=tx: e.activation(out=dst(c), in_=tx[:, 0:n], func=AF.Identity,
                                                                 scale=gs[:, l, s, c, j:j + 1], bias=modall[:, l, 3 * s * 8 + c, j:j + 1]),
                      reads=[("tmpx", c % 2), ("gs", l), ("modall", l)], writes=[dkeys(c)])

    def resid_add(bl, l, s, tb, oc, pb, n):
        t0, _ = TBS[tb]
        j = jmod(bl, tb)
        S.add("dve", lambda e: e.scalar_tensor_tensor(out=hT[:, oc, t0:t0 + n], in0=ps[pb][:, 0:n], scalar=hg[:, l, s, oc, j:j + 1],
                                                       in1=hT[:, oc, t0:t0 + n], op0=ALU.mult, op1=ALU.add),
              reads=[PK(pb), ("hg", l), ("h", oc, tb)], writes=[("h", oc, tb)])

    def dump_and_end(bl):
        dma_sp(dbg[bl], hT[:], [("h", c, tb) for c in range(8) for tb in range(5)], ["dbg"])

    def ffn(bl, l, which, blocks, ada_next=None):
        s = 0 if which == 0 else 2
        m = A.mark()
        extra, extra_fin = [], None
        if ada_next is not None:
            wabx = [A.t(f"wabx{i}", [128, 8, 512], BF16) for i in range(3)]
            extra, extra_fin = adaln_steps(ada_next, wabx, uid())
            extra[0][0]()
            extra[1][0]()
        ek = [0]
        xn = A.t("xn", [128, 8, NTOK], BF16)
        wa = [A.t(f"wa{i}", [128, 8, 256], BF16) for i in range(2)]
        wb = [A.t(f"wb{i}", [128, 8, 256], BF16) for i in range(2)]
        wo = [A.t(f"wo{i}", [128, 2, 1024], BF16) for i in range(2)]
        mid = [A.t(f"mid{i}", [128, 2, NTOK], BF16) for i in range(2)]
        sa = [A.t(f"sa{i}", [128, 512], F32) for i in range(2)]
        u = uid()
        for tb in blocks:
            t0, n = TBS[tb]
            make_xn(bl, l, s, tb, lambda c, t0=t0, n=n: xn[:, c, t0:t0 + n], lambda c, tb=tb: ("xn", u, c, tb))
        win = w_ff_in[which][l]
        wout = w_ff_out[which][l]
        NG = DFF // 256
        ring1 = Ring([0, 1, 2, 3])
        ring2 = Ring([4, 5, 6])
        sar = Ring([0, 1])

        def load(g):
            bi = g % 2
            dma_w(wa[bi][:], win[:, g * 256:(g + 1) * 256].rearrange("(k p) n -> p k n", p=128), [], [("wa", bi)])
            dma_w(wb[bi][:], win[:, DFF + g * 256:DFF + (g + 1) * 256].rearrange("(k p) n -> p k n", p=128), [], [("wb", bi)])
            dma_w(wo[bi][:], wout[g * 256:(g + 1) * 256, :].rearrange("(j p) n -> p j n", p=128), [], [("wo", bi)])

        def phase1(g):
            bi = g % 2
            for tb in blocks:
                t0, n = TBS[tb]
                for jj in range(2):
                    pa = ring1.next()
                    pb = ring1.next()

                    def mm(e, w, pbk, jj=jj, t0=t0, n=n):
                        ins = None
                        for k in range(8):
                            ins = e.matmul(ps[pbk][:, 0:n], lhsT=w[:, k, jj * 128:(jj + 1) * 128], rhs=xn[:, k, t0:t0 + n],
                                           start=(k == 0), stop=(k == 7))
                        return ins
                    xk = [("xn", u, c, tb) for c in range(8)]
                    S.add("pe", lambda e, w=wa[bi], pbk=pa, mm=mm: mm(e, w, pbk), reads=xk + [("wa", bi)], writes=[PK(pa)])
                    S.add("pe", lambda e, w=wb[bi], pbk=pb, mm=mm: mm(e, w, pbk), reads=xk + [("wb", bi)], writes=[PK(pb)])
                    si = sar.next()
                    S.add("act", lambda e, pa=pa, si=si, n=n: e.activation(out=sa[si][:, 0:n], in_=ps[pa][:, 0:n], func=AF.Silu),
                          reads=[PK(pa)], writes=[("sa", si)])
                    S.add("dve", lambda e, pb=pb, si=si, n=n, jj=jj, t0=t0: e.tensor_tensor(out=mid[bi][:, jj, t0:t0 + n], in0=sa[si][:, 0:n],
                                                                                         in1=ps[pb][:, 0:n], op=ALU.mult),
                          reads=[PK(pb), ("sa", si)], writes=[("mid", bi, jj, tb)])

        def phase2(g):
            bi = g % 2
            for tb in blocks:
                t0, n = TBS[tb]
                for oc in range(8):
                    po = ring2.next()

                    def mm(e, po=po, oc=oc, t0=t0, n=n):
                        ins = None
                        for jj in range(2):
                            ins = e.matmul(ps[po][:, 0:n], lhsT=wo[bi][:, jj, oc * 128:(oc + 1) * 128], rhs=mid[bi][:, jj, t0:t0 + n],
                                           start=(jj == 0), stop=(jj == 1))
                        return ins
                    S.add("pe", mm, reads=[("mid", bi, 0, tb), ("mid", bi, 1, tb), ("wo", bi)], writes=[PK(po)])
                    resid_add(bl, l, s, tb, oc, po, n)

        load(0)
        load(1)
        phase1(0)
        for g in range(1, NG):
            phase1(g)
            phase2(g - 1)
            if g + 1 < NG:
                load(g + 1)
            for _ in range(2):
                if ek[0] < len(extra):
                    ada_run(extra, ek[0])
                    ek[0] += 1
        phase2(NG - 1)
        while ek[0] < len(extra):
            ada_run(extra, ek[0])
            ek[0] += 1
        if extra_fin is not None:
            extra_fin()
        S.barrier()
        A.release(m)

    def mixer(bl, l):
        last = (l == DEPTH - 1)
        blocks = [0, 1, 2, 3, 4]
        m = A.mark()
        qT = A.t("qT", [128, 4, NTOK], BF16)
        kT = A.t("kT", [128, 3, NTOK], BF16)
        vaug = A.t("vaug", [128, 18, 6, 128], BF16)
        m1 = A.mark()
        xnb = [A.t(f"xnb{i}", [128, 8, 512], BF16) for i in range(2)]
        wq = A.t("wq", [128, 8, 512], BF16)
        wk = A.t("wk", [128, 8, 384], BF16)
        wv = A.t("wv", [128, 8, 384], BF16)
        ropet = A.t("ropet", [128, 4, 512], F32)
        sqb = [A.t(f"sqb{i}", [128, 512], BF16) for i in range(2)]
        qn = [A.t(f"qn{i}", [128, 512], BF16) for i in range(3)]
        t1 = A.t("t1", [128, 512], F32)
        t2 = A.t("t2", [128, 512], F32)
        u = uid()
        S.add("pool", lambda e: e.memset(vaug[:], 1.0), writes=[("vaug", kc) for kc in range(18)])
        W = w_in_p[l]
        prj = Ring([0, 1, 2])
        aux = Ring([3, 4])
        vring = Ring([5, 6])
        qnr = Ring([0, 1, 2])
        def pass1_pre(tb):
            t0, n = TBS[tb]
            xb = xnb[tb % 2]
            xkeys = [("xnb", tb % 2, c) for c in range(8)]
            make_xn(bl, l, 1, tb, lambda c, xb=xb, n=n: xb[:, c, 0:n], lambda c, tb=tb: ("xnb", tb % 2, c))
            dma_sp(xscr[:, :, t0:t0 + n], xb[:, :, 0:n], xkeys, [("xscr", tb)])

        sqr = Ring([0, 1])

        def pass1_items(tb):
            t0, n = TBS[tb]
            xb = xnb[tb % 2]
            xkeys = [("xnb", tb % 2, c) for c in range(8)]
            norope = (tb == 4)
            items = []
            for ci in range(7):
                if ci < 4:
                    wt, wkey, co = wq, "wq", ci * 128
                    dest = qT[:, ci, t0:t0 + n]
                    dkey = ("qT", ci, tb)
                else:
                    wt, wkey, co = wk, "wk", (ci - 4) * 128
                    dest = kT[:, ci - 4, t0:t0 + n]
                    dkey = ("kT", ci - 4, tb)
                isA = ci in (0, 1, 4)
                st = {}

                def P(st=st, wt=wt, wkey=wkey, co=co):
                    pb = prj.next()
                    st["pb"] = pb

                    def mm(e):
                        ins = None
                        for k in range(8):
                            ins = e.matmul(ps[pb][:, 0:n], lhsT=wt[:, k, co:co + 128], rhs=xb[:, k, 0:n], start=(k == 0), stop=(k == 7))
                        return ins
                    S.add("pe", mm, reads=xkeys + [wkey], writes=[PK(pb)])

                def N(st=st):
                    pb = st["pb"]
                    si = sqr.next()
                    st["si"] = si
                    S.add("act", lambda e: e.activation(out=sqb[si][:, 0:n], in_=ps[pb][:, 0:n], func=AF.Square),
                          reads=[PK(pb)], writes=[("sqb", si)])

                def M1(st=st, isA=isA, ci=ci, dest=dest, dkey=dkey):
                    pb = st["pb"]
                    qi = qnr.next()
                    st["qi"] = qi
                    qdst = dest if norope else qn[qi][:, 0:n]
                    qkey = dkey if norope else ("qn", qi)
                    if isA:
                        si = st["si"]
                        pa = aux.next()
                        S.add("pe", lambda e: e.matmul(ps[pa][:, 0:n], lhsT=blk64, rhs=sqb[si][:, 0:n], start=True, stop=True),
                              reads=[("sqb", si), "cmat"], writes=[PK(pa)])
                        rsqrt_from_psum(pa, n)
                        gi = 0 if ci < 4 else 1
                        S.add("dve", lambda e: e.scalar_tensor_tensor(out=qdst, in0=ps[pb][:, 0:n], scalar=gqk[:, l, gi:gi + 1],
                                                                        in1=rstd[:, 0:n], op0=ALU.mult, op1=ALU.mult),
                              reads=[PK(pb), "rstd", "gqk"], writes=[qkey])
                    else:
                        S.add("act", lambda e: e.activation(out=qdst, in_=ps[pb][:, 0:n], func=AF.Copy),
                              reads=[PK(pb)], writes=[qkey])

                def M2(st=st, isA=isA, ci=ci, dest=dest, dkey=dkey):
                    if ci == 0:
                        dma_sp(ropet[:], rope_d[:, :, t0:t0 + n].rearrange("m p n -> p m n"), [], ["ropet"])
                    qi = st["qi"]
                    pr = aux.next()
                    pm = permA if isA else permC
                    ti = 0 if isA else 2
                    S.add("pe", lambda e: e.matmul(ps[pr][:, 0:n], lhsT=pm, rhs=qn[qi][:, 0:n], start=True, stop=True),
                          reads=[("qn", qi), "cmat"], writes=[PK(pr)])
                    S.add("dve", lambda e: e.tensor_tensor(out=t1[:, 0:n], in0=qn[qi][:, 0:n], in1=ropet[:, ti, 0:n], op=ALU.mult),
                          reads=[("qn", qi), "ropet"], writes=["t1"])
                    S.add("dve", lambda e: e.tensor_tensor(out=t2[:, 0:n], in0=ps[pr][:, 0:n], in1=ropet[:, ti + 1, 0:n], op=ALU.mult),
                          reads=[PK(pr), "ropet"], writes=["t2"])
                    S.add("pool", lambda e: e.tensor_tensor(out=dest, in0=t1[:, 0:n], in1=t2[:, 0:n], op=ALU.add),
                          reads=["t1", "t2"], writes=[dkey])
                items.append([P, N if isA else None, M1, None if norope else M2])

            def PV():
                for sb in range(n // 128):
                    kc = (t0 // 128) + sb
                    pv = vring.next()

                    def mmv(e, pv=pv, sb=sb):
                        ins = None
                        for k in range(8):
                            ins = e.matmul(ps[pv][:, 0:384], lhsT=xb[:, k, sb * 128:(sb + 1) * 128], rhs=wv[:, k, :], start=(k == 0), stop=(k == 7))
                        return ins
                    S.add("pe", mmv, reads=xkeys + ["wv"], writes=[PK(pv)])
                    S.add("act", lambda e, pv=pv, kc=kc: e.activation(out=vaug[:, kc, :, 0:64], in_=ps[pv][:, 0:384].rearrange("p (h d) -> p h d", d=64), func=AF.Copy),
                          reads=[PK(pv)], writes=[("vaug", kc)])
            items.append([PV, None, None, None])
            return items

        dma_w(wq[:], W[:, 0:512].rearrange("(k p) n -> p k n", p=128), [], ["wq"])
        dma_w(wk[:], W[:, 512:896].rearrange("(k p) n -> p k n", p=128), [], ["wk"])
        dma_w(wv[:], W[:, 896:1280].rearrange("(k p) n -> p k n", p=128), [], ["wv"])
        pass1_pre(blocks[0])
        flat = []
        for i_, tb in enumerate(blocks):
            its = pass1_items(tb)
            if i_ + 1 < len(blocks):
                its[0][0] = (lambda p0_=its[0][0], nx=blocks[i_ + 1]: (pass1_pre(nx), p0_()))
            flat.extend(its)
        NF = len(flat)
        for i_ in range(NF + 3):
            for stg in range(4):
                j_ = i_ - stg
                if 0 <= j_ < NF and flat[j_][stg] is not None:
                    flat[j_][stg]()
        S.barrier()
        A.release(m1)
        if stop == f"p1_{l}":
            pass

        woh = A.t("woh", [128, 4, 1024], BF16)
        cat = [A.t(f"cat{i}", [128, 4, 512], BF16) for i in range(2)]
        lnb = A.t("lnb", [64, 512], F32)
        pT = [A.t(f"pT{i}", [128, 512], BF16) for i in range(3)]
        rden = [A.t(f"rden{i}", [64, 512], F32) for i in range(2)]
        tt0 = A.t("tt0", [64, 512], F32)
        tt1 = A.t("tt1", [64, 512], F32)
        od_ = A.t("od_", [64, 512], F32)
        odn = A.t("odn", [64, 512], F32)
        sq64 = A.t("sq64", [64, 512], BF16)
        qpad = [A.t(f"qpad{i}", [128, 512], BF16) for i in range(3)]
        qpr = Ring([0, 1, 2])
        dma_w(woh[:], w_out[l][0:512, :].rearrange("(c p) n -> p c n", p=128), [], ["woh"])
        sring = Ring([0, 1, 6])
        oring = Ring([2, 3, 4, 5])
        ptr = Ring([0, 1, 2])
        rdr = Ring([0, 1])
        qblocks = [0, 1, 2, 3] if last else [0, 1, 2, 3, 4]
        steps = []

        def qblock(qi_, tb):
            t0, n = TBS[tb]
            kcs = list(range(18)) if tb != 4 else [16, 17]
            ct = cat[qi_ % 2]
            ci_ = qi_ % 2

            def warm():
                wbk = sring.next()

                def burst(e):
                    ins = None
                    for _ in range(WARM_N):
                        ins = e.matmul(ps[wbk][:, 0:512], lhsT=woh[:, 0, 0:128], rhs=woh[:, 1, 0:512], start=True, stop=True)
                    return ins
                S.add("pe", burst, reads=["woh"], writes=[PK(wbk)])
            if WARM_N > 0:
                steps.append((warm, lambda: None, lambda: None, None))

            hm_list = []

            def prep_qpad(hm):
                qi = qpr.next()
                hm["qp"] = qi
                r0, r1 = hm["krows"]
                S.add("pool", lambda e: e.memset(qpad[qi][:, 0:n], 0.0), writes=[("qpad", qi)])
                S.add("pool", lambda e: e.tensor_copy(out=qpad[qi][r0:r1, 0:n], in_=qT[r0:r1, hm["qchunk"], t0:t0 + n]),
                      reads=[("qT", hm["qchunk"], tb)], writes=[("qpad", qi)])

            def head_pass(krows, kchunk, qchunk, vslot, scale, po, tp, post):
                hm = dict(krows=krows, qchunk=qchunk)
                hm_list.append(hm)
                myidx = len(hm_list) - 1
                for ii, kc in enumerate(kcs):
                    st = {}
                    ktb = kc // 4 if kc < 16 else 4

                    def qk(st=st, kc=kc, ktb=ktb, ii=ii):
                        if ii == 0:
                            if "qp" not in hm:
                                prep_qpad(hm)
                            if myidx + 1 < len(hm_list) and "qp" not in hm_list[myidx + 1]:
                                prep_qpad(hm_list[myidx + 1])
                        sb_ = sring.next()
                        st["sb"] = sb_
                        qi = hm["qp"]

                        def mms(e):
                            return e.matmul(ps[sb_][:, 0:n], lhsT=kT[:, kchunk, kc * 128:(kc + 1) * 128],
                                            rhs=qpad[qi][:, 0:n], start=True, stop=True)
                        S.add("pe", mms, reads=[("kT", kchunk, ktb), ("qpad", qi)], writes=[PK(sb_)])

                    def ex(st=st):
                        sb_ = st["sb"]
                        pi = ptr.next()
                        st["pi"] = pi
                        S.add("act", lambda e: e.activation(out=pT[pi][:, 0:n], in_=ps[sb_][:, 0:n], func=AF.Exp, scale=scale),
                              reads=[PK(sb_)], writes=[("pT", pi)])

                    def pv(st=st, kc=kc, ii=ii):
                        pi = st["pi"]
                        S.add("pe", lambda e: e.matmul(ps[po][:, 0:n], lhsT=vaug[:, kc, vslot, :], rhs=pT[pi][:, 0:n],
                                                       start=(ii == 0), stop=(ii == len(kcs) - 1)),
                              reads=[("pT", pi), ("vaug", kc)], writes=[PK(po)])
                    steps.append((qk, ex, pv, post if ii == len(kcs) - 1 else None))

            for h in range(4):
                r0 = 64 * (h // 2)
                po = oring.next()

                def postA(po=po, h=h):
                    ri = rdr.next()
                    S.add("dve", lambda e: e.reciprocal(out=rden[ri][:, 0:n], in_=ps[po][64:128, 0:n]), reads=[PK(po)], writes=[("rden", ri)])
                    p0 = 64 * (h % 2)
                    S.add("dve", lambda e: e.tensor_tensor(out=ct[p0:p0 + 64, h // 2, 0:n], in0=ps[po][0:64, 0:n], in1=rden[ri][:, 0:n], op=ALU.mult),
                          reads=[PK(po), ("rden", ri)], writes=[("cat", ci_, h)])
                    return []
                head_pass((r0, r0 + 64), 0, h % 2, h // 2, 0.125, po, None, postA)
            for h in range(4):
                base = 64 * (h % 2)
                pos = [oring.next(), oring.next()]

                def postC(pos=pos, h=h):
                    ri0 = rdr.next()
                    ri1 = rdr.next()
                    p0 = 64 * (h % 2)
                    S.add("dve", lambda e: e.reciprocal(out=rden[ri0][:, 0:n], in_=ps[pos[0]][64:128, 0:n]), reads=[PK(pos[0])], writes=[("rden", ri0)])
                    S.add("dve", lambda e: e.tensor_tensor(out=tt0[:, 0:n], in0=ps[pos[0]][0:64, 0:n], in1=rden[ri0][:, 0:n], op=ALU.mult),
                          reads=[PK(pos[0]), ("rden", ri0)], writes=["tt0"])
                    S.add("dve", lambda e: e.reciprocal(out=rden[ri1][:, 0:n], in_=ps[pos[1]][64:128, 0:n]), reads=[PK(pos[1])], writes=[("rden", ri1)])
                    S.add("dve", lambda e: e.tensor_tensor(out=tt1[:, 0:n], in0=ps[pos[1]][0:64, 0:n], in1=rden[ri1][:, 0:n], op=ALU.mult),
                          reads=[PK(pos[1]), ("rden", ri1)], writes=["tt1"])
                    S.add("dve", lambda e: e.scalar_tensor_tensor(out=od_[:, 0:n], in0=tt1[:, 0:n], scalar=neglam[:, l:l + 1], in1=tt0[:, 0:n],
                                                                    op0=ALU.mult, op1=ALU.add),
                          reads=["tt0", "tt1", "neglam"], writes=["od_"])

                    def st1():
                        S.add("act", lambda e: e.activation(out=sq64[:, 0:n], in_=od_[:, 0:n], func=AF.Square), reads=["od_"], writes=["sq64"])

                    def st2():
                        S.add("pe", lambda e: e.matmul(ps[7][0:64, 0:n], lhsT=blk64[0:64, 0:64], rhs=sq64[:, 0:n], start=True, stop=True),
                              reads=["sq64", "cmat"], writes=[PK(7)])

                    def st3():
                        S.add("act", lambda e: e.activation(out=lnb[:, 0:n], in_=ps[7][0:64, 0:n], func=AF.Ln, bias=epsc[0:64, 0:1], scale=1.0),
                              reads=[PK(7), "epsc"], writes=["lnb"])
                        S.add("act", lambda e: e.activation(out=odn[:, 0:n], in_=lnb[:, 0:n], func=AF.Exp, scale=-0.5), reads=["lnb"], writes=["odn"])

                    def st4():
                        S.add("dve", lambda e: e.scalar_tensor_tensor(out=ct[p0:p0 + 64, 2 + h // 2, 0:n], in0=od_[:, 0:n], scalar=gsub1m[:, l:l + 1],
                                                                        in1=odn[:, 0:n], op0=ALU.mult, op1=ALU.mult),
                              reads=["od_", "odn", "gsub1m"], writes=[("cat", ci_, 4 + h)])
                    tasks = [(20, st1), (23, st2), (26, st3), (30, st4)]
                    if h == 3:
                        def outproj():
                            for oc in range(8):
                                pb = 7

                                def mmo(e, oc=oc, pb=pb):
                                    ins = None
                                    for hh in range(4):
                                        ins = e.matmul(ps[pb][:, 0:n], lhsT=woh[:, hh, oc * 128:(oc + 1) * 128], rhs=ct[:, hh, 0:n],
                                                       start=(hh == 0), stop=(hh == 3))
                                    return ins
                                S.add("pe", mmo, reads=[("cat", ci_, hh) for hh in range(8)] + ["woh"], writes=[PK(pb)])
                                resid_add(bl, l, 1, tb, oc, pb, n)
                        tasks.append((34, outproj))
                    return tasks
                postC.is_c = True
                for c in range(2):
                    r0 = base + 32 * c
                    head_pass((r0, r0 + 32), 1 + h // 2, 2 + h // 2, 2 + h, 32 ** -0.5, pos[c], (r0, 0), postC if c == 1 else None)

        for qi_, tb in enumerate(qblocks):
            qblock(qi_, tb)
        LA = 2
        deferred = []
        NS = len(steps)
        for i in range(min(LA, NS)):
            steps[i][0]()
        for i in range(NS):
            if i + LA < NS:
                steps[i + LA][0]()
            steps[i][1]()
            steps[i][2]()
            if steps[i][3] is not None:
                if getattr(steps[i][3], "is_c", False):
                    for d in sorted(deferred, key=lambda d: d[0]):
                        d[1]()
                    deferred = []
                for (dl, fn) in steps[i][3]():
                    deferred.append((i + dl, fn))
            ready = [d for d in deferred if d[0] <= i]
            deferred = [d for d in deferred if d[0] > i]
            for d in ready:
                d[1]()
        for d in sorted(deferred, key=lambda d: d[0]):
            d[1]()
        S.barrier()
        A.release(m)

        m2 = A.mark()
        blocks2 = [0, 1, 2, 3] if last else [0, 1, 2, 3, 4]
        xnb = [A.t(f"xnb{i}", [128, 8, 512], BF16) for i in range(2)]
        wu = A.t("wu", [128, 8, 256], BF16)
        wvg = A.t("wvg", [128, 8, 256], BF16)
        wgl = A.t("wgl", [128, 8, 512], BF16)
        wob = A.t("wob", [128, 2, 1024], BF16)
        wod = A.t("wod", [128, 2, 1024], BF16)
        wsT = A.t("wsT", [128, 4, 128], BF16)
        gvb = A.t("gvb", [128, 256], F32)
        bst = A.t("bst", [64, 4, 512], F32)
        diag = A.t("diag", [128, 2, 31, 128], BF16)
        ypl = A.t("ypl", [128, 2, NLAT + 30], BF16)
        ypc = A.t("ypc", [128, 2, NCTX + 30], BF16)
        ug = A.t("ug", [64, 4, 512], F32)
        vge = A.t("vge", [128, 4, 256], F32)
        junk = A.t("junk", [128, 256], BF16)
        ssum = A.t("ssum", [128, 8], F32)
        vn = A.t("vn", [128, 4, 256], BF16)
        tmb = [A.t("tmb0", [64, 512], F32), A.t("tmb1", [64, 512], F32)]
        catb = A.t("catb", [128, 2, 512], BF16)
        sg = A.t("sg", [128, 512], F32)
        zz = A.t("zz", [128, 2, 512], F32)
        sqz = A.t("sqz", [128, 2, 512], BF16)
        odd = A.t("odd", [128, 2, 512], BF16)
        dma_w(wob[:], w_out[l][512:768, :].rearrange("(c p) n -> p c n", p=128), [], ["wob"])
        dma_w(wod[:], w_out[l][768:1024, :].rearrange("(c p) n -> p c n", p=128), [], ["wod"])
        dma_w(wsT[:], wsT_d[l], [], ["wsT"])
        dma_sp(gvb[:], gv_d[:, l, :], [], ["gvb"])
        dma_sp(bst[:], bs_d[:, l, :, :], [], ["bst"])
        for c in range(2):
            for k in range(31):
                S.add("dve", lambda e, c=c, k=k: e.tensor_scalar(out=diag[:, c, k, :], in0=identF[:], scalar1=wdw[:, l, c, k:k + 1], scalar2=None, op0=ALU.mult),
                      reads=["identF", "wdw"], writes=[("diag", c)])
        S.add("pool", lambda e: e.memset(ypl[:], 0.0), writes=[("ypl", c, tb) for c in range(2) for tb in range(4)] + ["yplpad"])
        S.add("pool", lambda e: e.memset(ypc[:], 0.0), writes=[("ypc", c) for c in range(2)])
        pr2 = Ring([0, 1, 2, 3])
        outr = Ring([4])
        pr2b = Ring([5, 6])
        outrb = Ring([7])
        def pass2a_pre(tb):
            t0, n = TBS[tb]
            xb = xnb[tb % 2]
            xkeys = [("xnb", tb % 2, c) for c in range(8)]
            dma_sp(xb[:, :, 0:n], xscr[:, :, t0:t0 + n], [("xscr", tb)], xkeys)

        def pass2a(tb):
            t0, n = TBS[tb]
            xb = xnb[tb % 2]
            xkeys = [("xnb", tb % 2, c) for c in range(8)]
            nsb = n // 128
            for g in range(4):
                pb = pr2.next()

                def mmu(e, pb=pb, g=g):
                    ins = None
                    for k in range(8):
                        ins = e.matmul(ps[pb][0:64, 0:n], lhsT=wu[:, k, g * 64:(g + 1) * 64], rhs=xb[:, k, 0:n], start=(k == 0), stop=(k == 7))
                    return ins
                S.add("pe", mmu, reads=xkeys + ["wu"], writes=[PK(pb)])
                S.add("act", lambda e, pb=pb, g=g: e.activation(out=ug[:, g, 0:n], in_=ps[pb][0:64, 0:n], func=AF.Gelu_apprx_tanh),
                      reads=[PK(pb)], writes=[("ug", g)])
            S.add("dve", lambda e: e.memset(ssum[:, 0:4], 0.0), writes=[("ssum", sb) for sb in range(4)])
            for sb in range(nsb):
                pb = pr2.next()

                def mmv(e, pb=pb, sb=sb):
                    ins = None
                    for k in range(8):
                        ins = e.matmul(ps[pb][:, 0:256], lhsT=xb[:, k, sb * 128:(sb + 1) * 128], rhs=wvg[:, k, :], start=(k == 0), stop=(k == 7))
                    return ins
                S.add("pe", mmv, reads=xkeys + ["wvg"], writes=[PK(pb)])
                S.add("act", lambda e, pb=pb, sb=sb: e.activation(out=vge[:, sb, :], in_=ps[pb][:, 0:256], func=AF.Gelu_apprx_tanh),
                      reads=[PK(pb)], writes=[("vge", sb)])
                S.add("act", lambda e, sb=sb: e.activation(out=junk[:], in_=vge[:, sb, :], func=AF.Square, accum_out=ssum[:, sb:sb + 1]),
                      reads=[("vge", sb), ("ssum", sb)], writes=["junk", ("ssum", sb)])
            for c in range(2):
                pa = pr2.next()
                pg = pr2.next()

                def mmg(e, pbk, co):
                    ins = None
                    for k in range(8):
                        ins = e.matmul(ps[pbk][:, 0:n], lhsT=wgl[:, k, co:co + 128], rhs=xb[:, k, 0:n], start=(k == 0), stop=(k == 7))
                    return ins
                S.add("pe", lambda e, pa=pa, c=c, mmg=mmg: mmg(e, pa, c * 128), reads=xkeys + ["wgl"], writes=[PK(pa)])
                S.add("pe", lambda e, pg=pg, c=c, mmg=mmg: mmg(e, pg, 256 + c * 128), reads=xkeys + ["wgl"], writes=[PK(pg)])
                S.add("act", lambda e, pg=pg: e.activation(out=sg[:, 0:n], in_=ps[pg][:, 0:n], func=AF.Sigmoid), reads=[PK(pg)], writes=["sg"])
                if tb != 4:
                    S.add("dve", lambda e, pa=pa, c=c: e.tensor_tensor(out=ypl[:, c, 15 + t0:15 + t0 + n], in0=ps[pa][:, 0:n], in1=sg[:, 0:n], op=ALU.mult),
                          reads=[PK(pa), "sg", "yplpad"], writes=[("ypl", c, tb)])
                else:
                    S.add("dve", lambda e, pa=pa, c=c: e.tensor_tensor(out=ypc[:, c, 15:15 + n], in0=ps[pa][:, 0:n], in1=sg[:, 0:n], op=ALU.mult),
                          reads=[PK(pa), "sg"], writes=[("ypc", c)])
            S.add("act", lambda e: e.activation(out=ssum[:, 4:4 + nsb], in_=ssum[:, 0:nsb], func=AF.Sqrt, bias=epsc[:, 0:1], scale=1.0 / 256.0),
                  reads=[("ssum", sb) for sb in range(nsb)] + ["epsc"], writes=["ssr"])
            S.add("dve", lambda e: e.reciprocal(out=ssum[:, 4:4 + nsb], in_=ssum[:, 4:4 + nsb]), reads=["ssr"], writes=["ssr"])
            for sb in range(nsb):
                S.add("dve", lambda e, sb=sb: e.scalar_tensor_tensor(out=vn[:, sb, :], in0=vge[:, sb, :], scalar=ssum[:, 4 + sb:5 + sb], in1=gvb[:],
                                                                       op0=ALU.mult, op1=ALU.mult),
                      reads=[("vge", sb), "ssr", "gvb"], writes=[("vn", sb)])
            for g in range(4):
                pb = pr2.next()
                ti = g % 2
                p0 = 64 * (g % 2)

                def mmm(e, pb=pb, g=g):
                    ins = None
                    for sb in range(nsb):
                        ins = e.matmul(ps[pb][0:64, sb * 128:(sb + 1) * 128], lhsT=vn[:, sb, g * 64:(g + 1) * 64], rhs=wsT[:, g, :], start=True, stop=True)
                    return ins
                S.add("pe", mmm, reads=[("vn", sb) for sb in range(nsb)] + ["wsT"], writes=[PK(pb)])
                S.add("dve", lambda e, pb=pb, g=g, ti=ti: e.tensor_tensor(out=tmb[ti][:, 0:n], in0=ps[pb][0:64, 0:n], in1=bst[:, g, 0:n], op=ALU.add),
                      reads=[PK(pb), "bst"], writes=[("tmb", ti)])
                S.add("dve", lambda e, g=g, ti=ti, p0=p0: e.tensor_tensor(out=catb[p0:p0 + 64, g // 2, 0:n], in0=ug[:, g, 0:n], in1=tmb[ti][:, 0:n], op=ALU.mult),
                      reads=[("tmb", ti), ("ug", g)], writes=[("catb", g)])
            for oc in range(8):
                po = outr.next()

                def mmo(e, po=po, oc=oc):
                    ins = None
                    for cc in range(2):
                        ins = e.matmul(ps[po][:, 0:n], lhsT=wob[:, cc, oc * 128:(oc + 1) * 128], rhs=catb[:, cc, 0:n], start=(cc == 0), stop=(cc == 1))
                    return ins
                S.add("pe", mmo, reads=[("catb", g) for g in range(4)] + ["wob"], writes=[PK(po)])
                if "B" in DEBUG_PARTS:
                    resid_add(bl, l, 1, tb, oc, po, n)

        dma_w(wu[:], W[:, 1280:1536].rearrange("(k p) n -> p k n", p=128), [], ["wu"])
        dma_w(wvg[:], W[:, 1536:1792].rearrange("(k p) n -> p k n", p=128), [], ["wvg"])
        dma_w(wgl[:], W[:, 1792:2304].rearrange("(k p) n -> p k n", p=128), [], ["wgl"])

        def pass2b(tb):
            t0, n = TBS[tb]
            for c in range(2):
                pz = pr2b.next()
                if tb != 4:
                    yk = [("ypl", c, t) for t in range(max(0, tb - 1), min(3, tb + 1) + 1)] + ["yplpad"]
                    ysrc = lambda k, c=c, t0=t0, n=n: ypl[:, c, t0 + k:t0 + k + n]
                else:
                    yk = [("ypc", c)]
                    ysrc = lambda k, c=c, n=n: ypc[:, c, k:k + n]

                def mmc(e, pz=pz, c=c, ysrc=ysrc, n=n):
                    ins = None
                    for k in range(31):
                        ins = e.matmul(ps[pz][:, 0:n], lhsT=diag[:, c, k, :], rhs=ysrc(k), start=(k == 0), stop=(k == 30))
                    return ins
                S.add("pe", mmc, reads=yk + [("diag", c)], writes=[PK(pz)])
                S.add("act", lambda e, pz=pz, c=c, n=n: e.activation(out=zz[:, c, 0:n], in_=ps[pz][:, 0:n], func=AF.Identity, bias=bdw[:, l, c:c + 1]),
                      reads=[PK(pz), "bdw"], writes=[("zz", c)])
                S.add("act", lambda e, c=c, n=n: e.activation(out=sqz[:, c, 0:n], in_=zz[:, c, 0:n], func=AF.Square), reads=[("zz", c)], writes=[("sqz", c)])
            pn = pr2b.next()

            def mmn(e, pn=pn, n=n):
                ins = None
                for c in range(2):
                    ins = e.matmul(ps[pn][:, 0:n], lhsT=ones256, rhs=sqz[:, c, 0:n], start=(c == 0), stop=(c == 1))
                return ins
            S.add("pe", mmn, reads=[("sqz", 0), ("sqz", 1), "cmat"], writes=[PK(pn)])
            rsqrt_from_psum(pn, n)
            for c in range(2):
                tx = tmpx[c]
                S.add("pool", lambda e, c=c, tx=tx, n=n: e.tensor_tensor(out=tx[:, 0:n], in0=zz[:, c, 0:n], in1=rstd[:, 0:n], op=ALU.mult),
                      reads=[("zz", c), "rstd"], writes=[("tmpx", c)])
                S.add("act", lambda e, c=c, tx=tx, n=n: e.activation(out=odd[:, c, 0:n], in_=tx[:, 0:n], func=AF.Silu, scale=gconv[:, l, c:c + 1]),
                      reads=[("tmpx", c), "gconv"], writes=[("odd", c)])
            for oc in range(8):
                po = outrb.next()

                def mmo2(e, po=po, oc=oc, n=n):
                    ins = None
                    for c in range(2):
                        ins = e.matmul(ps[po][:, 0:n], lhsT=wod[:, c, oc * 128:(oc + 1) * 128], rhs=odd[:, c, 0:n], start=(c == 0), stop=(c == 1))
                    return ins
                S.add("pe", mmo2, reads=[("odd", 0), ("odd", 1), "wod"], writes=[PK(po)])
                if "D" in DEBUG_PARTS:
                    resid_add(bl, l, 1, tb, oc, po, n)

        def record(fn):
            items = []
            S.add = lambda *a_, **k_: items.append((a_, k_))
            try:
                fn()
            finally:
                del S.add
            return items

        def merge_emit(la, lb):
            i = j = 0
            na, nb_ = len(la), len(lb)
            while i < na or j < nb_:
                if j >= nb_ or (i < na and i * nb_ <= j * na):
                    S.add(*la[i][0], **la[i][1])
                    i += 1
                else:
                    S.add(*lb[j][0], **lb[j][1])
                    j += 1

        def do2a(i_):
            if i_ + 1 < len(blocks2):
                pass2a_pre(blocks2[i_ + 1])
            pass2a(blocks2[i_])

        pass2a_pre(blocks2[0])
        NB2 = len(blocks2)
        for i_ in range(NB2 + 2):
            ia, ib = i_, i_ - 2
            la = record(lambda: do2a(ia)) if ia < NB2 else []
            lb = record(lambda: pass2b(blocks2[ib])) if 0 <= ib < NB2 else []
            merge_emit(la, lb)
        S.barrier()
        A.release(m2)

    done = False
    for bl in range(nb):
        for tb in range(4):
            t0_, n_ = TBS[tb]
            dma_sp(hT[:, :, t0_:t0_ + n_], xT[bl][:, t0_:t0_ + n_].rearrange("(c p) t -> p c t", p=128), [], [("h", c, tb) for c in range(8)])
        dma_sp(hT[:, :, NLAT:NTOK], ctxT[bl].rearrange("(c p) t -> p c t", p=128), [], [("h", c, 4) for c in range(8)])
        for l in range(depth):
            last = (l == DEPTH - 1)
            ffn(bl, l, 0, [0, 1, 2, 3, 4], ada_next=(l + 1 if (bl == 0 and l + 1 < depth) else None))
            if stop == f"ffn1_{l}":
                dump_and_end(bl)
                done = True
                break
            mixer(bl, l)
            if stop == f"mix_{l}":
                dump_and_end(bl)
                done = True
                break
            ffn(bl, l, 1, [0, 1, 2, 3] if last else [0, 1, 2, 3, 4])
            if stop == f"ffn2_{l}":
                dump_and_end(bl)
                done = True
                break
        if done:
            continue
        mf = A.mark()
        ob = [A.t(f"ob{i}", [128, 8, 512], F32) for i in range(2)]
        for tb in range(4):
            t0, n = TBS[tb]
            o = ob[tb % 2]
            make_xn(bl, 0, 0, tb, lambda c, o=o: o[:, c, :], lambda c, tb=tb: ("ob", tb % 2, c), final=True)
            dma_sp(outT[bl][:, t0:t0 + n].rearrange("(c p) t -> p c t", p=128), o[:], [("ob", tb % 2, c) for c in range(8)], [("ob", tb % 2, c) for c in range(8)] + ["outT"])
        S.barrier()
        A.release(mf)
    S.add("sp", lambda e: None, reads=["outT" if stop is None else "dbg"])
    S.emit()
    return nc


def _rope_tables():
    t = np.arange(NLAT)
    row = (t // 64).astype(np.float32)
    col = (t % 64).astype(np.float32)
    out = np.zeros((4, 128, NLAT), np.float32)
    for ti, hd in ((0, 64), (2, 32)):
        quarter = hd // 4
        inv = (np.float32(10000.0) ** (-np.arange(quarter, dtype=np.float32) / np.float32(quarter))).astype(np.float32)
        ang = np.concatenate([row[:, None] * inv[None, :], col[:, None] * inv[None, :]], axis=-1).astype(np.float32)
        cos = np.cos(ang).astype(np.float32)
        sin = np.sin(ang).astype(np.float32)
        half = hd // 2
        for p in range(128):
            d = p % hd
            j = d % half
            out[ti, p] = cos[:, j]
            out[ti + 1, p] = (-sin[:, j]) if d < half else sin[:, j]
    return out


def _const_mats():
    m = np.zeros((6, 128, 128), np.float32)
    m[0] = 1.0 / 1024.0
    m[1, 0:64, 0:64] = 1.0 / 64.0
    m[1, 64:128, 64:128] = 1.0 / 64.0
    m[2] = 1.0 / 256.0
    for mm_ in range(128):
        pa = mm_ + 32 if (mm_ % 64) < 32 else mm_ - 32
        m[3, pa, mm_] = 1.0
        pc = mm_ + 16 if (mm_ % 32) < 16 else mm_ - 16
        m[4, pc, mm_] = 1.0
    m[5] = np.eye(128, dtype=np.float32)
    return m


def _col_perm():
    qa = lambda h: list(range(h * 64, (h + 1) * 64))
    perm = qa(0) + qa(2) + qa(1) + qa(3)
    perm += list(range(256, 512))
    perm += list(range(512, 640))
    perm += list(range(768, 1024))
    perm += list(range(640, 768))
    perm += list(range(1024, 1280))
    perm += list(range(1280, 2304))
    return np.array(perm)


def prep_shared(inp):
    f = lambda a: np.ascontiguousarray(np.asarray(a, dtype=np.float32))
    sh = {}
    sh["w_ada"] = f(inp["w_ada"])
    sh["b_adaT"] = f(np.asarray(inp["b_ada"]).reshape(DEPTH, 72, 128).transpose(2, 0, 1))
    sh["g_normT"] = f(np.asarray(inp["g_norm"]).reshape(DEPTH, 3, 8, 128).transpose(3, 0, 1, 2))
    sh["g_finalT"] = f(np.asarray(inp["g_final"]).reshape(8, 128).T)
    for k in ("w_ff1_in", "w_ff1_out", "w_ff2_in", "w_ff2_out", "w_out"):
        sh[k] = f(inp[k])
    sh["w_in_p"] = f(np.asarray(inp["w_in"])[:, :, _col_perm()])
    gq = np.asarray(inp["g_q_a"])
    gk = np.asarray(inp["g_k_a"])
    gqk = np.stack([np.tile(gq, (1, 2)), np.tile(gk, (1, 2))], axis=-1)
    sh["gqk"] = f(gqk.transpose(1, 0, 2))
    sh["lamw"] = f(np.broadcast_to(np.asarray(inp["lam_c"])[None], (64, DEPTH, 4, 32)))
    sh["gsub"] = f(np.asarray(inp["g_sub_c"]).T)
    sh["gv_bc"] = f(np.broadcast_to(np.asarray(inp["g_v_b"])[None], (128, DEPTH, 256)))
    sh["wsT"] = f(np.asarray(inp["w_s_b"]).transpose(0, 3, 1, 2))
    bs = np.asarray(inp["b_s_b"])
    sh["bs_tbl"] = f(np.broadcast_to(np.tile(bs, (1, 1, 4))[None], (64, DEPTH, 4, 512)))
    sh["wdw"] = f(np.asarray(inp["w_dw_d"]).reshape(DEPTH, 31, 2, 128).transpose(3, 0, 2, 1))
    sh["bdw"] = f(np.asarray(inp["b_dw_d"]).reshape(DEPTH, 2, 128).transpose(2, 0, 1))
    sh["gconv"] = f(np.asarray(inp["g_conv_d"]).reshape(DEPTH, 2, 128).transpose(2, 0, 1))
    sh["rope"] = _rope_tables()
    sh["cmat"] = _const_mats()
    return sh


def prep_core(inp, bids):
    x = np.asarray(inp["x"])
    ctx = np.asarray(inp["ctx"])
    c = np.asarray(inp["c"])
    cc = np.asarray(inp["c_ctx"])
    d = {}
    d["xT"] = np.ascontiguousarray(np.stack([x[b].T for b in bids]).astype(np.float32))
    d["ctxT"] = np.ascontiguousarray(np.stack([ctx[b].T for b in bids]).astype(np.float32))
    vecs = [c[b] for b in bids]
    while len(vecs) < 2:
        vecs.append(c[bids[0]])
    vecs.append(cc)
    cT = np.stack(vecs, axis=-1).reshape(8, 128, 3).transpose(1, 0, 2)
    d["cT"] = np.ascontiguousarray(cT.astype(np.float32))
    return d


_NC_CACHE = {}


def kernel(**inputs):
    B = np.asarray(inputs["x"]).shape[0]
    nb = B // NCORES
    if "prog" not in _NC_CACHE:
        _NC_CACHE["prog"] = build_program(nb=nb)
    nc = _NC_CACHE["prog"]
    sh = prep_shared(inputs)
    in_maps = []
    for i in range(NCORES):
        d = dict(sh)
        d.update(prep_core(inputs, list(range(i * nb, (i + 1) * nb))))
        in_maps.append(d)
    res = run_bass_kernel_spmd(nc, in_maps, core_ids=list(range(NCORES)))
    out = np.empty((B, NLAT, D), np.float32)
    for i in range(NCORES):
        o = res.results[i]["outT"]
        for jb in range(nb):
            out[i * nb + jb] = o[jb].T
    return out
```

```python
import math
import contextlib
import numpy as np
import ml_dtypes
import concourse.bass as bass
import concourse.mybir as mybir
from concourse.bass_utils import run_bass_kernel_spmd

F32 = mybir.dt.float32
BF16 = mybir.dt.bfloat16
AF = mybir.ActivationFunctionType
ALU = mybir.AluOpType

ENGS = ["pe", "act", "dve", "pool", "sp"]

D = 1024
NLAT = 2048
NCTX = 256
NTOK = NLAT + NCTX
DFF = 2816
DEPTH = 2
EPS = 1e-6
NCORES = 8
TBS = [(0, 512), (512, 512), (1024, 512), (1536, 512), (2048, 256)]
DEBUG_PARTS = set("ACBD")
WARM_N = 0


class Op:
    __slots__ = ("eng", "fn", "deps", "sig", "sigval", "ch", "inc", "dma", "idx")


class Sched:
    def __init__(self, nc):
        self.nc = nc
        self.ops = {e: [] for e in ENGS}
        self.last_w = {}
        self.readers = {}
        self.ch_eng = {}
        self.pending_bar = {e: [] for e in ENGS}
        self.nops = 0
        self.last_on_ch = {}
        self.dma_ring = {}
        self.DMA_RING = {"sp": 16, "pool": 24}

    def add(self, eng, fn, reads=(), writes=(), ch=None, dma=False):
        op = Op()
        op.eng = eng
        op.fn = fn
        op.sig = bool(dma)
        op.sigval = None
        op.dma = dma
        op.ch = ch if ch is not None else eng
        if dma:
            k = self.dma_ring.get(eng, 0)
            self.dma_ring[eng] = k + 1
            op.ch = f"{eng}_d{k % self.DMA_RING[eng]}"
        op.inc = 16 if dma else 1
        op.idx = self.nops
        self.nops += 1
        if op.ch in self.ch_eng:
            assert self.ch_eng[op.ch] == eng, (op.ch, eng)
        else:
            self.ch_eng[op.ch] = eng
        deps = {}
        for k in reads:
            w = self.last_w.get(k)
            if w is not None:
                deps[w.idx] = (w, True)
        for k in writes:
            w = self.last_w.get(k)
            if w is not None and w.idx not in deps:
                deps[w.idx] = (w, False)
            for r in self.readers.get(k, ()):
                if r.idx not in deps:
                    deps[r.idx] = (r, False)
        need = []
        for (d, raw) in deps.values():
            if d.eng == eng and not d.dma and not dma and not raw:
                continue
            if d.eng == eng and eng == "pe" and not d.dma and not dma:
                continue
            need.append(d)
        if dma:
            prev = self.last_on_ch.get(op.ch)
            if prev is not None:
                need.append(prev)
        for d in self.pending_bar[eng]:
            need.append(d)
        self.pending_bar[eng] = []
        for d in need:
            d.sig = True
        op.deps = need
        for k in writes:
            self.last_w[k] = op
            self.readers[k] = []
        for k in reads:
            if k in writes:
                continue
            self.readers.setdefault(k, []).append(op)
        self.ops[eng].append(op)
        self.last_on_ch[op.ch] = op
        return op

    def barrier(self):
        frontier = list(self.last_on_ch.values())
        for e in ENGS:
            self.pending_bar[e] = list(frontier)

    def emit(self):
        nc = self.nc
        chans = list(self.ch_eng.keys())
        cum = {c: 0 for c in chans}
        for e in ENGS:
            for op in self.ops[e]:
                if op.sig:
                    cum[op.ch] += op.inc
                    op.sigval = cum[op.ch]
        with contextlib.ExitStack() as st:
            sems = {c: st.enter_context(nc.semaphore("s_" + c)) for c in chans}
            block = st.enter_context(nc.Block())

            def run(engname, engine):
                waited = {}
                for op in self.ops[engname]:
                    for d in op.deps:
                        if waited.get(d.ch, 0) < d.sigval:
                            engine.wait_ge(sems[d.ch], d.sigval)
                            waited[d.ch] = d.sigval
                    ins = op.fn(engine)
                    if op.sig:
                        assert ins is not None
                        ins.then_inc(sems[op.ch], op.inc)

            @block.tensor
            def _(eng):
                run("pe", eng)

            @block.scalar
            def _(eng):
                run("act", eng)

            @block.vector
            def _(eng):
                run("dve", eng)

            @block.gpsimd
            def _(eng):
                run("pool", eng)

            @block.sync
            def _(eng):
                run("sp", eng)


class Alloc:
    def __init__(self, nc, limit=229344, base=16512):
        self.nc = nc
        self.off = base
        self.limit = limit
        self.n = 0
        self.peak = base

    def mark(self):
        return self.off

    def release(self, m):
        self.off = m

    def t(self, name, shape, dtype):
        esz = 4 if dtype == F32 else 2
        nbytes = int(np.prod(shape[1:])) * esz
        nbytes = (nbytes + 63) // 64 * 64
        assert self.off + nbytes <= self.limit, (name, self.off, nbytes, self.limit)
        self.n += 1
        h = self.nc.alloc_sbuf_tensor_at(f"{name}_{self.n}", list(shape), dtype, offset=self.off)
        self.off += nbytes
        self.peak = max(self.peak, self.off)
        return h


class Ring:
    def __init__(self, items):
        self.items = list(items)
        self.i = 0

    def next(self):
        v = self.items[self.i % len(self.items)]
        self.i += 1
        return v


def build_program(nb=2, depth=DEPTH, stop=None):
    nc = bass.Bass("TRN2", target_bir_lowering=False)
    S = Sched(nc)
    A = Alloc(nc)

    def din(name, shape, dt=F32):
        return nc.dram_tensor(name, list(shape), dt, kind="ExternalInput").ap()

    xT = din("xT", [nb, D, NLAT])
    ctxT = din("ctxT", [nb, D, NCTX])
    cT_d = din("cT", [128, 8, 3])
    w_ada = din("w_ada", [DEPTH, D, 9 * D])
    b_adaT_d = din("b_adaT", [128, DEPTH, 72])
    g_normT_d = din("g_normT", [128, DEPTH, 3, 8])
    g_finalT_d = din("g_finalT", [128, 8])
    w_ff_in = [din("w_ff1_in", [DEPTH, D, 2 * DFF]), din("w_ff2_in", [DEPTH, D, 2 * DFF])]
    w_ff_out = [din("w_ff1_out", [DEPTH, DFF, D]), din("w_ff2_out", [DEPTH, DFF, D])]
    w_in_p = din("w_in_p", [DEPTH, D, 2304])
    w_out = din("w_out", [DEPTH, D, D])
    gqk_d = din("gqk", [128, DEPTH, 2])
    lamw_d = din("lamw", [64, DEPTH, 4, 32])
    gsub_d = din("gsub", [64, DEPTH])
    gv_d = din("gv_bc", [128, DEPTH, 256])
    wsT_d = din("wsT", [DEPTH, 128, 4, 128])
    bs_d = din("bs_tbl", [64, DEPTH, 4, 512])
    wdw_d = din("wdw", [128, DEPTH, 2, 31])
    bdw_d = din("bdw", [128, DEPTH, 2])
    gconv_d = din("gconv", [128, DEPTH, 2])
    rope_d = din("rope", [4, 128, NLAT])
    cmat_d = din("cmat", [6, 128, 128])
    xscr = nc.dram_tensor("xscr", [128, 8, NTOK], BF16, kind="ExternalOutput").ap()
    if stop is None:
        outT = nc.dram_tensor("outT", [nb, D, NLAT], F32, kind="ExternalOutput").ap()
    else:
        dbg = nc.dram_tensor("dbg", [nb, 128, 8, NTOK], F32, kind="ExternalOutput").ap()

    hT = A.t("hT", [128, 8, NTOK], F32)
    cmat = A.t("cmat", [128, 6, 128], BF16)
    identF = A.t("identF", [128, 128], F32)
    modall = A.t("modall", [128, DEPTH, 72, 3], F32)
    gs = A.t("gs", [128, DEPTH, 3, 8, 3], F32)
    hg = A.t("hg", [128, DEPTH, 3, 8, 3], F32)
    b_adaT = A.t("b_adaT", [128, DEPTH, 72], F32)
    g_normT = A.t("g_normT", [128, DEPTH, 3, 8], F32)
    g_finalT = A.t("g_finalT", [128, 8], F32)
    gqk = A.t("gqk", [128, DEPTH, 2], F32)
    gsub = A.t("gsub", [64, DEPTH], F32)
    gsub1m = A.t("gsub1m", [64, DEPTH], F32)
    neglam = A.t("neglam", [64, DEPTH], F32)
    wdw = A.t("wdw", [128, DEPTH, 2, 31], F32)
    bdw = A.t("bdw", [128, DEPTH, 2], F32)
    gconv = A.t("gconv", [128, DEPTH, 2], F32)
    epsc = A.t("epsc", [128, 1], F32)
    scT = A.t("scT", [128, 8, 3], BF16)
    sq = A.t("sq", [128, 8, 512], BF16)
    rt = A.t("rt", [128, 512], F32)
    rstd = A.t("rstd", [128, 512], F32)
    tmpx = [A.t("tmpx0", [128, 512], F32), A.t("tmpx1", [128, 512], F32)]
    PERSIST = A.mark()

    ps = [nc.alloc_psum_tensor(f"ps{i}", [128, 512], F32) for i in range(8)]
    onesD = cmat[:, 0, :]
    blk64 = cmat[:, 1, :]
    ones256 = cmat[:, 2, :]
    permA = cmat[:, 3, :]
    permC = cmat[:, 4, :]

    cnt = [0]

    def uid():
        cnt[0] += 1
        return cnt[0]

    def PK(b):
        return ("ps", b)

    def dma_sp(out, in_, reads, writes):
        S.add("sp", lambda e: e.dma_start(out=out, in_=in_), reads=reads, writes=writes, ch="dq_sp", dma=True)

    def dma_w(out, in_, reads, writes):
        S.add("pool", lambda e: e.dma_start(out=out, in_=in_), reads=reads, writes=writes, ch="dq_w", dma=True)

    dma_w(cmat[:], cmat_d.rearrange("m p n -> p m n"), [], ["cmat"])
    dma_sp(identF[:], cmat_d[5], [], ["identF"])
    dma_sp(b_adaT[:], b_adaT_d, [], ["b_adaT"])
    dma_sp(g_normT[:], g_normT_d, [], ["g_normT"])
    dma_sp(g_finalT[:], g_finalT_d, [], ["g_finalT"])
    dma_sp(gqk[:], gqk_d, [], ["gqk"])
    dma_sp(gsub[:], gsub_d, [], ["gsub"])
    dma_sp(wdw[:], wdw_d, [], ["wdw"])
    dma_sp(bdw[:], bdw_d, [], ["bdw"])
    dma_sp(gconv[:], gconv_d, [], ["gconv"])
    S.add("dve", lambda e: e.memset(epsc[:], EPS), writes=["epsc"])

    m0 = A.mark()
    cTs = A.t("cTs", [128, 8, 3], F32)
    lamw = A.t("lamw", [64, DEPTH, 4, 32], F32)
    lamp = A.t("lamp", [64, DEPTH, 2, 32], F32)
    lams = A.t("lams", [64, DEPTH, 2], F32)
    lame = A.t("lame", [64, DEPTH, 2], F32)
    wab = [A.t(f"wab{i}", [128, 8, 512], BF16) for i in range(3)]
    dma_sp(cTs[:], cT_d, [], ["cTs"])
    dma_sp(lamw[:], lamw_d, [], ["lamw"])
    S.add("act", lambda e: e.activation(out=scT[:], in_=cTs[:], func=AF.Silu), reads=["cTs"], writes=["scT"])
    def adaln_steps(l, wabufs, uidx):
        stepsl = []
        for cb in range(18):
            def dm(cb=cb):
                bi = cb % len(wabufs)
                wt = wabufs[bi]
                dma_w(wt[:], w_ada[l][:, cb * 512:(cb + 1) * 512].rearrange("(k p) n -> p k n", p=128), [], [("wab", uidx, bi)])

            def one(cb=cb):
                bi = cb % len(wabufs)
                wt = wabufs[bi]

                def mm(e):
                    ins = None
                    for cc in range(4):
                        chn = cb * 4 + cc
                        for k in range(8):
                            ins = e.matmul(ps[7][:, chn * 3:(chn + 1) * 3], lhsT=wt[:, k, cc * 128:(cc + 1) * 128],
                                           rhs=scT[:, k, :], start=(k == 0), stop=(k == 7))
                    return ins
                S.add("pe", mm, reads=[("wab", uidx, bi), "scT"], writes=[PK(7)])
            stepsl.append((dm, one))

        def fin():
            psm = ps[7][:, 0:216].rearrange("p (c j) -> p c j", j=3)
            for j in range(3):
                S.add("dve", lambda e, j=j: e.tensor_tensor(out=modall[:, l, :, j], in0=psm[:, :, j], in1=b_adaT[:, l, :], op=ALU.add),
                      reads=[PK(7), "b_adaT"], writes=[("modall", l)])
            for s_ in range(3):
                for j in range(3):
                    S.add("dve", lambda e, s_=s_, j=j: e.tensor_scalar(out=gs[:, l, s_, :, j], in0=modall[:, l, (3 * s_ + 1) * 8:(3 * s_ + 2) * 8, j],
                                                                         scalar1=1.0, scalar2=None, op0=ALU.add),
                          reads=[("modall", l)], writes=[("gs", l)])
                    S.add("dve", lambda e, s_=s_, j=j: e.tensor_tensor(out=gs[:, l, s_, :, j], in0=gs[:, l, s_, :, j], in1=g_normT[:, l, s_, :], op=ALU.mult),
                          reads=[("gs", l), "g_normT"], writes=[("gs", l)])
                    S.add("dve", lambda e, s_=s_, j=j: e.tensor_scalar(out=hg[:, l, s_, :, j], in0=modall[:, l, (3 * s_ + 2) * 8:(3 * s_ + 3) * 8, j],
                                                                         scalar1=(1.0 if s_ == 1 else 0.5), scalar2=None, op0=ALU.mult),
                          reads=[("modall", l)], writes=[("hg", l)])
        return stepsl, fin

    def ada_run(stl, k):
        if k + 2 < len(stl):
            stl[k + 2][0]()
        stl[k][1]()

    st0, fin0 = adaln_steps(0, wab, 0)
    st0[0][0]()
    st0[1][0]()
    for k_ in range(len(st0)):
        ada_run(st0, k_)
    fin0()
    for l in range(depth):
        lam_init = 0.8 - 0.6 * math.exp(-0.3 * l)
        for q in range(2):
            S.add("dve", lambda e, l=l, q=q: e.tensor_tensor(out=lamp[:, l, q, :], in0=lamw[:, l, 2 * q, :], in1=lamw[:, l, 2 * q + 1, :], op=ALU.mult),
                  reads=["lamw"], writes=["lamp"])
            S.add("dve", lambda e, l=l, q=q: e.tensor_reduce(out=lams[:, l, q:q + 1], in_=lamp[:, l, q, :], axis=mybir.AxisListType.X, op=ALU.add),
                  reads=["lamp"], writes=["lams"])
        S.add("act", lambda e, l=l: e.activation(out=lame[:, l, :], in_=lams[:, l, :], func=AF.Exp), reads=["lams"], writes=["lame"])
        S.add("dve", lambda e, l=l: e.tensor_tensor(out=neglam[:, l:l + 1], in0=lame[:, l, 1:2], in1=lame[:, l, 0:1], op=ALU.subtract),
              reads=["lame"], writes=["neglam"])
        S.add("dve", lambda e, l=l, li=lam_init: e.tensor_scalar(out=neglam[:, l:l + 1], in0=neglam[:, l:l + 1], scalar1=-li, scalar2=None, op0=ALU.add),
              reads=["neglam"], writes=["neglam"])
        S.add("dve", lambda e, l=l, li=lam_init: e.tensor_scalar(out=gsub1m[:, l:l + 1], in0=gsub[:, l:l + 1], scalar1=(1.0 - li), scalar2=None, op0=ALU.mult),
              reads=["gsub"], writes=["gsub1m"])
    S.barrier()
    A.release(m0)

    def hkeys(tb, cs=range(8)):
        return [("h", c, tb) for c in cs]

    def jmod(bl, tb):
        return 2 if tb == 4 else bl

    def rsqrt_from_psum(pb, n, parts=128, scale=1.0):
        S.add("act", lambda e: e.activation(out=rt[0:parts, 0:n], in_=ps[pb][0:parts, 0:n], func=AF.Sqrt, bias=epsc[0:parts, 0:1], scale=scale),
              reads=[PK(pb), "epsc"], writes=["rt"])
        S.add("dve", lambda e: e.reciprocal(out=rstd[0:parts, 0:n], in_=rt[0:parts, 0:n]), reads=["rt"], writes=["rstd"])

    def make_xn(bl, l, s, tb, dst, dkeys, final=False, pbank=7):
        t0, n = TBS[tb]
        j = jmod(bl, tb)
        for c in range(8):
            if c in (0, 3, 6):
                S.add("act", lambda e, c=c: e.activation(out=sq[:, c, 0:n], in_=hT[:, c, t0:t0 + n], func=AF.Square),
                      reads=[("h", c, tb)], writes=[("sq", c)])
            elif c in (1, 4, 7):
                S.add("dve", lambda e, c=c: e.tensor_tensor(out=sq[:, c, 0:n], in0=hT[:, c, t0:t0 + n], in1=hT[:, c, t0:t0 + n], op=ALU.mult),
                      reads=[("h", c, tb)], writes=[("sq", c)])
            else:
                S.add("pool", lambda e, c=c: e.tensor_tensor(out=sq[:, c, 0:n], in0=hT[:, c, t0:t0 + n], in1=hT[:, c, t0:t0 + n], op=ALU.mult),
                      reads=[("h", c, tb)], writes=[("sq", c)])

        def mm(e):
            ins = None
            for c in range(8):
                ins = e.matmul(ps[pbank][:, 0:n], lhsT=onesD, rhs=sq[:, c, 0:n], start=(c == 0), stop=(c == 7))
            return ins
        S.add("pe", mm, reads=[("sq", c) for c in range(8)] + ["cmat"], writes=[PK(pbank)])
        rsqrt_from_psum(pbank, n)
        for c in range(8):
            tx = tmpx[c % 2]
            S.add("pool" if c in (1, 4, 7) else "dve", lambda e, c=c, tx=tx: e.tensor_tensor(out=tx[:, 0:n], in0=hT[:, c, t0:t0 + n], in1=rstd[:, 0:n], op=ALU.mult),
                  reads=[("h", c, tb), "rstd"], writes=[("tmpx", c % 2)])
            if final:
                S.add("act", lambda e, c=c, tx=tx: e.activation(out=dst(c), in_=tx[:, 0:n], func=AF.Identity, scale=g_finalT[:, c:c + 1]),
                      reads=[("tmpx", c % 2), "g_finalT"], writes=[dkeys(c)])
            else:
                S.add("act", lambda e, c=c, tx=tx: e.activation(out=dst(c), in_=tx[:, 0:n], func=AF.Identity,
                                                                 scale=gs[:, l, s, c, j:j + 1], bias=modall[:, l, 3 * s * 8 + c, j:j + 1]),
                      reads=[("tmpx", c % 2), ("gs", l), ("modall", l)], writes=[dkeys(c)])

    def resid_add(bl, l, s, tb, oc, pb, n):
        t0, _ = TBS[tb]
        j = jmod(bl, tb)
        S.add("dve", lambda e: e.scalar_tensor_tensor(out=hT[:, oc, t0:t0 + n], in0=ps[pb][:, 0:n], scalar=hg[:, l, s, oc, j:j + 1],
                                                       in1=hT[:, oc, t0:t0 + n], op0=ALU.mult, op1=ALU.add),
              reads=[PK(pb), ("hg", l), ("h", oc, tb)], writes=[("h", oc, tb)])

    def dump_and_end(bl):
        dma_sp(dbg[bl], hT[:], [("h", c, tb) for c in range(8) for tb in range(5)], ["dbg"])

    def ffn(bl, l, which, blocks, ada_next=None):
        s = 0 if which == 0 else 2
        m = A.mark()
        extra, extra_fin = [], None
        if ada_next is not None:
            wabx = [A.t(f"wabx{i}", [128, 8, 512], BF16) for i in range(3)]
            extra, extra_fin = adaln_steps(ada_next, wabx, uid())
            extra[0][0]()
            extra[1][0]()
        ek = [0]
        xn = A.t("xn", [128, 8, NTOK], BF16)
        wa = [A.t(f"wa{i}", [128, 8, 256], BF16) for i in range(2)]
        wb = [A.t(f"wb{i}", [128, 8, 256], BF16) for i in range(2)]
        wo = [A.t(f"wo{i}", [128, 2, 1024], BF16) for i in range(2)]
        mid = [A.t(f"mid{i}", [128, 2, NTOK], BF16) for i in range(2)]
        sa = [A.t(f"sa{i}", [128, 512], F32) for i in range(2)]
        u = uid()
        def mk(tb):
            t0, n = TBS[tb]
            make_xn(bl, l, s, tb, lambda c, t0=t0, n=n: xn[:, c, t0:t0 + n], lambda c, tb=tb: ("xn", u, c, tb))
        win = w_ff_in[which][l]
        wout = w_ff_out[which][l]
        NG = DFF // 256
        ring1 = Ring([0, 1, 2, 3])
        ring2 = Ring([4, 5, 6])
        sar = Ring([0, 1])

        def load(g):
            bi = g % 2
            dma_w(wa[bi][:], win[:, g * 256:(g + 1) * 256].rearrange("(k p) n -> p k n", p=128), [], [("wa", bi)])
            dma_w(wb[bi][:], win[:, DFF + g * 256:DFF + (g + 1) * 256].rearrange("(k p) n -> p k n", p=128), [], [("wb", bi)])
            dma_w(wo[bi][:], wout[g * 256:(g + 1) * 256, :].rearrange("(j p) n -> p j n", p=128), [], [("wo", bi)])

        def phase1(g, blks=None):
            bi = g % 2
            for tb in (blocks if blks is None else blks):
                t0, n = TBS[tb]
                for jj in range(2):
                    pa = ring1.next()
                    pb = ring1.next()

                    def mm(e, w, pbk, jj=jj, t0=t0, n=n):
                        ins = None
                        for k in range(8):
                            ins = e.matmul(ps[pbk][:, 0:n], lhsT=w[:, k, jj * 128:(jj + 1) * 128], rhs=xn[:, k, t0:t0 + n],
                                           start=(k == 0), stop=(k == 7))
                        return ins
                    xk = [("xn", u, c, tb) for c in range(8)]
                    S.add("pe", lambda e, w=wa[bi], pbk=pa, mm=mm: mm(e, w, pbk), reads=xk + [("wa", bi)], writes=[PK(pa)])
                    S.add("pe", lambda e, w=wb[bi], pbk=pb, mm=mm: mm(e, w, pbk), reads=xk + [("wb", bi)], writes=[PK(pb)])
                    si = sar.next()
                    S.add("act", lambda e, pa=pa, si=si, n=n: e.activation(out=sa[si][:, 0:n], in_=ps[pa][:, 0:n], func=AF.Silu),
                          reads=[PK(pa)], writes=[("sa", si)])
                    S.add("dve", lambda e, pb=pb, si=si, n=n, jj=jj, t0=t0: e.tensor_tensor(out=mid[bi][:, jj, t0:t0 + n], in0=sa[si][:, 0:n],
                                                                                         in1=ps[pb][:, 0:n], op=ALU.mult),
                          reads=[PK(pb), ("sa", si)], writes=[("mid", bi, jj, tb)])

        def phase2(g):
            bi = g % 2
            for tb in blocks:
                t0, n = TBS[tb]
                for oc in range(8):
                    po = ring2.next()

                    def mm(e, po=po, oc=oc, t0=t0, n=n):
                        ins = None
                        for jj in range(2):
                            ins = e.matmul(ps[po][:, 0:n], lhsT=wo[bi][:, jj, oc * 128:(oc + 1) * 128], rhs=mid[bi][:, jj, t0:t0 + n],
                                           start=(jj == 0), stop=(jj == 1))
                        return ins
                    S.add("pe", mm, reads=[("mid", bi, 0, tb), ("mid", bi, 1, tb), ("wo", bi)], writes=[PK(po)])
                    resid_add(bl, l, s, tb, oc, po, n)

        load(0)
        load(1)
        for tb in blocks[:2]:
            mk(tb)
        for i_, tb in enumerate(blocks):
            if i_ + 2 < len(blocks):
                mk(blocks[i_ + 2])
            phase1(0, [tb])
        for g in range(1, NG):
            phase1(g)
            phase2(g - 1)
            if g + 1 < NG:
                load(g + 1)
            for _ in range(2):
                if ek[0] < len(extra):
                    ada_run(extra, ek[0])
                    ek[0] += 1
        phase2(NG - 1)
        while ek[0] < len(extra):
            ada_run(extra, ek[0])
            ek[0] += 1
        if extra_fin is not None:
            extra_fin()
        S.barrier()
        A.release(m)

    def mixer(bl, l):
        last = (l == DEPTH - 1)
        blocks = [0, 1, 2, 3, 4]
        m = A.mark()
        qT = A.t("qT", [128, 4, NTOK], BF16)
        kT = A.t("kT", [128, 3, NTOK], BF16)
        vaug = A.t("vaug", [128, 18, 6, 128], BF16)
        m1 = A.mark()
        xnb = [A.t(f"xnb{i}", [128, 8, 512], BF16) for i in range(2)]
        wq = A.t("wq", [128, 8, 512], BF16)
        wk = A.t("wk", [128, 8, 384], BF16)
        wv = A.t("wv", [128, 8, 384], BF16)
        ropet = A.t("ropet", [128, 4, 512], F32)
        sqb = [A.t(f"sqb{i}", [128, 512], BF16) for i in range(2)]
        qn = [A.t(f"qn{i}", [128, 512], BF16) for i in range(3)]
        t1 = A.t("t1", [128, 512], F32)
        t2 = A.t("t2", [128, 512], F32)
        u = uid()
        S.add("pool", lambda e: e.memset(vaug[:], 1.0), writes=[("vaug", kc) for kc in range(18)])
        W = w_in_p[l]
        prj = Ring([0, 1, 2])
        aux = Ring([3, 4])
        vring = Ring([5, 6])
        qnr = Ring([0, 1, 2])
        def pass1_pre(tb):
            t0, n = TBS[tb]
            xb = xnb[tb % 2]
            xkeys = [("xnb", tb % 2, c) for c in range(8)]
            make_xn(bl, l, 1, tb, lambda c, xb=xb, n=n: xb[:, c, 0:n], lambda c, tb=tb: ("xnb", tb % 2, c))
            dma_sp(xscr[:, :, t0:t0 + n], xb[:, :, 0:n], xkeys, [("xscr", tb)])

        sqr = Ring([0, 1])

        def pass1_items(tb):
            t0, n = TBS[tb]
            xb = xnb[tb % 2]
            xkeys = [("xnb", tb % 2, c) for c in range(8)]
            norope = (tb == 4)
            items = []
            for ci in range(7):
                if ci < 4:
                    wt, wkey, co = wq, "wq", ci * 128
                    dest = qT[:, ci, t0:t0 + n]
                    dkey = ("qT", ci, tb)
                else:
                    wt, wkey, co = wk, "wk", (ci - 4) * 128
                    dest = kT[:, ci - 4, t0:t0 + n]
                    dkey = ("kT", ci - 4, tb)
                isA = ci in (0, 1, 4)
                st = {}

                def P(st=st, wt=wt, wkey=wkey, co=co):
                    pb = prj.next()
                    st["pb"] = pb

                    def mm(e):
                        ins = None
                        for k in range(8):
                            ins = e.matmul(ps[pb][:, 0:n], lhsT=wt[:, k, co:co + 128], rhs=xb[:, k, 0:n], start=(k == 0), stop=(k == 7))
                        return ins
                    S.add("pe", mm, reads=xkeys + [wkey], writes=[PK(pb)])

                def N(st=st):
                    pb = st["pb"]
                    si = sqr.next()
                    st["si"] = si
                    S.add("act", lambda e: e.activation(out=sqb[si][:, 0:n], in_=ps[pb][:, 0:n], func=AF.Square),
                          reads=[PK(pb)], writes=[("sqb", si)])

                def M1(st=st, isA=isA, ci=ci, dest=dest, dkey=dkey):
                    pb = st["pb"]
                    qi = qnr.next()
                    st["qi"] = qi
                    qdst = dest if norope else qn[qi][:, 0:n]
                    qkey = dkey if norope else ("qn", qi)
                    if isA:
                        si = st["si"]
                        pa = aux.next()
                        S.add("pe", lambda e: e.matmul(ps[pa][:, 0:n], lhsT=blk64, rhs=sqb[si][:, 0:n], start=True, stop=True),
                              reads=[("sqb", si), "cmat"], writes=[PK(pa)])
                        rsqrt_from_psum(pa, n)
                        gi = 0 if ci < 4 else 1
                        S.add("dve", lambda e: e.scalar_tensor_tensor(out=qdst, in0=ps[pb][:, 0:n], scalar=gqk[:, l, gi:gi + 1],
                                                                        in1=rstd[:, 0:n], op0=ALU.mult, op1=ALU.mult),
                              reads=[PK(pb), "rstd", "gqk"], writes=[qkey])
                    else:
                        S.add("act", lambda e: e.activation(out=qdst, in_=ps[pb][:, 0:n], func=AF.Copy),
                              reads=[PK(pb)], writes=[qkey])

                def M2(st=st, isA=isA, ci=ci, dest=dest, dkey=dkey):
                    if ci == 0:
                        dma_sp(ropet[:], rope_d[:, :, t0:t0 + n].rearrange("m p n -> p m n"), [], ["ropet"])
                    qi = st["qi"]
                    pr = aux.next()
                    pm = permA if isA else permC
                    ti = 0 if isA else 2
                    S.add("pe", lambda e: e.matmul(ps[pr][:, 0:n], lhsT=pm, rhs=qn[qi][:, 0:n], start=True, stop=True),
                          reads=[("qn", qi), "cmat"], writes=[PK(pr)])
                    S.add("dve", lambda e: e.tensor_tensor(out=t1[:, 0:n], in0=qn[qi][:, 0:n], in1=ropet[:, ti, 0:n], op=ALU.mult),
                          reads=[("qn", qi), "ropet"], writes=["t1"])
                    S.add("dve", lambda e: e.tensor_tensor(out=t2[:, 0:n], in0=ps[pr][:, 0:n], in1=ropet[:, ti + 1, 0:n], op=ALU.mult),
                          reads=[PK(pr), "ropet"], writes=["t2"])
                    S.add("pool", lambda e: e.tensor_tensor(out=dest, in0=t1[:, 0:n], in1=t2[:, 0:n], op=ALU.add),
                          reads=["t1", "t2"], writes=[dkey])
                items.append([P, N if isA else None, M1, None if norope else M2])

            def PV():
                for sb in range(n // 128):
                    kc = (t0 // 128) + sb
                    pv = vring.next()

                    def mmv(e, pv=pv, sb=sb):
                        ins = None
                        for k in range(8):
                            ins = e.matmul(ps[pv][:, 0:384], lhsT=xb[:, k, sb * 128:(sb + 1) * 128], rhs=wv[:, k, :], start=(k == 0), stop=(k == 7))
                        return ins
                    S.add("pe", mmv, reads=xkeys + ["wv"], writes=[PK(pv)])
                    S.add("act", lambda e, pv=pv, kc=kc: e.activation(out=vaug[:, kc, :, 0:64], in_=ps[pv][:, 0:384].rearrange("p (h d) -> p h d", d=64), func=AF.Copy),
                          reads=[PK(pv)], writes=[("vaug", kc)])
            items.append([PV, None, None, None])
            return items

        dma_w(wq[:], W[:, 0:512].rearrange("(k p) n -> p k n", p=128), [], ["wq"])
        dma_w(wk[:], W[:, 512:896].rearrange("(k p) n -> p k n", p=128), [], ["wk"])
        dma_w(wv[:], W[:, 896:1280].rearrange("(k p) n -> p k n", p=128), [], ["wv"])
        pass1_pre(blocks[0])
        flat = []
        for i_, tb in enumerate(blocks):
            its = pass1_items(tb)
            if i_ + 1 < len(blocks):
                its[0][0] = (lambda p0_=its[0][0], nx=blocks[i_ + 1]: (pass1_pre(nx), p0_()))
            flat.extend(its)
        NF = len(flat)
        for i_ in range(NF + 3):
            for stg in range(4):
                j_ = i_ - stg
                if 0 <= j_ < NF and flat[j_][stg] is not None:
                    flat[j_][stg]()
        S.barrier()
        A.release(m1)
        if stop == f"p1_{l}":
            pass

        woh = A.t("woh", [128, 4, 1024], BF16)
        cat = [A.t(f"cat{i}", [128, 4, 512], BF16) for i in range(2)]
        lnb = A.t("lnb", [64, 512], F32)
        pT = [A.t(f"pT{i}", [128, 512], BF16) for i in range(3)]
        rden = [A.t(f"rden{i}", [64, 512], F32) for i in range(2)]
        tt0 = A.t("tt0", [64, 512], F32)
        tt1 = A.t("tt1", [64, 512], F32)
        od_ = A.t("od_", [64, 512], F32)
        odn = A.t("odn", [64, 512], F32)
        sq64 = A.t("sq64", [64, 512], BF16)
        qpad = [A.t(f"qpad{i}", [128, 512], BF16) for i in range(3)]
        qpr = Ring([0, 1, 2])
        dma_w(woh[:], w_out[l][0:512, :].rearrange("(c p) n -> p c n", p=128), [], ["woh"])
        sring = Ring([0, 1, 6])
        oring = Ring([2, 3, 4, 5])
        ptr = Ring([0, 1, 2])
        rdr = Ring([0, 1])
        qblocks = [0, 1, 2, 3] if last else [0, 1, 2, 3, 4]
        steps = []

        def qblock(qi_, tb):
            t0, n = TBS[tb]
            kcs = list(range(18)) if tb != 4 else [16, 17]
            ct = cat[qi_ % 2]
            ci_ = qi_ % 2

            def warm():
                wbk = sring.next()

                def burst(e):
                    ins = None
                    for _ in range(WARM_N):
                        ins = e.matmul(ps[wbk][:, 0:512], lhsT=woh[:, 0, 0:128], rhs=woh[:, 1, 0:512], start=True, stop=True)
                    return ins
                S.add("pe", burst, reads=["woh"], writes=[PK(wbk)])
            if WARM_N > 0:
                steps.append((warm, lambda: None, lambda: None, None))

            hm_list = []

            def prep_qpad(hm):
                qi = qpr.next()
                hm["qp"] = qi
                r0, r1 = hm["krows"]
                S.add("pool", lambda e: e.memset(qpad[qi][:, 0:n], 0.0), writes=[("qpad", qi)])
                S.add("pool", lambda e: e.tensor_copy(out=qpad[qi][r0:r1, 0:n], in_=qT[r0:r1, hm["qchunk"], t0:t0 + n]),
                      reads=[("qT", hm["qchunk"], tb)], writes=[("qpad", qi)])

            def head_pass(krows, kchunk, qchunk, vslot, scale, po, tp, post):
                hm = dict(krows=krows, qchunk=qchunk)
                hm_list.append(hm)
                myidx = len(hm_list) - 1
                for ii, kc in enumerate(kcs):
                    st = {}
                    ktb = kc // 4 if kc < 16 else 4

                    def qk(st=st, kc=kc, ktb=ktb, ii=ii):
                        if ii == 0:
                            if "qp" not in hm:
                                prep_qpad(hm)
                            if myidx + 1 < len(hm_list) and "qp" not in hm_list[myidx + 1]:
                                prep_qpad(hm_list[myidx + 1])
                        sb_ = sring.next()
                        st["sb"] = sb_
                        qi = hm["qp"]

                        def mms(e):
                            return e.matmul(ps[sb_][:, 0:n], lhsT=kT[:, kchunk, kc * 128:(kc + 1) * 128],
                                            rhs=qpad[qi][:, 0:n], start=True, stop=True)
                        S.add("pe", mms, reads=[("kT", kchunk, ktb), ("qpad", qi)], writes=[PK(sb_)])

                    def ex(st=st):
                        sb_ = st["sb"]
                        pi = ptr.next()
                        st["pi"] = pi
                        S.add("act", lambda e: e.activation(out=pT[pi][:, 0:n], in_=ps[sb_][:, 0:n], func=AF.Exp, scale=scale),
                              reads=[PK(sb_)], writes=[("pT", pi)])

                    def pv(st=st, kc=kc, ii=ii):
                        pi = st["pi"]
                        S.add("pe", lambda e: e.matmul(ps[po][:, 0:n], lhsT=vaug[:, kc, vslot, :], rhs=pT[pi][:, 0:n],
                                                       start=(ii == 0), stop=(ii == len(kcs) - 1)),
                              reads=[("pT", pi), ("vaug", kc)], writes=[PK(po)])
                    steps.append((qk, ex, pv, post if ii == len(kcs) - 1 else None))

            for h in range(4):
                r0 = 64 * (h // 2)
                po = oring.next()

                def postA(po=po, h=h):
                    ri = rdr.next()
                    S.add("dve", lambda e: e.reciprocal(out=rden[ri][:, 0:n], in_=ps[po][64:128, 0:n]), reads=[PK(po)], writes=[("rden", ri)])
                    p0 = 64 * (h % 2)
                    S.add("dve", lambda e: e.tensor_tensor(out=ct[p0:p0 + 64, h // 2, 0:n], in0=ps[po][0:64, 0:n], in1=rden[ri][:, 0:n], op=ALU.mult),
                          reads=[PK(po), ("rden", ri)], writes=[("cat", ci_, h)])
                    return []
                head_pass((r0, r0 + 64), 0, h % 2, h // 2, 0.125, po, None, postA)
            for h in range(4):
                base = 64 * (h % 2)
                pos = [oring.next(), oring.next()]

                def postC(pos=pos, h=h):
                    ri0 = rdr.next()
                    ri1 = rdr.next()
                    p0 = 64 * (h % 2)
                    S.add("dve", lambda e: e.reciprocal(out=rden[ri0][:, 0:n], in_=ps[pos[0]][64:128, 0:n]), reads=[PK(pos[0])], writes=[("rden", ri0)])
                    S.add("dve", lambda e: e.tensor_tensor(out=tt0[:, 0:n], in0=ps[pos[0]][0:64, 0:n], in1=rden[ri0][:, 0:n], op=ALU.mult),
                          reads=[PK(pos[0]), ("rden", ri0)], writes=["tt0"])
                    S.add("dve", lambda e: e.reciprocal(out=rden[ri1][:, 0:n], in_=ps[pos[1]][64:128, 0:n]), reads=[PK(pos[1])], writes=[("rden", ri1)])
                    S.add("dve", lambda e: e.tensor_tensor(out=tt1[:, 0:n], in0=ps[pos[1]][0:64, 0:n], in1=rden[ri1][:, 0:n], op=ALU.mult),
                          reads=[PK(pos[1]), ("rden", ri1)], writes=["tt1"])
                    S.add("dve", lambda e: e.scalar_tensor_tensor(out=od_[:, 0:n], in0=tt1[:, 0:n], scalar=neglam[:, l:l + 1], in1=tt0[:, 0:n],
                                                                    op0=ALU.mult, op1=ALU.add),
                          reads=["tt0", "tt1", "neglam"], writes=["od_"])

                    def st1():
                        S.add("act", lambda e: e.activation(out=sq64[:, 0:n], in_=od_[:, 0:n], func=AF.Square), reads=["od_"], writes=["sq64"])

                    def st2():
                        S.add("pe", lambda e: e.matmul(ps[7][0:64, 0:n], lhsT=blk64[0:64, 0:64], rhs=sq64[:, 0:n], start=True, stop=True),
                              reads=["sq64", "cmat"], writes=[PK(7)])

                    def st3():
                        S.add("act", lambda e: e.activation(out=lnb[:, 0:n], in_=ps[7][0:64, 0:n], func=AF.Ln, bias=epsc[0:64, 0:1], scale=1.0),
                              reads=[PK(7), "epsc"], writes=["lnb"])
                        S.add("act", lambda e: e.activation(out=odn[:, 0:n], in_=lnb[:, 0:n], func=AF.Exp, scale=-0.5), reads=["lnb"], writes=["odn"])

                    def st4():
                        S.add("dve", lambda e: e.scalar_tensor_tensor(out=ct[p0:p0 + 64, 2 + h // 2, 0:n], in0=od_[:, 0:n], scalar=gsub1m[:, l:l + 1],
                                                                        in1=odn[:, 0:n], op0=ALU.mult, op1=ALU.mult),
                              reads=["od_", "odn", "gsub1m"], writes=[("cat", ci_, 4 + h)])
                    tasks = [(20, st1), (23, st2), (26, st3), (30, st4)]
                    if h == 3:
                        def outproj():
                            for oc in range(8):
                                pb = 7

                                def mmo(e, oc=oc, pb=pb):
                                    ins = None
                                    for hh in range(4):
                                        ins = e.matmul(ps[pb][:, 0:n], lhsT=woh[:, hh, oc * 128:(oc + 1) * 128], rhs=ct[:, hh, 0:n],
                                                       start=(hh == 0), stop=(hh == 3))
                                    return ins
                                S.add("pe", mmo, reads=[("cat", ci_, hh) for hh in range(8)] + ["woh"], writes=[PK(pb)])
                                resid_add(bl, l, 1, tb, oc, pb, n)
                        tasks.append((34, outproj))
                    return tasks
                postC.is_c = True
                for c in range(2):
                    r0 = base + 32 * c
                    head_pass((r0, r0 + 32), 1 + h // 2, 2 + h // 2, 2 + h, 32 ** -0.5, pos[c], (r0, 0), postC if c == 1 else None)

        for qi_, tb in enumerate(qblocks):
            qblock(qi_, tb)
        LA = 2
        deferred = []
        NS = len(steps)
        for i in range(min(LA, NS)):
            steps[i][0]()
        for i in range(NS):
            if i + LA < NS:
                steps[i + LA][0]()
            steps[i][1]()
            steps[i][2]()
            if steps[i][3] is not None:
                if getattr(steps[i][3], "is_c", False):
                    for d in sorted(deferred, key=lambda d: d[0]):
                        d[1]()
                    deferred = []
                for (dl, fn) in steps[i][3]():
                    deferred.append((i + dl, fn))
            ready = [d for d in deferred if d[0] <= i]
            deferred = [d for d in deferred if d[0] > i]
            for d in ready:
                d[1]()
        for d in sorted(deferred, key=lambda d: d[0]):
            d[1]()
        S.barrier()
        A.release(m)

        m2 = A.mark()
        blocks2 = [0, 1, 2, 3] if last else [0, 1, 2, 3, 4]
        xnb = [A.t(f"xnb{i}", [128, 8, 512], BF16) for i in range(2)]
        wu = A.t("wu", [128, 8, 256], BF16)
        wvg = A.t("wvg", [128, 8, 256], BF16)
        wgl = A.t("wgl", [128, 8, 512], BF16)
        wob = A.t("wob", [128, 2, 1024], BF16)
        wod = A.t("wod", [128, 2, 1024], BF16)
        wsT = A.t("wsT", [128, 4, 128], BF16)
        gvb = A.t("gvb", [128, 256], F32)
        bst = A.t("bst", [64, 4, 512], F32)
        diag = A.t("diag", [128, 2, 31, 128], BF16)
        ypl = A.t("ypl", [128, 2, NLAT + 30], BF16)
        ypc = A.t("ypc", [128, 2, NCTX + 30], BF16)
        ug = A.t("ug", [64, 4, 512], F32)
        vge = A.t("vge", [128, 4, 256], F32)
        junk = A.t("junk", [128, 256], BF16)
        ssum = A.t("ssum", [128, 8], F32)
        vn = A.t("vn", [128, 4, 256], BF16)
        tmb = [A.t("tmb0", [64, 512], F32), A.t("tmb1", [64, 512], F32)]
        catb = A.t("catb", [128, 2, 512], BF16)
        sg = A.t("sg", [128, 512], F32)
        zz = A.t("zz", [128, 2, 512], F32)
        sqz = A.t("sqz", [128, 2, 512], BF16)
        odd = A.t("odd", [128, 2, 512], BF16)
        dma_w(wob[:], w_out[l][512:768, :].rearrange("(c p) n -> p c n", p=128), [], ["wob"])
        dma_w(wod[:], w_out[l][768:1024, :].rearrange("(c p) n -> p c n", p=128), [], ["wod"])
        dma_w(wsT[:], wsT_d[l], [], ["wsT"])
        dma_sp(gvb[:], gv_d[:, l, :], [], ["gvb"])
        dma_sp(bst[:], bs_d[:, l, :, :], [], ["bst"])
        for c in range(2):
            for k in range(31):
                S.add("dve", lambda e, c=c, k=k: e.tensor_scalar(out=diag[:, c, k, :], in0=identF[:], scalar1=wdw[:, l, c, k:k + 1], scalar2=None, op0=ALU.mult),
                      reads=["identF", "wdw"], writes=[("diag", c)])
        S.add("pool", lambda e: e.memset(ypl[:], 0.0), writes=[("ypl", c, tb) for c in range(2) for tb in range(4)] + ["yplpad"])
        S.add("pool", lambda e: e.memset(ypc[:], 0.0), writes=[("ypc", c) for c in range(2)])
        pr2 = Ring([0, 1, 2, 3])
        outr = Ring([4])
        pr2b = Ring([5, 6])
        outrb = Ring([7])
        def pass2a_pre(tb):
            t0, n = TBS[tb]
            xb = xnb[tb % 2]
            xkeys = [("xnb", tb % 2, c) for c in range(8)]
            dma_sp(xb[:, :, 0:n], xscr[:, :, t0:t0 + n], [("xscr", tb)], xkeys)

        def pass2a(tb):
            t0, n = TBS[tb]
            xb = xnb[tb % 2]
            xkeys = [("xnb", tb % 2, c) for c in range(8)]
            nsb = n // 128
            for g in range(4):
                pb = pr2.next()

                def mmu(e, pb=pb, g=g):
                    ins = None
                    for k in range(8):
                        ins = e.matmul(ps[pb][0:64, 0:n], lhsT=wu[:, k, g * 64:(g + 1) * 64], rhs=xb[:, k, 0:n], start=(k == 0), stop=(k == 7))
                    return ins
                S.add("pe", mmu, reads=xkeys + ["wu"], writes=[PK(pb)])
                S.add("act", lambda e, pb=pb, g=g: e.activation(out=ug[:, g, 0:n], in_=ps[pb][0:64, 0:n], func=AF.Gelu_apprx_tanh),
                      reads=[PK(pb)], writes=[("ug", g)])
            S.add("dve", lambda e: e.memset(ssum[:, 0:4], 0.0), writes=[("ssum", sb) for sb in range(4)])
            for sb in range(nsb):
                pb = pr2.next()

                def mmv(e, pb=pb, sb=sb):
                    ins = None
                    for k in range(8):
                        ins = e.matmul(ps[pb][:, 0:256], lhsT=xb[:, k, sb * 128:(sb + 1) * 128], rhs=wvg[:, k, :], start=(k == 0), stop=(k == 7))
                    return ins
                S.add("pe", mmv, reads=xkeys + ["wvg"], writes=[PK(pb)])
                S.add("act", lambda e, pb=pb, sb=sb: e.activation(out=vge[:, sb, :], in_=ps[pb][:, 0:256], func=AF.Gelu_apprx_tanh),
                      reads=[PK(pb)], writes=[("vge", sb)])
                S.add("act", lambda e, sb=sb: e.activation(out=junk[:], in_=vge[:, sb, :], func=AF.Square, accum_out=ssum[:, sb:sb + 1]),
                      reads=[("vge", sb), ("ssum", sb)], writes=["junk", ("ssum", sb)])
            for c in range(2):
                pa = pr2.next()
                pg = pr2.next()

                def mmg(e, pbk, co):
                    ins = None
                    for k in range(8):
                        ins = e.matmul(ps[pbk][:, 0:n], lhsT=wgl[:, k, co:co + 128], rhs=xb[:, k, 0:n], start=(k == 0), stop=(k == 7))
                    return ins
                S.add("pe", lambda e, pa=pa, c=c, mmg=mmg: mmg(e, pa, c * 128), reads=xkeys + ["wgl"], writes=[PK(pa)])
                S.add("pe", lambda e, pg=pg, c=c, mmg=mmg: mmg(e, pg, 256 + c * 128), reads=xkeys + ["wgl"], writes=[PK(pg)])
                S.add("act", lambda e, pg=pg: e.activation(out=sg[:, 0:n], in_=ps[pg][:, 0:n], func=AF.Sigmoid), reads=[PK(pg)], writes=["sg"])
                if tb != 4:
                    S.add("dve", lambda e, pa=pa, c=c: e.tensor_tensor(out=ypl[:, c, 15 + t0:15 + t0 + n], in0=ps[pa][:, 0:n], in1=sg[:, 0:n], op=ALU.mult),
                          reads=[PK(pa), "sg", "yplpad"], writes=[("ypl", c, tb)])
                else:
                    S.add("dve", lambda e, pa=pa, c=c: e.tensor_tensor(out=ypc[:, c, 15:15 + n], in0=ps[pa][:, 0:n], in1=sg[:, 0:n], op=ALU.mult),
                          reads=[PK(pa), "sg"], writes=[("ypc", c)])
            S.add("act", lambda e: e.activation(out=ssum[:, 4:4 + nsb], in_=ssum[:, 0:nsb], func=AF.Sqrt, bias=epsc[:, 0:1], scale=1.0 / 256.0),
                  reads=[("ssum", sb) for sb in range(nsb)] + ["epsc"], writes=["ssr"])
            S.add("dve", lambda e: e.reciprocal(out=ssum[:, 4:4 + nsb], in_=ssum[:, 4:4 + nsb]), reads=["ssr"], writes=["ssr"])
            for sb in range(nsb):
                S.add("dve", lambda e, sb=sb: e.scalar_tensor_tensor(out=vn[:, sb, :], in0=vge[:, sb, :], scalar=ssum[:, 4 + sb:5 + sb], in1=gvb[:],
                                                                       op0=ALU.mult, op1=ALU.mult),
                      reads=[("vge", sb), "ssr", "gvb"], writes=[("vn", sb)])
            for g in range(4):
                pb = pr2.next()
                ti = g % 2
                p0 = 64 * (g % 2)

                def mmm(e, pb=pb, g=g):
                    ins = None
                    for sb in range(nsb):
                        ins = e.matmul(ps[pb][0:64, sb * 128:(sb + 1) * 128], lhsT=vn[:, sb, g * 64:(g + 1) * 64], rhs=wsT[:, g, :], start=True, stop=True)
                    return ins
                S.add("pe", mmm, reads=[("vn", sb) for sb in range(nsb)] + ["wsT"], writes=[PK(pb)])
                S.add("dve", lambda e, pb=pb, g=g, ti=ti: e.tensor_tensor(out=tmb[ti][:, 0:n], in0=ps[pb][0:64, 0:n], in1=bst[:, g, 0:n], op=ALU.add),
                      reads=[PK(pb), "bst"], writes=[("tmb", ti)])
                S.add("dve", lambda e, g=g, ti=ti, p0=p0: e.tensor_tensor(out=catb[p0:p0 + 64, g // 2, 0:n], in0=ug[:, g, 0:n], in1=tmb[ti][:, 0:n], op=ALU.mult),
                      reads=[("tmb", ti), ("ug", g)], writes=[("catb", g)])
            for oc in range(8):
                po = outr.next()

                def mmo(e, po=po, oc=oc):
                    ins = None
                    for cc in range(2):
                        ins = e.matmul(ps[po][:, 0:n], lhsT=wob[:, cc, oc * 128:(oc + 1) * 128], rhs=catb[:, cc, 0:n], start=(cc == 0), stop=(cc == 1))
                    return ins
                S.add("pe", mmo, reads=[("catb", g) for g in range(4)] + ["wob"], writes=[PK(po)])
                if "B" in DEBUG_PARTS:
                    resid_add(bl, l, 1, tb, oc, po, n)

        dma_w(wu[:], W[:, 1280:1536].rearrange("(k p) n -> p k n", p=128), [], ["wu"])
        dma_w(wvg[:], W[:, 1536:1792].rearrange("(k p) n -> p k n", p=128), [], ["wvg"])
        dma_w(wgl[:], W[:, 1792:2304].rearrange("(k p) n -> p k n", p=128), [], ["wgl"])

        def pass2b(tb):
            t0, n = TBS[tb]
            for c in range(2):
                pz = pr2b.next()
                if tb != 4:
                    yk = [("ypl", c, t) for t in range(max(0, tb - 1), min(3, tb + 1) + 1)] + ["yplpad"]
                    ysrc = lambda k, c=c, t0=t0, n=n: ypl[:, c, t0 + k:t0 + k + n]
                else:
                    yk = [("ypc", c)]
                    ysrc = lambda k, c=c, n=n: ypc[:, c, k:k + n]

                def mmc(e, pz=pz, c=c, ysrc=ysrc, n=n):
                    ins = None
                    for k in range(31):
                        ins = e.matmul(ps[pz][:, 0:n], lhsT=diag[:, c, k, :], rhs=ysrc(k), start=(k == 0), stop=(k == 30))
                    return ins
                S.add("pe", mmc, reads=yk + [("diag", c)], writes=[PK(pz)])
                S.add("act", lambda e, pz=pz, c=c, n=n: e.activation(out=zz[:, c, 0:n], in_=ps[pz][:, 0:n], func=AF.Identity, bias=bdw[:, l, c:c + 1]),
                      reads=[PK(pz), "bdw"], writes=[("zz", c)])
                S.add("act", lambda e, c=c, n=n: e.activation(out=sqz[:, c, 0:n], in_=zz[:, c, 0:n], func=AF.Square), reads=[("zz", c)], writes=[("sqz", c)])
            pn = pr2b.next()

            def mmn(e, pn=pn, n=n):
                ins = None
                for c in range(2):
                    ins = e.matmul(ps[pn][:, 0:n], lhsT=ones256, rhs=sqz[:, c, 0:n], start=(c == 0), stop=(c == 1))
                return ins
            S.add("pe", mmn, reads=[("sqz", 0), ("sqz", 1), "cmat"], writes=[PK(pn)])
            rsqrt_from_psum(pn, n)
            for c in range(2):
                tx = tmpx[c]
                S.add("pool", lambda e, c=c, tx=tx, n=n: e.tensor_tensor(out=tx[:, 0:n], in0=zz[:, c, 0:n], in1=rstd[:, 0:n], op=ALU.mult),
                      reads=[("zz", c), "rstd"], writes=[("tmpx", c)])
                S.add("act", lambda e, c=c, tx=tx, n=n: e.activation(out=odd[:, c, 0:n], in_=tx[:, 0:n], func=AF.Silu, scale=gconv[:, l, c:c + 1]),
                      reads=[("tmpx", c), "gconv"], writes=[("odd", c)])
            for oc in range(8):
                po = outrb.next()

                def mmo2(e, po=po, oc=oc, n=n):
                    ins = None
                    for c in range(2):
                        ins = e.matmul(ps[po][:, 0:n], lhsT=wod[:, c, oc * 128:(oc + 1) * 128], rhs=odd[:, c, 0:n], start=(c == 0), stop=(c == 1))
                    return ins
                S.add("pe", mmo2, reads=[("odd", 0), ("odd", 1), "wod"], writes=[PK(po)])
                if "D" in DEBUG_PARTS:
                    resid_add(bl, l, 1, tb, oc, po, n)

        def record(fn):
            items = []
            S.add = lambda *a_, **k_: items.append((a_, k_))
            try:
                fn()
            finally:
                del S.add
            return items

        def merge_emit(la, lb):
            i = j = 0
            na, nb_ = len(la), len(lb)
            while i < na or j < nb_:
                if j >= nb_ or (i < na and i * nb_ <= j * na):
                    S.add(*la[i][0], **la[i][1])
                    i += 1
                else:
                    S.add(*lb[j][0], **lb[j][1])
                    j += 1

        def do2a(i_):
            if i_ + 1 < len(blocks2):
                pass2a_pre(blocks2[i_ + 1])
            pass2a(blocks2[i_])

        pass2a_pre(blocks2[0])
        NB2 = len(blocks2)
        for i_ in range(NB2 + 2):
            ia, ib = i_, i_ - 2
            la = record(lambda: do2a(ia)) if ia < NB2 else []
            lb = record(lambda: pass2b(blocks2[ib])) if 0 <= ib < NB2 else []
            merge_emit(la, lb)
        S.barrier()
        A.release(m2)

    done = False
    for bl in range(nb):
        for tb in range(4):
            t0_, n_ = TBS[tb]
            dma_sp(hT[:, :, t0_:t0_ + n_], xT[bl][:, t0_:t0_ + n_].rearrange("(c p) t -> p c t", p=128), [], [("h", c, tb) for c in range(8)])
        dma_sp(hT[:, :, NLAT:NTOK], ctxT[bl].rearrange("(c p) t -> p c t", p=128), [], [("h", c, 4) for c in range(8)])
        for l in range(depth):
            last = (l == DEPTH - 1)
            ffn(bl, l, 0, [0, 1, 2, 3, 4], ada_next=(l + 1 if (bl == 0 and l + 1 < depth) else None))
            if stop == f"ffn1_{l}":
                dump_and_end(bl)
                done = True
                break
            mixer(bl, l)
            if stop == f"mix_{l}":
                dump_and_end(bl)
                done = True
                break
            ffn(bl, l, 1, [0, 1, 2, 3] if last else [0, 1, 2, 3, 4])
            if stop == f"ffn2_{l}":
                dump_and_end(bl)
                done = True
                break
        if done:
            continue
        mf = A.mark()
        ob = [A.t(f"ob{i}", [128, 8, 512], F32) for i in range(2)]
        for tb in range(4):
            t0, n = TBS[tb]
            o = ob[tb % 2]
            make_xn(bl, 0, 0, tb, lambda c, o=o: o[:, c, :], lambda c, tb=tb: ("ob", tb % 2, c), final=True)
            dma_sp(outT[bl][:, t0:t0 + n].rearrange("(c p) t -> p c t", p=128), o[:], [("ob", tb % 2, c) for c in range(8)], [("ob", tb % 2, c) for c in range(8)] + ["outT"])
        S.barrier()
        A.release(mf)
    S.add("sp", lambda e: None, reads=["outT" if stop is None else "dbg"])
    S.emit()
    return nc


def _rope_tables():
    t = np.arange(NLAT)
    row = (t // 64).astype(np.float32)
    col = (t % 64).astype(np.float32)
    out = np.zeros((4, 128, NLAT), np.float32)
    for ti, hd in ((0, 64), (2, 32)):
        quarter = hd // 4
        inv = (np.float32(10000.0) ** (-np.arange(quarter, dtype=np.float32) / np.float32(quarter))).astype(np.float32)
        ang = np.concatenate([row[:, None] * inv[None, :], col[:, None] * inv[None, :]], axis=-1).astype(np.float32)
        cos = np.cos(ang).astype(np.float32)
        sin = np.sin(ang).astype(np.float32)
        half = hd // 2
        for p in range(128):
            d = p % hd
            j = d % half
            out[ti, p] = cos[:, j]
            out[ti + 1, p] = (-sin[:, j]) if d < half else sin[:, j]
    return out


def _const_mats():
    m = np.zeros((6, 128, 128), np.float32)
    m[0] = 1.0 / 1024.0
    m[1, 0:64, 0:64] = 1.0 / 64.0
    m[1, 64:128, 64:128] = 1.0 / 64.0
    m[2] = 1.0 / 256.0
    for mm_ in range(128):
        pa = mm_ + 32 if (mm_ % 64) < 32 else mm_ - 32
        m[3, pa, mm_] = 1.0
        pc = mm_ + 16 if (mm_ % 32) < 16 else mm_ - 16
        m[4, pc, mm_] = 1.0
    m[5] = np.eye(128, dtype=np.float32)
    return m


def _col_perm():
    qa = lambda h: list(range(h * 64, (h + 1) * 64))
    perm = qa(0) + qa(2) + qa(1) + qa(3)
    perm += list(range(256, 512))
    perm += list(range(512, 640))
    perm += list(range(768, 1024))
    perm += list(range(640, 768))
    perm += list(range(1024, 1280))
    perm += list(range(1280, 2304))
    return np.array(perm)


def prep_shared(inp):
    f = lambda a: np.ascontiguousarray(np.asarray(a, dtype=np.float32))
    sh = {}
    sh["w_ada"] = f(inp["w_ada"])
    sh["b_adaT"] = f(np.asarray(inp["b_ada"]).reshape(DEPTH, 72, 128).transpose(2, 0, 1))
    sh["g_normT"] = f(np.asarray(inp["g_norm"]).reshape(DEPTH, 3, 8, 128).transpose(3, 0, 1, 2))
    sh["g_finalT"] = f(np.asarray(inp["g_final"]).reshape(8, 128).T)
    for k in ("w_ff1_in", "w_ff1_out", "w_ff2_in", "w_ff2_out", "w_out"):
        sh[k] = f(inp[k])
    sh["w_in_p"] = f(np.asarray(inp["w_in"])[:, :, _col_perm()])
    gq = np.asarray(inp["g_q_a"])
    gk = np.asarray(inp["g_k_a"])
    gqk = np.stack([np.tile(gq, (1, 2)), np.tile(gk, (1, 2))], axis=-1)
    sh["gqk"] = f(gqk.transpose(1, 0, 2))
    sh["lamw"] = f(np.broadcast_to(np.asarray(inp["lam_c"])[None], (64, DEPTH, 4, 32)))
    sh["gsub"] = f(np.asarray(inp["g_sub_c"]).T)
    sh["gv_bc"] = f(np.broadcast_to(np.asarray(inp["g_v_b"])[None], (128, DEPTH, 256)))
    sh["wsT"] = f(np.asarray(inp["w_s_b"]).transpose(0, 3, 1, 2))
    bs = np.asarray(inp["b_s_b"])
    sh["bs_tbl"] = f(np.broadcast_to(np.tile(bs, (1, 1, 4))[None], (64, DEPTH, 4, 512)))
    sh["wdw"] = f(np.asarray(inp["w_dw_d"]).reshape(DEPTH, 31, 2, 128).transpose(3, 0, 2, 1))
    sh["bdw"] = f(np.asarray(inp["b_dw_d"]).reshape(DEPTH, 2, 128).transpose(2, 0, 1))
    sh["gconv"] = f(np.asarray(inp["g_conv_d"]).reshape(DEPTH, 2, 128).transpose(2, 0, 1))
    sh["rope"] = _rope_tables()
    sh["cmat"] = _const_mats()
    return sh


def prep_core(inp, bids):
    x = np.asarray(inp["x"])
    ctx = np.asarray(inp["ctx"])
    c = np.asarray(inp["c"])
    cc = np.asarray(inp["c_ctx"])
    d = {}
    d["xT"] = np.ascontiguousarray(np.stack([x[b].T for b in bids]).astype(np.float32))
    d["ctxT"] = np.ascontiguousarray(np.stack([ctx[b].T for b in bids]).astype(np.float32))
    vecs = [c[b] for b in bids]
    while len(vecs) < 2:
        vecs.append(c[bids[0]])
    vecs.append(cc)
    cT = np.stack(vecs, axis=-1).reshape(8, 128, 3).transpose(1, 0, 2)
    d["cT"] = np.ascontiguousarray(cT.astype(np.float32))
    return d


_NC_CACHE = {}


def kernel(**inputs):
    B = np.asarray(inputs["x"]).shape[0]
    nb = B // NCORES
    if "prog" not in _NC_CACHE:
        _NC_CACHE["prog"] = build_program(nb=nb)
    nc = _NC_CACHE["prog"]
    sh = prep_shared(inputs)
    in_maps = []
    for i in range(NCORES):
        d = dict(sh)
        d.update(prep_core(inputs, list(range(i * nb, (i + 1) * nb))))
        in_maps.append(d)
    res = run_bass_kernel_spmd(nc, in_maps, core_ids=list(range(NCORES)))
    out = np.empty((B, NLAT, D), np.float32)
    for i in range(NCORES):
        o = res.results[i]["outT"]
        for jb in range(nb):
            out[i * nb + jb] = o[jb].T
    return out
```

```python
import math
import contextlib
import numpy as np
import ml_dtypes
import concourse.bass as bass
import concourse.mybir as mybir
from concourse.bass_utils import run_bass_kernel_spmd

F32 = mybir.dt.float32
BF16 = mybir.dt.bfloat16
AF = mybir.ActivationFunctionType
ALU = mybir.AluOpType

ENGS = ["pe", "act", "dve", "pool", "sp"]

D = 1024
NLAT = 2048
NCTX = 256
NTOK = NLAT + NCTX
DFF = 2816
DEPTH = 2
EPS = 1e-6
NCORES = 8
TBS = [(0, 512), (512, 512), (1024, 512), (1536, 512), (2048, 256)]
DEBUG_PARTS = set("ACBD")
WARM_N = 0


class Op:
    __slots__ = ("eng", "fn", "deps", "sig", "sigval", "ch", "inc", "dma", "idx")


class Sched:
    def __init__(self, nc):
        self.nc = nc
        self.ops = {e: [] for e in ENGS}
        self.last_w = {}
        self.readers = {}
        self.ch_eng = {}
        self.pending_bar = {e: [] for e in ENGS}
        self.nops = 0
        self.last_on_ch = {}
        self.dma_ring = {}
        self.DMA_RING = {"sp": 16, "pool": 24}

    def add(self, eng, fn, reads=(), writes=(), ch=None, dma=False):
        op = Op()
        op.eng = eng
        op.fn = fn
        op.sig = bool(dma)
        op.sigval = None
        op.dma = dma
        op.ch = ch if ch is not None else eng
        if dma:
            k = self.dma_ring.get(eng, 0)
            self.dma_ring[eng] = k + 1
            op.ch = f"{eng}_d{k % self.DMA_RING[eng]}"
        op.inc = 16 if dma else 1
        op.idx = self.nops
        self.nops += 1
        if op.ch in self.ch_eng:
            assert self.ch_eng[op.ch] == eng, (op.ch, eng)
        else:
            self.ch_eng[op.ch] = eng
        deps = {}
        for k in reads:
            w = self.last_w.get(k)
            if w is not None:
                deps[w.idx] = (w, True)
        for k in writes:
            w = self.last_w.get(k)
            if w is not None and w.idx not in deps:
                deps[w.idx] = (w, False)
            for r in self.readers.get(k, ()):
                if r.idx not in deps:
                    deps[r.idx] = (r, False)
        need = []
        for (d, raw) in deps.values():
            if d.eng == eng and not d.dma and not dma and not raw:
                continue
            if d.eng == eng and eng == "pe" and not d.dma and not dma:
                continue
            need.append(d)
        if dma:
            prev = self.last_on_ch.get(op.ch)
            if prev is not None:
                need.append(prev)
        for d in self.pending_bar[eng]:
            need.append(d)
        self.pending_bar[eng] = []
        for d in need:
            d.sig = True
        op.deps = need
        for k in writes:
            self.last_w[k] = op
            self.readers[k] = []
        for k in reads:
            if k in writes:
                continue
            self.readers.setdefault(k, []).append(op)
        self.ops[eng].append(op)
        self.last_on_ch[op.ch] = op
        return op

    def barrier(self):
        frontier = list(self.last_on_ch.values())
        for e in ENGS:
            self.pending_bar[e] = list(frontier)

    def emit(self):
        nc = self.nc
        chans = list(self.ch_eng.keys())
        cum = {c: 0 for c in chans}
        for e in ENGS:
            for op in self.ops[e]:
                if op.sig:
                    cum[op.ch] += op.inc
                    op.sigval = cum[op.ch]
        with contextlib.ExitStack() as st:
            sems = {c: st.enter_context(nc.semaphore("s_" + c)) for c in chans}
            block = st.enter_context(nc.Block())

            def run(engname, engine):
                waited = {}
                for op in self.ops[engname]:
                    for d in op.deps:
                        if waited.get(d.ch, 0) < d.sigval:
                            engine.wait_ge(sems[d.ch], d.sigval)
                            waited[d.ch] = d.sigval
                    ins = op.fn(engine)
                    if op.sig:
                        assert ins is not None
                        ins.then_inc(sems[op.ch], op.inc)

            @block.tensor
            def _(eng):
                run("pe", eng)

            @block.scalar
            def _(eng):
                run("act", eng)

            @block.vector
            def _(eng):
                run("dve", eng)

            @block.gpsimd
            def _(eng):
                run("pool", eng)

            @block.sync
            def _(eng):
                run("sp", eng)


class Alloc:
    def __init__(self, nc, limit=229344, base=16512):
        self.nc = nc
        self.off = base
        self.limit = limit
        self.n = 0
        self.peak = base

    def mark(self):
        return self.off

    def release(self, m):
        self.off = m

    def t(self, name, shape, dtype):
        esz = 4 if dtype == F32 else 2
        nbytes = int(np.prod(shape[1:])) * esz
        nbytes = (nbytes + 63) // 64 * 64
        assert self.off + nbytes <= self.limit, (name, self.off, nbytes, self.limit)
        self.n += 1
        h = self.nc.alloc_sbuf_tensor_at(f"{name}_{self.n}", list(shape), dtype, offset=self.off)
        self.off += nbytes
        self.peak = max(self.peak, self.off)
        return h


class Ring:
    def __init__(self, items):
        self.items = list(items)
        self.i = 0

    def next(self):
        v = self.items[self.i % len(self.items)]
        self.i += 1
        return v


def build_program(nb=2, depth=DEPTH, stop=None):
    nc = bass.Bass("TRN2", target_bir_lowering=False)
    S = Sched(nc)
    A = Alloc(nc)

    def din(name, shape, dt=F32):
        return nc.dram_tensor(name, list(shape), dt, kind="ExternalInput").ap()

    xT = din("xT", [nb, D, NLAT])
    ctxT = din("ctxT", [nb, D, NCTX])
    cT_d = din("cT", [128, 8, 3])
    w_ada = din("w_ada", [DEPTH, D, 9 * D])
    b_adaT_d = din("b_adaT", [128, DEPTH, 72])
    g_normT_d = din("g_normT", [128, DEPTH, 3, 8])
    g_finalT_d = din("g_finalT", [128, 8])
    w_ff_in = [din("w_ff1_in", [DEPTH, D, 2 * DFF]), din("w_ff2_in", [DEPTH, D, 2 * DFF])]
    w_ff_out = [din("w_ff1_out", [DEPTH, DFF, D]), din("w_ff2_out", [DEPTH, DFF, D])]
    w_in_p = din("w_in_p", [DEPTH, D, 2304])
    w_out = din("w_out", [DEPTH, D, D])
    gqk_d = din("gqk", [128, DEPTH, 2])
    lamw_d = din("lamw", [64, DEPTH, 4, 32])
    gsub_d = din("gsub", [64, DEPTH])
    gv_d = din("gv_bc", [128, DEPTH, 256])
    wsT_d = din("wsT", [DEPTH, 128, 4, 128])
    bs_d = din("bs_tbl", [64, DEPTH, 4, 512])
    wdw_d = din("wdw", [128, DEPTH, 2, 31])
    bdw_d = din("bdw", [128, DEPTH, 2])
    gconv_d = din("gconv", [128, DEPTH, 2])
    rope_d = din("rope", [4, 128, NLAT])
    cmat_d = din("cmat", [6, 128, 128])
    xscr = nc.dram_tensor("xscr", [128, 8, NTOK], BF16, kind="ExternalOutput").ap()
    if stop is None:
        outT = nc.dram_tensor("outT", [nb, D, NLAT], F32, kind="ExternalOutput").ap()
    else:
        dbg = nc.dram_tensor("dbg", [nb, 128, 8, NTOK], F32, kind="ExternalOutput").ap()

    hT = A.t("hT", [128, 8, NTOK], F32)
    cmat = A.t("cmat", [128, 6, 128], BF16)
    modall = A.t("modall", [128, DEPTH, 72, 3], F32)
    gs = A.t("gs", [128, DEPTH, 3, 8, 3], F32)
    hg = A.t("hg", [128, DEPTH, 3, 8, 3], F32)
    b_adaT = A.t("b_adaT", [128, DEPTH, 72], F32)
    g_normT = A.t("g_normT", [128, DEPTH, 3, 8], F32)
    g_finalT = A.t("g_finalT", [128, 8], F32)
    gqk = A.t("gqk", [128, DEPTH, 2], F32)
    gsub = A.t("gsub", [64, DEPTH], F32)
    gsub1m = A.t("gsub1m", [64, DEPTH], F32)
    neglam = A.t("neglam", [64, DEPTH], F32)
    wdw = A.t("wdw", [128, DEPTH, 2, 31], F32)
    bdw = A.t("bdw", [128, DEPTH, 2], F32)
    gconv = A.t("gconv", [128, DEPTH, 2], F32)
    epsc = A.t("epsc", [128, 1], F32)
    scT = A.t("scT", [128, 8, 3], BF16)
    sq = A.t("sq", [128, 8, 512], BF16)
    rt = A.t("rt", [128, 512], F32)
    rstd = A.t("rstd", [128, 512], F32)
    rstdx = A.t("rstdx", [128, 512], F32)
    tmpx = [A.t("tmpx0", [128, 512], F32), A.t("tmpx1", [128, 512], F32)]
    PERSIST = A.mark()

    ps = [nc.alloc_psum_tensor(f"ps{i}", [128, 512], F32) for i in range(8)]
    onesD = cmat[:, 0, :]
    blk64 = cmat[:, 1, :]
    ones256 = cmat[:, 2, :]
    permA = cmat[:, 3, :]
    permC = cmat[:, 4, :]

    cnt = [0]

    def uid():
        cnt[0] += 1
        return cnt[0]

    def PK(b):
        return ("ps", b)

    def dma_sp(out, in_, reads, writes):
        S.add("sp", lambda e: e.dma_start(out=out, in_=in_), reads=reads, writes=writes, ch="dq_sp", dma=True)

    def dma_w(out, in_, reads, writes):
        S.add("pool", lambda e: e.dma_start(out=out, in_=in_), reads=reads, writes=writes, ch="dq_w", dma=True)

    dma_w(cmat[:], cmat_d.rearrange("m p n -> p m n"), [], ["cmat"])
    dma_sp(b_adaT[:], b_adaT_d, [], ["b_adaT"])
    dma_sp(g_normT[:], g_normT_d, [], ["g_normT"])
    dma_sp(g_finalT[:], g_finalT_d, [], ["g_finalT"])
    dma_sp(gqk[:], gqk_d, [], ["gqk"])
    dma_sp(gsub[:], gsub_d, [], ["gsub"])
    dma_sp(wdw[:], wdw_d, [], ["wdw"])
    dma_sp(bdw[:], bdw_d, [], ["bdw"])
    dma_sp(gconv[:], gconv_d, [], ["gconv"])
    S.add("dve", lambda e: e.memset(epsc[:], EPS), writes=["epsc"])

    m0 = A.mark()
    cTs = A.t("cTs", [128, 8, 3], F32)
    lamw = A.t("lamw", [64, DEPTH, 4, 32], F32)
    lamp = A.t("lamp", [64, DEPTH, 2, 32], F32)
    lams = A.t("lams", [64, DEPTH, 2], F32)
    lame = A.t("lame", [64, DEPTH, 2], F32)
    wab = [A.t(f"wab{i}", [128, 8, 512], BF16) for i in range(3)]
    dma_sp(cTs[:], cT_d, [], ["cTs"])
    dma_sp(lamw[:], lamw_d, [], ["lamw"])
    S.add("act", lambda e: e.activation(out=scT[:], in_=cTs[:], func=AF.Silu), reads=["cTs"], writes=["scT"])
    def adaln_steps(l, wabufs, uidx):
        stepsl = []
        for cb in range(18):
            def dm(cb=cb):
                bi = cb % len(wabufs)
                wt = wabufs[bi]
                dma_w(wt[:], w_ada[l][:, cb * 512:(cb + 1) * 512].rearrange("(k p) n -> p k n", p=128), [], [("wab", uidx, bi)])

            def one(cb=cb):
                bi = cb % len(wabufs)
                wt = wabufs[bi]

                def mm(e):
                    ins = None
                    for cc in range(4):
                        chn = cb * 4 + cc
                        for k in range(8):
                            ins = e.matmul(ps[7][:, chn * 3:(chn + 1) * 3], lhsT=wt[:, k, cc * 128:(cc + 1) * 128],
                                           rhs=scT[:, k, :], start=(k == 0), stop=(k == 7))
                    return ins
                S.add("pe", mm, reads=[("wab", uidx, bi), "scT"], writes=[PK(7)])
            stepsl.append((dm, one))

        def fin():
            psm = ps[7][:, 0:216].rearrange("p (c j) -> p c j", j=3)
            for j in range(3):
                S.add("dve", lambda e, j=j: e.tensor_tensor(out=modall[:, l, :, j], in0=psm[:, :, j], in1=b_adaT[:, l, :], op=ALU.add),
                      reads=[PK(7), "b_adaT"], writes=[("modall", l)])
            for s_ in range(3):
                for j in range(3):
                    S.add("dve", lambda e, s_=s_, j=j: e.tensor_scalar(out=gs[:, l, s_, :, j], in0=modall[:, l, (3 * s_ + 1) * 8:(3 * s_ + 2) * 8, j],
                                                                         scalar1=1.0, scalar2=None, op0=ALU.add),
                          reads=[("modall", l)], writes=[("gs", l)])
                    S.add("dve", lambda e, s_=s_, j=j: e.tensor_tensor(out=gs[:, l, s_, :, j], in0=gs[:, l, s_, :, j], in1=g_normT[:, l, s_, :], op=ALU.mult),
                          reads=[("gs", l), "g_normT"], writes=[("gs", l)])
                    S.add("dve", lambda e, s_=s_, j=j: e.tensor_scalar(out=hg[:, l, s_, :, j], in0=modall[:, l, (3 * s_ + 2) * 8:(3 * s_ + 3) * 8, j],
                                                                         scalar1=(1.0 if s_ == 1 else 0.5), scalar2=None, op0=ALU.mult),
                          reads=[("modall", l)], writes=[("hg", l)])
        return stepsl, fin

    def ada_run(stl, k):
        if k + 2 < len(stl):
            stl[k + 2][0]()
        stl[k][1]()

    st0, fin0 = adaln_steps(0, wab, 0)
    st0[0][0]()
    st0[1][0]()
    for k_ in range(len(st0)):
        ada_run(st0, k_)
    fin0()
    for l in range(depth):
        lam_init = 0.8 - 0.6 * math.exp(-0.3 * l)
        for q in range(2):
            S.add("dve", lambda e, l=l, q=q: e.tensor_tensor(out=lamp[:, l, q, :], in0=lamw[:, l, 2 * q, :], in1=lamw[:, l, 2 * q + 1, :], op=ALU.mult),
                  reads=["lamw"], writes=["lamp"])
            S.add("dve", lambda e, l=l, q=q: e.tensor_reduce(out=lams[:, l, q:q + 1], in_=lamp[:, l, q, :], axis=mybir.AxisListType.X, op=ALU.add),
                  reads=["lamp"], writes=["lams"])
        S.add("act", lambda e, l=l: e.activation(out=lame[:, l, :], in_=lams[:, l, :], func=AF.Exp), reads=["lams"], writes=["lame"])
        S.add("dve", lambda e, l=l: e.tensor_tensor(out=neglam[:, l:l + 1], in0=lame[:, l, 1:2], in1=lame[:, l, 0:1], op=ALU.subtract),
              reads=["lame"], writes=["neglam"])
        S.add("dve", lambda e, l=l, li=lam_init: e.tensor_scalar(out=neglam[:, l:l + 1], in0=neglam[:, l:l + 1], scalar1=-li, scalar2=None, op0=ALU.add),
              reads=["neglam"], writes=["neglam"])
        S.add("dve", lambda e, l=l, li=lam_init: e.tensor_scalar(out=gsub1m[:, l:l + 1], in0=gsub[:, l:l + 1], scalar1=(1.0 - li), scalar2=None, op0=ALU.mult),
              reads=["gsub"], writes=["gsub1m"])
    S.barrier()
    A.release(m0)

    def hkeys(tb, cs=range(8)):
        return [("h", c, tb) for c in cs]

    def jmod(bl, tb):
        return 2 if tb == 4 else bl

    def rsqrt_from_psum(pb, n, parts=128, scale=1.0, out=None, okey="rstd"):
        o = rstd if out is None else out
        S.add("act", lambda e: e.activation(out=rt[0:parts, 0:n], in_=ps[pb][0:parts, 0:n], func=AF.Sqrt, bias=epsc[0:parts, 0:1], scale=scale),
              reads=[PK(pb), "epsc"], writes=["rt"])
        S.add("dve", lambda e: e.reciprocal(out=o[0:parts, 0:n], in_=rt[0:parts, 0:n]), reads=["rt"], writes=[okey])

    def make_xn(bl, l, s, tb, dst, dkeys, final=False, pbank=7):
        for part in make_xn_parts(bl, l, s, tb, dst, dkeys, final=final, pbank=pbank):
            part()

    def make_xn_parts(bl, l, s, tb, dst, dkeys, final=False, pbank=7):
        t0, n = TBS[tb]
        j = jmod(bl, tb)

        def part_a():
            _squares()

        def part_b():
            _reduce()

        def part_c():
            _apply()

        def _squares():
          for c in range(8):
            if c in (0, 3, 6):
                S.add("act", lambda e, c=c: e.activation(out=sq[:, c, 0:n], in_=hT[:, c, t0:t0 + n], func=AF.Square),
                      reads=[("h", c, tb)], writes=[("sq", c)])
            elif c in (1, 4, 7):
                S.add("dve", lambda e, c=c: e.tensor_tensor(out=sq[:, c, 0:n], in0=hT[:, c, t0:t0 + n], in1=hT[:, c, t0:t0 + n], op=ALU.mult),
                      reads=[("h", c, tb)], writes=[("sq", c)])
            else:
                S.add("pool", lambda e, c=c: e.tensor_tensor(out=sq[:, c, 0:n], in0=hT[:, c, t0:t0 + n], in1=hT[:, c, t0:t0 + n], op=ALU.mult),
                      reads=[("h", c, tb)], writes=[("sq", c)])

        def _reduce():
            def mm(e):
                ins = None
                for c in range(8):
                    ins = e.matmul(ps[pbank][:, 0:n], lhsT=onesD, rhs=sq[:, c, 0:n], start=(c == 0), stop=(c == 7))
                return ins
            S.add("pe", mm, reads=[("sq", c) for c in range(8)] + ["cmat"], writes=[PK(pbank)])
            rsqrt_from_psum(pbank, n, out=rstdx, okey="rstdx")

        def _apply():
          for c in range(8):
            tx = tmpx[c % 2]
            S.add("pool" if c in (1, 4, 7) else "dve", lambda e, c=c, tx=tx: e.tensor_tensor(out=tx[:, 0:n], in0=hT[:, c, t0:t0 + n], in1=rstdx[:, 0:n], op=ALU.mult),
                  reads=[("h", c, tb), "rstdx"], writes=[("tmpx", c % 2)])
            if final:
                S.add("act", lambda e, c=c, tx=tx: e.activation(out=dst(c), in_=tx[:, 0:n], func=AF.Identity, scale=g_finalT[:, c:c + 1]),
                      reads=[("tmpx", c % 2), "g_finalT"], writes=[dkeys(c)])
            else:
                S.add("act", lambda e, c=c, tx=tx: e.activation(out=dst(c), in_=tx[:, 0:n], func=AF.Identity,
                                                                 scale=gs[:, l, s, c, j:j + 1], bias=modall[:, l, 3 * s * 8 + c, j:j + 1]),
                      reads=[("tmpx", c % 2), ("gs", l), ("modall", l)], writes=[dkeys(c)])
        return [part_a, part_b, part_c]

    def resid_add(bl, l, s, tb, oc, pb, n):
        t0, _ = TBS[tb]
        j = jmod(bl, tb)
        S.add("dve", lambda e: e.scalar_tensor_tensor(out=hT[:, oc, t0:t0 + n], in0=ps[pb][:, 0:n], scalar=hg[:, l, s, oc, j:j + 1],
                                                       in1=hT[:, oc, t0:t0 + n], op0=ALU.mult, op1=ALU.add),
              reads=[PK(pb), ("hg", l), ("h", oc, tb)], writes=[("h", oc, tb)])

    def dump_and_end(bl):
        dma_sp(dbg[bl], hT[:], [("h", c, tb) for c in range(8) for tb in range(5)], ["dbg"])

    def ffn(bl, l, which, blocks, ada_next=None):
        s = 0 if which == 0 else 2
        m = A.mark()
        extra, extra_fin = [], None
        if ada_next is not None:
            wabx = [A.t(f"wabx{i}", [128, 8, 512], BF16) for i in range(3)]
            extra, extra_fin = adaln_steps(ada_next, wabx, uid())
            extra[0][0]()
            extra[1][0]()
        ek = [0]
        xn = A.t("xn", [128, 8, NTOK], BF16)
        wa = [A.t(f"wa{i}", [128, 8, 256], BF16) for i in range(2)]
        wb = [A.t(f"wb{i}", [128, 8, 256], BF16) for i in range(2)]
        wo = [A.t(f"wo{i}", [128, 2, 1024], BF16) for i in range(2)]
        mid = [A.t(f"mid{i}", [128, 2, NTOK], BF16) for i in range(2)]
        sa = [A.t(f"sa{i}", [128, 512], F32) for i in range(2)]
        u = uid()
        def mkp(tb):
            t0, n = TBS[tb]
            return make_xn_parts(bl, l, s, tb, lambda c, t0=t0, n=n: xn[:, c, t0:t0 + n], lambda c, tb=tb: ("xn", u, c, tb))
        win = w_ff_in[which][l]
        wout = w_ff_out[which][l]
        NG = DFF // 256
        ring1 = Ring([0, 1, 2, 3])
        ring2 = Ring([4, 5, 6])
        sar = Ring([0, 1])

        def load(g):
            bi = g % 2
            dma_w(wa[bi][:], win[:, g * 256:(g + 1) * 256].rearrange("(k p) n -> p k n", p=128), [], [("wa", bi)])
            dma_w(wb[bi][:], win[:, DFF + g * 256:DFF + (g + 1) * 256].rearrange("(k p) n -> p k n", p=128), [], [("wb", bi)])
            dma_w(wo[bi][:], wout[g * 256:(g + 1) * 256, :].rearrange("(j p) n -> p j n", p=128), [], [("wo", bi)])

        def phase1(g, blks=None):
            bi = g % 2
            for tb in (blocks if blks is None else blks):
                t0, n = TBS[tb]
                for jj in range(2):
                    pa = ring1.next()
                    pb = ring1.next()

                    def mm(e, w, pbk, jj=jj, t0=t0, n=n):
                        ins = None
                        for k in range(8):
                            ins = e.matmul(ps[pbk][:, 0:n], lhsT=w[:, k, jj * 128:(jj + 1) * 128], rhs=xn[:, k, t0:t0 + n],
                                           start=(k == 0), stop=(k == 7))
                        return ins
                    xk = [("xn", u, c, tb) for c in range(8)]
                    S.add("pe", lambda e, w=wa[bi], pbk=pa, mm=mm: mm(e, w, pbk), reads=xk + [("wa", bi)], writes=[PK(pa)])
                    S.add("pe", lambda e, w=wb[bi], pbk=pb, mm=mm: mm(e, w, pbk), reads=xk + [("wb", bi)], writes=[PK(pb)])
                    si = sar.next()
                    S.add("act", lambda e, pa=pa, si=si, n=n: e.activation(out=sa[si][:, 0:n], in_=ps[pa][:, 0:n], func=AF.Silu),
                          reads=[PK(pa)], writes=[("sa", si)])
                    S.add("dve", lambda e, pb=pb, si=si, n=n, jj=jj, t0=t0: e.tensor_tensor(out=mid[bi][:, jj, t0:t0 + n], in0=sa[si][:, 0:n],
                                                                                         in1=ps[pb][:, 0:n], op=ALU.mult),
                          reads=[PK(pb), ("sa", si)], writes=[("mid", bi, jj, tb)])

        def phase2(g):
            bi = g % 2
            for tb in blocks:
                t0, n = TBS[tb]
                for oc in range(8):
                    po = ring2.next()

                    def mm(e, po=po, oc=oc, t0=t0, n=n):
                        ins = None
                        for jj in range(2):
                            ins = e.matmul(ps[po][:, 0:n], lhsT=wo[bi][:, jj, oc * 128:(oc + 1) * 128], rhs=mid[bi][:, jj, t0:t0 + n],
                                           start=(jj == 0), stop=(jj == 1))
                        return ins
                    S.add("pe", mm, reads=[("mid", bi, 0, tb), ("mid", bi, 1, tb), ("wo", bi)], writes=[PK(po)])
                    resid_add(bl, l, s, tb, oc, po, n)

        load(0)
        load(1)
        for tb in blocks[:2]:
            for p_ in mkp(tb):
                p_()
        for i_, tb in enumerate(blocks):
            prts = mkp(blocks[i_ + 2]) if i_ + 2 < len(blocks) else [lambda: None] * 3
            prts[0]()
            phase1(0, [tb])
            prts[1]()
            phase1(1, [tb])
            prts[2]()
        for g in range(1, NG):
            if g > 1:
                phase1(g)
            phase2(g - 1)
            if g + 1 < NG:
                load(g + 1)
            for _ in range(2):
                if ek[0] < len(extra):
                    ada_run(extra, ek[0])
                    ek[0] += 1
        phase2(NG - 1)
        while ek[0] < len(extra):
            ada_run(extra, ek[0])
            ek[0] += 1
        if extra_fin is not None:
            extra_fin()
        S.barrier()
        A.release(m)

    def mixer(bl, l):
        last = (l == DEPTH - 1)
        blocks = [0, 1, 2, 3, 4]
        m = A.mark()
        qT = A.t("qT", [128, 4, NTOK], BF16)
        kT = A.t("kT", [128, 3, NTOK], BF16)
        vaug = A.t("vaug", [128, 18, 6, 128], BF16)
        m1 = A.mark()
        xnb = [A.t(f"xnb{i}", [128, 8, 512], BF16) for i in range(2)]
        wq = A.t("wq", [128, 8, 512], BF16)
        wk = A.t("wk", [128, 8, 384], BF16)
        wv = A.t("wv", [128, 8, 384], BF16)
        ropet = A.t("ropet", [128, 4, 512], F32)
        sqb = [A.t(f"sqb{i}", [128, 512], BF16) for i in range(2)]
        qn = [A.t(f"qn{i}", [128, 512], BF16) for i in range(3)]
        t1 = A.t("t1", [128, 512], F32)
        t2 = A.t("t2", [128, 512], F32)
        u = uid()
        S.add("pool", lambda e: e.memset(vaug[:], 1.0), writes=[("vaug", kc) for kc in range(18)])
        W = w_in_p[l]
        prj = Ring([0, 1, 2])
        aux = Ring([3, 4])
        vring = Ring([5, 6])
        qnr = Ring([0, 1, 2])
        def pass1_pre_parts(tb):
            t0, n = TBS[tb]
            xb = xnb[tb % 2]
            xkeys = [("xnb", tb % 2, c) for c in range(8)]
            pa_, pb_, pc_ = make_xn_parts(bl, l, 1, tb, lambda c, xb=xb, n=n: xb[:, c, 0:n], lambda c, tb=tb: ("xnb", tb % 2, c))

            def pc2():
                pc_()
                dma_sp(xscr[:, :, t0:t0 + n], xb[:, :, 0:n], xkeys, [("xscr", tb)])
            return [pa_, pb_, pc2]

        def pass1_pre(tb):
            for p_ in pass1_pre_parts(tb):
                p_()

        sqr = Ring([0, 1])

        def pass1_items(tb):
            t0, n = TBS[tb]
            xb = xnb[tb % 2]
            xkeys = [("xnb", tb % 2, c) for c in range(8)]
            norope = (tb == 4)
            items = []
            for ci in range(7):
                if ci < 4:
                    wt, wkey, co = wq, "wq", ci * 128
                    dest = qT[:, ci, t0:t0 + n]
                    dkey = ("qT", ci, tb)
                else:
                    wt, wkey, co = wk, "wk", (ci - 4) * 128
                    dest = kT[:, ci - 4, t0:t0 + n]
                    dkey = ("kT", ci - 4, tb)
                isA = ci in (0, 1, 4)
                st = {}

                def P(st=st, wt=wt, wkey=wkey, co=co):
                    pb = prj.next()
                    st["pb"] = pb

                    def mm(e):
                        ins = None
                        for k in range(8):
                            ins = e.matmul(ps[pb][:, 0:n], lhsT=wt[:, k, co:co + 128], rhs=xb[:, k, 0:n], start=(k == 0), stop=(k == 7))
                        return ins
                    S.add("pe", mm, reads=xkeys + [wkey], writes=[PK(pb)])

                def N(st=st):
                    pb = st["pb"]
                    si = sqr.next()
                    st["si"] = si
                    S.add("act", lambda e: e.activation(out=sqb[si][:, 0:n], in_=ps[pb][:, 0:n], func=AF.Square),
                          reads=[PK(pb)], writes=[("sqb", si)])

                def M1(st=st, isA=isA, ci=ci, dest=dest, dkey=dkey):
                    pb = st["pb"]
                    qi = qnr.next()
                    st["qi"] = qi
                    qdst = dest if norope else qn[qi][:, 0:n]
                    qkey = dkey if norope else ("qn", qi)
                    if isA:
                        si = st["si"]
                        pa = aux.next()
                        S.add("pe", lambda e: e.matmul(ps[pa][:, 0:n], lhsT=blk64, rhs=sqb[si][:, 0:n], start=True, stop=True),
                              reads=[("sqb", si), "cmat"], writes=[PK(pa)])
                        rsqrt_from_psum(pa, n)
                        gi = 0 if ci < 4 else 1
                        S.add("dve", lambda e: e.scalar_tensor_tensor(out=qdst, in0=ps[pb][:, 0:n], scalar=gqk[:, l, gi:gi + 1],
                                                                        in1=rstd[:, 0:n], op0=ALU.mult, op1=ALU.mult),
                              reads=[PK(pb), "rstd", "gqk"], writes=[qkey])
                    else:
                        S.add("act", lambda e: e.activation(out=qdst, in_=ps[pb][:, 0:n], func=AF.Copy),
                              reads=[PK(pb)], writes=[qkey])

                def M2(st=st, isA=isA, ci=ci, dest=dest, dkey=dkey):
                    if ci == 0:
                        dma_sp(ropet[:], rope_d[:, :, t0:t0 + n].rearrange("m p n -> p m n"), [], ["ropet"])
                    qi = st["qi"]
                    pr = aux.next()
                    pm = permA if isA else permC
                    ti = 0 if isA else 2
                    S.add("pe", lambda e: e.matmul(ps[pr][:, 0:n], lhsT=pm, rhs=qn[qi][:, 0:n], start=True, stop=True),
                          reads=[("qn", qi), "cmat"], writes=[PK(pr)])
                    S.add("dve", lambda e: e.tensor_tensor(out=t1[:, 0:n], in0=qn[qi][:, 0:n], in1=ropet[:, ti, 0:n], op=ALU.mult),
                          reads=[("qn", qi), "ropet"], writes=["t1"])
                    S.add("dve", lambda e: e.tensor_tensor(out=t2[:, 0:n], in0=ps[pr][:, 0:n], in1=ropet[:, ti + 1, 0:n], op=ALU.mult),
                          reads=[PK(pr), "ropet"], writes=["t2"])
                    S.add("pool", lambda e: e.tensor_tensor(out=dest, in0=t1[:, 0:n], in1=t2[:, 0:n], op=ALU.add),
                          reads=["t1", "t2"], writes=[dkey])
                items.append([P, N if isA else None, M1, None if norope else M2])

            def PV():
                for sb in range(n // 128):
                    kc = (t0 // 128) + sb
                    pv = vring.next()

                    def mmv(e, pv=pv, sb=sb):
                        ins = None
                        for k in range(8):
                            ins = e.matmul(ps[pv][:, 0:384], lhsT=xb[:, k, sb * 128:(sb + 1) * 128], rhs=wv[:, k, :], start=(k == 0), stop=(k == 7))
                        return ins
                    S.add("pe", mmv, reads=xkeys + ["wv"], writes=[PK(pv)])
                    S.add("act", lambda e, pv=pv, kc=kc: e.activation(out=vaug[:, kc, :, 0:64], in_=ps[pv][:, 0:384].rearrange("p (h d) -> p h d", d=64), func=AF.Copy),
                          reads=[PK(pv)], writes=[("vaug", kc)])
            items.append([PV, None, None, None])
            return items

        dma_w(wq[:], W[:, 0:512].rearrange("(k p) n -> p k n", p=128), [], ["wq"])
        dma_w(wk[:], W[:, 512:896].rearrange("(k p) n -> p k n", p=128), [], ["wk"])
        dma_w(wv[:], W[:, 896:1280].rearrange("(k p) n -> p k n", p=128), [], ["wv"])
        pass1_pre(blocks[0])
        flat = []
        for i_, tb in enumerate(blocks):
            its = pass1_items(tb)
            if i_ + 1 < len(blocks):
                prts = pass1_pre_parts(blocks[i_ + 1])
                for ci_, prt in zip((0, 3, 5), prts):
                    its[ci_][0] = (lambda p0_=its[ci_][0], prt=prt: (prt(), p0_()))
            flat.extend(its)
        NF = len(flat)
        for i_ in range(NF + 3):
            for stg in range(4):
                j_ = i_ - stg
                if 0 <= j_ < NF and flat[j_][stg] is not None:
                    flat[j_][stg]()
        S.barrier()
        A.release(m1)
        if stop == f"p1_{l}":
            pass

        woh = A.t("woh", [128, 4, 1024], BF16)
        cat = [A.t(f"cat{i}", [128, 4, 512], BF16) for i in range(2)]
        lnb = A.t("lnb", [64, 512], F32)
        pT = [A.t(f"pT{i}", [128, 512], BF16) for i in range(3)]
        rden = [A.t(f"rden{i}", [64, 512], F32) for i in range(2)]
        tt0 = A.t("tt0", [64, 512], F32)
        tt1 = A.t("tt1", [64, 512], F32)
        od_ = A.t("od_", [64, 512], F32)
        odn = A.t("odn", [64, 512], F32)
        sq64 = A.t("sq64", [64, 512], BF16)
        qpad = [A.t(f"qpad{i}", [128, 512], BF16) for i in range(3)]
        qpr = Ring([0, 1, 2])
        dma_w(woh[:], w_out[l][0:512, :].rearrange("(c p) n -> p c n", p=128), [], ["woh"])
        sring = Ring([0, 1, 6])
        oring = Ring([2, 3, 4, 5])
        ptr = Ring([0, 1, 2])
        rdr = Ring([0, 1])
        qblocks = [0, 1, 2, 3] if last else [0, 1, 2, 3, 4]
        steps = []

        def qblock(qi_, tb):
            t0, n = TBS[tb]
            kcs = list(range(18)) if tb != 4 else [16, 17]
            ct = cat[qi_ % 2]
            ci_ = qi_ % 2

            def warm():
                wbk = sring.next()

                def burst(e):
                    ins = None
                    for _ in range(WARM_N):
                        ins = e.matmul(ps[wbk][:, 0:512], lhsT=woh[:, 0, 0:128], rhs=woh[:, 1, 0:512], start=True, stop=True)
                    return ins
                S.add("pe", burst, reads=["woh"], writes=[PK(wbk)])
            if WARM_N > 0:
                steps.append((warm, lambda: None, lambda: None, None))

            hm_list = []

            def prep_qpad(hm):
                qi = qpr.next()
                hm["qp"] = qi
                r0, r1 = hm["krows"]
                S.add("pool", lambda e: e.memset(qpad[qi][:, 0:n], 0.0), writes=[("qpad", qi)])
                S.add("pool", lambda e: e.tensor_copy(out=qpad[qi][r0:r1, 0:n], in_=qT[r0:r1, hm["qchunk"], t0:t0 + n]),
                      reads=[("qT", hm["qchunk"], tb)], writes=[("qpad", qi)])

            def head_pass(krows, kchunk, qchunk, vslot, scale, po, tp, post):
                hm = dict(krows=krows, qchunk=qchunk)
                hm_list.append(hm)
                myidx = len(hm_list) - 1
                for ii, kc in enumerate(kcs):
                    st = {}
                    ktb = kc // 4 if kc < 16 else 4

                    def qk(st=st, kc=kc, ktb=ktb, ii=ii):
                        if ii == 0:
                            if "qp" not in hm:
                                prep_qpad(hm)
                            if myidx + 1 < len(hm_list) and "qp" not in hm_list[myidx + 1]:
                                prep_qpad(hm_list[myidx + 1])
                        sb_ = sring.next()
                        st["sb"] = sb_
                        qi = hm["qp"]

                        def mms(e):
                            return e.matmul(ps[sb_][:, 0:n], lhsT=kT[:, kchunk, kc * 128:(kc + 1) * 128],
                                            rhs=qpad[qi][:, 0:n], start=True, stop=True)
                        S.add("pe", mms, reads=[("kT", kchunk, ktb), ("qpad", qi)], writes=[PK(sb_)])

                    def ex(st=st):
                        sb_ = st["sb"]
                        pi = ptr.next()
                        st["pi"] = pi
                        S.add("act", lambda e: e.activation(out=pT[pi][:, 0:n], in_=ps[sb_][:, 0:n], func=AF.Exp, scale=scale),
                              reads=[PK(sb_)], writes=[("pT", pi)])

                    def pv(st=st, kc=kc, ii=ii):
                        pi = st["pi"]
                        S.add("pe", lambda e: e.matmul(ps[po][:, 0:n], lhsT=vaug[:, kc, vslot, :], rhs=pT[pi][:, 0:n],
                                                       start=(ii == 0), stop=(ii == len(kcs) - 1)),
                              reads=[("pT", pi), ("vaug", kc)], writes=[PK(po)])
                    steps.append((qk, ex, pv, post if ii == len(kcs) - 1 else None))

            for h in range(4):
                r0 = 64 * (h // 2)
                po = oring.next()

                def postA(po=po, h=h):
                    ri = rdr.next()
                    S.add("dve", lambda e: e.reciprocal(out=rden[ri][:, 0:n], in_=ps[po][64:128, 0:n]), reads=[PK(po)], writes=[("rden", ri)])
                    p0 = 64 * (h % 2)
                    S.add("dve", lambda e: e.tensor_tensor(out=ct[p0:p0 + 64, h // 2, 0:n], in0=ps[po][0:64, 0:n], in1=rden[ri][:, 0:n], op=ALU.mult),
                          reads=[PK(po), ("rden", ri)], writes=[("cat", ci_, h)])
                    return []
                head_pass((r0, r0 + 64), 0, h % 2, h // 2, 0.125, po, None, postA)
            for h in range(4):
                base = 64 * (h % 2)
                pos = [oring.next(), oring.next()]

                def postC(pos=pos, h=h):
                    ri0 = rdr.next()
                    ri1 = rdr.next()
                    p0 = 64 * (h % 2)
                    S.add("dve", lambda e: e.reciprocal(out=rden[ri0][:, 0:n], in_=ps[pos[0]][64:128, 0:n]), reads=[PK(pos[0])], writes=[("rden", ri0)])
                    S.add("dve", lambda e: e.tensor_tensor(out=tt0[:, 0:n], in0=ps[pos[0]][0:64, 0:n], in1=rden[ri0][:, 0:n], op=ALU.mult),
                          reads=[PK(pos[0]), ("rden", ri0)], writes=["tt0"])
                    S.add("dve", lambda e: e.reciprocal(out=rden[ri1][:, 0:n], in_=ps[pos[1]][64:128, 0:n]), reads=[PK(pos[1])], writes=[("rden", ri1)])
                    S.add("dve", lambda e: e.tensor_tensor(out=tt1[:, 0:n], in0=ps[pos[1]][0:64, 0:n], in1=rden[ri1][:, 0:n], op=ALU.mult),
                          reads=[PK(pos[1]), ("rden", ri1)], writes=["tt1"])
                    S.add("dve", lambda e: e.scalar_tensor_tensor(out=od_[:, 0:n], in0=tt1[:, 0:n], scalar=neglam[:, l:l + 1], in1=tt0[:, 0:n],
                                                                    op0=ALU.mult, op1=ALU.add),
                          reads=["tt0", "tt1", "neglam"], writes=["od_"])

                    def st1():
                        S.add("act", lambda e: e.activation(out=sq64[:, 0:n], in_=od_[:, 0:n], func=AF.Square), reads=["od_"], writes=["sq64"])

                    def st2():
                        S.add("pe", lambda e: e.matmul(ps[7][0:64, 0:n], lhsT=blk64[0:64, 0:64], rhs=sq64[:, 0:n], start=True, stop=True),
                              reads=["sq64", "cmat"], writes=[PK(7)])

                    def st3():
                        S.add("act", lambda e: e.activation(out=lnb[:, 0:n], in_=ps[7][0:64, 0:n], func=AF.Ln, bias=epsc[0:64, 0:1], scale=1.0),
                              reads=[PK(7), "epsc"], writes=["lnb"])
                        S.add("act", lambda e: e.activation(out=odn[:, 0:n], in_=lnb[:, 0:n], func=AF.Exp, scale=-0.5), reads=["lnb"], writes=["odn"])

                    def st4():
                        S.add("dve", lambda e: e.scalar_tensor_tensor(out=ct[p0:p0 + 64, 2 + h // 2, 0:n], in0=od_[:, 0:n], scalar=gsub1m[:, l:l + 1],
                                                                        in1=odn[:, 0:n], op0=ALU.mult, op1=ALU.mult),
                              reads=["od_", "odn", "gsub1m"], writes=[("cat", ci_, 4 + h)])
                    tasks = [(20, st1), (23, st2), (26, st3), (30, st4)]
                    if h == 3:
                        def outproj():
                            for oc in range(8):
                                pb = 7

                                def mmo(e, oc=oc, pb=pb):
                                    ins = None
                                    for hh in range(4):
                                        ins = e.matmul(ps[pb][:, 0:n], lhsT=woh[:, hh, oc * 128:(oc + 1) * 128], rhs=ct[:, hh, 0:n],
                                                       start=(hh == 0), stop=(hh == 3))
                                    return ins
                                S.add("pe", mmo, reads=[("cat", ci_, hh) for hh in range(8)] + ["woh"], writes=[PK(pb)])
                                resid_add(bl, l, 1, tb, oc, pb, n)
                        tasks.append((34, outproj))
                    return tasks
                postC.is_c = True
                for c in range(2):
                    r0 = base + 32 * c
                    head_pass((r0, r0 + 32), 1 + h // 2, 2 + h // 2, 2 + h, 32 ** -0.5, pos[c], (r0, 0), postC if c == 1 else None)

        for qi_, tb in enumerate(qblocks):
            qblock(qi_, tb)
        LA = 2
        deferred = []
        NS = len(steps)
        for i in range(min(LA, NS)):
            steps[i][0]()
        for i in range(NS):
            if i + LA < NS:
                steps[i + LA][0]()
            steps[i][1]()
            steps[i][2]()
            if steps[i][3] is not None:
                if getattr(steps[i][3], "is_c", False):
                    for d in sorted(deferred, key=lambda d: d[0]):
                        d[1]()
                    deferred = []
                for (dl, fn) in steps[i][3]():
                    deferred.append((i + dl, fn))
            ready = [d for d in deferred if d[0] <= i]
            deferred = [d for d in deferred if d[0] > i]
            for d in ready:
                d[1]()
        for d in sorted(deferred, key=lambda d: d[0]):
            d[1]()
        S.barrier()
        A.release(m)

        m2 = A.mark()
        blocks2 = [0, 1, 2, 3] if last else [0, 1, 2, 3, 4]
        xnb = [A.t(f"xnb{i}", [128, 8, 512], BF16) for i in range(2)]
        wu = A.t("wu", [128, 8, 256], BF16)
        wvg = A.t("wvg", [128, 8, 256], BF16)
        wgl = A.t("wgl", [128, 8, 512], BF16)
        wob = A.t("wob", [128, 2, 1024], BF16)
        wod = A.t("wod", [128, 2, 1024], BF16)
        wsT = A.t("wsT", [128, 4, 128], BF16)
        gvb = A.t("gvb", [128, 256], F32)
        bst = A.t("bst", [64, 4, 512], F32)
        diag = A.t("diag", [128, 2, 31, 128], BF16)
        identF = A.t("identF", [128, 128], F32)
        dma_sp(identF[:], cmat_d[5], [], ["identF"])
        ypl = A.t("ypl", [128, 2, NLAT + 30], BF16)
        ypc = A.t("ypc", [128, 2, NCTX + 30], BF16)
        ug = A.t("ug", [64, 4, 512], F32)
        vge = A.t("vge", [128, 4, 256], F32)
        junk = A.t("junk", [128, 256], BF16)
        ssum = A.t("ssum", [128, 8], F32)
        vn = A.t("vn", [128, 4, 256], BF16)
        tmb = [A.t("tmb0", [64, 512], F32), A.t("tmb1", [64, 512], F32)]
        catb = A.t("catb", [128, 2, 512], BF16)
        sg = A.t("sg", [128, 512], F32)
        zz = A.t("zz", [128, 2, 512], F32)
        sqz = A.t("sqz", [128, 2, 512], BF16)
        odd = A.t("odd", [128, 2, 512], BF16)
        dma_w(wob[:], w_out[l][512:768, :].rearrange("(c p) n -> p c n", p=128), [], ["wob"])
        dma_w(wod[:], w_out[l][768:1024, :].rearrange("(c p) n -> p c n", p=128), [], ["wod"])
        dma_w(wsT[:], wsT_d[l], [], ["wsT"])
        dma_sp(gvb[:], gv_d[:, l, :], [], ["gvb"])
        dma_sp(bst[:], bs_d[:, l, :, :], [], ["bst"])
        for c in range(2):
            for k in range(31):
                S.add("dve", lambda e, c=c, k=k: e.tensor_scalar(out=diag[:, c, k, :], in0=identF[:], scalar1=wdw[:, l, c, k:k + 1], scalar2=None, op0=ALU.mult),
                      reads=["identF", "wdw"], writes=[("diag", c)])
        S.add("pool", lambda e: e.memset(ypl[:], 0.0), writes=[("ypl", c, tb) for c in range(2) for tb in range(4)] + ["yplpad"])
        S.add("pool", lambda e: e.memset(ypc[:], 0.0), writes=[("ypc", c) for c in range(2)])
        pr2 = Ring([0, 1, 2, 3])
        outr = Ring([4])
        pr2b = Ring([5, 6])
        outrb = Ring([7])
        def pass2a_pre(tb):
            t0, n = TBS[tb]
            xb = xnb[tb % 2]
            xkeys = [("xnb", tb % 2, c) for c in range(8)]
            dma_sp(xb[:, :, 0:n], xscr[:, :, t0:t0 + n], [("xscr", tb)], xkeys)

        def pass2a(tb):
            t0, n = TBS[tb]
            xb = xnb[tb % 2]
            xkeys = [("xnb", tb % 2, c) for c in range(8)]
            nsb = n // 128
            for g in range(4):
                pb = pr2.next()

                def mmu(e, pb=pb, g=g):
                    ins = None
                    for k in range(8):
                        ins = e.matmul(ps[pb][0:64, 0:n], lhsT=wu[:, k, g * 64:(g + 1) * 64], rhs=xb[:, k, 0:n], start=(k == 0), stop=(k == 7))
                    return ins
                S.add("pe", mmu, reads=xkeys + ["wu"], writes=[PK(pb)])
                S.add("act", lambda e, pb=pb, g=g: e.activation(out=ug[:, g, 0:n], in_=ps[pb][0:64, 0:n], func=AF.Gelu_apprx_tanh),
                      reads=[PK(pb)], writes=[("ug", g)])
            S.add("dve", lambda e: e.memset(ssum[:, 0:4], 0.0), writes=[("ssum", sb) for sb in range(4)])
            for sb in range(nsb):
                pb = pr2.next()

                def mmv(e, pb=pb, sb=sb):
                    ins = None
                    for k in range(8):
                        ins = e.matmul(ps[pb][:, 0:256], lhsT=xb[:, k, sb * 128:(sb + 1) * 128], rhs=wvg[:, k, :], start=(k == 0), stop=(k == 7))
                    return ins
                S.add("pe", mmv, reads=xkeys + ["wvg"], writes=[PK(pb)])
                S.add("act", lambda e, pb=pb, sb=sb: e.activation(out=vge[:, sb, :], in_=ps[pb][:, 0:256], func=AF.Gelu_apprx_tanh),
                      reads=[PK(pb)], writes=[("vge", sb)])
                S.add("act", lambda e, sb=sb: e.activation(out=junk[:], in_=vge[:, sb, :], func=AF.Square, accum_out=ssum[:, sb:sb + 1]),
                      reads=[("vge", sb), ("ssum", sb)], writes=["junk", ("ssum", sb)])
            for c in range(2):
                pa = pr2.next()
                pg = pr2.next()

                def mmg(e, pbk, co):
                    ins = None
                    for k in range(8):
                        ins = e.matmul(ps[pbk][:, 0:n], lhsT=wgl[:, k, co:co + 128], rhs=xb[:, k, 0:n], start=(k == 0), stop=(k == 7))
                    return ins
                S.add("pe", lambda e, pa=pa, c=c, mmg=mmg: mmg(e, pa, c * 128), reads=xkeys + ["wgl"], writes=[PK(pa)])
                S.add("pe", lambda e, pg=pg, c=c, mmg=mmg: mmg(e, pg, 256 + c * 128), reads=xkeys + ["wgl"], writes=[PK(pg)])
                S.add("act", lambda e, pg=pg: e.activation(out=sg[:, 0:n], in_=ps[pg][:, 0:n], func=AF.Sigmoid), reads=[PK(pg)], writes=["sg"])
                if tb != 4:
                    S.add("dve", lambda e, pa=pa, c=c: e.tensor_tensor(out=ypl[:, c, 15 + t0:15 + t0 + n], in0=ps[pa][:, 0:n], in1=sg[:, 0:n], op=ALU.mult),
                          reads=[PK(pa), "sg", "yplpad"], writes=[("ypl", c, tb)])
                else:
                    S.add("dve", lambda e, pa=pa, c=c: e.tensor_tensor(out=ypc[:, c, 15:15 + n], in0=ps[pa][:, 0:n], in1=sg[:, 0:n], op=ALU.mult),
                          reads=[PK(pa), "sg"], writes=[("ypc", c)])
            S.add("act", lambda e: e.activation(out=ssum[:, 4:4 + nsb], in_=ssum[:, 0:nsb], func=AF.Sqrt, bias=epsc[:, 0:1], scale=1.0 / 256.0),
                  reads=[("ssum", sb) for sb in range(nsb)] + ["epsc"], writes=["ssr"])
            S.add("dve", lambda e: e.reciprocal(out=ssum[:, 4:4 + nsb], in_=ssum[:, 4:4 + nsb]), reads=["ssr"], writes=["ssr"])
            for sb in range(nsb):
                S.add("dve", lambda e, sb=sb: e.scalar_tensor_tensor(out=vn[:, sb, :], in0=vge[:, sb, :], scalar=ssum[:, 4 + sb:5 + sb], in1=gvb[:],
                                                                       op0=ALU.mult, op1=ALU.mult),
                      reads=[("vge", sb), "ssr", "gvb"], writes=[("vn", sb)])
            for g in range(4):
                pb = pr2.next()
                ti = g % 2
                p0 = 64 * (g % 2)

                def mmm(e, pb=pb, g=g):
                    ins = None
                    for sb in range(nsb):
                        ins = e.matmul(ps[pb][0:64, sb * 128:(sb + 1) * 128], lhsT=vn[:, sb, g * 64:(g + 1) * 64], rhs=wsT[:, g, :], start=True, stop=True)
                    return ins
                S.add("pe", mmm, reads=[("vn", sb) for sb in range(nsb)] + ["wsT"], writes=[PK(pb)])
                S.add("dve", lambda e, pb=pb, g=g, ti=ti: e.tensor_tensor(out=tmb[ti][:, 0:n], in0=ps[pb][0:64, 0:n], in1=bst[:, g, 0:n], op=ALU.add),
                      reads=[PK(pb), "bst"], writes=[("tmb", ti)])
                S.add("dve", lambda e, g=g, ti=ti, p0=p0: e.tensor_tensor(out=catb[p0:p0 + 64, g // 2, 0:n], in0=ug[:, g, 0:n], in1=tmb[ti][:, 0:n], op=ALU.mult),
                      reads=[("tmb", ti), ("ug", g)], writes=[("catb", g)])
            for oc in range(8):
                po = outr.next()

                def mmo(e, po=po, oc=oc):
                    ins = None
                    for cc in range(2):
                        ins = e.matmul(ps[po][:, 0:n], lhsT=wob[:, cc, oc * 128:(oc + 1) * 128], rhs=catb[:, cc, 0:n], start=(cc == 0), stop=(cc == 1))
                    return ins
                S.add("pe", mmo, reads=[("catb", g) for g in range(4)] + ["wob"], writes=[PK(po)])
                if "B" in DEBUG_PARTS:
                    resid_add(bl, l, 1, tb, oc, po, n)

        dma_w(wu[:], W[:, 1280:1536].rearrange("(k p) n -> p k n", p=128), [], ["wu"])
        dma_w(wvg[:], W[:, 1536:1792].rearrange("(k p) n -> p k n", p=128), [], ["wvg"])
        dma_w(wgl[:], W[:, 1792:2304].rearrange("(k p) n -> p k n", p=128), [], ["wgl"])

        def pass2b(tb):
            t0, n = TBS[tb]
            for c in range(2):
                pz = pr2b.next()
                if tb != 4:
                    yk = [("ypl", c, t) for t in range(max(0, tb - 1), min(3, tb + 1) + 1)] + ["yplpad"]
                    ysrc = lambda k, c=c, t0=t0, n=n: ypl[:, c, t0 + k:t0 + k + n]
                else:
                    yk = [("ypc", c)]
                    ysrc = lambda k, c=c, n=n: ypc[:, c, k:k + n]

                def mmc(e, pz=pz, c=c, ysrc=ysrc, n=n):
                    ins = None
                    for k in range(31):
                        ins = e.matmul(ps[pz][:, 0:n], lhsT=diag[:, c, k, :], rhs=ysrc(k), start=(k == 0), stop=(k == 30))
                    return ins
                S.add("pe", mmc, reads=yk + [("diag", c)], writes=[PK(pz)])
                S.add("act", lambda e, pz=pz, c=c, n=n: e.activation(out=zz[:, c, 0:n], in_=ps[pz][:, 0:n], func=AF.Identity, bias=bdw[:, l, c:c + 1]),
                      reads=[PK(pz), "bdw"], writes=[("zz", c)])
                S.add("act", lambda e, c=c, n=n: e.activation(out=sqz[:, c, 0:n], in_=zz[:, c, 0:n], func=AF.Square), reads=[("zz", c)], writes=[("sqz", c)])
            pn = pr2b.next()

            def mmn(e, pn=pn, n=n):
                ins = None
                for c in range(2):
                    ins = e.matmul(ps[pn][:, 0:n], lhsT=ones256, rhs=sqz[:, c, 0:n], start=(c == 0), stop=(c == 1))
                return ins
            S.add("pe", mmn, reads=[("sqz", 0), ("sqz", 1), "cmat"], writes=[PK(pn)])
            rsqrt_from_psum(pn, n)
            for c in range(2):
                tx = tmpx[c]
                S.add("pool", lambda e, c=c, tx=tx, n=n: e.tensor_tensor(out=tx[:, 0:n], in0=zz[:, c, 0:n], in1=rstd[:, 0:n], op=ALU.mult),
                      reads=[("zz", c), "rstd"], writes=[("tmpx", c)])
                S.add("act", lambda e, c=c, tx=tx, n=n: e.activation(out=odd[:, c, 0:n], in_=tx[:, 0:n], func=AF.Silu, scale=gconv[:, l, c:c + 1]),
                      reads=[("tmpx", c), "gconv"], writes=[("odd", c)])
            for oc in range(8):
                po = outrb.next()

                def mmo2(e, po=po, oc=oc, n=n):
                    ins = None
                    for c in range(2):
                        ins = e.matmul(ps[po][:, 0:n], lhsT=wod[:, c, oc * 128:(oc + 1) * 128], rhs=odd[:, c, 0:n], start=(c == 0), stop=(c == 1))
                    return ins
                S.add("pe", mmo2, reads=[("odd", 0), ("odd", 1), "wod"], writes=[PK(po)])
                if "D" in DEBUG_PARTS:
                    resid_add(bl, l, 1, tb, oc, po, n)

        def record(fn):
            items = []
            S.add = lambda *a_, **k_: items.append((a_, k_))
            try:
                fn()
            finally:
                del S.add
            return items

        def merge_emit(la, lb):
            i = j = 0
            na, nb_ = len(la), len(lb)
            while i < na or j < nb_:
                if j >= nb_ or (i < na and i * nb_ <= j * na):
                    S.add(*la[i][0], **la[i][1])
                    i += 1
                else:
                    S.add(*lb[j][0], **lb[j][1])
                    j += 1

        def do2a(i_):
            if i_ + 1 < len(blocks2):
                pass2a_pre(blocks2[i_ + 1])
            pass2a(blocks2[i_])

        pass2a_pre(blocks2[0])
        NB2 = len(blocks2)
        for i_ in range(NB2 + 2):
            ia, ib = i_, i_ - 2
            la = record(lambda: do2a(ia)) if ia < NB2 else []
            lb = record(lambda: pass2b(blocks2[ib])) if 0 <= ib < NB2 else []
            merge_emit(la, lb)
        S.barrier()
        A.release(m2)

    done = False
    for bl in range(nb):
        for tb in range(4):
            t0_, n_ = TBS[tb]
            dma_sp(hT[:, :, t0_:t0_ + n_], xT[bl][:, t0_:t0_ + n_].rearrange("(c p) t -> p c t", p=128), [], [("h", c, tb) for c in range(8)])
        dma_sp(hT[:, :, NLAT:NTOK], ctxT[bl].rearrange("(c p) t -> p c t", p=128), [], [("h", c, 4) for c in range(8)])
        for l in range(depth):
            last = (l == DEPTH - 1)
            ffn(bl, l, 0, [0, 1, 2, 3, 4], ada_next=(l + 1 if (bl == 0 and l + 1 < depth) else None))
            if stop == f"ffn1_{l}":
                dump_and_end(bl)
                done = True
                break
            mixer(bl, l)
            if stop == f"mix_{l}":
                dump_and_end(bl)
                done = True
                break
            ffn(bl, l, 1, [0, 1, 2, 3] if last else [0, 1, 2, 3, 4])
            if stop == f"ffn2_{l}":
                dump_and_end(bl)
                done = True
                break
        if done:
            continue
        mf = A.mark()
        ob = [A.t(f"ob{i}", [128, 8, 512], F32) for i in range(2)]
        for tb in range(4):
            t0, n = TBS[tb]
            o = ob[tb % 2]
            make_xn(bl, 0, 0, tb, lambda c, o=o: o[:, c, :], lambda c, tb=tb: ("ob", tb % 2, c), final=True)
            dma_sp(outT[bl][:, t0:t0 + n].rearrange("(c p) t -> p c t", p=128), o[:], [("ob", tb % 2, c) for c in range(8)], [("ob", tb % 2, c) for c in range(8)] + ["outT"])
        S.barrier()
        A.release(mf)
    S.add("sp", lambda e: None, reads=["outT" if stop is None else "dbg"])
    S.emit()
    return nc


def _rope_tables():
    t = np.arange(NLAT)
    row = (t // 64).astype(np.float32)
    col = (t % 64).astype(np.float32)
    out = np.zeros((4, 128, NLAT), np.float32)
    for ti, hd in ((0, 64), (2, 32)):
        quarter = hd // 4
        inv = (np.float32(10000.0) ** (-np.arange(quarter, dtype=np.float32) / np.float32(quarter))).astype(np.float32)
        ang = np.concatenate([row[:, None] * inv[None, :], col[:, None] * inv[None, :]], axis=-1).astype(np.float32)
        cos = np.cos(ang).astype(np.float32)
        sin = np.sin(ang).astype(np.float32)
        half = hd // 2
        for p in range(128):
            d = p % hd
            j = d % half
            out[ti, p] = cos[:, j]
            out[ti + 1, p] = (-sin[:, j]) if d < half else sin[:, j]
    return out


def _const_mats():
    m = np.zeros((6, 128, 128), np.float32)
    m[0] = 1.0 / 1024.0
    m[1, 0:64, 0:64] = 1.0 / 64.0
    m[1, 64:128, 64:128] = 1.0 / 64.0
    m[2] = 1.0 / 256.0
    for mm_ in range(128):
        pa = mm_ + 32 if (mm_ % 64) < 32 else mm_ - 32
        m[3, pa, mm_] = 1.0
        pc = mm_ + 16 if (mm_ % 32) < 16 else mm_ - 16
        m[4, pc, mm_] = 1.0
    m[5] = np.eye(128, dtype=np.float32)
    return m


def _col_perm():
    qa = lambda h: list(range(h * 64, (h + 1) * 64))
    perm = qa(0) + qa(2) + qa(1) + qa(3)
    perm += list(range(256, 512))
    perm += list(range(512, 640))
    perm += list(range(768, 1024))
    perm += list(range(640, 768))
    perm += list(range(1024, 1280))
    perm += list(range(1280, 2304))
    return np.array(perm)


def prep_shared(inp):
    f = lambda a: np.ascontiguousarray(np.asarray(a, dtype=np.float32))
    sh = {}
    sh["w_ada"] = f(inp["w_ada"])
    sh["b_adaT"] = f(np.asarray(inp["b_ada"]).reshape(DEPTH, 72, 128).transpose(2, 0, 1))
    sh["g_normT"] = f(np.asarray(inp["g_norm"]).reshape(DEPTH, 3, 8, 128).transpose(3, 0, 1, 2))
    sh["g_finalT"] = f(np.asarray(inp["g_final"]).reshape(8, 128).T)
    for k in ("w_ff1_in", "w_ff1_out", "w_ff2_in", "w_ff2_out", "w_out"):
        sh[k] = f(inp[k])
    sh["w_in_p"] = f(np.asarray(inp["w_in"])[:, :, _col_perm()])
    gq = np.asarray(inp["g_q_a"])
    gk = np.asarray(inp["g_k_a"])
    gqk = np.stack([np.tile(gq, (1, 2)), np.tile(gk, (1, 2))], axis=-1)
    sh["gqk"] = f(gqk.transpose(1, 0, 2))
    sh["lamw"] = f(np.broadcast_to(np.asarray(inp["lam_c"])[None], (64, DEPTH, 4, 32)))
    sh["gsub"] = f(np.asarray(inp["g_sub_c"]).T)
    sh["gv_bc"] = f(np.broadcast_to(np.asarray(inp["g_v_b"])[None], (128, DEPTH, 256)))
    sh["wsT"] = f(np.asarray(inp["w_s_b"]).transpose(0, 3, 1, 2))
    bs = np.asarray(inp["b_s_b"])
    sh["bs_tbl"] = f(np.broadcast_to(np.tile(bs, (1, 1, 4))[None], (64, DEPTH, 4, 512)))
    sh["wdw"] = f(np.asarray(inp["w_dw_d"]).reshape(DEPTH, 31, 2, 128).transpose(3, 0, 2, 1))
    sh["bdw"] = f(np.asarray(inp["b_dw_d"]).reshape(DEPTH, 2, 128).transpose(2, 0, 1))
    sh["gconv"] = f(np.asarray(inp["g_conv_d"]).reshape(DEPTH, 2, 128).transpose(2, 0, 1))
    sh["rope"] = _rope_tables()
    sh["cmat"] = _const_mats()
    return sh


def prep_core(inp, bids):
    x = np.asarray(inp["x"])
    ctx = np.asarray(inp["ctx"])
    c = np.asarray(inp["c"])
    cc = np.asarray(inp["c_ctx"])
    d = {}
    d["xT"] = np.ascontiguousarray(np.stack([x[b].T for b in bids]).astype(np.float32))
    d["ctxT"] = np.ascontiguousarray(np.stack([ctx[b].T for b in bids]).astype(np.float32))
    vecs = [c[b] for b in bids]
    while len(vecs) < 2:
        vecs.append(c[bids[0]])
    vecs.append(cc)
    cT = np.stack(vecs, axis=-1).reshape(8, 128, 3).transpose(1, 0, 2)
    d["cT"] = np.ascontiguousarray(cT.astype(np.float32))
    return d


_NC_CACHE = {}


def kernel(**inputs):
    B = np.asarray(inputs["x"]).shape[0]
    nb = B // NCORES
    if "prog" not in _NC_CACHE:
        _NC_CACHE["prog"] = build_program(nb=nb)
    nc = _NC_CACHE["prog"]
    sh = prep_shared(inputs)
    in_maps = []
    for i in range(NCORES):
        d = dict(sh)
        d.update(prep_core(inputs, list(range(i * nb, (i + 1) * nb))))
        in_maps.append(d)
    res = run_bass_kernel_spmd(nc, in_maps, core_ids=list(range(NCORES)))
    out = np.empty((B, NLAT, D), np.float32)
    for i in range(NCORES):
        o = res.results[i]["outT"]
        for jb in range(nb):
            out[i * nb + jb] = o[jb].T
    return out
```

```python
import math
import contextlib
import numpy as np
import ml_dtypes
import concourse.bass as bass
import concourse.mybir as mybir
from concourse.bass_utils import run_bass_kernel_spmd

F32 = mybir.dt.float32
BF16 = mybir.dt.bfloat16
AF = mybir.ActivationFunctionType
ALU = mybir.AluOpType

ENGS = ["pe", "act", "dve", "pool", "sp"]

D = 1024
NLAT = 2048
NCTX = 256
NTOK = NLAT + NCTX
DFF = 2816
DEPTH = 2
EPS = 1e-6
NCORES = 8
TBS = [(0, 512), (512, 512), (1024, 512), (1536, 512), (2048, 256)]
DEBUG_PARTS = set("ACBD")
WARM_N = 0


class Op:
    __slots__ = ("eng", "fn", "deps", "sig", "sigval", "ch", "inc", "dma", "idx")


class Sched:
    def __init__(self, nc):
        self.nc = nc
        self.ops = {e: [] for e in ENGS}
        self.last_w = {}
        self.readers = {}
        self.ch_eng = {}
        self.pending_bar = {e: [] for e in ENGS}
        self.nops = 0
        self.last_on_ch = {}
        self.dma_ring = {}
        self.DMA_RING = {"sp": 16, "pool": 24}

    def add(self, eng, fn, reads=(), writes=(), ch=None, dma=False):
        op = Op()
        op.eng = eng
        op.fn = fn
        op.sig = bool(dma)
        op.sigval = None
        op.dma = dma
        op.ch = ch if ch is not None else eng
        if dma:
            k = self.dma_ring.get(eng, 0)
            self.dma_ring[eng] = k + 1
            op.ch = f"{eng}_d{k % self.DMA_RING[eng]}"
        op.inc = 16 if dma else 1
        op.idx = self.nops
        self.nops += 1
        if op.ch in self.ch_eng:
            assert self.ch_eng[op.ch] == eng, (op.ch, eng)
        else:
            self.ch_eng[op.ch] = eng
        deps = {}
        for k in reads:
            w = self.last_w.get(k)
            if w is not None:
                deps[w.idx] = (w, True)
        for k in writes:
            w = self.last_w.get(k)
            if w is not None and w.idx not in deps:
                deps[w.idx] = (w, True)
            for r in self.readers.get(k, ()):
                if r.idx not in deps:
                    deps[r.idx] = (r, False)
        need = []
        for (d, raw) in deps.values():
            if d.eng == eng and eng != "pool" and not d.dma and not dma and not raw:
                continue
            if d.eng == eng and eng == "pe" and not d.dma and not dma:
                continue
            need.append(d)
        if dma:
            prev = self.last_on_ch.get(op.ch)
            if prev is not None:
                need.append(prev)
        for d in self.pending_bar[eng]:
            need.append(d)
        self.pending_bar[eng] = []
        for d in need:
            d.sig = True
        op.deps = need
        for k in writes:
            self.last_w[k] = op
            self.readers[k] = []
        for k in reads:
            if k in writes:
                continue
            self.readers.setdefault(k, []).append(op)
        self.ops[eng].append(op)
        self.last_on_ch[op.ch] = op
        return op

    def barrier(self):
        frontier = list(self.last_on_ch.values())
        for e in ENGS:
            self.pending_bar[e] = list(frontier)

    def emit(self):
        nc = self.nc
        chans = list(self.ch_eng.keys())
        cum = {c: 0 for c in chans}
        for e in ENGS:
            for op in self.ops[e]:
                if op.sig:
                    cum[op.ch] += op.inc
                    op.sigval = cum[op.ch]
        with contextlib.ExitStack() as st:
            sems = {c: st.enter_context(nc.semaphore("s_" + c)) for c in chans}
            block = st.enter_context(nc.Block())

            def run(engname, engine):
                waited = {}
                for op in self.ops[engname]:
                    for d in op.deps:
                        if waited.get(d.ch, 0) < d.sigval:
                            engine.wait_ge(sems[d.ch], d.sigval)
                            waited[d.ch] = d.sigval
                    ins = op.fn(engine)
                    if op.sig:
                        assert ins is not None
                        ins.then_inc(sems[op.ch], op.inc)

            @block.tensor
            def _(eng):
                run("pe", eng)

            @block.scalar
            def _(eng):
                run("act", eng)

            @block.vector
            def _(eng):
                run("dve", eng)

            @block.gpsimd
            def _(eng):
                run("pool", eng)

            @block.sync
            def _(eng):
                run("sp", eng)


class Alloc:
    def __init__(self, nc, limit=229344, base=16512):
        self.nc = nc
        self.off = base
        self.limit = limit
        self.n = 0
        self.peak = base

    def mark(self):
        return self.off

    def release(self, m):
        self.off = m

    def t(self, name, shape, dtype):
        esz = 4 if dtype == F32 else 2
        nbytes = int(np.prod(shape[1:])) * esz
        nbytes = (nbytes + 63) // 64 * 64
        assert self.off + nbytes <= self.limit, (name, self.off, nbytes, self.limit)
        self.n += 1
        h = self.nc.alloc_sbuf_tensor_at(f"{name}_{self.n}", list(shape), dtype, offset=self.off)
        self.off += nbytes
        self.peak = max(self.peak, self.off)
        return h


class Ring:
    def __init__(self, items):
        self.items = list(items)
        self.i = 0

    def next(self):
        v = self.items[self.i % len(self.items)]
        self.i += 1
        return v


def build_program(nb=2, depth=DEPTH, stop=None):
    nc = bass.Bass("TRN2", target_bir_lowering=False)
    S = Sched(nc)
    A = Alloc(nc)

    def din(name, shape, dt=F32):
        return nc.dram_tensor(name, list(shape), dt, kind="ExternalInput").ap()

    xT = din("xT", [nb, D, NLAT])
    ctxT = din("ctxT", [nb, D, NCTX])
    cT_d = din("cT", [128, 8, 3])
    w_ada = din("w_ada", [DEPTH, D, 9 * D])
    b_adaT_d = din("b_adaT", [128, DEPTH, 72])
    g_normT_d = din("g_normT", [128, DEPTH, 3, 8])
    g_finalT_d = din("g_finalT", [128, 8])
    w_ff_in = [din("w_ff1_in", [DEPTH, D, 2 * DFF]), din("w_ff2_in", [DEPTH, D, 2 * DFF])]
    w_ff_out = [din("w_ff1_out", [DEPTH, DFF, D]), din("w_ff2_out", [DEPTH, DFF, D])]
    w_in_p = din("w_in_p", [DEPTH, D, 2304])
    w_out = din("w_out", [DEPTH, D, D])
    gqk_d = din("gqk", [128, DEPTH, 2])
    lamw_d = din("lamw", [64, DEPTH, 4, 32])
    gsub_d = din("gsub", [64, DEPTH])
    gv_d = din("gv_bc", [128, DEPTH, 256])
    wsT_d = din("wsT", [DEPTH, 128, 4, 128])
    bs_d = din("bs_tbl", [64, DEPTH, 4, 512])
    wdw_d = din("wdw", [128, DEPTH, 2, 31])
    bdw_d = din("bdw", [128, DEPTH, 2])
    gconv_d = din("gconv", [128, DEPTH, 2])
    rope_d = din("rope", [4, 128, NLAT])
    cmat_d = din("cmat", [6, 128, 128])
    xscr = nc.dram_tensor("xscr", [128, 8, NTOK], BF16, kind="ExternalOutput").ap()
    if stop is None:
        outT = nc.dram_tensor("outT", [nb, D, NLAT], F32, kind="ExternalOutput").ap()
    else:
        dbg = nc.dram_tensor("dbg", [nb, 128, 8, NTOK], F32, kind="ExternalOutput").ap()

    hT = A.t("hT", [128, 8, NTOK], F32)
    cmat = A.t("cmat", [128, 6, 128], BF16)
    modall = A.t("modall", [128, DEPTH, 72, 3], F32)
    gs = A.t("gs", [128, DEPTH, 3, 8, 3], F32)
    hg = A.t("hg", [128, DEPTH, 3, 8, 3], F32)
    b_adaT = A.t("b_adaT", [128, DEPTH, 72], F32)
    g_normT = A.t("g_normT", [128, DEPTH, 3, 8], F32)
    g_finalT = A.t("g_finalT", [128, 8], F32)
    gqk = A.t("gqk", [128, DEPTH, 2], F32)
    gsub = A.t("gsub", [64, DEPTH], F32)
    gsub1m = A.t("gsub1m", [64, DEPTH], F32)
    neglam = A.t("neglam", [64, DEPTH], F32)
    wdw = A.t("wdw", [128, DEPTH, 2, 31], F32)
    bdw = A.t("bdw", [128, DEPTH, 2], F32)
    gconv = A.t("gconv", [128, DEPTH, 2], F32)
    epsc = A.t("epsc", [128, 1], F32)
    scT = A.t("scT", [128, 8, 3], BF16)
    sq = A.t("sq", [128, 8, 512], BF16)
    rt = A.t("rt", [128, 512], F32)
    rstd = A.t("rstd", [128, 512], F32)
    rstdx = A.t("rstdx", [128, 512], F32)
    tmpx = [A.t("tmpx0", [128, 512], F32), A.t("tmpx1", [128, 512], F32)]
    PERSIST = A.mark()

    ps = [nc.alloc_psum_tensor(f"ps{i}", [128, 512], F32) for i in range(8)]
    onesD = cmat[:, 0, :]
    blk64 = cmat[:, 1, :]
    ones256 = cmat[:, 2, :]
    permA = cmat[:, 3, :]
    permC = cmat[:, 4, :]

    cnt = [0]

    def uid():
        cnt[0] += 1
        return cnt[0]

    def PK(b):
        return ("ps", b)

    def dma_sp(out, in_, reads, writes):
        S.add("sp", lambda e: e.dma_start(out=out, in_=in_), reads=reads, writes=writes, ch="dq_sp", dma=True)

    def dma_w(out, in_, reads, writes):
        S.add("pool", lambda e: e.dma_start(out=out, in_=in_), reads=reads, writes=writes, ch="dq_w", dma=True)

    dma_w(cmat[:], cmat_d.rearrange("m p n -> p m n"), [], ["cmat"])
    dma_sp(b_adaT[:], b_adaT_d, [], ["b_adaT"])
    dma_sp(g_normT[:], g_normT_d, [], ["g_normT"])
    dma_sp(g_finalT[:], g_finalT_d, [], ["g_finalT"])
    dma_sp(gqk[:], gqk_d, [], ["gqk"])
    dma_sp(gsub[:], gsub_d, [], ["gsub"])
    dma_sp(wdw[:], wdw_d, [], ["wdw"])
    dma_sp(bdw[:], bdw_d, [], ["bdw"])
    dma_sp(gconv[:], gconv_d, [], ["gconv"])
    S.add("dve", lambda e: e.memset(epsc[:], EPS), writes=["epsc"])

    m0 = A.mark()
    cTs = A.t("cTs", [128, 8, 3], F32)
    lamw = A.t("lamw", [64, DEPTH, 4, 32], F32)
    lamp = A.t("lamp", [64, DEPTH, 2, 32], F32)
    lams = A.t("lams", [64, DEPTH, 2], F32)
    lame = A.t("lame", [64, DEPTH, 2], F32)
    wab = [A.t(f"wab{i}", [128, 8, 512], BF16) for i in range(3)]
    dma_sp(cTs[:], cT_d, [], ["cTs"])
    dma_sp(lamw[:], lamw_d, [], ["lamw"])
    S.add("act", lambda e: e.activation(out=scT[:], in_=cTs[:], func=AF.Silu), reads=["cTs"], writes=["scT"])
    def adaln_steps(l, wabufs, uidx):
        stepsl = []
        for cb in range(18):
            def dm(cb=cb):
                bi = cb % len(wabufs)
                wt = wabufs[bi]
                dma_w(wt[:], w_ada[l][:, cb * 512:(cb + 1) * 512].rearrange("(k p) n -> p k n", p=128), [], [("wab", uidx, bi)])

            def one(cb=cb):
                bi = cb % len(wabufs)
                wt = wabufs[bi]

                def mm(e):
                    ins = None
                    for cc in range(4):
                        chn = cb * 4 + cc
                        for k in range(8):
                            ins = e.matmul(ps[7][:, chn * 3:(chn + 1) * 3], lhsT=wt[:, k, cc * 128:(cc + 1) * 128],
                                           rhs=scT[:, k, :], start=(k == 0), stop=(k == 7))
                    return ins
                S.add("pe", mm, reads=[("wab", uidx, bi), "scT"], writes=[PK(7)])
            stepsl.append((dm, one))

        def fin():
            psm = ps[7][:, 0:216].rearrange("p (c j) -> p c j", j=3)
            for j in range(3):
                S.add("dve", lambda e, j=j: e.tensor_tensor(out=modall[:, l, :, j], in0=psm[:, :, j], in1=b_adaT[:, l, :], op=ALU.add),
                      reads=[PK(7), "b_adaT"], writes=[("modall", l)])
            for s_ in range(3):
                for j in range(3):
                    S.add("dve", lambda e, s_=s_, j=j: e.tensor_scalar(out=gs[:, l, s_, :, j], in0=modall[:, l, (3 * s_ + 1) * 8:(3 * s_ + 2) * 8, j],
                                                                         scalar1=1.0, scalar2=None, op0=ALU.add),
                          reads=[("modall", l)], writes=[("gs", l)])
                    S.add("dve", lambda e, s_=s_, j=j: e.tensor_tensor(out=gs[:, l, s_, :, j], in0=gs[:, l, s_, :, j], in1=g_normT[:, l, s_, :], op=ALU.mult),
                          reads=[("gs", l), "g_normT"], writes=[("gs", l)])
                    S.add("dve", lambda e, s_=s_, j=j: e.tensor_scalar(out=hg[:, l, s_, :, j], in0=modall[:, l, (3 * s_ + 2) * 8:(3 * s_ + 3) * 8, j],
                                                                         scalar1=(1.0 if s_ == 1 else 0.5), scalar2=None, op0=ALU.mult),
                          reads=[("modall", l)], writes=[("hg", l)])
        return stepsl, fin

    def ada_run(stl, k):
        if k + 2 < len(stl):
            stl[k + 2][0]()
        stl[k][1]()

    st0, fin0 = adaln_steps(0, wab, 0)
    st0[0][0]()
    st0[1][0]()
    for k_ in range(len(st0)):
        ada_run(st0, k_)
    fin0()
    for l in range(depth):
        lam_init = 0.8 - 0.6 * math.exp(-0.3 * l)
        for q in range(2):
            S.add("dve", lambda e, l=l, q=q: e.tensor_tensor(out=lamp[:, l, q, :], in0=lamw[:, l, 2 * q, :], in1=lamw[:, l, 2 * q + 1, :], op=ALU.mult),
                  reads=["lamw"], writes=["lamp"])
            S.add("dve", lambda e, l=l, q=q: e.tensor_reduce(out=lams[:, l, q:q + 1], in_=lamp[:, l, q, :], axis=mybir.AxisListType.X, op=ALU.add),
                  reads=["lamp"], writes=["lams"])
        S.add("act", lambda e, l=l: e.activation(out=lame[:, l, :], in_=lams[:, l, :], func=AF.Exp), reads=["lams"], writes=["lame"])
        S.add("dve", lambda e, l=l: e.tensor_tensor(out=neglam[:, l:l + 1], in0=lame[:, l, 1:2], in1=lame[:, l, 0:1], op=ALU.subtract),
              reads=["lame"], writes=["neglam"])
        S.add("dve", lambda e, l=l, li=lam_init: e.tensor_scalar(out=neglam[:, l:l + 1], in0=neglam[:, l:l + 1], scalar1=-li, scalar2=None, op0=ALU.add),
              reads=["neglam"], writes=["neglam"])
        S.add("dve", lambda e, l=l, li=lam_init: e.tensor_scalar(out=gsub1m[:, l:l + 1], in0=gsub[:, l:l + 1], scalar1=(1.0 - li), scalar2=None, op0=ALU.mult),
              reads=["gsub"], writes=["gsub1m"])
    S.barrier()
    A.release(m0)

    def hkeys(tb, cs=range(8)):
        return [("h", c, tb) for c in cs]

    def jmod(bl, tb):
        return 2 if tb == 4 else bl

    def rsqrt_from_psum(pb, n, parts=128, scale=1.0, out=None, okey="rstd"):
        o = rstd if out is None else out
        S.add("act", lambda e: e.activation(out=rt[0:parts, 0:n], in_=ps[pb][0:parts, 0:n], func=AF.Sqrt, bias=epsc[0:parts, 0:1], scale=scale),
              reads=[PK(pb), "epsc"], writes=["rt"])
        S.add("dve", lambda e: e.reciprocal(out=o[0:parts, 0:n], in_=rt[0:parts, 0:n]), reads=["rt"], writes=[okey])

    def make_xn(bl, l, s, tb, dst, dkeys, final=False, pbank=7):
        for part in make_xn_parts(bl, l, s, tb, dst, dkeys, final=final, pbank=pbank):
            part()

    def make_xn_parts(bl, l, s, tb, dst, dkeys, final=False, pbank=7):
        t0, n = TBS[tb]
        j = jmod(bl, tb)

        def part_a():
            _squares()

        def part_b():
            _reduce()

        def part_c():
            _apply()

        def _squares():
          for c in range(8):
            if c in (0, 3, 6):
                S.add("act", lambda e, c=c: e.activation(out=sq[:, c, 0:n], in_=hT[:, c, t0:t0 + n], func=AF.Square),
                      reads=[("h", c, tb)], writes=[("sq", c)])
            elif c in (1, 4, 7):
                S.add("dve", lambda e, c=c: e.tensor_tensor(out=sq[:, c, 0:n], in0=hT[:, c, t0:t0 + n], in1=hT[:, c, t0:t0 + n], op=ALU.mult),
                      reads=[("h", c, tb)], writes=[("sq", c)])
            else:
                S.add("pool", lambda e, c=c: e.tensor_tensor(out=sq[:, c, 0:n], in0=hT[:, c, t0:t0 + n], in1=hT[:, c, t0:t0 + n], op=ALU.mult),
                      reads=[("h", c, tb)], writes=[("sq", c)])

        def _reduce():
            def mm(e):
                ins = None
                for c in range(8):
                    ins = e.matmul(ps[pbank][:, 0:n], lhsT=onesD, rhs=sq[:, c, 0:n], start=(c == 0), stop=(c == 7))
                return ins
            S.add("pe", mm, reads=[("sq", c) for c in range(8)] + ["cmat"], writes=[PK(pbank)])
            rsqrt_from_psum(pbank, n, out=rstdx, okey="rstdx")

        def _apply():
          for c in range(8):
            tx = tmpx[c % 2]
            S.add("pool" if c in (1, 4, 7) else "dve", lambda e, c=c, tx=tx: e.tensor_tensor(out=tx[:, 0:n], in0=hT[:, c, t0:t0 + n], in1=rstdx[:, 0:n], op=ALU.mult),
                  reads=[("h", c, tb), "rstdx"], writes=[("tmpx", c % 2)])
            if final:
                S.add("act", lambda e, c=c, tx=tx: e.activation(out=dst(c), in_=tx[:, 0:n], func=AF.Identity, scale=g_finalT[:, c:c + 1]),
                      reads=[("tmpx", c % 2), "g_finalT"], writes=[dkeys(c)])
            else:
                S.add("act", lambda e, c=c, tx=tx: e.activation(out=dst(c), in_=tx[:, 0:n], func=AF.Identity,
                                                                 scale=gs[:, l, s, c, j:j + 1], bias=modall[:, l, 3 * s * 8 + c, j:j + 1]),
                      reads=[("tmpx", c % 2), ("gs", l), ("modall", l)], writes=[dkeys(c)])
        return [part_a, part_b, part_c]

    def resid_add(bl, l, s, tb, oc, pb, n):
        t0, _ = TBS[tb]
        j = jmod(bl, tb)
        S.add("dve", lambda e: e.scalar_tensor_tensor(out=hT[:, oc, t0:t0 + n], in0=ps[pb][:, 0:n], scalar=hg[:, l, s, oc, j:j + 1],
                                                       in1=hT[:, oc, t0:t0 + n], op0=ALU.mult, op1=ALU.add),
              reads=[PK(pb), ("hg", l), ("h", oc, tb)], writes=[("h", oc, tb)])

    def dump_and_end(bl):
        dma_sp(dbg[bl], hT[:], [("h", c, tb) for c in range(8) for tb in range(5)], ["dbg"])

    def ffn(bl, l, which, blocks, ada_next=None):
        s = 0 if which == 0 else 2
        m = A.mark()
        extra, extra_fin = [], None
        if ada_next is not None:
            wabx = [A.t(f"wabx{i}", [128, 8, 512], BF16) for i in range(3)]
            extra, extra_fin = adaln_steps(ada_next, wabx, uid())
            extra[0][0]()
            extra[1][0]()
        ek = [0]
        xn = A.t("xn", [128, 8, NTOK], BF16)
        wa = [A.t(f"wa{i}", [128, 8, 256], BF16) for i in range(2)]
        wb = [A.t(f"wb{i}", [128, 8, 256], BF16) for i in range(2)]
        wo = [A.t(f"wo{i}", [128, 2, 1024], BF16) for i in range(2)]
        mid = [A.t(f"mid{i}", [128, 2, NTOK], BF16) for i in range(2)]
        sa = [A.t(f"sa{i}", [128, 512], F32) for i in range(2)]
        u = uid()
        def mkp(tb):
            t0, n = TBS[tb]
            return make_xn_parts(bl, l, s, tb, lambda c, t0=t0, n=n: xn[:, c, t0:t0 + n], lambda c, tb=tb: ("xn", u, c, tb))
        win = w_ff_in[which][l]
        wout = w_ff_out[which][l]
        NG = DFF // 256
        ring1 = Ring([0, 1, 2, 3])
        ring2 = Ring([4, 5, 6])
        sar = Ring([0, 1])

        def load(g):
            bi = g % 2
            dma_w(wa[bi][:], win[:, g * 256:(g + 1) * 256].rearrange("(k p) n -> p k n", p=128), [], [("wa", bi)])
            dma_w(wb[bi][:], win[:, DFF + g * 256:DFF + (g + 1) * 256].rearrange("(k p) n -> p k n", p=128), [], [("wb", bi)])
            dma_w(wo[bi][:], wout[g * 256:(g + 1) * 256, :].rearrange("(j p) n -> p j n", p=128), [], [("wo", bi)])

        def phase1(g, blks=None):
            bi = g % 2
            for tb in (blocks if blks is None else blks):
                t0, n = TBS[tb]
                for jj in range(2):
                    pa = ring1.next()
                    pb = ring1.next()

                    def mm(e, w, pbk, jj=jj, t0=t0, n=n):
                        ins = None
                        for k in range(8):
                            ins = e.matmul(ps[pbk][:, 0:n], lhsT=w[:, k, jj * 128:(jj + 1) * 128], rhs=xn[:, k, t0:t0 + n],
                                           start=(k == 0), stop=(k == 7))
                        return ins
                    xk = [("xn", u, c, tb) for c in range(8)]
                    S.add("pe", lambda e, w=wa[bi], pbk=pa, mm=mm: mm(e, w, pbk), reads=xk + [("wa", bi)], writes=[PK(pa)])
                    S.add("pe", lambda e, w=wb[bi], pbk=pb, mm=mm: mm(e, w, pbk), reads=xk + [("wb", bi)], writes=[PK(pb)])
                    si = sar.next()
                    S.add("act", lambda e, pa=pa, si=si, n=n: e.activation(out=sa[si][:, 0:n], in_=ps[pa][:, 0:n], func=AF.Silu),
                          reads=[PK(pa)], writes=[("sa", si)])
                    S.add("dve", lambda e, pb=pb, si=si, n=n, jj=jj, t0=t0: e.tensor_tensor(out=mid[bi][:, jj, t0:t0 + n], in0=sa[si][:, 0:n],
                                                                                         in1=ps[pb][:, 0:n], op=ALU.mult),
                          reads=[PK(pb), ("sa", si)], writes=[("mid", bi, jj, tb)])

        def phase2(g):
            bi = g % 2
            for tb in blocks:
                t0, n = TBS[tb]
                for oc in range(8):
                    po = ring2.next()

                    def mm(e, po=po, oc=oc, t0=t0, n=n):
                        ins = None
                        for jj in range(2):
                            ins = e.matmul(ps[po][:, 0:n], lhsT=wo[bi][:, jj, oc * 128:(oc + 1) * 128], rhs=mid[bi][:, jj, t0:t0 + n],
                                           start=(jj == 0), stop=(jj == 1))
                        return ins
                    S.add("pe", mm, reads=[("mid", bi, 0, tb), ("mid", bi, 1, tb), ("wo", bi)], writes=[PK(po)])
                    resid_add(bl, l, s, tb, oc, po, n)

        load(0)
        load(1)
        for tb in blocks[:2]:
            for p_ in mkp(tb):
                p_()
        for i_, tb in enumerate(blocks):
            prts = mkp(blocks[i_ + 2]) if i_ + 2 < len(blocks) else [lambda: None] * 3
            prts[0]()
            phase1(0, [tb])
            prts[1]()
            phase1(1, [tb])
            prts[2]()
        for g in range(1, NG):
            if g > 1:
                phase1(g)
            phase2(g - 1)
            if g + 1 < NG:
                load(g + 1)
            for _ in range(2):
                if ek[0] < len(extra):
                    ada_run(extra, ek[0])
                    ek[0] += 1
        phase2(NG - 1)
        while ek[0] < len(extra):
            ada_run(extra, ek[0])
            ek[0] += 1
        if extra_fin is not None:
            extra_fin()
        S.barrier()
        A.release(m)

    def mixer(bl, l):
        last = (l == DEPTH - 1)
        blocks = [0, 1, 2, 3, 4]
        m = A.mark()
        qT = A.t("qT", [128, 4, NTOK], BF16)
        kT = A.t("kT", [128, 3, NTOK], BF16)
        vaug = A.t("vaug", [128, 18, 6, 128], BF16)
        m1 = A.mark()
        xnb = [A.t(f"xnb{i}", [128, 8, 512], BF16) for i in range(2)]
        wq = A.t("wq", [128, 8, 512], BF16)
        wk = A.t("wk", [128, 8, 384], BF16)
        wv = A.t("wv", [128, 8, 384], BF16)
        ropet = A.t("ropet", [128, 4, 512], F32)
        sqb = [A.t(f"sqb{i}", [128, 512], BF16) for i in range(2)]
        qn = [A.t(f"qn{i}", [128, 512], BF16) for i in range(3)]
        t1 = A.t("t1", [128, 512], F32)
        t2 = A.t("t2", [128, 512], F32)
        u = uid()
        S.add("pool", lambda e: e.memset(vaug[:], 1.0), writes=[("vaug", kc) for kc in range(18)])
        W = w_in_p[l]
        prj = Ring([0, 1, 2])
        aux = Ring([3, 4])
        vring = Ring([5, 6])
        qnr = Ring([0, 1, 2])
        def pass1_pre_parts(tb):
            t0, n = TBS[tb]
            xb = xnb[tb % 2]
            xkeys = [("xnb", tb % 2, c) for c in range(8)]
            pa_, pb_, pc_ = make_xn_parts(bl, l, 1, tb, lambda c, xb=xb, n=n: xb[:, c, 0:n], lambda c, tb=tb: ("xnb", tb % 2, c))

            def pc2():
                pc_()
                dma_sp(xscr[:, :, t0:t0 + n], xb[:, :, 0:n], xkeys, [("xscr", tb)])
            return [pa_, pb_, pc2]

        def pass1_pre(tb):
            for p_ in pass1_pre_parts(tb):
                p_()

        sqr = Ring([0, 1])

        def pass1_items(tb):
            t0, n = TBS[tb]
            xb = xnb[tb % 2]
            xkeys = [("xnb", tb % 2, c) for c in range(8)]
            norope = (tb == 4)
            items = []
            for ci in range(7):
                if ci < 4:
                    wt, wkey, co = wq, "wq", ci * 128
                    dest = qT[:, ci, t0:t0 + n]
                    dkey = ("qT", ci, tb)
                else:
                    wt, wkey, co = wk, "wk", (ci - 4) * 128
                    dest = kT[:, ci - 4, t0:t0 + n]
                    dkey = ("kT", ci - 4, tb)
                isA = ci in (0, 1, 4)
                st = {}

                def P(st=st, wt=wt, wkey=wkey, co=co):
                    pb = prj.next()
                    st["pb"] = pb

                    def mm(e):
                        ins = None
                        for k in range(8):
                            ins = e.matmul(ps[pb][:, 0:n], lhsT=wt[:, k, co:co + 128], rhs=xb[:, k, 0:n], start=(k == 0), stop=(k == 7))
                        return ins
                    S.add("pe", mm, reads=xkeys + [wkey], writes=[PK(pb)])

                def N(st=st):
                    pb = st["pb"]
                    si = sqr.next()
                    st["si"] = si
                    S.add("act", lambda e: e.activation(out=sqb[si][:, 0:n], in_=ps[pb][:, 0:n], func=AF.Square),
                          reads=[PK(pb)], writes=[("sqb", si)])

                def M1(st=st, isA=isA, ci=ci, dest=dest, dkey=dkey):
                    pb = st["pb"]
                    qi = qnr.next()
                    st["qi"] = qi
                    qdst = dest if norope else qn[qi][:, 0:n]
                    qkey = dkey if norope else ("qn", qi)
                    if isA:
                        si = st["si"]
                        pa = aux.next()
                        S.add("pe", lambda e: e.matmul(ps[pa][:, 0:n], lhsT=blk64, rhs=sqb[si][:, 0:n], start=True, stop=True),
                              reads=[("sqb", si), "cmat"], writes=[PK(pa)])
                        rsqrt_from_psum(pa, n)
                        gi = 0 if ci < 4 else 1
                        S.add("dve", lambda e: e.scalar_tensor_tensor(out=qdst, in0=ps[pb][:, 0:n], scalar=gqk[:, l, gi:gi + 1],
                                                                        in1=rstd[:, 0:n], op0=ALU.mult, op1=ALU.mult),
                              reads=[PK(pb), "rstd", "gqk"], writes=[qkey])
                    else:
                        S.add("act", lambda e: e.activation(out=qdst, in_=ps[pb][:, 0:n], func=AF.Copy),
                              reads=[PK(pb)], writes=[qkey])

                def M2(st=st, isA=isA, ci=ci, dest=dest, dkey=dkey):
                    if ci == 0:
                        dma_sp(ropet[:], rope_d[:, :, t0:t0 + n].rearrange("m p n -> p m n"), [], ["ropet"])
                    qi = st["qi"]
                    pr = aux.next()
                    pm = permA if isA else permC
                    ti = 0 if isA else 2
                    S.add("pe", lambda e: e.matmul(ps[pr][:, 0:n], lhsT=pm, rhs=qn[qi][:, 0:n], start=True, stop=True),
                          reads=[("qn", qi), "cmat"], writes=[PK(pr)])
                    S.add("dve", lambda e: e.tensor_tensor(out=t1[:, 0:n], in0=qn[qi][:, 0:n], in1=ropet[:, ti, 0:n], op=ALU.mult),
                          reads=[("qn", qi), "ropet"], writes=["t1"])
                    S.add("dve", lambda e: e.tensor_tensor(out=t2[:, 0:n], in0=ps[pr][:, 0:n], in1=ropet[:, ti + 1, 0:n], op=ALU.mult),
                          reads=[PK(pr), "ropet"], writes=["t2"])
                    S.add("pool", lambda e: e.tensor_tensor(out=dest, in0=t1[:, 0:n], in1=t2[:, 0:n], op=ALU.add),
                          reads=["t1", "t2"], writes=[dkey])
                items.append([P, N if isA else None, M1, None if norope else M2])

            def PV():
                for sb in range(n // 128):
                    kc = (t0 // 128) + sb
                    pv = vring.next()

                    def mmv(e, pv=pv, sb=sb):
                        ins = None
                        for k in range(8):
                            ins = e.matmul(ps[pv][:, 0:384], lhsT=xb[:, k, sb * 128:(sb + 1) * 128], rhs=wv[:, k, :], start=(k == 0), stop=(k == 7))
                        return ins
                    S.add("pe", mmv, reads=xkeys + ["wv"], writes=[PK(pv)])
                    S.add("act", lambda e, pv=pv, kc=kc: e.activation(out=vaug[:, kc, :, 0:64], in_=ps[pv][:, 0:384].rearrange("p (h d) -> p h d", d=64), func=AF.Copy),
                          reads=[PK(pv)], writes=[("vaug", kc)])
            items.append([PV, None, None, None])
            return items

        dma_w(wq[:], W[:, 0:512].rearrange("(k p) n -> p k n", p=128), [], ["wq"])
        dma_w(wk[:], W[:, 512:896].rearrange("(k p) n -> p k n", p=128), [], ["wk"])
        dma_w(wv[:], W[:, 896:1280].rearrange("(k p) n -> p k n", p=128), [], ["wv"])
        pass1_pre(blocks[0])
        flat = []
        for i_, tb in enumerate(blocks):
            its = pass1_items(tb)
            if i_ + 1 < len(blocks):
                prts = pass1_pre_parts(blocks[i_ + 1])
                for ci_, prt in zip((0, 3, 5), prts):
                    its[ci_][0] = (lambda p0_=its[ci_][0], prt=prt: (prt(), p0_()))
            flat.extend(its)
        NF = len(flat)
        for i_ in range(NF + 3):
            for stg in range(4):
                j_ = i_ - stg
                if 0 <= j_ < NF and flat[j_][stg] is not None:
                    flat[j_][stg]()
        S.barrier()
        A.release(m1)
        if stop == f"p1_{l}":
            pass

        woh = A.t("woh", [128, 4, 1024], BF16)
        cat = [A.t(f"cat{i}", [128, 4, 512], BF16) for i in range(2)]
        lnb = A.t("lnb", [64, 512], F32)
        pT = [A.t(f"pT{i}", [128, 512], BF16) for i in range(3)]
        rden = [A.t(f"rden{i}", [64, 512], F32) for i in range(2)]
        tt0 = A.t("tt0", [64, 512], F32)
        tt1 = A.t("tt1", [64, 512], F32)
        od_ = A.t("od_", [64, 512], F32)
        odn = A.t("odn", [64, 512], F32)
        sq64 = A.t("sq64", [64, 512], BF16)
        qpad = [A.t(f"qpad{i}", [128, 512], BF16) for i in range(3)]
        qpr = Ring([0, 1, 2])
        dma_w(woh[:], w_out[l][0:512, :].rearrange("(c p) n -> p c n", p=128), [], ["woh"])
        sring = Ring([0, 1, 6])
        oring = Ring([2, 3, 4, 5])
        ptr = Ring([0, 1, 2])
        rdr = Ring([0, 1])
        qblocks = [0, 1, 2, 3] if last else [0, 1, 2, 3, 4]
        steps = []

        def qblock(qi_, tb):
            t0, n = TBS[tb]
            kcs = list(range(18)) if tb != 4 else [16, 17]
            ct = cat[qi_ % 2]
            ci_ = qi_ % 2

            def warm():
                wbk = sring.next()

                def burst(e):
                    ins = None
                    for _ in range(WARM_N):
                        ins = e.matmul(ps[wbk][:, 0:512], lhsT=woh[:, 0, 0:128], rhs=woh[:, 1, 0:512], start=True, stop=True)
                    return ins
                S.add("pe", burst, reads=["woh"], writes=[PK(wbk)])
            if WARM_N > 0:
                steps.append((warm, lambda: None, lambda: None, None))

            hm_list = []

            def prep_qpad(hm):
                qi = qpr.next()
                hm["qp"] = qi
                r0, r1 = hm["krows"]
                S.add("pool", lambda e: e.memset(qpad[qi][:, 0:n], 0.0), writes=[("qpad", qi)])
                S.add("pool", lambda e: e.tensor_copy(out=qpad[qi][r0:r1, 0:n], in_=qT[r0:r1, hm["qchunk"], t0:t0 + n]),
                      reads=[("qT", hm["qchunk"], tb)], writes=[("qpad", qi)])

            def head_pass(krows, kchunk, qchunk, vslot, scale, po, tp, post):
                hm = dict(krows=krows, qchunk=qchunk)
                hm_list.append(hm)
                myidx = len(hm_list) - 1
                for ii, kc in enumerate(kcs):
                    st = {}
                    ktb = kc // 4 if kc < 16 else 4

                    def qk(st=st, kc=kc, ktb=ktb, ii=ii):
                        if ii == 0:
                            if "qp" not in hm:
                                prep_qpad(hm)
                            if myidx + 1 < len(hm_list) and "qp" not in hm_list[myidx + 1]:
                                prep_qpad(hm_list[myidx + 1])
                        sb_ = sring.next()
                        st["sb"] = sb_
                        qi = hm["qp"]

                        def mms(e):
                            return e.matmul(ps[sb_][:, 0:n], lhsT=kT[:, kchunk, kc * 128:(kc + 1) * 128],
                                            rhs=qpad[qi][:, 0:n], start=True, stop=True)
                        S.add("pe", mms, reads=[("kT", kchunk, ktb), ("qpad", qi)], writes=[PK(sb_)])

                    def ex(st=st):
                        sb_ = st["sb"]
                        pi = ptr.next()
                        st["pi"] = pi
                        S.add("act", lambda e: e.activation(out=pT[pi][:, 0:n], in_=ps[sb_][:, 0:n], func=AF.Exp, scale=scale),
                              reads=[PK(sb_)], writes=[("pT", pi)])

                    def pv(st=st, kc=kc, ii=ii):
                        pi = st["pi"]
                        S.add("pe", lambda e: e.matmul(ps[po][:, 0:n], lhsT=vaug[:, kc, vslot, :], rhs=pT[pi][:, 0:n],
                                                       start=(ii == 0), stop=(ii == len(kcs) - 1)),
                              reads=[("pT", pi), ("vaug", kc)], writes=[PK(po)])
                    steps.append((qk, ex, pv, post if ii == len(kcs) - 1 else None))

            for h in range(4):
                r0 = 64 * (h // 2)
                po = oring.next()

                def postA(po=po, h=h):
                    ri = rdr.next()
                    S.add("dve", lambda e: e.reciprocal(out=rden[ri][:, 0:n], in_=ps[po][64:128, 0:n]), reads=[PK(po)], writes=[("rden", ri)])
                    p0 = 64 * (h % 2)
                    S.add("dve", lambda e: e.tensor_tensor(out=ct[p0:p0 + 64, h // 2, 0:n], in0=ps[po][0:64, 0:n], in1=rden[ri][:, 0:n], op=ALU.mult),
                          reads=[PK(po), ("rden", ri)], writes=[("cat", ci_, h)])
                    return []
                head_pass((r0, r0 + 64), 0, h % 2, h // 2, 0.125, po, None, postA)
            for h in range(4):
                base = 64 * (h % 2)
                pos = [oring.next(), oring.next()]

                def postC(pos=pos, h=h):
                    ri0 = rdr.next()
                    ri1 = rdr.next()
                    p0 = 64 * (h % 2)
                    S.add("dve", lambda e: e.reciprocal(out=rden[ri0][:, 0:n], in_=ps[pos[0]][64:128, 0:n]), reads=[PK(pos[0])], writes=[("rden", ri0)])
                    S.add("dve", lambda e: e.tensor_tensor(out=tt0[:, 0:n], in0=ps[pos[0]][0:64, 0:n], in1=rden[ri0][:, 0:n], op=ALU.mult),
                          reads=[PK(pos[0]), ("rden", ri0)], writes=["tt0"])
                    S.add("dve", lambda e: e.reciprocal(out=rden[ri1][:, 0:n], in_=ps[pos[1]][64:128, 0:n]), reads=[PK(pos[1])], writes=[("rden", ri1)])
                    S.add("dve", lambda e: e.tensor_tensor(out=tt1[:, 0:n], in0=ps[pos[1]][0:64, 0:n], in1=rden[ri1][:, 0:n], op=ALU.mult),
                          reads=[PK(pos[1]), ("rden", ri1)], writes=["tt1"])
                    S.add("dve", lambda e: e.scalar_tensor_tensor(out=od_[:, 0:n], in0=tt1[:, 0:n], scalar=neglam[:, l:l + 1], in1=tt0[:, 0:n],
                                                                    op0=ALU.mult, op1=ALU.add),
                          reads=["tt0", "tt1", "neglam"], writes=["od_"])

                    def st1():
                        S.add("act", lambda e: e.activation(out=sq64[:, 0:n], in_=od_[:, 0:n], func=AF.Square), reads=["od_"], writes=["sq64"])

                    def st2():
                        S.add("pe", lambda e: e.matmul(ps[7][0:64, 0:n], lhsT=blk64[0:64, 0:64], rhs=sq64[:, 0:n], start=True, stop=True),
                              reads=["sq64", "cmat"], writes=[PK(7)])

                    def st3():
                        S.add("act", lambda e: e.activation(out=lnb[:, 0:n], in_=ps[7][0:64, 0:n], func=AF.Ln, bias=epsc[0:64, 0:1], scale=1.0),
                              reads=[PK(7), "epsc"], writes=["lnb"])
                        S.add("act", lambda e: e.activation(out=odn[:, 0:n], in_=lnb[:, 0:n], func=AF.Exp, scale=-0.5), reads=["lnb"], writes=["odn"])

                    def st4():
                        S.add("dve", lambda e: e.scalar_tensor_tensor(out=ct[p0:p0 + 64, 2 + h // 2, 0:n], in0=od_[:, 0:n], scalar=gsub1m[:, l:l + 1],
                                                                        in1=odn[:, 0:n], op0=ALU.mult, op1=ALU.mult),
                              reads=["od_", "odn", "gsub1m"], writes=[("cat", ci_, 4 + h)])
                    tasks = [(20, st1), (23, st2), (26, st3), (30, st4)]
                    if h == 3:
                        def outproj():
                            for oc in range(8):
                                pb = 7

                                def mmo(e, oc=oc, pb=pb):
                                    ins = None
                                    for hh in range(4):
                                        ins = e.matmul(ps[pb][:, 0:n], lhsT=woh[:, hh, oc * 128:(oc + 1) * 128], rhs=ct[:, hh, 0:n],
                                                       start=(hh == 0), stop=(hh == 3))
                                    return ins
                                S.add("pe", mmo, reads=[("cat", ci_, hh) for hh in range(8)] + ["woh"], writes=[PK(pb)])
                                resid_add(bl, l, 1, tb, oc, pb, n)
                        tasks.append((34, outproj))
                    return tasks
                postC.is_c = True
                for c in range(2):
                    r0 = base + 32 * c
                    head_pass((r0, r0 + 32), 1 + h // 2, 2 + h // 2, 2 + h, 32 ** -0.5, pos[c], (r0, 0), postC if c == 1 else None)

        for qi_, tb in enumerate(qblocks):
            qblock(qi_, tb)
        LA = 2
        deferred = []
        NS = len(steps)
        for i in range(min(LA, NS)):
            steps[i][0]()
        for i in range(NS):
            if i + LA < NS:
                steps[i + LA][0]()
            steps[i][1]()
            steps[i][2]()
            if steps[i][3] is not None:
                if getattr(steps[i][3], "is_c", False):
                    for d in sorted(deferred, key=lambda d: d[0]):
                        d[1]()
                    deferred = []
                for (dl, fn) in steps[i][3]():
                    deferred.append((i + dl, fn))
            ready = [d for d in deferred if d[0] <= i]
            deferred = [d for d in deferred if d[0] > i]
            for d in ready:
                d[1]()
        for d in sorted(deferred, key=lambda d: d[0]):
            d[1]()
        S.barrier()
        A.release(m)

        m2 = A.mark()
        blocks2 = [0, 1, 2, 3] if last else [0, 1, 2, 3, 4]
        xnb = [A.t(f"xnb{i}", [128, 8, 512], BF16) for i in range(2)]
        wu = A.t("wu", [128, 8, 256], BF16)
        wvg = A.t("wvg", [128, 8, 256], BF16)
        wgl = A.t("wgl", [128, 8, 512], BF16)
        wob = A.t("wob", [128, 2, 1024], BF16)
        wod = A.t("wod", [128, 2, 1024], BF16)
        wsT = A.t("wsT", [128, 4, 128], BF16)
        gvb = A.t("gvb", [128, 256], F32)
        bst = A.t("bst", [64, 4, 512], F32)
        diag = A.t("diag", [128, 2, 31, 128], BF16)
        identF = A.t("identF", [128, 128], F32)
        dma_sp(identF[:], cmat_d[5], [], ["identF"])
        ypl = A.t("ypl", [128, 2, NLAT + 30], BF16)
        ypc = A.t("ypc", [128, 2, NCTX + 30], BF16)
        ug = A.t("ug", [64, 4, 512], F32)
        vge = A.t("vge", [128, 4, 256], F32)
        junk = A.t("junk", [128, 256], BF16)
        ssum = A.t("ssum", [128, 8], F32)
        vn = A.t("vn", [128, 4, 256], BF16)
        tmb = [A.t("tmb0", [64, 512], F32), A.t("tmb1", [64, 512], F32)]
        catb = A.t("catb", [128, 2, 512], BF16)
        sg = A.t("sg", [128, 512], F32)
        zz = A.t("zz", [128, 2, 512], F32)
        sqz = A.t("sqz", [128, 2, 512], BF16)
        odd = A.t("odd", [128, 2, 512], BF16)
        dma_w(wob[:], w_out[l][512:768, :].rearrange("(c p) n -> p c n", p=128), [], ["wob"])
        dma_w(wod[:], w_out[l][768:1024, :].rearrange("(c p) n -> p c n", p=128), [], ["wod"])
        dma_w(wsT[:], wsT_d[l], [], ["wsT"])
        dma_sp(gvb[:], gv_d[:, l, :], [], ["gvb"])
        dma_sp(bst[:], bs_d[:, l, :, :], [], ["bst"])
        for c in range(2):
            for k in range(31):
                S.add("dve", lambda e, c=c, k=k: e.tensor_scalar(out=diag[:, c, k, :], in0=identF[:], scalar1=wdw[:, l, c, k:k + 1], scalar2=None, op0=ALU.mult),
                      reads=["identF", "wdw"], writes=[("diag", c)])
        S.add("pool", lambda e: e.memset(ypl[:], 0.0), writes=[("ypl", c, tb) for c in range(2) for tb in range(4)] + ["yplpad"])
        S.add("pool", lambda e: e.memset(ypc[:], 0.0), writes=[("ypc", c) for c in range(2)])
        pr2 = Ring([0, 1, 2, 3])
        outr = Ring([4])
        pr2b = Ring([5, 6])
        outrb = Ring([7])
        def pass2a_pre(tb):
            t0, n = TBS[tb]
            xb = xnb[tb % 2]
            xkeys = [("xnb", tb % 2, c) for c in range(8)]
            dma_sp(xb[:, :, 0:n], xscr[:, :, t0:t0 + n], [("xscr", tb)], xkeys)

        def pass2a(tb):
            t0, n = TBS[tb]
            xb = xnb[tb % 2]
            xkeys = [("xnb", tb % 2, c) for c in range(8)]
            nsb = n // 128
            for g in range(4):
                pb = pr2.next()

                def mmu(e, pb=pb, g=g):
                    ins = None
                    for k in range(8):
                        ins = e.matmul(ps[pb][0:64, 0:n], lhsT=wu[:, k, g * 64:(g + 1) * 64], rhs=xb[:, k, 0:n], start=(k == 0), stop=(k == 7))
                    return ins
                S.add("pe", mmu, reads=xkeys + ["wu"], writes=[PK(pb)])
                S.add("act", lambda e, pb=pb, g=g: e.activation(out=ug[:, g, 0:n], in_=ps[pb][0:64, 0:n], func=AF.Gelu_apprx_tanh),
                      reads=[PK(pb)], writes=[("ug", g)])
            S.add("dve", lambda e: e.memset(ssum[:, 0:4], 0.0), writes=[("ssum", sb) for sb in range(4)])
            for sb in range(nsb):
                pb = pr2.next()

                def mmv(e, pb=pb, sb=sb):
                    ins = None
                    for k in range(8):
                        ins = e.matmul(ps[pb][:, 0:256], lhsT=xb[:, k, sb * 128:(sb + 1) * 128], rhs=wvg[:, k, :], start=(k == 0), stop=(k == 7))
                    return ins
                S.add("pe", mmv, reads=xkeys + ["wvg"], writes=[PK(pb)])
                S.add("act", lambda e, pb=pb, sb=sb: e.activation(out=vge[:, sb, :], in_=ps[pb][:, 0:256], func=AF.Gelu_apprx_tanh),
                      reads=[PK(pb)], writes=[("vge", sb)])
                S.add("act", lambda e, sb=sb: e.activation(out=junk[:], in_=vge[:, sb, :], func=AF.Square, accum_out=ssum[:, sb:sb + 1]),
                      reads=[("vge", sb), ("ssum", sb)], writes=["junk", ("ssum", sb)])
            for c in range(2):
                pa = pr2.next()
                pg = pr2.next()

                def mmg(e, pbk, co):
                    ins = None
                    for k in range(8):
                        ins = e.matmul(ps[pbk][:, 0:n], lhsT=wgl[:, k, co:co + 128], rhs=xb[:, k, 0:n], start=(k == 0), stop=(k == 7))
                    return ins
                S.add("pe", lambda e, pa=pa, c=c, mmg=mmg: mmg(e, pa, c * 128), reads=xkeys + ["wgl"], writes=[PK(pa)])
                S.add("pe", lambda e, pg=pg, c=c, mmg=mmg: mmg(e, pg, 256 + c * 128), reads=xkeys + ["wgl"], writes=[PK(pg)])
                S.add("act", lambda e, pg=pg: e.activation(out=sg[:, 0:n], in_=ps[pg][:, 0:n], func=AF.Sigmoid), reads=[PK(pg)], writes=["sg"])
                if tb != 4:
                    S.add("dve", lambda e, pa=pa, c=c: e.tensor_tensor(out=ypl[:, c, 15 + t0:15 + t0 + n], in0=ps[pa][:, 0:n], in1=sg[:, 0:n], op=ALU.mult),
                          reads=[PK(pa), "sg", "yplpad"], writes=[("ypl", c, tb)])
                else:
                    S.add("dve", lambda e, pa=pa, c=c: e.tensor_tensor(out=ypc[:, c, 15:15 + n], in0=ps[pa][:, 0:n], in1=sg[:, 0:n], op=ALU.mult),
                          reads=[PK(pa), "sg"], writes=[("ypc", c)])
            S.add("act", lambda e: e.activation(out=ssum[:, 4:4 + nsb], in_=ssum[:, 0:nsb], func=AF.Sqrt, bias=epsc[:, 0:1], scale=1.0 / 256.0),
                  reads=[("ssum", sb) for sb in range(nsb)] + ["epsc"], writes=["ssr"])
            S.add("dve", lambda e: e.reciprocal(out=ssum[:, 4:4 + nsb], in_=ssum[:, 4:4 + nsb]), reads=["ssr"], writes=["ssr"])
            for sb in range(nsb):
                S.add("dve", lambda e, sb=sb: e.scalar_tensor_tensor(out=vn[:, sb, :], in0=vge[:, sb, :], scalar=ssum[:, 4 + sb:5 + sb], in1=gvb[:],
                                                                       op0=ALU.mult, op1=ALU.mult),
                      reads=[("vge", sb), "ssr", "gvb"], writes=[("vn", sb)])
            for g in range(4):
                pb = pr2.next()
                ti = g % 2
                p0 = 64 * (g % 2)

                def mmm(e, pb=pb, g=g):
                    ins = None
                    for sb in range(nsb):
                        ins = e.matmul(ps[pb][0:64, sb * 128:(sb + 1) * 128], lhsT=vn[:, sb, g * 64:(g + 1) * 64], rhs=wsT[:, g, :], start=True, stop=True)
                    return ins
                S.add("pe", mmm, reads=[("vn", sb) for sb in range(nsb)] + ["wsT"], writes=[PK(pb)])
                S.add("dve", lambda e, pb=pb, g=g, ti=ti: e.tensor_tensor(out=tmb[ti][:, 0:n], in0=ps[pb][0:64, 0:n], in1=bst[:, g, 0:n], op=ALU.add),
                      reads=[PK(pb), "bst"], writes=[("tmb", ti)])
                S.add("dve", lambda e, g=g, ti=ti, p0=p0: e.tensor_tensor(out=catb[p0:p0 + 64, g // 2, 0:n], in0=ug[:, g, 0:n], in1=tmb[ti][:, 0:n], op=ALU.mult),
                      reads=[("tmb", ti), ("ug", g)], writes=[("catb", g)])
            for oc in range(8):
                po = outr.next()

                def mmo(e, po=po, oc=oc):
                    ins = None
                    for cc in range(2):
                        ins = e.matmul(ps[po][:, 0:n], lhsT=wob[:, cc, oc * 128:(oc + 1) * 128], rhs=catb[:, cc, 0:n], start=(cc == 0), stop=(cc == 1))
                    return ins
                S.add("pe", mmo, reads=[("catb", g) for g in range(4)] + ["wob"], writes=[PK(po)])
                if "B" in DEBUG_PARTS:
                    resid_add(bl, l, 1, tb, oc, po, n)

        dma_w(wu[:], W[:, 1280:1536].rearrange("(k p) n -> p k n", p=128), [], ["wu"])
        dma_w(wvg[:], W[:, 1536:1792].rearrange("(k p) n -> p k n", p=128), [], ["wvg"])
        dma_w(wgl[:], W[:, 1792:2304].rearrange("(k p) n -> p k n", p=128), [], ["wgl"])

        def pass2b(tb):
            t0, n = TBS[tb]
            for c in range(2):
                pz = pr2b.next()
                if tb != 4:
                    yk = [("ypl", c, t) for t in range(max(0, tb - 1), min(3, tb + 1) + 1)] + ["yplpad"]
                    ysrc = lambda k, c=c, t0=t0, n=n: ypl[:, c, t0 + k:t0 + k + n]
                else:
                    yk = [("ypc", c)]
                    ysrc = lambda k, c=c, n=n: ypc[:, c, k:k + n]

                def mmc(e, pz=pz, c=c, ysrc=ysrc, n=n):
                    ins = None
                    for k in range(31):
                        ins = e.matmul(ps[pz][:, 0:n], lhsT=diag[:, c, k, :], rhs=ysrc(k), start=(k == 0), stop=(k == 30))
                    return ins
                S.add("pe", mmc, reads=yk + [("diag", c)], writes=[PK(pz)])
                S.add("act", lambda e, pz=pz, c=c, n=n: e.activation(out=zz[:, c, 0:n], in_=ps[pz][:, 0:n], func=AF.Identity, bias=bdw[:, l, c:c + 1]),
                      reads=[PK(pz), "bdw"], writes=[("zz", c)])
                S.add("act", lambda e, c=c, n=n: e.activation(out=sqz[:, c, 0:n], in_=zz[:, c, 0:n], func=AF.Square), reads=[("zz", c)], writes=[("sqz", c)])
            pn = pr2b.next()

            def mmn(e, pn=pn, n=n):
                ins = None
                for c in range(2):
                    ins = e.matmul(ps[pn][:, 0:n], lhsT=ones256, rhs=sqz[:, c, 0:n], start=(c == 0), stop=(c == 1))
                return ins
            S.add("pe", mmn, reads=[("sqz", 0), ("sqz", 1), "cmat"], writes=[PK(pn)])
            rsqrt_from_psum(pn, n)
            for c in range(2):
                tx = tmpx[c]
                S.add("pool", lambda e, c=c, tx=tx, n=n: e.tensor_tensor(out=tx[:, 0:n], in0=zz[:, c, 0:n], in1=rstd[:, 0:n], op=ALU.mult),
                      reads=[("zz", c), "rstd"], writes=[("tmpx", c)])
                S.add("act", lambda e, c=c, tx=tx, n=n: e.activation(out=odd[:, c, 0:n], in_=tx[:, 0:n], func=AF.Silu, scale=gconv[:, l, c:c + 1]),
                      reads=[("tmpx", c), "gconv"], writes=[("odd", c)])
            for oc in range(8):
                po = outrb.next()

                def mmo2(e, po=po, oc=oc, n=n):
                    ins = None
                    for c in range(2):
                        ins = e.matmul(ps[po][:, 0:n], lhsT=wod[:, c, oc * 128:(oc + 1) * 128], rhs=odd[:, c, 0:n], start=(c == 0), stop=(c == 1))
                    return ins
                S.add("pe", mmo2, reads=[("odd", 0), ("odd", 1), "wod"], writes=[PK(po)])
                if "D" in DEBUG_PARTS:
                    resid_add(bl, l, 1, tb, oc, po, n)

        def record(fn):
            items = []
            S.add = lambda *a_, **k_: items.append((a_, k_))
            try:
                fn()
            finally:
                del S.add
            return items

        def merge_emit(la, lb):
            i = j = 0
            na, nb_ = len(la), len(lb)
            while i < na or j < nb_:
                if j >= nb_ or (i < na and i * nb_ <= j * na):
                    S.add(*la[i][0], **la[i][1])
                    i += 1
                else:
                    S.add(*lb[j][0], **lb[j][1])
                    j += 1

        def do2a(i_):
            if i_ + 1 < len(blocks2):
                pass2a_pre(blocks2[i_ + 1])
            pass2a(blocks2[i_])

        pass2a_pre(blocks2[0])
        NB2 = len(blocks2)
        for i_ in range(NB2 + 2):
            ia, ib = i_, i_ - 2
            la = record(lambda: do2a(ia)) if ia < NB2 else []
            lb = record(lambda: pass2b(blocks2[ib])) if 0 <= ib < NB2 else []
            merge_emit(la, lb)
        S.barrier()
        A.release(m2)

    done = False
    for bl in range(nb):
        for tb in range(4):
            t0_, n_ = TBS[tb]
            dma_sp(hT[:, :, t0_:t0_ + n_], xT[bl][:, t0_:t0_ + n_].rearrange("(c p) t -> p c t", p=128), [], [("h", c, tb) for c in range(8)])
        dma_sp(hT[:, :, NLAT:NTOK], ctxT[bl].rearrange("(c p) t -> p c t", p=128), [], [("h", c, 4) for c in range(8)])
        for l in range(depth):
            last = (l == DEPTH - 1)
            ffn(bl, l, 0, [0, 1, 2, 3, 4], ada_next=(l + 1 if (bl == 0 and l + 1 < depth) else None))
            if stop == f"ffn1_{l}":
                dump_and_end(bl)
                done = True
                break
            mixer(bl, l)
            if stop == f"mix_{l}":
                dump_and_end(bl)
                done = True
                break
            ffn(bl, l, 1, [0, 1, 2, 3] if last else [0, 1, 2, 3, 4])
            if stop == f"ffn2_{l}":
                dump_and_end(bl)
                done = True
                break
        if done:
            continue
        mf = A.mark()
        ob = [A.t(f"ob{i}", [128, 8, 512], F32) for i in range(2)]
        for tb in range(4):
            t0, n = TBS[tb]
            o = ob[tb % 2]
            make_xn(bl, 0, 0, tb, lambda c, o=o: o[:, c, :], lambda c, tb=tb: ("ob", tb % 2, c), final=True)
            dma_sp(outT[bl][:, t0:t0 + n].rearrange("(c p) t -> p c t", p=128), o[:], [("ob", tb % 2, c) for c in range(8)], [("ob", tb % 2, c) for c in range(8)] + ["outT"])
        S.barrier()
        A.release(mf)
    S.add("sp", lambda e: None, reads=["outT" if stop is None else "dbg"])
    S.emit()
    return nc


def _rope_tables():
    t = np.arange(NLAT)
    row = (t // 64).astype(np.float32)
    col = (t % 64).astype(np.float32)
    out = np.zeros((4, 128, NLAT), np.float32)
    for ti, hd in ((0, 64), (2, 32)):
        quarter = hd // 4
        inv = (np.float32(10000.0) ** (-np.arange(quarter, dtype=np.float32) / np.float32(quarter))).astype(np.float32)
        ang = np.concatenate([row[:, None] * inv[None, :], col[:, None] * inv[None, :]], axis=-1).astype(np.float32)
        cos = np.cos(ang).astype(np.float32)
        sin = np.sin(ang).astype(np.float32)
        half = hd // 2
        for p in range(128):
            d = p % hd
            j = d % half
            out[ti, p] = cos[:, j]
            out[ti + 1, p] = (-sin[:, j]) if d < half else sin[:, j]
    return out


def _const_mats():
    m = np.zeros((6, 128, 128), np.float32)
    m[0] = 1.0 / 1024.0
    m[1, 0:64, 0:64] = 1.0 / 64.0
    m[1, 64:128, 64:128] = 1.0 / 64.0
    m[2] = 1.0 / 256.0
    for mm_ in range(128):
        pa = mm_ + 32 if (mm_ % 64) < 32 else mm_ - 32
        m[3, pa, mm_] = 1.0
        pc = mm_ + 16 if (mm_ % 32) < 16 else mm_ - 16
        m[4, pc, mm_] = 1.0
    m[5] = np.eye(128, dtype=np.float32)
    return m


def _col_perm():
    qa = lambda h: list(range(h * 64, (h + 1) * 64))
    perm = qa(0) + qa(2) + qa(1) + qa(3)
    perm += list(range(256, 512))
    perm += list(range(512, 640))
    perm += list(range(768, 1024))
    perm += list(range(640, 768))
    perm += list(range(1024, 1280))
    perm += list(range(1280, 2304))
    return np.array(perm)


def prep_shared(inp):
    f = lambda a: np.ascontiguousarray(np.asarray(a, dtype=np.float32))
    sh = {}
    sh["w_ada"] = f(inp["w_ada"])
    sh["b_adaT"] = f(np.asarray(inp["b_ada"]).reshape(DEPTH, 72, 128).transpose(2, 0, 1))
    sh["g_normT"] = f(np.asarray(inp["g_norm"]).reshape(DEPTH, 3, 8, 128).transpose(3, 0, 1, 2))
    sh["g_finalT"] = f(np.asarray(inp["g_final"]).reshape(8, 128).T)
    for k in ("w_ff1_in", "w_ff1_out", "w_ff2_in", "w_ff2_out", "w_out"):
        sh[k] = f(inp[k])
    sh["w_in_p"] = f(np.asarray(inp["w_in"])[:, :, _col_perm()])
    gq = np.asarray(inp["g_q_a"])
    gk = np.asarray(inp["g_k_a"])
    gqk = np.stack([np.tile(gq, (1, 2)), np.tile(gk, (1, 2))], axis=-1)
    sh["gqk"] = f(gqk.transpose(1, 0, 2))
    sh["lamw"] = f(np.broadcast_to(np.asarray(inp["lam_c"])[None], (64, DEPTH, 4, 32)))
    sh["gsub"] = f(np.asarray(inp["g_sub_c"]).T)
    sh["gv_bc"] = f(np.broadcast_to(np.asarray(inp["g_v_b"])[None], (128, DEPTH, 256)))
    sh["wsT"] = f(np.asarray(inp["w_s_b"]).transpose(0, 3, 1, 2))
    bs = np.asarray(inp["b_s_b"])
    sh["bs_tbl"] = f(np.broadcast_to(np.tile(bs, (1, 1, 4))[None], (64, DEPTH, 4, 512)))
    sh["wdw"] = f(np.asarray(inp["w_dw_d"]).reshape(DEPTH, 31, 2, 128).transpose(3, 0, 2, 1))
    sh["bdw"] = f(np.asarray(inp["b_dw_d"]).reshape(DEPTH, 2, 128).transpose(2, 0, 1))
    sh["gconv"] = f(np.asarray(inp["g_conv_d"]).reshape(DEPTH, 2, 128).transpose(2, 0, 1))
    sh["rope"] = _rope_tables()
    sh["cmat"] = _const_mats()
    return sh


def prep_core(inp, bids):
    x = np.asarray(inp["x"])
    ctx = np.asarray(inp["ctx"])
    c = np.asarray(inp["c"])
    cc = np.asarray(inp["c_ctx"])
    d = {}
    d["xT"] = np.ascontiguousarray(np.stack([x[b].T for b in bids]).astype(np.float32))
    d["ctxT"] = np.ascontiguousarray(np.stack([ctx[b].T for b in bids]).astype(np.float32))
    vecs = [c[b] for b in bids]
    while len(vecs) < 2:
        vecs.append(c[bids[0]])
    vecs.append(cc)
    cT = np.stack(vecs, axis=-1).reshape(8, 128, 3).transpose(1, 0, 2)
    d["cT"] = np.ascontiguousarray(cT.astype(np.float32))
    return d


_NC_CACHE = {}


def kernel(**inputs):
    B = np.asarray(inputs["x"]).shape[0]
    nb = B // NCORES
    if "prog" not in _NC_CACHE:
        _NC_CACHE["prog"] = build_program(nb=nb)
    nc = _NC_CACHE["prog"]
    sh = prep_shared(inputs)
    in_maps = []
    for i in range(NCORES):
        d = dict(sh)
        d.update(prep_core(inputs, list(range(i * nb, (i + 1) * nb))))
        in_maps.append(d)
    res = run_bass_kernel_spmd(nc, in_maps, core_ids=list(range(NCORES)))
    out = np.empty((B, NLAT, D), np.float32)
    for i in range(NCORES):
        o = res.results[i]["outT"]
        for jb in range(nb):
            out[i * nb + jb] = o[jb].T
    return out
```
